# Optimizing a Trainium2 kernel written in Bass

```python
import jax, jax.numpy as jnp
from jax import lax
import numpy as np

D_MODEL = 1024
BATCH = 8
SEQ = 2048
DEPTH = 1
DEC_BATCH = 128
DEC_SEQ = 1
PAST_LEN = 2048
PAGE_SIZE = 128

HEAD_DIM = 64
N_HEADS = 8
N_KV = 2
GQA = N_HEADS // N_KV
WIDTH_A = N_HEADS * HEAD_DIM
KV_WIDTH = N_KV * HEAD_DIM
L_CMP = 32
L_SLC = 64
N_SEL = 8
WINDOW = 512
Q_BLOCK = 128
FORCE_BONUS = 1.0e4
ROPE_THETA = 10000.0
CHUNK = 128
N_GROUPS_B = 4
WIDTH_B = 512
GROUP_W_B = WIDTH_B // N_GROUPS_B
D_IN = WIDTH_A + 6 * KV_WIDTH + 3 * N_HEADS + WIDTH_A + 3 * WIDTH_B + 2 * D_MODEL
EPS = 1e-6
NEG = -1e30

kernel_name = 'nsa_gmlp_hybrid_step'


def rms_norm(x, g):
    xf = x.astype(jnp.float32)
    y = xf * lax.rsqrt(jnp.mean(xf * xf, axis=-1, keepdims=True) + EPS)
    return (y * g.astype(jnp.float32)).astype(x.dtype)


def layer_norm(x, g, b):
    xf = x.astype(jnp.float32)
    mu = jnp.mean(xf, axis=-1, keepdims=True)
    var = jnp.mean(jnp.square(xf - mu), axis=-1, keepdims=True)
    y = (xf - mu) * lax.rsqrt(var + EPS) * g.astype(jnp.float32) + b.astype(jnp.float32)
    return y.astype(x.dtype)


def rope(x, pos):
    half = HEAD_DIM // 2
    inv = ROPE_THETA ** (-jnp.arange(half, dtype=jnp.float32) * 2.0 / HEAD_DIM)
    ang = pos.astype(jnp.float32)[:, None] * inv[None, :]
    shape = (pos.shape[0],) + (1,) * (x.ndim - 3) + (half,)
    cos, sin = jnp.cos(ang).reshape(shape), jnp.sin(ang).reshape(shape)
    xf = x.astype(jnp.float32)
    x1, x2 = xf[..., :half], xf[..., half:]
    return jnp.concatenate([x1 * cos - x2 * sin, x2 * cos + x1 * sin], axis=-1).astype(x.dtype)


def compress(rows, pos_emb, w):
    B, T = rows.shape[:2]
    blocks = rows.reshape(B, T // L_CMP, L_CMP, N_KV, HEAD_DIM)
    pooled = jnp.mean(blocks + pos_emb[:, None, :], axis=2)
    return pooled @ w


def nsa_core(q, qpos, kc, vc, ks, vs, kw, vw, kwpos):
    B, Tq = q.shape[:2]
    dt = q.dtype
    scale = HEAD_DIM ** -0.5
    qg = q.reshape(B, Tq, N_KV, GQA, HEAD_DIM)
    NC = kc.shape[1]
    cend = (jnp.arange(NC, dtype=jnp.int32) + 1) * L_CMP - 1
    mc = (cend[None, :] <= qpos[:, None])[None, :, None, None, :]
    s_c = jnp.einsum('bqgrd,bcgd->bqgrc', qg, kc).astype(jnp.float32) * scale
    p_c = jax.nn.softmax(jnp.where(mc, s_c, NEG), axis=-1) * mc
    o_c = jnp.einsum('bqgrc,bcgd->bqgrd', p_c.astype(dt), vc)
    NS = ks.shape[1] // L_SLC
    ratio = L_SLC // L_CMP
    imp = p_c.sum(axis=3).reshape(B, Tq, N_KV, NS, ratio).sum(axis=-1)
    blk = jnp.arange(NS, dtype=jnp.int32)
    qblk = qpos // L_SLC
    forced = ((blk[None, :] == 0) | (blk[None, :] == qblk[:, None])).astype(jnp.float32)
    causal_blk = blk[None, :] <= qblk[:, None]
    imp = jnp.where(causal_blk[None, :, None, :], imp + FORCE_BONUS * forced[None, :, None, :], NEG)
    n_sel = min(N_SEL, NS)
    _, idx = lax.top_k(imp, n_sel)
    ksb = ks.reshape(B, NS, L_SLC, N_KV, HEAD_DIM).transpose(0, 3, 1, 2, 4)
    vsb = vs.reshape(B, NS, L_SLC, N_KV, HEAD_DIM).transpose(0, 3, 1, 2, 4)
    idx_t = idx.transpose(0, 2, 1, 3)
    gather = jax.vmap(jax.vmap(lambda blocks, ids: blocks[ids]))
    k_sel = gather(ksb, idx_t)
    v_sel = gather(vsb, idx_t)
    s_s = jnp.einsum('bqgrd,bgqnld->bqgrnl', qg, k_sel).astype(jnp.float32) * scale
    kpos_s = idx_t[..., None] * L_SLC + jnp.arange(L_SLC, dtype=jnp.int32)
    m_s = (kpos_s <= qpos[None, None, :, None, None]).transpose(0, 2, 1, 3, 4)[:, :, :, None]
    s_s = jnp.where(m_s, s_s, NEG).reshape(B, Tq, N_KV, GQA, n_sel * L_SLC)
    p_s = jax.nn.softmax(s_s, axis=-1).reshape(B, Tq, N_KV, GQA, n_sel, L_SLC)
    o_s = jnp.einsum('bqgrnl,bgqnld->bqgrd', p_s.astype(dt), v_sel)
    m_w = ((kwpos[None, :] <= qpos[:, None]) & (kwpos[None, :] > qpos[:, None] - WINDOW)
           & (kwpos[None, :] >= 0))[None, :, None, None, :]
    s_w = jnp.einsum('bqgrd,bkgd->bqgrk', qg, kw).astype(jnp.float32) * scale
    p_w = jax.nn.softmax(jnp.where(m_w, s_w, NEG), axis=-1)
    o_w = jnp.einsum('bqgrk,bkgd->bqgrd', p_w.astype(dt), vw)
    shp = (B, Tq, N_HEADS, HEAD_DIM)
    return (o_c.reshape(shp), o_s.reshape(shp), o_w.reshape(shp))


def spatial_mix(vn, w_s, b_s):
    B, T, _ = vn.shape
    Lc = min(CHUNK, T)
    vc = vn.reshape(B, T // Lc, Lc, N_GROUPS_B, GROUP_W_B)
    ws = jnp.tril(w_s[:, :Lc, :Lc])
    s = jnp.einsum('gij,bcjgd->bcigd', ws, vc) + b_s[:, :Lc].T[None, None, :, :, None]
    return s.reshape(B, T, WIDTH_B)


def mixer_inputs(x, c, pos, w_ada, b_ada, norm_g, w_in, q_norm_g, k_norm_g, vnorm_g, vnorm_b):
    B, T, _ = x.shape
    mod = c @ w_ada + b_ada
    shift, scale, gate = jnp.split(mod, 3, axis=-1)
    h = rms_norm(x, norm_g) * (1.0 + scale[:, None]) + shift[:, None]
    proj = h @ w_in
    sizes = (WIDTH_A, 6 * KV_WIDTH, 3 * N_HEADS, WIDTH_A, WIDTH_B, WIDTH_B, WIDTH_B, D_MODEL, D_MODEL)
    cuts = [int(v) for v in np.cumsum(sizes)[:-1]]
    q, kv, nsa_g, z_a, u, v, z_b, g_a, g_b = jnp.split(proj, cuts, axis=-1)
    q = rope(rms_norm(q.reshape(B, T, N_HEADS, HEAD_DIM), q_norm_g), pos)
    kv = kv.reshape(B, T, 2, 3, N_KV, HEAD_DIM)
    keys = rope(rms_norm(kv[:, :, 0], k_norm_g), pos)
    vals = kv[:, :, 1]
    vn = layer_norm(v, vnorm_g, vnorm_b)
    return gate, q, keys, vals, nsa_g, z_a, u, vn, z_b, g_a, g_b


def mixer_output(x, gate, o_c, o_s, o_w, nsa_g, z_a, u, s_b, z_b, g_a, g_b, w_br_a, w_br_b, w_out):
    B, T, _ = x.shape
    gw = jax.nn.sigmoid(nsa_g).reshape(B, T, N_HEADS, 3, 1)
    o_a = gw[:, :, :, 0] * o_c + gw[:, :, :, 1] * o_s + gw[:, :, :, 2] * o_w
    a = (o_a.reshape(B, T, WIDTH_A) * jax.nn.silu(z_a)) @ w_br_a
    b = (u * s_b * jax.nn.silu(z_b)) @ w_br_b
    m = jax.nn.sigmoid(g_a) * a + jax.nn.sigmoid(g_b) * b
    return x + gate[:, None] * (m @ w_out)


def prompt_layer(x, c, lw):
    (w_ada, b_ada, norm_g, w_in, q_norm_g, k_norm_g, pe_k, pe_v, w_ck, w_cv,
     vnorm_g, vnorm_b, w_s, b_s, w_br_a, w_br_b, w_out) = lw
    B, S, _ = x.shape
    pos = jnp.arange(S, dtype=jnp.int32)
    gate, q, keys, vals, nsa_g, z_a, u, vn, z_b, g_a, g_b = mixer_inputs(
        x, c, pos, w_ada, b_ada, norm_g, w_in, q_norm_g, k_norm_g, vnorm_g, vnorm_b)
    k_cmp, k_slc, k_win = keys[:, :, 0], keys[:, :, 1], keys[:, :, 2]
    v_cmp, v_slc, v_win = vals[:, :, 0], vals[:, :, 1], vals[:, :, 2]
    kc = compress(k_cmp, pe_k, w_ck)
    vc = compress(v_cmp, pe_v, w_cv)
    pad = ((0, 0), (WINDOW, 0), (0, 0), (0, 0))
    kw_pad, vw_pad = jnp.pad(k_win, pad), jnp.pad(v_win, pad)
    band = WINDOW + Q_BLOCK

    def block(i):
        start = i * Q_BLOCK
        qb = lax.dynamic_slice_in_dim(q, start, Q_BLOCK, axis=1)
        qpos = start + jnp.arange(Q_BLOCK, dtype=jnp.int32)
        kwb = lax.dynamic_slice_in_dim(kw_pad, start, band, axis=1)
        vwb = lax.dynamic_slice_in_dim(vw_pad, start, band, axis=1)
        kwpos = start - WINDOW + jnp.arange(band, dtype=jnp.int32)
        return nsa_core(qb, qpos, kc, vc, k_slc, v_slc, kwb, vwb, kwpos)

    outs = lax.map(block, jnp.arange(S // Q_BLOCK, dtype=jnp.int32))
    o_c, o_s, o_w = [o.transpose(1, 0, 2, 3, 4).reshape(B, S, N_HEADS, HEAD_DIM) for o in outs]
    s_b = spatial_mix(vn, w_s, b_s)
    y = mixer_output(x, gate, o_c, o_s, o_w, nsa_g, z_a, u, s_b, z_b, g_a, g_b, w_br_a, w_br_b, w_out)
    wb = min(WINDOW, S)
    return y, (k_cmp, v_cmp, k_slc, v_slc, k_win[:, S - wb:], v_win[:, S - wb:])


def sample_layer(x, c, caches, page_table, lw):
    (w_ada, b_ada, norm_g, w_in, q_norm_g, k_norm_g, pe_k, pe_v, w_ck, w_cv,
     vnorm_g, vnorm_b, w_s, b_s, w_br_a, w_br_b, w_out) = lw
    cache_k_cmp, cache_v_cmp, cache_k_slc, cache_v_slc, cache_k_win, cache_v_win = caches
    B, T_new, _ = x.shape
    past = page_table.shape[1] * PAGE_SIZE
    pos = past + jnp.arange(T_new, dtype=jnp.int32)
    gate, q, keys, vals, nsa_g, z_a, u, vn, z_b, g_a, g_b = mixer_inputs(
        x, c, pos, w_ada, b_ada, norm_g, w_in, q_norm_g, k_norm_g, vnorm_g, vnorm_b)
    k_cmp, k_slc, k_win = keys[:, :, 0], keys[:, :, 1], keys[:, :, 2]
    v_cmp, v_slc, v_win = vals[:, :, 0], vals[:, :, 1], vals[:, :, 2]
    total = past + T_new
    t_pad = -(-total // L_SLC) * L_SLC

    def full_rows(pool, new):
        rows = pool[page_table].reshape(B, past, N_KV, HEAD_DIM).astype(new.dtype)
        tail = jnp.zeros((B, t_pad - total, N_KV, HEAD_DIM), new.dtype)
        return jnp.concatenate([rows, new, tail], axis=1)

    kc = compress(full_rows(cache_k_cmp, k_cmp), pe_k, w_ck)
    vc = compress(full_rows(cache_v_cmp, v_cmp), pe_v, w_cv)
    ks_full = full_rows(cache_k_slc, k_slc)
    vs_full = full_rows(cache_v_slc, v_slc)
    wb = cache_k_win.shape[1]
    kw = jnp.concatenate([cache_k_win.astype(k_win.dtype), k_win], axis=1)
    vw = jnp.concatenate([cache_v_win.astype(v_win.dtype), v_win], axis=1)
    kwpos = past - wb + jnp.arange(wb + T_new, dtype=jnp.int32)
    o_c, o_s, o_w = nsa_core(q, pos, kc, vc, ks_full, vs_full, kw, vw, kwpos)
    s_b = spatial_mix(vn, w_s, b_s)
    y = mixer_output(x, gate, o_c, o_s, o_w, nsa_g, z_a, u, s_b, z_b, g_a, g_b, w_br_a, w_br_b, w_out)
    return y, (k_cmp, v_cmp, k_slc, v_slc, kw[:, T_new:], vw[:, T_new:], vn)


def setup_inputs(seed: int = 0) -> dict:
    key = jax.random.key(seed)
    ks = jax.random.split(key, 28)
    n_pages = PAST_LEN // PAGE_SIZE
    n_pool = (DEC_BATCH * n_pages * 5) // 4
    wb = min(WINDOW, PAST_LEN)

    def nrm(k, shape, s=1.0):
        return jax.random.normal(k, shape, jnp.float32) * s

    page_table = jax.random.permutation(ks[0], n_pool)[: DEC_BATCH * n_pages]
    page_table = page_table.reshape(DEC_BATCH, n_pages).astype(jnp.int32)
    paged = (DEPTH, n_pool, PAGE_SIZE, N_KV, HEAD_DIM)
    win = (DEPTH, DEC_BATCH, wb, N_KV, HEAD_DIM)
    return {
        'x_prompt': nrm(ks[1], (BATCH, SEQ, D_MODEL)),
        'x_sample': nrm(ks[2], (DEC_BATCH, DEC_SEQ, D_MODEL)),
        'cache_k_cmp': nrm(ks[3], paged),
        'cache_v_cmp': nrm(ks[4], paged),
        'cache_k_slc': nrm(ks[5], paged),
        'cache_v_slc': nrm(ks[6], paged),
        'cache_k_win': nrm(ks[7], win),
        'cache_v_win': nrm(ks[8], win),
        'page_table': page_table,
        'c_prompt': nrm(ks[9], (BATCH, D_MODEL)),
        'c_sample': nrm(ks[10], (DEC_BATCH, D_MODEL)),
        'w_ada': nrm(ks[11], (DEPTH, D_MODEL, 3 * D_MODEL), 0.5 * D_MODEL ** -0.5),
        'b_ada': nrm(ks[12], (DEPTH, 3 * D_MODEL), 0.02),
        'norm_g': 1.0 + nrm(ks[13], (DEPTH, D_MODEL), 0.02),
        'w_in': nrm(ks[14], (DEPTH, D_MODEL, D_IN), D_MODEL ** -0.5),
        'q_norm_g': 1.0 + nrm(ks[15], (DEPTH, HEAD_DIM), 0.02),
        'k_norm_g': 1.0 + nrm(ks[16], (DEPTH, HEAD_DIM), 0.02),
        'cmp_pos_k': nrm(ks[17], (DEPTH, L_CMP, HEAD_DIM), 0.1),
        'cmp_pos_v': nrm(ks[18], (DEPTH, L_CMP, HEAD_DIM), 0.1),
        'w_cmp_k': nrm(ks[19], (DEPTH, HEAD_DIM, HEAD_DIM), HEAD_DIM ** -0.5),
        'w_cmp_v': nrm(ks[20], (DEPTH, HEAD_DIM, HEAD_DIM), HEAD_DIM ** -0.5),
        'vnorm_g': 1.0 + nrm(ks[21], (DEPTH, WIDTH_B), 0.02),
        'vnorm_b': nrm(ks[22], (DEPTH, WIDTH_B), 0.02),
        'w_s': nrm(ks[23], (DEPTH, N_GROUPS_B, CHUNK, CHUNK), CHUNK ** -0.5),
        'b_s': 1.0 + nrm(ks[24], (DEPTH, N_GROUPS_B, CHUNK), 0.02),
        'w_br_a': nrm(ks[25], (DEPTH, WIDTH_A, D_MODEL), WIDTH_A ** -0.5),
        'w_br_b': nrm(ks[26], (DEPTH, WIDTH_B, D_MODEL), WIDTH_B ** -0.5),
        'w_out': nrm(ks[27], (DEPTH, D_MODEL, D_MODEL), D_MODEL ** -0.5),
    }


def reference(x_prompt, x_sample, cache_k_cmp, cache_v_cmp, cache_k_slc, cache_v_slc,
              cache_k_win, cache_v_win, page_table, c_prompt, c_sample,
              w_ada, b_ada, norm_g, w_in, q_norm_g, k_norm_g, cmp_pos_k, cmp_pos_v,
              w_cmp_k, w_cmp_v, vnorm_g, vnorm_b, w_s, b_s, w_br_a, w_br_b, w_out):
    xp, xs = x_prompt, x_sample
    p_states, s_states = [], []
    for l in range(DEPTH):
        lw = (w_ada[l], b_ada[l], norm_g[l], w_in[l], q_norm_g[l], k_norm_g[l], cmp_pos_k[l],
              cmp_pos_v[l], w_cmp_k[l], w_cmp_v[l], vnorm_g[l], vnorm_b[l], w_s[l], b_s[l],
              w_br_a[l], w_br_b[l], w_out[l])
        xp, st_p = prompt_layer(xp, c_prompt, lw)
        caches = (cache_k_cmp[l], cache_v_cmp[l], cache_k_slc[l], cache_v_slc[l],
                  cache_k_win[l], cache_v_win[l])
        xs, st_s = sample_layer(xs, c_sample, caches, page_table, lw)
        p_states.append(st_p)
        s_states.append(st_s)
    p_k_cmp, p_v_cmp, p_k_slc, p_v_slc, p_k_win, p_v_win = [jnp.stack(z) for z in zip(*p_states)]
    s_k_cmp, s_v_cmp, s_k_slc, s_v_slc, s_k_win, s_v_win, s_v_chunk = [jnp.stack(z) for z in zip(*s_states)]
    return (xp, xs, p_k_cmp, p_v_cmp, p_k_slc, p_v_slc, p_k_win, p_v_win,
            s_k_cmp, s_v_cmp, s_k_slc, s_v_slc, s_k_win, s_v_win, s_v_chunk)
```

```python
import contextlib
import numpy as np
import concourse.bass as bass
import concourse.mybir as mybir
from concourse.bass_utils import run_bass_kernel_spmd

F32 = mybir.dt.float32
BF16 = mybir.dt.bfloat16
I32 = mybir.dt.int32
ALU = mybir.AluOpType
AF = mybir.ActivationFunctionType
AX = mybir.AxisListType

D = 1024
DIN = 5400
S = 2048
NT = 16
NSMP = 16
EPS = 1e-6
SCL = 0.125
C_Q, C_K, C_V, C_NSA, C_ZA, C_U, C_VV, C_ZB, C_GA, C_GB = 0, 512, 896, 1280, 1304, 1816, 2328, 2840, 3352, 4376


class _Op:
    __slots__ = ("eng", "fn", "deps", "idx", "signal", "is_dma", "sem", "target", "count")


class Prog:
    ENGS = ("pe", "act", "dve", "pool", "sp")
    DMA_POOL = {"sp": 16, "pool": 12, "act": 2}

    def __init__(self, nc, tag):
        self.nc = nc
        self.tag = tag
        self.q = {e: [] for e in self.ENGS}
        self.last_w = {}
        self.readers = {}
        self.dma_n = {e: 0 for e in self.DMA_POOL}
        self.out_dmas = []

    def add(self, eng, fn, r=(), w=(), dma=False, out=False):
        op = _Op()
        op.eng, op.fn, op.is_dma, op.signal = eng, fn, dma, False
        op.deps = set()
        op.sem = None
        op.target = 0
        op.count = 0
        for b in r:
            lw = self.last_w.get(b)
            if lw is not None:
                op.deps.add(lw)
        for b in w:
            lw = self.last_w.get(b)
            if lw is not None:
                op.deps.add(lw)
            for rd in self.readers.get(b, ()):
                op.deps.add(rd)
        for b in r:
            self.readers.setdefault(b, []).append(op)
        for b in w:
            self.last_w[b] = op
            self.readers[b] = []
        op.deps.discard(op)
        op.idx = len(self.q[eng])
        self.q[eng].append(op)
        if dma:
            j = self.dma_n[eng]
            self.dma_n[eng] += 1
            op.sem = (eng, j % self.DMA_POOL[eng])
            op.target = 16 * (j // self.DMA_POOL[eng] + 1)
            if out:
                self.out_dmas.append(op)
        return op

    def _needs_wait(self, op, dep):
        if dep.is_dma:
            return True
        if dep.eng == "pe" and op.eng == "pe" and not op.is_dma:
            return False
        return True

    def emit(self, es):
        nc = self.nc
        fin = self.add("sp", None)
        for o in self.out_dmas:
            fin.deps.add(o)
        for e in self.DMA_POOL:
            for op in self.q[e]:
                if op.is_dma:
                    fin.deps.add(op)
        for e in self.ENGS:
            for op in self.q[e]:
                for d in op.deps:
                    if (not d.is_dma) and self._needs_wait(op, d):
                        d.signal = True
        for e in self.ENGS:
            c = 0
            for op in self.q[e]:
                if (not op.is_dma) and op.signal:
                    c += 1
                    op.count = c
        esem = {e: es.enter_context(nc.semaphore("s%s_%s" % (self.tag, e))) for e in ("pe", "act", "dve", "pool")}
        dsem = {}
        for e, n in self.DMA_POOL.items():
            for i in range(n):
                dsem[(e, i)] = es.enter_context(nc.semaphore("d%s_%s_%d" % (self.tag, e, i)))
        prog = self
        with nc.Block() as block:
            def run_queue(ename, eng):
                waited = {}
                for op in prog.q[ename]:
                    waits = {}
                    for d in op.deps:
                        if not prog._needs_wait(op, d):
                            continue
                        if d.is_dma:
                            key, val = ("d", d.sem), d.target
                        else:
                            key, val = ("e", d.eng), d.count
                        if waits.get(key, 0) < val:
                            waits[key] = val
                    if op.is_dma and op.target > 16:
                        key = ("d", op.sem)
                        if waits.get(key, 0) < op.target - 16:
                            waits[key] = op.target - 16
                    for key, val in waits.items():
                        if waited.get(key, 0) >= val:
                            continue
                        waited[key] = val
                        s = dsem[key[1]] if key[0] == "d" else esem[key[1]]
                        eng.wait_ge(s, val)
                    if op.fn is None:
                        continue
                    ins = op.fn(eng)
                    if op.is_dma:
                        ins.then_inc(dsem[op.sem], 16)
                    elif op.signal:
                        ins.then_inc(esem[ename], 1)

            @block.sync
            def _(eng):
                run_queue("sp", eng)

            @block.tensor
            def _(eng):
                run_queue("pe", eng)

            @block.scalar
            def _(eng):
                run_queue("act", eng)

            @block.vector
            def _(eng):
                run_queue("dve", eng)

            @block.gpsimd
            def _(eng):
                run_queue("pool", eng)


class K:
    def __init__(self, P):
        self.P = P

    def mm(self, out, lhsT, rhs, start=True, stop=True, r=(), w=()):
        self.P.add("pe", lambda e: e.matmul(out, lhsT=lhsT, rhs=rhs, start=start, stop=stop, skip_group_check=True), r, w)

    def tr(self, out, in_, ident, r=(), w=()):
        self.P.add("pe", lambda e: e.transpose(out=out, in_=in_, identity=ident), r, w)

    def act(self, out, in_, func, r=(), w=(), scale=1.0, accum=None):
        if accum is None:
            self.P.add("act", lambda e: e.activation(out=out, in_=in_, func=func, scale=scale), r, w)
        else:
            self.P.add("act", lambda e: e.activation(out=out, in_=in_, func=func, scale=scale, accum_out=accum), r, w)

    def tt(self, eng, out, in0, in1, op, r=(), w=()):
        self.P.add(eng, lambda e: e.tensor_tensor(out=out, in0=in0, in1=in1, op=op), r, w)

    def ts(self, eng, out, in0, s1, s2, op0, op1=None, r=(), w=()):
        if op1 is None:
            self.P.add(eng, lambda e: e.tensor_scalar(out=out, in0=in0, scalar1=s1, scalar2=None, op0=op0), r, w)
        else:
            self.P.add(eng, lambda e: e.tensor_scalar(out=out, in0=in0, scalar1=s1, scalar2=s2, op0=op0, op1=op1), r, w)

    def stt(self, out, in0, scalar, in1, op0, op1, r=(), w=()):
        self.P.add("dve", lambda e: e.scalar_tensor_tensor(out=out, in0=in0, scalar=scalar, in1=in1, op0=op0, op1=op1), r, w)

    def cp(self, eng, out, in_, r=(), w=()):
        if eng == "act":
            self.P.add("act", lambda e: e.activation(out=out, in_=in_, func=AF.Copy), r, w)
        else:
            self.P.add(eng, lambda e: e.tensor_copy(out=out, in_=in_), r, w)

    def red(self, out, in_, op, r=(), w=()):
        self.P.add("dve", lambda e: e.tensor_reduce(out=out, in_=in_, axis=AX.X, op=op), r, w)

    def memset(self, eng, ap, val, w=()):
        self.P.add(eng, lambda e: e.memset(ap, val), (), w)

    def dma(self, q, out, in_, r=(), w=(), is_out=False, slow=False):
        if slow:
            self.P.add(q, lambda e: e.dma_start(out=out, in_=in_, allow_slow_non_contiguous=True), r, w, dma=True, out=is_out)
        else:
            self.P.add(q, lambda e: e.dma_start(out=out, in_=in_), r, w, dma=True, out=is_out)

    def gather(self, out, in_, idx, r=(), w=()):
        self.P.add("pool", lambda e: e.indirect_dma_start(out=out, out_offset=None, in_=in_,
                                                          in_offset=bass.IndirectOffsetOnAxis(ap=idx, axis=0)),
                   r, w, dma=True)


def v3(ap, a):
    return ap.rearrange("p (a b) -> p a b", a=a)


def bc(ap, shape):
    return ap.to_broadcast(shape)


def win_names(c0, n):
    return ["win%d" % j for j in range(c0 // 512, (c0 + n - 1) // 512 + 1)]


def front(k, t, n, x_src, cs_name, sfx, is_prompt):
    P = k.P
    N = slice(0, n)
    x, B1, B2, B3, h, hT = t["x"], t["B1"], t["B2"], t["B3"], t["h"], t["hT"]
    TH, ZB, vn_bf = B1[:, 0:512], B1[:, 512:1024], h[:, 0:512]
    tg, za_s, bmix, gw = t["tg" + sfx], t["za_s" + sfx], t["bmix" + sfx], t["gw" + sfx]
    n_tg, n_za, n_bm, n_gw = "tg" + sfx, "za_s" + sfx, "bmix" + sfx, "gw" + sfx
    B1W = ["B1a", "B1b"]
    st, st2, st3 = t["st"], t["st2"], t["st3"]
    pA, pB, pT = t["pA"], t["pB"], t["pT"]
    pTv = pT[:].rearrange("p (a b) -> p a b", a=8)
    k.dma("sp", x[N, :], x_src, w=["x"])
    k.act(B1[N, :], x[N, :], AF.Square, r=["x"], w=B1W + ["st"], accum=st[N, 0:1])
    k.ts("dve", st[N, 1:2], st[N, 0:1], 1.0 / D, EPS, ALU.mult, ALU.add, r=["st"], w=["st"])
    k.tt("pool", st[N, 2:3], st[N, 1:2], t["nh"][N, 0:1], ALU.pow, r=["st", "nh"], w=["st"])
    k.stt(B1[N, :], x[N, :], st[N, 2:3], t["sc1"][N, :], ALU.mult, ALU.mult, r=["x", "st", "sc1"], w=B1W)
    k.tt("dve", h[N, :], B1[N, :], t["sh"][N, :], ALU.add, r=B1W + ["sh"], w=["h"])
    yield
    for kc in range(8):
        k.tr(pTv[:, kc, 0:n], h[N, kc * 128:(kc + 1) * 128], t["identb"][N, N], r=["h", "identb"], w=["pT"])
    k.cp("act", hT[:, :, 0:n], pTv[:, :, 0:n], r=["pT"], w=["hT"])
    yield

    def proj(bank, bname, dst, c0, ncol):
        for kc in range(8):
            k.mm(bank[N, dst:dst + ncol], hT[:, kc, 0:n], t["Win"][:, kc, c0:c0 + ncol], start=(kc == 0), stop=(kc == 7),
                 r=["hT"] + win_names(c0, ncol), w=[bname])

    def normrope(bank, bname, A, Bd, gain, dst_ap, dst_name):
        H = A * Bd
        HW = H * 64

        def v4(ap):
            return ap.rearrange("p (a b d) -> p a b d", a=A, b=Bd)

        k.cp("act", B2[N, 0:HW], bank[N, 0:HW], r=[bname], w=["B2"])
        k.act(B3[N, 0:HW], B2[N, 0:HW], AF.Square, r=["B2"], w=["B3"])
        k.red(st2[N, 0:H], v3(B3[N, 0:HW], H), ALU.add, r=["B3"], w=["st2"])
        k.ts("dve", st2[N, 8:8 + H], st2[N, 0:H], 1.0 / 64, EPS, ALU.mult, ALU.add, r=["st2"], w=["st2"])
        k.tt("pool", st2[N, 16:16 + H], st2[N, 8:8 + H], t["nh"][N, 0:H], ALU.pow, r=["st2", "nh"], w=["st2"])
        b2 = v4(B2[N, 0:HW])
        rs = st2[N, 16:16 + H].rearrange("p (a b) -> p a b", a=A).unsqueeze(3)
        k.tt("dve", b2, b2, bc(rs, [n, A, Bd, 64]), ALU.mult, r=["B2", "st2"], w=["B2"])
        k.tt("dve", b2, b2, bc(gain[N, :].unsqueeze(1).unsqueeze(1), [n, A, Bd, 64]), ALU.mult, r=["B2", "gains"], w=["B2"])
        cs = t["cs"]
        cosb = bc(cs[N, 0:32].unsqueeze(1).unsqueeze(1), [n, A, Bd, 32])
        sinb = bc(cs[N, 32:64].unsqueeze(1).unsqueeze(1), [n, A, Bd, 32])
        x1, x2 = b2[:, :, :, 0:32], b2[:, :, :, 32:64]
        r1 = t["R1"][N, 0:H * 32].rearrange("p (a b d) -> p a b d", a=A, b=Bd)
        r2 = t["R2"][N, 0:H * 32].rearrange("p (a b d) -> p a b d", a=A, b=Bd)
        k.tt("dve", r1, x1, cosb, ALU.mult, r=["B2", cs_name], w=["R1"])
        k.tt("dve", r2, x2, sinb, ALU.mult, r=["B2", cs_name], w=["R2"])
        k.tt("dve", dst_ap[:, :, :, 0:32], r1, r2, ALU.subtract, r=["R1", "R2"], w=[dst_name])
        k.tt("dve", r1, x2, cosb, ALU.mult, r=["B2", cs_name], w=["R1"])
        k.tt("dve", r2, x1, sinb, ALU.mult, r=["B2", cs_name], w=["R2"])
        k.tt("dve", dst_ap[:, :, :, 32:64], r1, r2, ALU.add, r=["R1", "R2"], w=[dst_name])

    proj(pA, "pA", 0, C_Q, 512)
    qdst = t["q_bf"][N, :].rearrange("p (r g d) -> p g r d", r=4, g=2)
    normrope(pA, "pA", 2, 4, t["qg"], qdst, "q_bf")
    yield
    proj(pB, "pB", 0, C_K, 384)
    normrope(pB, "pB", 6, 1, t["kg"], t["kf"][N, :].rearrange("p (a b d) -> p a b d", a=6, b=1), "kf")
    k.cp("act", t["k_bf"][N, :], t["kf"][N, :], r=["kf"], w=["k_bf"])
    yield
    proj(pA, "pA", 0, C_V, 384)
    proj(pA, "pA", 384, C_NSA, 24)
    k.cp("act", t["vf"][N, :], pA[N, 0:384], r=["pA"], w=["vf"])
    k.act(t["gwt"][N, :], pA[N, 384:408], AF.Tanh, r=["pA"], w=["gwt"], scale=0.5)
    k.ts("dve", gw[N, :], t["gwt"][N, :], 0.5, 0.5, ALU.mult, ALU.add, r=["gwt"], w=[n_gw])
    yield
    proj(pB, "pB", 0, C_ZA, 512)
    k.act(TH[N, :], pB[N, :], AF.Tanh, r=["pB"], w=["B1a"], scale=0.5)
    k.stt(za_s[N, :], TH[N, :], 1.0, pB[N, :], ALU.add, ALU.mult, r=["B1a", "pB"], w=[n_za])
    yield
    proj(pA, "pA", 0, C_VV, 512)
    k.cp("act", t["vn_f"][N, :], pA[N, :], r=["pA"], w=["vn_f"])
    P.add("dve", lambda e: e.bn_stats(out=st3[N, 0:6], in_=t["vn_f"][N, :]), ["vn_f"], ["st3"])
    P.add("dve", lambda e: e.bn_aggr(out=st3[N, 6:8], in_=st3[N, 0:6]), ["st3"], ["st3"])
    k.ts("dve", st3[N, 8:9], st3[N, 7:8], EPS, None, ALU.add, r=["st3"], w=["st3"])
    k.tt("pool", st3[N, 9:10], st3[N, 8:9], t["nh"][N, 0:1], ALU.pow, r=["st3", "nh"], w=["st3"])
    k.ts("dve", t["vn_f"][N, :], t["vn_f"][N, :], st3[N, 6:7], st3[N, 9:10], ALU.subtract, ALU.mult, r=["vn_f", "st3"], w=["vn_f"])
    k.tt("dve", t["vn_f"][N, :], t["vn_f"][N, :], t["vng"][N, :], ALU.mult, r=["vn_f", "gains"], w=["vn_f"])
    k.tt("dve", t["vn_f"][N, :], t["vn_f"][N, :], t["vnb"][N, :], ALU.add, r=["vn_f", "gains"], w=["vn_f"])
    yield
    proj(pB, "pB", 0, C_ZB, 512)
    k.act(TH[N, :], pB[N, :], AF.Tanh, r=["pB"], w=["B1a"], scale=0.5)
    k.stt(ZB[N, :], TH[N, :], 1.0, pB[N, :], ALU.add, ALU.mult, r=["B1a", "pB"], w=["B1b"])
    yield
    proj(pA, "pA", 0, C_U, 512)
    k.tt("dve", ZB[N, :], pA[N, :], ZB[N, :], ALU.mult, r=["pA", "B1b"], w=["B1b"])
    yield
    if is_prompt:
        k.cp("act", vn_bf[N, :], t["vn_f"][N, :], r=["vn_f"], w=["h"])
        for g in range(4):
            k.mm(pB[N, g * 128:(g + 1) * 128], t["wsT"][:, g, :], vn_bf[N, g * 128:(g + 1) * 128],
                 r=["wsT", "h"], w=["pB"])
        k.tt("dve", v3(B2[N, :], 4), v3(pB[N, :], 4), bc(t["bsT"][N, :].unsqueeze(2), [n, 4, 128]), ALU.add,
             r=["pB", "bsT"], w=["B2"])
    else:
        k.tt("dve", v3(B2[N, :], 4), v3(t["vn_f"][N, :], 4), bc(t["w00"][N, :].unsqueeze(2), [n, 4, 128]), ALU.mult,
             r=["vn_f", "w00"], w=["B2"])
        k.tt("dve", v3(B2[N, :], 4), v3(B2[N, :], 4), bc(t["b0"][N, :].unsqueeze(2), [n, 4, 128]), ALU.add,
             r=["B2", "w00"], w=["B2"])
    k.tt("dve", bmix[N, :], B2[N, :], ZB[N, :], ALU.mult, r=["B2", "B1b"], w=[n_bm])
    yield
    banks = [(pB, "pB"), (pA, "pA")]
    for j in range(4):
        bank, bname = banks[j % 2]
        proj(bank, bname, 0, C_GA + j * 512, 512)
        k.act(tg[N, j * 512:(j + 1) * 512], bank[N, :], AF.Tanh, r=[bname], w=[n_tg], scale=0.5)
        yield


def tail(k, t, n, y_dst, x_src, sfx, gate=None):
    N = slice(0, n)
    pA, pB, pT = t["pA"], t["pB"], t["pT"]
    pTv8 = pT[:].rearrange("p (a b) -> p a b", a=8)
    tg, za_s, bmix = t["tg" + sfx], t["za_s" + sfx], t["bmix" + sfx]
    n_tg, n_za, n_bm = "tg" + sfx, "za_s" + sfx, "bmix" + sfx
    mc = t["mc"]
    B1W = ["B1a", "B1b"]
    ozd = t["oz"][N, :].rearrange("p (r g d) -> p g r d", r=4, g=2)
    k.tt("dve", ozd, t["o_a"][N, :].rearrange("p (g r d) -> p g r d", g=2, r=4),
         za_s[N, :].rearrange("p (g r d) -> p g r d", g=2, r=4), ALU.mult, r=["o_a", n_za], w=["oz"])
    for r in range(4):
        k.tr(pTv8[:, r, 0:n], t["oz"][N, r * 128:(r + 1) * 128], t["identb"][N, N], r=["oz", "identb"], w=["pT"])
    for c in range(4):
        k.tr(pTv8[:, 4 + c, 0:n], bmix[N, c * 128:(c + 1) * 128], t["identb"][N, N], r=[n_bm, "identb"], w=["pT"])
    k.cp("act", t["ozT"][:, :, 0:n], pTv8[:, :, 0:n], r=["pT"], w=["ozT"])
    yield
    for half in range(2):
        cs = slice(half * 512, (half + 1) * 512)
        for r in range(4):
            k.mm(pA[N, :], t["ozT"][:, r, 0:n], t["WA"][:, r, cs], start=(r == 0), stop=(r == 3), r=["ozT", "WA"], w=["pA"])
        for c in range(4):
            k.mm(pB[N, :], t["ozT"][:, 4 + c, 0:n], t["WB"][:, c, cs], start=(c == 0), stop=(c == 3), r=["ozT", "WB"], w=["pB"])
        k.stt(t["B2"][N, :], tg[N, half * 512:(half + 1) * 512], 1.0, pA[N, :], ALU.add, ALU.mult,
              r=[n_tg, "pA"], w=["B2"])
        k.stt(t["B3"][N, :], tg[N, 1024 + half * 512:1024 + (half + 1) * 512], 1.0, pB[N, :], ALU.add, ALU.mult,
              r=[n_tg, "pB"], w=["B3"])
        k.tt("dve", mc[N, cs], t["B2"][N, :], t["B3"][N, :], ALU.add, r=["B2", "B3"], w=["mc"])
        yield
    for kc in range(8):
        k.tr(pTv8[:, kc, 0:n], mc[N, kc * 128:(kc + 1) * 128], t["identb"][N, N], r=["mc", "identb"], w=["pT"])
    k.cp("act", t["hT"][:, :, 0:n], pTv8[:, :, 0:n], r=["pT"], w=["hT"])
    yield
    banks = [(pA, "pA"), (pB, "pB")]
    for half in range(2):
        bank, bname = banks[half]
        cs = slice(half * 512, (half + 1) * 512)
        for kc in range(8):
            k.mm(bank[N, :], t["hT"][:, kc, 0:n], t["Wout"][:, kc, cs], start=(kc == 0), stop=(kc == 7),
                 r=["hT", "Wout"], w=[bname])
        if gate is None:
            gap, gname = t["gq"][N, cs], "gq"
        else:
            gap, gname = gate[half][0][N, 0:512], gate[half][1]
        k.tt("dve", t["B1"][N, cs], bank[N, :], gap, ALU.mult, r=[bname, gname], w=["B1a" if half == 0 else "B1b"])
        yield
    k.dma("sp", t["x"][N, :], x_src, w=["x"])
    k.tt("dve", t["x"][N, :], t["B1"][N, :], t["x"][N, :], ALU.add, r=B1W + ["x"], w=["x"])
    k.dma("sp", y_dst, t["x"][N, :], r=["x"], is_out=True)


def mod_pass(k, t, d, n, cT, cT_name, bcast):
    N = slice(0, n)
    stg = [t["G0"], t["G1"]]
    banks = [(t["pA"], "pA"), (t["pB"], "pB")]
    wada = d["w_ada"].rearrange("(kc p) n -> p kc n", p=128)
    for j in range(12):
        sg, sname = stg[j % 2], "G%d" % (j % 2)
        bank, bname = banks[j % 2]
        sgv = sg[:].rearrange("p (kc n) -> p kc n", kc=8)
        k.dma("sp", sgv, wada[:, :, j * 256:(j + 1) * 256], w=[sname])
        k.dma("sp", t["bada"][0:1, :], d["b_ada"][0:1, j * 256:(j + 1) * 256], w=["bada"])
        if not bcast:
            for kc in range(8):
                k.mm(bank[N, 0:256], cT[:, kc, 0:n], sgv[:, kc, :], start=(kc == 0), stop=False, r=[cT_name, sname], w=[bname])
            k.mm(bank[N, 0:256], t["ones0"][:, 0:n], t["bada"][:, :], start=False, stop=True, r=["ones0", "bada"], w=[bname])
            src = bank[N, 0:256]
        else:
            for kc in range(8):
                k.mm(bank[0:1, 0:256], cT[:, kc:kc + 1], sgv[:, kc, :], start=(kc == 0), stop=False, r=[cT_name, sname], w=[bname])
            k.mm(bank[0:1, 0:256], t["ones0"][:, 0:1], t["bada"][:, :], start=False, stop=True, r=["ones0", "bada"], w=[bname])
            k.cp("act", t["modrow"][0:1, :], bank[0:1, 0:256], r=[bname], w=["modrow"])
            k.mm(bank[N, 256:512], t["onesf"][0:1, 0:n], t["modrow"][0:1, :], r=["onesf", "modrow"], w=[bname])
            src = bank[N, 256:512]
        cs = slice((j % 4) * 256, (j % 4 + 1) * 256)
        if j < 4:
            k.cp("act", t["sh"][N, cs], src, r=[bname], w=["sh"])
        elif j < 8:
            k.stt(t["sc1"][N, cs], src, 1.0, t["normg"][N, cs], ALU.add, ALU.mult, r=[bname, "normg"], w=["sc1"])
        else:
            k.act(t["gq"][N, cs], src, AF.Copy, r=[bname], w=["gq"], scale=0.25)


def build_program():
    nc = bass.Bass("TRN2", target_bir_lowering=False)
    d = {}

    def din(name, shape, dt=F32):
        d[name] = nc.dram_tensor(name, shape, dt, kind="ExternalInput").ap()

    def dout(name, shape):
        d[name] = nc.dram_tensor(name, shape, F32, kind="ExternalOutput").ap()

    din("xp", [S, D]); din("xs", [NSMP, D]); din("cpv", [D]); din("csv", [NSMP, D])
    for nm in ("kcmp", "vcmp", "kslc", "vslc"):
        din(nm, [2560 * 8, 2048])
    din("kwin", [NSMP, 512, 128]); din("vwin", [NSMP, 512, 128])
    din("ptrep", [128, NSMP], I32)
    din("w_ada", [D, 3 * D]); din("b_ada", [1, 3 * D]); din("norm_g", [1, D]); din("w_in", [D, DIN])
    din("qg", [1, 64]); din("kg", [1, 64]); din("pek", [32, 64]); din("pev", [32, 64])
    din("wck", [64, 64]); din("wcv", [64, 64]); din("vng", [1, 512]); din("vnb", [1, 512])
    din("ws", [4, 128, 128]); din("bs", [4, 128]); din("wbra", [512, D]); din("wbrb", [512, D]); din("wout", [D, D])
    din("rope", [S + 1, 64]); din("tri_le", [128, 128]); din("tri_gt", [128, 128]); din("identf_c", [128, 128])
    din("pool4", [128, 4]); din("pair", [64, 32]); din("e2", [64, 2048]); din("cmpbase", [128, 512]); din("trineg_le", [128, 512]); din("trineg_gt", [128, 512])
    din("impbias", [NT, 128, 32]); din("pool2", [128, 64]); din("e4", [32, 128]); din("rmod8", [128, 1])
    d["gqs"] = nc.dram_tensor("gqs", [NSMP, D], F32, kind="Internal").ap()
    dout("yp", [S, D]); dout("ys", [NSMP, D])
    for nm in ("pk_cmp", "pv_cmp", "pk_slc", "pv_slc"):
        dout(nm, [S, 128])
    dout("pk_win", [512, 128]); dout("pv_win", [512, 128])
    for nm in ("sk_cmp", "sv_cmp", "sk_slc", "sv_slc"):
        dout(nm, [NSMP, 128])
    dout("sk_win", [NSMP, 512, 128]); dout("sv_win", [NSMP, 512, 128]); dout("svch", [NSMP, 512])

    with contextlib.ExitStack() as es:
        t = {}

        def sb(name, shape, dt, scope=es):
            t[name] = scope.enter_context(nc.sbuf_tensor("sb_" + name, shape, dt))
            return t[name]

        def ps(name, shape, dt, scope=es):
            t[name] = scope.enter_context(nc.psum_tensor("ps_" + name, shape, dt))
            return t[name]

        sb("Win", [128, 8, DIN], BF16)
        sb("sc1", [128, D], F32); sb("sh", [128, D], F32); sb("gq", [128, D], F32)
        sb("identb", [128, 128], BF16); sb("identf", [128, 128], F32); sb("tri_le", [128, 128], BF16); sb("tri_gt", [128, 128], BF16)
        sb("qg", [128, 64], F32); sb("kg", [128, 64], F32); sb("vng", [128, 512], F32); sb("vnb", [128, 512], F32)
        sb("nh", [128, 8], F32); sb("onesf", [128, 128], F32)
        sb("wsT", [128, 4, 128], BF16); sb("bsT", [128, 4], F32)
        sb("Wbdk", [128, 128], BF16); sb("Wbdv", [128, 128], BF16); sb("pebar", [128, 2], F32)
        sb("pool4", [128, 4], BF16); sb("e2", [128, 2048], BF16)
        sb("x", [128, D], F32); sb("B1", [128, D], F32); sb("B2", [128, 512], F32); sb("B3", [128, 512], F32)
        sb("h", [128, D], BF16); sb("hT", [128, 8, 128], BF16); sb("R1", [128, 256], F32); sb("R2", [128, 256], F32)
        sb("st", [128, 4], F32); sb("st2", [128, 24], F32); sb("st3", [128, 12], F32)
        sb("q_bf", [128, 512], BF16); sb("kf", [128, 384], F32); sb("vf", [128, 384], F32); sb("k_bf", [128, 384], BF16)
        sb("gwt", [128, 24], F32); sb("gw0", [128, 24], F32); sb("gw1", [128, 24], F32); sb("za_s0", [128, 512], BF16); sb("za_s1", [128, 512], BF16)
        sb("vn_f", [128, 512], F32); sb("tg0", [128, 2048], BF16); sb("tg1", [128, 2048], BF16)
        sb("bmix0", [128, 512], BF16); sb("bmix1", [128, 512], BF16); sb("o_a", [128, 512], F32); sb("oz", [128, 512], BF16); sb("ozT", [128, 8, 128], BF16)
        sb("cs", [128, 64], F32); sb("mc", [128, D], BF16)
        sb("rden", [128, 8], F32); sb("cx", [128, 8], F32); t["tmpO"] = t["B3"]
        ps("pA", [128, 512], F32); ps("pB", [128, 512], F32); ps("pT", [128, 1024], BF16)
        ps("pM", [128, 512], F32); ps("pS0", [128, 512], F32); ps("pS1", [128, 512], F32)
        ps("pO0", [128, 512], F32); ps("pO1", [128, 512], F32)

        with contextlib.ExitStack() as s1:
            P = Prog(nc, "a")
            k = K(P)
            sb("normg", [128, D], F32, s1)
            sb("G0", [128, 2048], F32, s1); sb("G1", [128, 2048], F32, s1)
            sb("Gw0", [128, 512], F32, s1); sb("Gw1", [128, 512], F32, s1)
            sb("Gb0", [128, 2048], BF16, s1); sb("Gb1", [128, 2048], BF16, s1); sb("pool2b", [128, 64], BF16, s1); sb("Gwb0", [128, 512], BF16, s1); sb("Gwb1", [128, 512], BF16, s1)
            sb("PTb", [128, 160], BF16, s1); sb("vfb", [128, 256], BF16, s1); sb("Enewb", [128, 2 * NSMP * 8], BF16, s1)
            sb("bada", [128, 256], F32, s1); sb("modrow", [1, 256], F32, s1); sb("ones0", [128, 128], F32, s1)
            sb("cTs", [128, 8, NSMP], F32, s1)
            sb("cp8", [128, 8], F32, s1)
            sb("w00", [128, 4], F32, s1); sb("b0", [128, 4], F32, s1)
            sb("pe2", [32, 128], F32, s1); sb("o32", [32, 2], F32, s1)
            sb("ptrep", [128, NSMP], I32, s1); sb("rmod8", [128, 1], F32, s1); sb("idx", [128, NSMP], I32, s1)
            sb("pool2", [128, 64], F32, s1); sb("pairf", [64, 32], F32, s1); sb("e4", [32, 128], BF16, s1)
            sb("QTs", [128, 2, 4 * NSMP], BF16, s1); sb("enr", [NSMP, 16], F32, s1); sb("en", [NSMP, 16], F32, s1); sb("Enew", [128, 2 * NSMP * 8], F32, s1)
            sb("pTk", [128, 64], BF16, s1); sb("pTv", [128, 64], BF16, s1); sb("KcTs", [128, 64], BF16, s1); sb("Vcs", [64, 128], F32, s1)
            sb("PcT", [64, NSMP * 8], F32, s1); sb("KTs", [128, 20, 128], BF16, s1)
            sb("PTs", [128, 160], F32, s1); sb("PTsum", [128, 16], F32, s1)
            sb("rDc", [32, 128], F32, s1); sb("impn", [32, 128], F32, s1); sb("impT", [32, 32], F32, s1)
            sb("impS", [32, 32], F32, s1); sb("bias0", [32, 32], F32, s1); sb("m8s", [32, 8], F32, s1)
            sb("selS", [32, 32], BF16, s1); sb("selTs", [32, 32], BF16, s1); sb("Msk", [128, 32], F32, s1)
            t["OTn"] = t["B1"][:, 0:384]; t["rDall"] = t["B1"][:, 384:768]; t["OT1"] = t["B2"][0:64, 0:384]; t["wsl"] = t["B3"][:, 0:128]

            k.dma("pool", t["identb"][:], d["identf_c"], w=["identb"])
            k.dma("sp", t["identf"][:], d["identf_c"], w=["identf"])
            k.dma("pool", t["tri_le"][:], d["tri_le"], w=["tri_le"])
            k.dma("pool", t["tri_gt"][:], d["tri_gt"], w=["tri_gt"])
            k.dma("pool", t["pool4"][:], d["pool4"], w=["pool4"])
            k.memset("pool", t["e2"][64:128, :], 0.0, w=["e2"])
            k.dma("pool", t["e2"][0:64, :], d["e2"], w=["e2"])
            k.dma("pool", t["e4"][:], d["e4"], w=["e4"])
            k.dma("sp", t["pool2"][:], d["pool2"], w=["pool2"])
            k.dma("pool", t["pool2b"][:], d["pool2"], w=["pool2b"])
            k.dma("sp", t["pairf"][:], d["pair"], w=["pairf"])
            k.dma("sp", t["rmod8"][:], d["rmod8"], w=["rmod8"])
            k.dma("sp", t["ptrep"][:], d["ptrep"], w=["ptrep"])
            k.dma("sp", t["qg"][:], d["qg"].partition_broadcast(128), w=["gains"])
            k.dma("sp", t["kg"][:], d["kg"].partition_broadcast(128), w=["gains"])
            k.dma("sp", t["vng"][:], d["vng"].partition_broadcast(128), w=["gains"])
            k.dma("sp", t["vnb"][:], d["vnb"].partition_broadcast(128), w=["gains"])
            k.dma("sp", t["normg"][:], d["norm_g"].partition_broadcast(128), w=["normg"])
            k.dma("sp", t["bsT"][:], d["bs"].rearrange("g i -> i g"), w=["bsT"], slow=True)
            k.dma("sp", t["w00"][0:NSMP, :], d["ws"][:, 0, 0:1].rearrange("g o -> o g").partition_broadcast(NSMP), w=["w00"], slow=True)
            k.dma("sp", t["b0"][0:NSMP, :], d["bs"][:, 0:1].rearrange("g o -> o g").partition_broadcast(NSMP), w=["w00"], slow=True)
            k.memset("pool", t["nh"][:], -0.5, w=["nh"])
            k.memset("pool", t["Enew"][:], 0.0, w=["Enew"])
            k.memset("pool", t["bada"][:], 0.0, w=["bada"])
            k.memset("pool", t["ones0"][:], 0.0, w=["ones0"])
            k.memset("pool", t["ones0"][0:1, :], 1.0, w=["ones0"])
            k.memset("pool", t["QTs"][:], 0.0, w=["QTs"])
            k.memset("pool", t["vf"][:], 0.0, w=["vf"])
            k.memset("pool", t["Enewb"][:], 0.0, w=["Enewb"])
            k.memset("pool", t["onesf"][:], 1.0, w=["onesf"])
            k.memset("pool", t["o32"][:], 1.0 / 32, w=["o32"])
            k.memset("pool", t["Wbdk"][:], 0.0, w=["Wbdk"])
            k.memset("pool", t["Wbdv"][:], 0.0, w=["Wbdv"])
            k.memset("pool", t["bias0"][:], 0.0, w=["bias0"])
            k.memset("pool", t["bias0"][:, 0:1], 1.0e4, w=["bias0"])
            for g in range(2):
                gs = slice(g * 64, (g + 1) * 64)
                k.dma("pool", t["Wbdk"][gs, gs], d["wck"], w=["Wbdk"])
                k.dma("pool", t["Wbdv"][gs, gs], d["wcv"], w=["Wbdv"])
                k.dma("sp", t["pe2"][:, gs], d["pek"], w=["pe2k"])
            k.mm(t["pM"][:, 0:1], t["pe2"][:, :], t["o32"][:, 0:1], r=["pe2k", "o32"], w=["pM"])
            k.cp("dve", t["pebar"][:, 0:1], t["pM"][:, 0:1], r=["pM"], w=["pebar"])
            for g in range(2):
                gs = slice(g * 64, (g + 1) * 64)
                k.dma("sp", t["pe2"][:, gs], d["pev"], r=[], w=["pe2k"])
            k.mm(t["pM"][:, 0:1], t["pe2"][:, :], t["o32"][:, 0:1], r=["pe2k", "o32"], w=["pM"])
            k.cp("dve", t["pebar"][:, 1:2], t["pM"][:, 0:1], r=["pM"], w=["pebar"])
            for g in range(4):
                k.dma("sp", t["wsl"], d["ws"][g], w=["B3"])
                k.tr(t["pM"][:, 0:128], t["wsl"], t["identf"][:], r=["B3", "identf"], w=["pM"])
                k.tt("dve", t["wsT"][:, g, :], t["pM"][:, 0:128], t["tri_le"][:], ALU.mult, r=["pM", "tri_le"], w=["wsT"])
            winv = d["w_in"].rearrange("(kc p) n -> p kc n", p=128)
            for j in range(11):
                c0, c1 = j * 512, min(DIN, (j + 1) * 512)
                k.dma("pool", t["Win"][:, :, c0:c1], winv[:, :, c0:c1], w=["win%d" % j])
            k.dma("sp", t["cp8"][:], d["cpv"].rearrange("(kc p) -> p kc", p=128), w=["cp8"], slow=True)
            k.dma("sp", t["x"][0:NSMP, :], d["csv"], w=["x"])
            pMv = t["pM"][:, 0:8 * NSMP].rearrange("p (a b) -> p a b", a=8)
            for kc in range(8):
                k.tr(pMv[:, kc, :], t["x"][0:NSMP, kc * 128:(kc + 1) * 128], t["identf"][0:NSMP, 0:NSMP],
                     r=["x", "identf"], w=["pM"])
            k.cp("dve", t["cTs"][:], pMv, r=["pM"], w=["cTs"])
            mod_pass(k, t, d, NSMP, t["cTs"], "cTs", False)
            k.ts("dve", t["idx"][:], t["ptrep"][:], 8.0, t["rmod8"][:, 0:1], ALU.mult, ALU.add, r=["ptrep", "rmod8"], w=["idx"])
            k.dma("sp", t["cs"][0:NSMP, :], d["rope"][S:S + 1, :].partition_broadcast(NSMP), w=["cs"])
            for _ in front(k, t, NSMP, d["xs"], "cs", "0", False):
                pass
            n = NSMP
            N = slice(0, n)
            k.dma("sp", d["sk_cmp"], t["kf"][N, 0:128], r=["kf"], is_out=True)
            k.dma("sp", d["sk_slc"], t["kf"][N, 128:256], r=["kf"], is_out=True)
            k.dma("sp", d["sv_cmp"], t["vf"][N, 0:128], r=["vf"], is_out=True)
            k.dma("sp", d["sv_slc"], t["vf"][N, 128:256], r=["vf"], w=["d_svslc"], is_out=True)
            k.dma("sp", d["svch"], t["vn_f"][N, :], r=["vn_f"], is_out=True)
            k.dma("sp", d["sk_win"][:, 0:511, :], d["kwin"][:, 1:512, :], is_out=True)
            k.dma("sp", d["sv_win"][:, 0:511, :], d["vwin"][:, 1:512, :], is_out=True)
            k.dma("sp", d["sk_win"][:, 511, :], t["kf"][N, 256:384], r=["kf"], is_out=True)
            k.dma("sp", d["sv_win"][:, 511, :], t["vf"][N, 256:384], r=["vf"], w=["d_svwin"], is_out=True)
            pTv8 = t["pT"][:].rearrange("p (a b) -> p a b", a=8)
            for r in range(4):
                k.tr(pTv8[:, r, 0:n], t["q_bf"][N, r * 128:(r + 1) * 128], t["identb"][N, N], r=["q_bf", "identb"], w=["pT"])
            k.cp("act", t["QTs"][0:64, 0, :].rearrange("p (r s) -> p r s", r=4), pTv8[0:64, 0:4, 0:n], r=["pT"], w=["QTs"])
            k.cp("act", t["QTs"][64:128, 1, :].rearrange("p (r s) -> p r s", r=4), pTv8[64:128, 0:4, 0:n], r=["pT"], w=["QTs"])
            Env = t["Enew"][:].rearrange("p (x s h) -> p x s h", x=2, s=NSMP)
            Env16 = t["Enew"][0:NSMP, :].rearrange("p (x s h) -> p x s h", x=2, s=NSMP)
            for xx in range(2):
                kcol = t["k_bf"][N, 128 + xx * 128:256 + xx * 128].rearrange("p (g d) -> p g d", g=2).unsqueeze(2)
                k.tt("dve", t["B2"][N, :].rearrange("p (g r d) -> p g r d", g=2, r=4),
                     t["q_bf"][N, :].rearrange("p (r g d) -> p g r d", r=4, g=2), bc(kcol, [n, 2, 4, 64]), ALU.mult,
                     r=["q_bf", "k_bf"], w=["B2"])
                k.red(t["enr"][:, xx * 8:(xx + 1) * 8], v3(t["B2"][N, :], 8), ALU.add, r=["B2"], w=["enr"])
            k.act(t["en"][:], t["enr"][:], AF.Exp, r=["enr"], w=["en"], scale=SCL)
            for xx in range(2):
                k.tt("dve", Env16[:, xx, :, :], bc(t["identf"][N, N].unsqueeze(2), [n, NSMP, 8]),
                     bc(t["en"][:, xx * 8:(xx + 1) * 8].unsqueeze(1), [n, NSMP, 8]), ALU.mult, r=["identf", "en"], w=["Enew"])
            k.cp("act", t["vfb"][:], t["vf"][:, 128:384], r=["vf"], w=["vfb"])
            k.cp("act", t["Enewb"][0:NSMP, :], t["Enew"][0:NSMP, :], r=["Enew"], w=["Enewb"])
            Envb = t["Enewb"][:].rearrange("p (x s h) -> p x s h", x=2, s=NSMP)
            pS, pO, pD, pM = t["pS0"], t["pO0"], t["pO1"], t["pM"]
            pOv = pO[:, 0:384].rearrange("p (s x h) -> p s x h", s=NSMP, x=3)
            pDv = pD[:, 0:384].rearrange("p (s x h) -> p s x h", s=NSMP, x=3)
            PcTv = t["PcT"][:].rearrange("p (s h) -> p s h", s=NSMP)
            for s in range(NSMP):
                k.gather(t["G0"][:], d["kcmp"], t["idx"][:, s:s + 1], r=["idx"], w=["G0"])
                k.gather(t["G1"][:], d["vcmp"], t["idx"][:, s:s + 1], r=["idx"], w=["G1"])
                k.cp("act", t["Gb0"][:], t["G0"][:], r=["G0"], w=["Gb0"])
                k.cp("act", t["Gb1"][:], t["G1"][:], r=["G1"], w=["Gb1"])
                for tt_ in range(16):
                    k.mm(pM[:, 0:64], t["Gb0"][:, tt_ * 128:(tt_ + 1) * 128], t["pool2b"][:], start=(tt_ == 0), stop=(tt_ == 15),
                         r=["Gb0", "pool2b"], w=["pM"])
                for tt_ in range(16):
                    k.mm(pM[:, 64:128], t["Gb1"][:, tt_ * 128:(tt_ + 1) * 128], t["pool2b"][:], start=(tt_ == 0), stop=(tt_ == 15),
                         r=["Gb1", "pool2b"], w=["pM"])
                k.ts("dve", t["pTk"][:], pM[:, 0:64], t["pebar"][:, 0:1], None, ALU.add, r=["pM", "pebar"], w=["pTk"])
                k.ts("dve", t["pTv"][:], pM[:, 64:128], t["pebar"][:, 1:2], None, ALU.add, r=["pM", "pebar"], w=["pTv"])
                k.mm(pM[:, 128:192], t["Wbdk"][:], t["pTk"][:], r=["Wbdk", "pTk"], w=["pM"])
                k.mm(pM[0:64, 192:320], t["pTv"][:], t["Wbdv"][:], r=["Wbdv", "pTv"], w=["pM"])
                k.cp("act", t["KcTs"][:], pM[:, 128:192], r=["pM"], w=["KcTs"])
                k.cp("act", t["Vcs"][:], pM[0:64, 192:320], r=["pM"], w=["Vcs"])
                for g in range(2):
                    gs = slice(g * 64, (g + 1) * 64)
                    k.mm(pS[0:64, s * 8 + g * 4:s * 8 + g * 4 + 4], t["KcTs"][:, :],
                         t["QTs"][:, g, :].rearrange("p (r s) -> p r s", r=4)[:, :, s], r=["KcTs", "QTs"], w=["pS0"])
                k.act(PcTv[:, s, :], pS[0:64, s * 8:(s + 1) * 8], AF.Exp, r=["pS0"], w=["PcT"], scale=SCL)
                k.mm(pOv[:, s, 0, :], t["Vcs"][:], PcTv[:, s, :], r=["Vcs", "PcT"], w=["pO0"])
                k.mm(pDv[:, s, 0, :], t["onesf"][0:64, :], PcTv[:, s, :], r=["onesf", "PcT"], w=["pO1"])
                k.mm(pS[0:32, 128 + s * 8:128 + (s + 1) * 8], t["pairf"][:], PcTv[:, s, :], r=["pairf", "PcT"], w=["pS0"])
            k.P.add("dve", lambda e: e.reciprocal(out=t["rDc"][:].rearrange("p (s h) -> p s h", s=NSMP), in_=pDv[0:32, :, 0, :]),
                    ["pO1"], ["rDc"])
            k.tt("dve", t["impn"][:], pS[0:32, 128:256], t["rDc"][:], ALU.mult, r=["pS0", "rDc"], w=["impn"])
            k.red(t["impT"][:], t["impn"][:].rearrange("p (a r) -> p a r", r=4), ALU.add, r=["impn"], w=["impT"])
            k.tr(pS[0:32, 256:288], t["impT"][:], t["identf"][0:32, 0:32], r=["impT", "identf"], w=["pS0"])
            k.tt("dve", t["impS"][:], pS[0:32, 256:288], t["bias0"][:], ALU.add, r=["pS0", "bias0"], w=["impS"])
            k.P.add("dve", lambda e: e.max(out=t["m8s"][:], in_=t["impS"][:]), ["impS"], ["m8s"])
            k.ts("dve", t["selS"][:], t["impS"][:], t["m8s"][:, 6:7], None, ALU.is_ge, r=["impS", "m8s"], w=["selS"])
            k.tr(t["pT"][0:32, 0:32], t["selS"][:], t["identb"][0:32, 0:32], r=["selS", "identb"], w=["pT"])
            k.cp("act", t["selTs"][:], t["pT"][0:32, 0:32], r=["pT"], w=["selTs"])
            k.mm(pS[:, 320:352], t["e4"][:], t["selTs"][:], r=["e4", "selTs"], w=["pS0"])
            k.cp("act", t["Msk"][:], pS[:, 320:352], r=["pS0"], w=["Msk"])
            pS = t["pS1"]
            banks = [(t["pA"], "pA"), (t["pB"], "pB")]
            for s in range(NSMP):
                k.gather(t["G0"][:], d["kslc"], t["idx"][:, s:s + 1], r=["idx"], w=["G0"])
                k.gather(t["G1"][:], d["vslc"], t["idx"][:, s:s + 1], r=["idx"], w=["G1"])
                k.dma("sp", t["Gw0"][:].rearrange("p (a c) -> p a c", a=4), d["kwin"][s].rearrange("(p a) c -> p a c", a=4), w=["Gw0"])
                k.dma("sp", t["Gw1"][:].rearrange("p (a c) -> p a c", a=4), d["vwin"][s].rearrange("(p a) c -> p a c", a=4), w=["Gw1"])
                k.cp("act", t["Gb0"][:], t["G0"][:], r=["G0"], w=["Gb0"])
                k.cp("act", t["Gwb0"][:], t["Gw0"][:], r=["Gw0"], w=["Gwb0"])
                k.cp("act", t["Gb1"][:], t["G1"][:], r=["G1"], w=["Gb1"])
                k.cp("act", t["Gwb1"][:], t["Gw1"][:], r=["Gw1"], w=["Gwb1"])
                pTk8 = t["pT"][:].rearrange("p (a c) -> p a c", a=8)
                for q8 in range(3):
                    nt_ = 8 if q8 < 2 else 4
                    for a in range(nt_):
                        tix = q8 * 8 + a
                        if tix < 16:
                            src, sname, col = t["Gb0"], "Gb0", tix * 128
                        else:
                            src, sname, col = t["Gwb0"], "Gwb0", (tix - 16) * 128
                        k.tr(pTk8[:, a, :], src[:, col:col + 128], t["identb"][:], r=[sname, "identb"], w=["pT"])
                    k.cp("act", t["KTs"][:, q8 * 8:q8 * 8 + nt_, :], pTk8[:, 0:nt_, :], r=["pT"], w=["KTs"])
                for tt_ in range(20):
                    for g in range(2):
                        gs = slice(g * 64, (g + 1) * 64)
                        k.mm(pS[:, tt_ * 8 + g * 4:tt_ * 8 + g * 4 + 4], t["KTs"][:, tt_, :],
                             t["QTs"][:, g, :].rearrange("p (r s) -> p r s", r=4)[:, :, s], r=["KTs", "QTs"], w=["pS1"])
                k.act(t["PTs"][:], pS[:, 0:160], AF.Exp, r=["pS1"], w=["PTs"], scale=SCL)
                k.tt("dve", t["PTs"][:, 0:128].rearrange("p (a g r) -> p a g r", a=16, g=2),
                     t["PTs"][:, 0:128].rearrange("p (a g r) -> p a g r", a=16, g=2),
                     bc(t["Msk"][:, s * 2:(s + 1) * 2].unsqueeze(1).unsqueeze(3), [128, 16, 2, 4]), ALU.mult,
                     r=["PTs", "Msk"], w=["PTs"])
                k.memset("dve", t["PTs"][0:1, 128:136], 0.0, w=["PTs"])
                k.red(t["PTsum"][:, 0:8], t["PTs"][:, 0:128].rearrange("p (a h) -> p h a", a=16), ALU.add, r=["PTs"], w=["PTsum"])
                k.red(t["PTsum"][:, 8:16], t["PTs"][:, 128:160].rearrange("p (a h) -> p h a", a=4), ALU.add, r=["PTs"], w=["PTsum"])
                k.cp("act", t["PTb"][:], t["PTs"][:], r=["PTs"], w=["PTb"])
                for tt_ in range(16):
                    k.mm(pOv[:, s, 1, :], t["Gb1"][:, tt_ * 128:(tt_ + 1) * 128], t["PTb"][:, tt_ * 8:(tt_ + 1) * 8],
                         start=(tt_ == 0), stop=False, r=["Gb1", "PTb"], w=["pO0"])
                k.mm(pOv[:, s, 1, :], t["vfb"][:, 0:128], Envb[:, 0, s, :], start=False, stop=True,
                     r=["vfb", "Enewb"], w=["pO0"])
                for tt_ in range(4):
                    k.mm(pOv[:, s, 2, :], t["Gwb1"][:, tt_ * 128:(tt_ + 1) * 128], t["PTb"][:, 128 + tt_ * 8:128 + (tt_ + 1) * 8],
                         start=(tt_ == 0), stop=False, r=["Gwb1", "PTb"], w=["pO0"])
                k.mm(pOv[:, s, 2, :], t["vfb"][:, 128:256], Envb[:, 1, s, :], start=False, stop=True,
                     r=["vfb", "Enewb"], w=["pO0"])
                for xx in range(2):
                    k.mm(pDv[:, s, 1 + xx, :], t["onesf"][:, :], t["PTsum"][:, xx * 8:(xx + 1) * 8], start=True, stop=False,
                         r=["onesf", "PTsum"], w=["pO1"])
                    k.mm(pDv[:, s, 1 + xx, :], t["onesf"][:, :], Env[:, xx, s, :], start=False, stop=True,
                         r=["onesf", "Enew"], w=["pO1"])
            k.P.add("dve", lambda e: e.reciprocal(out=t["rDall"], in_=pD[:, 0:384]), ["pO1"], ["B1a", "B1b"])
            k.tt("dve", t["OTn"], pO[:, 0:384], t["rDall"], ALU.mult, r=["pO0", "B1a", "B1b"], w=["B1a", "B1b"])
            k.cp("dve", t["OT1"], t["OTn"][64:128, :], r=["B1a", "B1b"], w=["B2"])
            gwv = t["gw0"][N, :].rearrange("p (h x) -> p x h", x=3)
            for xx in range(3):
                bank, bname = banks[xx % 2]
                for hh in range(8):
                    src = t["OTn"] if hh < 4 else t["OT1"]
                    sname = "B1a" if hh < 4 else "B2"
                    inap = src[0:64, :].rearrange("p (s c) -> p s c", s=NSMP)[:, :, xx * 8 + hh]
                    k.tr(bank[0:n, hh * 64:(hh + 1) * 64], inap, t["identf"][0:64, 0:64], r=[sname, "identf"], w=[bname])
                if xx == 0:
                    k.tt("dve", v3(t["o_a"][N, :], 8), v3(bank[N, :], 8), bc(gwv[:, xx, :].unsqueeze(2), [n, 8, 64]), ALU.mult,
                         r=[bname, "gw0"], w=["o_a"])
                else:
                    k.tt("dve", v3(t["tmpO"][N, :], 8), v3(bank[N, :], 8), bc(gwv[:, xx, :].unsqueeze(2), [n, 8, 64]), ALU.mult,
                         r=[bname, "gw0"], w=["B3"])
                    k.tt("pool", t["o_a"][N, :], t["o_a"][N, :], t["tmpO"][N, :], ALU.add, r=["o_a", "B3"], w=["o_a"])
            k.dma("sp", d["gqs"], t["gq"][N, :], r=["gq"], w=["d_gqs"])
            mod_pass(k, t, d, 128, t["cp8"], "cp8", True)
            P.emit(es)

        with contextlib.ExitStack() as s2:
            P = Prog(nc, "b")
            k = K(P)
            sb("Wout", [128, 8, D], BF16, s2); sb("WA", [128, 4, D], BF16, s2); sb("WB", [128, 4, D], BF16, s2)
            k.dma("pool", t["Wout"][:], d["wout"].rearrange("(kc p) n -> p kc n", p=128), w=["Wout"])
            for g in range(2):
                k.dma("pool", t["WA"][g * 64:(g + 1) * 64, :, :],
                      d["wbra"][g * 256:(g + 1) * 256, :].rearrange("(r dd) n -> dd r n", dd=64), w=["WA"])
            k.dma("pool", t["WB"][:], d["wbrb"].rearrange("(c p) n -> p c n", p=128), w=["WB"])
            k.dma("sp", t["B1"][0:NSMP, :], d["gqs"], w=["B1a", "B1b"])
            for _ in tail(k, t, NSMP, d["ys"], d["xs"], "0", gate=[(t["B1"][:, 0:512], "B1a"), (t["B1"][:, 512:1024], "B1b")]):
                pass
            sb("KsT", [128, S], BF16, s2); sb("KwT", [128, 5 * 128], BF16, s2)
            sb("Vs", [128, NT, 2, 65], BF16, s2); sb("Vw", [128, 5, 2, 65], BF16, s2)
            sb("QT", [128, 2, 512], BF16, s2); sb("vcb", [128, 128], BF16, s2)
            sb("pTk2", [128, 64], BF16, s2); sb("pTv2", [128, 64], BF16, s2); sb("KcT", [128, 64], BF16, s2)
            sb("Vc", [64, 2, 97], BF16, s2)
            sb("PT0", [128, 512], BF16, s2); sb("PT1", [128, 512], BF16, s2)
            sb("nsel4", [128, 2, 512], BF16, s2); sb("imp", [128, 64], F32, s2)
            sb("sel", [128, 64], BF16, s2); sb("m8", [128, 16], F32, s2); sb("mctneg", [128, 512], BF16, s2); sb("ibias", [128, 32], F32, s2)
            sb("tnle", [128, 512], BF16, s2); sb("tngt", [128, 512], BF16, s2)
            k.dma("pool", t["tnle"][:], d["trineg_le"], w=["tnle"])
            k.dma("pool", t["tngt"][:], d["trineg_gt"], w=["tngt"])
            n = 128
            N = slice(0, 128)
            k.memset("pool", t["Vs"][:], 1.0, w=["Vs"])
            k.memset("pool", t["Vw"][:], 1.0, w=["Vw"])
            k.memset("pool", t["Vc"][:], 1.0, w=["Vc"])
            k.memset("pool", t["pTk2"][:], 0.0, w=["pTk2"])
            k.memset("pool", t["pTv2"][:], 0.0, w=["pTv2"])
            k.memset("pool", t["KcT"][:], 0.0, w=["KcT"])
            k.memset("pool", t["QT"][:], 0.0, w=["QT"])
            k.memset("pool", t["nsel4"][:], 0.0, w=["nsel4"])
            k.dma("pool", t["mctneg"][:], d["cmpbase"], w=["mctneg"])
            for g in range(2):
                k.dma("pool", t["Vc"][:, g, 65:97], d["pair"], w=["Vc"])
            pS = [(t["pS0"], "pS0"), (t["pS1"], "pS1")]
            pO = [(t["pO0"], "pO0"), (t["pO1"], "pO1")]
            PT = [(t["PT0"], "PT0"), (t["PT1"], "PT1")]
            pM = t["pM"]
            pTv8 = t["pT"][:].rearrange("p (a b) -> p a b", a=8)
            cnt = {"s": 0, "p": 0}
            cur = {}
            sfx_of = lambda ii: str((ii + 1) % 2)

            def branch_finish(xx):
                for g in range(2):
                    bank, bname = pO[g]
                    ov = bank[:, 0:388].rearrange("p (r c) -> p r c", r=4)
                    k.ts("dve", t["rden"][:, g * 4:(g + 1) * 4], ov[:, :, 64], 1e-30, None, ALU.max, r=[bname], w=["rden"])
                k.P.add("dve", lambda e: e.reciprocal(out=t["rden"][:], in_=t["rden"][:]), ["rden"], ["rden"])
                k.tt("dve", t["cx"][:], t["rden"][:], cur["gwv"][:, xx, :], ALU.mult, r=["rden", cur["gwn"]], w=["cx"])
                for g in range(2):
                    bank, bname = pO[g]
                    ov = bank[:, 0:388].rearrange("p (r c) -> p r c", r=4)
                    for r in range(4):
                        hs = slice((g * 4 + r) * 64, (g * 4 + r + 1) * 64)
                        k.stt(t["o_a"][:, hs], ov[:, r, 0:64], t["cx"][:, g * 4 + r:g * 4 + r + 1], t["o_a"][:, hs], ALU.mult, ALU.add,
                              r=[bname, "cx", "o_a"], w=["o_a"])

            k.dma("sp", t["cs"][:], d["rope"][0:128, :], w=["cs"])
            for _ in front(k, t, 128, d["xp"][0:128, :], "cs", sfx_of(0), True):
                pass
            tgen = {"g": None}

            def tail_hook(nit):
                if tgen["g"] is not None:
                    next(tgen["g"], None)

            for i in range(NT):
                rows = slice(i * 128, (i + 1) * 128)
                sfx = sfx_of(i)
                cur["gwn"] = "gw" + sfx
                cur["gwv"] = t["gw" + sfx][:, :].rearrange("p (h x) -> p x h", x=3)
                k.dma("sp", d["pk_cmp"][rows, :], t["kf"][:, 0:128], r=["kf"], is_out=True)
                k.dma("sp", d["pk_slc"][rows, :], t["kf"][:, 128:256], r=["kf"], is_out=True)
                k.dma("sp", d["pv_cmp"][rows, :], t["vf"][:, 0:128], r=["vf"], is_out=True)
                k.dma("sp", d["pv_slc"][rows, :], t["vf"][:, 128:256], r=["vf"], is_out=True)
                if i >= NT - 4:
                    wr = slice((i - (NT - 4)) * 128, (i - (NT - 4) + 1) * 128)
                    k.dma("sp", d["pk_win"][wr, :], t["kf"][:, 256:384], r=["kf"], is_out=True)
                    k.dma("sp", d["pv_win"][wr, :], t["vf"][:, 256:384], r=["vf"], is_out=True)
                slot = i % 5
                k.cp("pool", t["Vs"][:, i, :, 0:64], v3(t["vf"][:, 128:256], 2), r=["vf"], w=["Vs"])
                k.cp("pool", t["Vw"][:, slot, :, 0:64], v3(t["vf"][:, 256:384], 2), r=["vf"], w=["Vw"])
                k.cp("pool", t["vcb"][:], t["vf"][:, 0:128], r=["vf"], w=["vcb"])
                for r in range(4):
                    k.tr(pTv8[:, r, :], t["q_bf"][:, r * 128:(r + 1) * 128], t["identb"][:], r=["q_bf", "identb"], w=["pT"])
                k.tr(pTv8[:, 4, :], t["k_bf"][:, 128:256], t["identb"][:], r=["k_bf", "identb"], w=["pT"])
                k.tr(pTv8[:, 5, :], t["k_bf"][:, 256:384], t["identb"][:], r=["k_bf", "identb"], w=["pT"])
                k.cp("act", t["QT"][0:64, 0, :].rearrange("p (r q) -> p r q", r=4), pTv8[0:64, 0:4, :], r=["pT"], w=["QT"])
                k.cp("act", t["QT"][64:128, 1, :].rearrange("p (r q) -> p r q", r=4), pTv8[64:128, 0:4, :], r=["pT"], w=["QT"])
                k.cp("act", t["KsT"][:, rows], pTv8[:, 4, :], r=["pT"], w=["KsT"])
                k.cp("act", t["KwT"][:, slot * 128:(slot + 1) * 128], pTv8[:, 5, :], r=["pT"], w=["KwT"])
                cc = slice(4 * i, 4 * i + 4)
                k.mm(pM[:, 0:4], t["k_bf"][:, 0:128], t["pool4"][:], r=["k_bf", "pool4"], w=["pM"])
                k.mm(pM[:, 4:8], t["vcb"][:], t["pool4"][:], r=["vcb", "pool4"], w=["pM"])
                k.ts("dve", t["pTk2"][:, cc], pM[:, 0:4], t["pebar"][:, 0:1], None, ALU.add, r=["pM", "pebar"], w=["pTk2"])
                k.ts("dve", t["pTv2"][:, cc], pM[:, 4:8], t["pebar"][:, 1:2], None, ALU.add, r=["pM", "pebar"], w=["pTv2"])
                k.mm(pM[:, 8:12], t["Wbdk"][:], t["pTk2"][:, cc], r=["Wbdk", "pTk2"], w=["pM"])
                k.mm(pM[0:64, 16:144], t["pTv2"][:], t["Wbdv"][:], r=["Wbdv", "pTv2"], w=["pM"])
                k.cp("act", t["KcT"][:, cc], pM[:, 8:12], r=["pM"], w=["KcT"])
                k.cp("act", t["Vc"][:, :, 0:64], v3(pM[0:64, 16:144], 2), r=["pM"], w=["Vc"])

                k.dma("sp", t["ibias"][:], d["impbias"][i], w=["ibias"])
                gen = None
                if i + 1 < NT:
                    nrows = slice((i + 1) * 128, (i + 2) * 128)
                    k.dma("sp", t["cs"][:], d["rope"][nrows, :], w=["cs"])
                    gen = front(k, t, 128, d["xp"][nrows, :], "cs", sfx_of(i + 1), True)
                    next(gen, None)
                qt = {g: t["QT"][:, g, :] for g in range(2)}

                def run_branch(items, hook=None):
                    recs = []

                    def s_stage(it):
                        g, kp, mms, v_ap, v_name, first, last = it
                        sbank, sname = pS[cnt["s"] % 2]; cnt["s"] += 1
                        pt, pname = PT[cnt["p"] % 2]; cnt["p"] += 1
                        for mi, (lh, rh, nm) in enumerate(mms):
                            k.mm(sbank[0:kp, :], lh, rh, start=(mi == 0), stop=(mi == len(mms) - 1), r=nm, w=[sname])
                        recs.append((sbank, sname, pt, pname))

                    def e_stage(idx):
                        g, kp, mms, v_ap, v_name, first, last = items[idx]
                        sbank, sname, pt, pname = recs[idx]
                        k.act(pt[0:kp, :], sbank[0:kp, :], AF.Exp, r=[sname], w=[pname], scale=SCL)
                        obank, oname = pO[g]
                        ov = obank[:, 0:388].rearrange("p (r c) -> p r c", r=4)
                        nv = v_ap.shape[-1]
                        for r in range(4):
                            k.mm(ov[:, r, 0:nv], pt[0:kp, r * 128:(r + 1) * 128], v_ap, start=(first and r == 0), stop=(last and r == 3),
                                 r=[pname, v_name], w=[oname])

                    s_stage(items[0])
                    for idx in range(len(items)):
                        if idx + 1 < len(items):
                            s_stage(items[idx + 1])
                        e_stage(idx)
                        if hook is not None:
                            hook(len(items))

                items = []
                for g in range(2):
                    gs = slice(g * 64, (g + 1) * 64)
                    items.append((g, 64, [(t["KcT"][:, :], qt[g], ["KcT", "QT"]),
                                          (t["identb"][:, 64 - 4 * i:128 - 4 * i], t["mctneg"][:, :], ["identb", "mctneg"])],
                                  t["Vc"][:, g, :], "Vc", True, True))
                run_branch(items)
                for g in range(2):
                    bank, bname = pO[g]
                    ov = bank[:, 0:388].rearrange("p (r c) -> p r c", r=4)
                    k.ts("dve", t["rden"][:, g * 4:(g + 1) * 4], ov[:, :, 64], 1e-30, None, ALU.max, r=[bname], w=["rden"])
                k.P.add("dve", lambda e: e.reciprocal(out=t["rden"][:], in_=t["rden"][:]), ["rden"], ["rden"])
                for g in range(2):
                    bank, bname = pO[g]
                    ov = bank[:, 0:388].rearrange("p (r c) -> p r c", r=4)
                    ig = t["imp"][:, g * 32:(g + 1) * 32]
                    k.stt(ig, ov[:, 0, 65:97], t["rden"][:, g * 4:g * 4 + 1], t["ibias"][:], ALU.mult, ALU.add,
                          r=[bname, "rden", "ibias"], w=["imp"])
                    for r in range(1, 4):
                        k.stt(ig, ov[:, r, 65:97], t["rden"][:, g * 4 + r:g * 4 + r + 1], ig, ALU.mult, ALU.add,
                              r=[bname, "rden", "imp"], w=["imp"])
                k.tt("dve", t["cx"][:], t["rden"][:], cur["gwv"][:, 0, :], ALU.mult, r=["rden", cur["gwn"]], w=["cx"])
                for g in range(2):
                    bank, bname = pO[g]
                    ov = bank[:, 0:388].rearrange("p (r c) -> p r c", r=4)
                    k.tt("dve", v3(t["o_a"][:, g * 256:(g + 1) * 256], 4), ov[:, :, 0:64],
                         bc(t["cx"][:, g * 4:(g + 1) * 4].unsqueeze(2), [128, 4, 64]), ALU.mult, r=[bname, "cx"], w=["o_a"])
                for g in range(2):
                    ig = t["imp"][:, g * 32:(g + 1) * 32]
                    k.P.add("dve", (lambda g: lambda e: e.max(out=t["m8"][:, g * 8:(g + 1) * 8], in_=t["imp"][:, g * 32:(g + 1) * 32]))(g),
                            ["imp"], ["m8"])
                    k.ts("dve", t["sel"][:, g * 32:(g + 1) * 32], ig, t["m8"][:, g * 8 + 7:g * 8 + 8], None, ALU.is_ge,
                         r=["imp", "m8"], w=["sel"])
                if tgen["g"] is not None:
                    next(tgen["g"], None)
                    next(tgen["g"], None)
                j0 = max(0, i - 4)
                items = []
                for g in range(2):
                    gs = slice(g * 64, (g + 1) * 64)
                    for j in range(j0, i + 1):
                        sl = j % 5
                        mms = [(t["KwT"][:, sl * 128:(sl + 1) * 128], qt[g], ["KwT", "QT"])]
                        if j == i:
                            mms.append((t["identb"][:, :], t["tnle"][:, :], ["identb", "tnle"]))
                        elif j == i - 4:
                            mms.append((t["identb"][:, :], t["tngt"][:, :], ["identb", "tngt"]))
                        items.append((g, 128, mms, t["Vw"][:, sl, g, :], "Vw", j == j0, j == i))
                run_branch(items, tail_hook)
                if tgen["g"] is not None:
                    for _ in tgen["g"]:
                        pass
                    tgen["g"] = None
                branch_finish(2)
                k.tr(t["pT"][0:64, 0:128], t["sel"][:], t["identb"][:], r=["sel", "identb"], w=["pT"])
                for g in range(2):
                    g32 = slice(g * 32, (g + 1) * 32)
                    k.ts("dve", v3(t["nsel4"][g32, g, :], 4), bc(t["pT"][g32, 0:128].unsqueeze(1), [32, 4, 128]), -1.0, 30000.0,
                         ALU.add, ALU.mult, r=["pT"], w=["nsel4"])
                items = []
                for g in range(2):
                    gs = slice(g * 64, (g + 1) * 64)
                    g32 = slice(g * 32, (g + 1) * 32)
                    for j in range(i + 1):
                        mms = [(t["KsT"][:, j * 128:(j + 1) * 128], qt[g], ["KsT", "QT"]),
                               (t["e2"][:, j * 128:(j + 1) * 128], t["nsel4"][:, g, :], ["e2", "nsel4"])]
                        if j == i:
                            mms.append((t["identb"][:, :], t["tnle"][:, :], ["identb", "tnle"]))
                        items.append((g, 128, mms, t["Vs"][:, j, g, :], "Vs", j == 0, j == i))
                acc = {"a": 0.0}
                if gen is not None:
                    next(gen, None)
                    next(gen, None)

                def hook(nit):
                    if gen is None:
                        return
                    acc["a"] += 14.0 / nit
                    while acc["a"] >= 1.0:
                        acc["a"] -= 1.0
                        next(gen, None)

                run_branch(items, hook)
                if gen is not None:
                    for _ in gen:
                        pass
                branch_finish(1)
                tgen["g"] = tail(k, t, 128, d["yp"][rows, :], d["xp"][rows, :], sfx)
                next(tgen["g"], None)
            for _ in tgen["g"]:
                pass
            P.emit(es)
    return nc


def _consts():
    c = {}
    half = 32
    inv = (10000.0 ** (-np.arange(half, dtype=np.float32) * 2.0 / 64)).astype(np.float32)
    ang = np.arange(S + 1, dtype=np.float32)[:, None] * inv[None, :]
    c["rope"] = np.concatenate([np.cos(ang), np.sin(ang)], axis=1).astype(np.float32)
    kk = np.arange(128)[:, None]
    qq = np.arange(128)[None, :]
    c["tri_le"] = (kk <= qq).astype(np.float32)
    c["tri_gt"] = (kk > qq).astype(np.float32)
    c["identf_c"] = np.eye(128, dtype=np.float32)
    c["pool4"] = ((np.arange(128)[:, None] // 32) == np.arange(4)[None, :]).astype(np.float32) / 32.0
    c["pair"] = ((np.arange(64)[:, None] // 2) == np.arange(32)[None, :]).astype(np.float32)
    e = ((np.arange(2048)[None, :] // 64) == np.arange(32)[:, None]).astype(np.float32)
    c["e2"] = np.concatenate([e, e], axis=0)
    m = np.zeros((NT, 64, 128), np.float32)
    ib = np.zeros((NT, 128, 32), np.float32)
    for i in range(NT):
        pos = i * 128 + np.arange(128)
        cend = (np.arange(64) + 1) * 32 - 1
        m[i] = (cend[:, None] <= pos[None, :]).astype(np.float32)
        qblk = pos // 64
        blk = np.arange(32)
        forced = (blk[None, :] == 0) | (blk[None, :] == qblk[:, None])
        causal = blk[None, :] <= qblk[:, None]
        ib[i] = np.where(causal, 1.0e4 * forced, -1.0e30).astype(np.float32)
    base = np.zeros((128, 128), np.float32)
    base[68:, :] = -30000.0
    for u in range(64, 68):
        base[u, :] = np.where(np.arange(128) < 32 * (u - 64) + 31, -30000.0, 0.0)
    c["cmpbase"] = np.ascontiguousarray(np.tile(base[:, None, :], (1, 4, 1)).reshape(128, 512)).astype(np.float32)
    c["trineg_le"] = np.ascontiguousarray(np.tile(((1.0 - c["tri_le"]) * -30000.0)[:, None, :], (1, 4, 1)).reshape(128, 512)).astype(np.float32)
    c["trineg_gt"] = np.ascontiguousarray(np.tile(((1.0 - c["tri_gt"]) * -30000.0)[:, None, :], (1, 4, 1)).reshape(128, 512)).astype(np.float32)
    c["impbias"] = ib
    c["pool2"] = ((np.arange(128)[:, None] // 2) == np.arange(64)[None, :]).astype(np.float32) / 32.0
    c["e4"] = ((np.arange(128)[None, :] // 4) == np.arange(32)[:, None]).astype(np.float32)
    c["rmod8"] = (np.arange(128) % 8).astype(np.float32).reshape(128, 1)
    return c


_NC_CACHE = {}


def kernel(x_prompt, x_sample, cache_k_cmp, cache_v_cmp, cache_k_slc, cache_v_slc,
           cache_k_win, cache_v_win, page_table, c_prompt, c_sample,
           w_ada, b_ada, norm_g, w_in, q_norm_g, k_norm_g, cmp_pos_k, cmp_pos_v,
           w_cmp_k, w_cmp_v, vnorm_g, vnorm_b, w_s, b_s, w_br_a, w_br_b, w_out):
    f = lambda a: np.ascontiguousarray(np.asarray(a), dtype=np.float32)
    if "nc" not in _NC_CACHE:
        _NC_CACHE["nc"] = build_program()
    nc = _NC_CACHE["nc"]
    consts = _consts()
    pools = {nm: f(a).reshape(2560 * 8, 2048) for nm, a in
             (("kcmp", cache_k_cmp), ("vcmp", cache_v_cmp), ("kslc", cache_k_slc), ("vslc", cache_v_slc))}
    shared = dict(
        w_ada=f(w_ada)[0], b_ada=f(b_ada)[0].reshape(1, -1), norm_g=f(norm_g)[0].reshape(1, -1), w_in=f(w_in)[0],
        qg=f(q_norm_g)[0].reshape(1, -1), kg=f(k_norm_g)[0].reshape(1, -1), pek=f(cmp_pos_k)[0], pev=f(cmp_pos_v)[0],
        wck=f(w_cmp_k)[0], wcv=f(w_cmp_v)[0], vng=f(vnorm_g)[0].reshape(1, -1), vnb=f(vnorm_b)[0].reshape(1, -1),
        ws=f(w_s)[0], bs=f(b_s)[0], wbra=f(w_br_a)[0], wbrb=f(w_br_b)[0], wout=f(w_out)[0])
    shared.update(pools)
    shared.update(consts)
    xp, xs = f(x_prompt), f(x_sample)
    kw, vw = f(cache_k_win)[0], f(cache_v_win)[0]
    pt = np.asarray(page_table).astype(np.int32)
    cp, cs = f(c_prompt), f(c_sample)
    in_maps = []
    for c in range(8):
        sl = slice(c * NSMP, (c + 1) * NSMP)
        m = dict(shared)
        m["xp"] = xp[c]
        m["xs"] = np.ascontiguousarray(xs[sl, 0, :])
        m["cpv"] = np.ascontiguousarray(cp[c])
        m["csv"] = np.ascontiguousarray(cs[sl])
        m["kwin"] = np.ascontiguousarray(kw[sl].reshape(NSMP, 512, 128))
        m["vwin"] = np.ascontiguousarray(vw[sl].reshape(NSMP, 512, 128))
        m["ptrep"] = np.ascontiguousarray(np.repeat(pt[sl].T, 8, axis=0))
        in_maps.append(m)
    res = run_bass_kernel_spmd(nc, in_maps, core_ids=list(range(8)))
    R = res.results
    cat = lambda nm: np.stack([R[c][nm] for c in range(8)], axis=0)
    y_p = cat("yp")
    y_s = np.concatenate([R[c]["ys"] for c in range(8)], axis=0).reshape(128, 1, D)
    outs = [y_p, y_s]
    for nm in ("pk_cmp", "pv_cmp", "pk_slc", "pv_slc"):
        outs.append(cat(nm).reshape(1, 8, S, 2, 64))
    for nm in ("pk_win", "pv_win"):
        outs.append(cat(nm).reshape(1, 8, 512, 2, 64))
    for nm in ("sk_cmp", "sv_cmp", "sk_slc", "sv_slc"):
        outs.append(np.concatenate([R[c][nm] for c in range(8)], axis=0).reshape(1, 128, 1, 2, 64))
    for nm in ("sk_win", "sv_win"):
        outs.append(np.concatenate([R[c][nm] for c in range(8)], axis=0).reshape(1, 128, 512, 2, 64))
    outs.append(np.concatenate([R[c]["svch"] for c in range(8)], axis=0).reshape(1, 128, 1, 512))
    return tuple(o.astype(np.float32) for o in outs)
```

```python
import contextlib
import numpy as np
import concourse.bass as bass
import concourse.mybir as mybir
from concourse.bass_utils import run_bass_kernel_spmd

F32 = mybir.dt.float32
BF16 = mybir.dt.bfloat16
I32 = mybir.dt.int32
ALU = mybir.AluOpType
AF = mybir.ActivationFunctionType
AX = mybir.AxisListType

D = 1024
DIN = 5400
S = 2048
NT = 16
NSMP = 16
EPS = 1e-6
SCL = 0.125
C_Q, C_K, C_V, C_NSA, C_ZA, C_U, C_VV, C_ZB, C_GA, C_GB = 0, 512, 896, 1280, 1304, 1816, 2328, 2840, 3352, 4376


class _Op:
    __slots__ = ("eng", "fn", "deps", "idx", "signal", "is_dma", "sem", "target", "count")


class Prog:
    ENGS = ("pe", "act", "dve", "pool", "sp")
    DMA_POOL = {"sp": 16, "pool": 12, "act": 2}

    def __init__(self, nc, tag):
        self.nc = nc
        self.tag = tag
        self.q = {e: [] for e in self.ENGS}
        self.last_w = {}
        self.readers = {}
        self.dma_n = {e: 0 for e in self.DMA_POOL}
        self.out_dmas = []

    def add(self, eng, fn, r=(), w=(), dma=False, out=False):
        op = _Op()
        op.eng, op.fn, op.is_dma, op.signal = eng, fn, dma, False
        op.deps = set()
        op.sem = None
        op.target = 0
        op.count = 0
        for b in r:
            lw = self.last_w.get(b)
            if lw is not None:
                op.deps.add(lw)
        for b in w:
            lw = self.last_w.get(b)
            if lw is not None:
                op.deps.add(lw)
            for rd in self.readers.get(b, ()):
                op.deps.add(rd)
        for b in r:
            self.readers.setdefault(b, []).append(op)
        for b in w:
            self.last_w[b] = op
            self.readers[b] = []
        op.deps.discard(op)
        op.idx = len(self.q[eng])
        self.q[eng].append(op)
        if dma:
            j = self.dma_n[eng]
            self.dma_n[eng] += 1
            op.sem = (eng, j % self.DMA_POOL[eng])
            op.target = 16 * (j // self.DMA_POOL[eng] + 1)
            if out:
                self.out_dmas.append(op)
        return op

    def _needs_wait(self, op, dep):
        if dep.is_dma:
            return True
        if dep.eng == "pe" and op.eng == "pe" and not op.is_dma:
            return False
        return True

    def emit(self, es):
        nc = self.nc
        fin = self.add("sp", None)
        for o in self.out_dmas:
            fin.deps.add(o)
        for e in self.DMA_POOL:
            for op in self.q[e]:
                if op.is_dma:
                    fin.deps.add(op)
        for e in self.ENGS:
            for op in self.q[e]:
                for d in op.deps:
                    if (not d.is_dma) and self._needs_wait(op, d):
                        d.signal = True
        for e in self.ENGS:
            c = 0
            for op in self.q[e]:
                if (not op.is_dma) and op.signal:
                    c += 1
                    op.count = c
        esem = {e: es.enter_context(nc.semaphore("s%s_%s" % (self.tag, e))) for e in ("pe", "act", "dve", "pool")}
        dsem = {}
        for e, n in self.DMA_POOL.items():
            for i in range(n):
                dsem[(e, i)] = es.enter_context(nc.semaphore("d%s_%s_%d" % (self.tag, e, i)))
        prog = self
        with nc.Block() as block:
            def run_queue(ename, eng):
                waited = {}
                for op in prog.q[ename]:
                    waits = {}
                    for d in op.deps:
                        if not prog._needs_wait(op, d):
                            continue
                        if d.is_dma:
                            key, val = ("d", d.sem), d.target
                        else:
                            key, val = ("e", d.eng), d.count
                        if waits.get(key, 0) < val:
                            waits[key] = val
                    if op.is_dma and op.target > 16:
                        key = ("d", op.sem)
                        if waits.get(key, 0) < op.target - 16:
                            waits[key] = op.target - 16
                    for key, val in waits.items():
                        if waited.get(key, 0) >= val:
                            continue
                        waited[key] = val
                        s = dsem[key[1]] if key[0] == "d" else esem[key[1]]
                        eng.wait_ge(s, val)
                    if op.fn is None:
                        continue
                    ins = op.fn(eng)
                    if op.is_dma:
                        ins.then_inc(dsem[op.sem], 16)
                    elif op.signal:
                        ins.then_inc(esem[ename], 1)

            @block.sync
            def _(eng):
                run_queue("sp", eng)

            @block.tensor
            def _(eng):
                run_queue("pe", eng)

            @block.scalar
            def _(eng):
                run_queue("act", eng)

            @block.vector
            def _(eng):
                run_queue("dve", eng)

            @block.gpsimd
            def _(eng):
                run_queue("pool", eng)


class K:
    def __init__(self, P):
        self.P = P

    def mm(self, out, lhsT, rhs, start=True, stop=True, r=(), w=()):
        self.P.add("pe", lambda e: e.matmul(out, lhsT=lhsT, rhs=rhs, start=start, stop=stop, skip_group_check=True), r, w)

    def tr(self, out, in_, ident, r=(), w=()):
        self.P.add("pe", lambda e: e.transpose(out=out, in_=in_, identity=ident), r, w)

    def act(self, out, in_, func, r=(), w=(), scale=1.0, accum=None):
        if accum is None:
            self.P.add("act", lambda e: e.activation(out=out, in_=in_, func=func, scale=scale), r, w)
        else:
            self.P.add("act", lambda e: e.activation(out=out, in_=in_, func=func, scale=scale, accum_out=accum), r, w)

    def tt(self, eng, out, in0, in1, op, r=(), w=()):
        self.P.add(eng, lambda e: e.tensor_tensor(out=out, in0=in0, in1=in1, op=op), r, w)

    def ts(self, eng, out, in0, s1, s2, op0, op1=None, r=(), w=()):
        if op1 is None:
            self.P.add(eng, lambda e: e.tensor_scalar(out=out, in0=in0, scalar1=s1, scalar2=None, op0=op0), r, w)
        else:
            self.P.add(eng, lambda e: e.tensor_scalar(out=out, in0=in0, scalar1=s1, scalar2=s2, op0=op0, op1=op1), r, w)

    def stt(self, out, in0, scalar, in1, op0, op1, r=(), w=()):
        self.P.add("dve", lambda e: e.scalar_tensor_tensor(out=out, in0=in0, scalar=scalar, in1=in1, op0=op0, op1=op1), r, w)

    def cp(self, eng, out, in_, r=(), w=()):
        if eng == "act":
            self.P.add("act", lambda e: e.activation(out=out, in_=in_, func=AF.Copy), r, w)
        else:
            self.P.add(eng, lambda e: e.tensor_copy(out=out, in_=in_), r, w)

    def red(self, out, in_, op, r=(), w=()):
        self.P.add("dve", lambda e: e.tensor_reduce(out=out, in_=in_, axis=AX.X, op=op), r, w)

    def memset(self, eng, ap, val, w=()):
        self.P.add(eng, lambda e: e.memset(ap, val), (), w)

    def dma(self, q, out, in_, r=(), w=(), is_out=False, slow=False):
        if slow:
            self.P.add(q, lambda e: e.dma_start(out=out, in_=in_, allow_slow_non_contiguous=True), r, w, dma=True, out=is_out)
        else:
            self.P.add(q, lambda e: e.dma_start(out=out, in_=in_), r, w, dma=True, out=is_out)

    def gather(self, out, in_, idx, r=(), w=()):
        self.P.add("pool", lambda e: e.indirect_dma_start(out=out, out_offset=None, in_=in_,
                                                          in_offset=bass.IndirectOffsetOnAxis(ap=idx, axis=0)),
                   r, w, dma=True)


def v3(ap, a):
    return ap.rearrange("p (a b) -> p a b", a=a)


def bc(ap, shape):
    return ap.to_broadcast(shape)


def win_names(c0, n):
    return ["win%d" % j for j in range(c0 // 512, (c0 + n - 1) // 512 + 1)]


def front(k, t, n, x_src, cs_name, sfx, is_prompt):
    P = k.P
    N = slice(0, n)
    x, B1, B2, B3, h, hT = t["x"], t["B1"], t["B2"], t["B3"], t["h"], t["hT"]
    TH, ZB, vn_bf = B1[:, 0:512], B1[:, 512:1024], h[:, 0:512]
    tg, za_s, bmix, gw = t["tg" + sfx], t["za_s" + sfx], t["bmix" + sfx], t["gw" + sfx]
    n_tg, n_za, n_bm, n_gw = "tg" + sfx, "za_s" + sfx, "bmix" + sfx, "gw" + sfx
    B1W = ["B1a", "B1b"]
    st, st2, st3 = t["st"], t["st2"], t["st3"]
    pA, pB, pT = t["pA"], t["pB"], t["pT"]
    pTv = pT[:].rearrange("p (a b) -> p a b", a=8)
    k.dma("sp", x[N, :], x_src, w=["x"])
    k.act(B1[N, :], x[N, :], AF.Square, r=["x"], w=B1W + ["st"], accum=st[N, 0:1])
    k.ts("dve", st[N, 1:2], st[N, 0:1], 1.0 / D, EPS, ALU.mult, ALU.add, r=["st"], w=["st"])
    k.tt("pool", st[N, 2:3], st[N, 1:2], t["nh"][N, 0:1], ALU.pow, r=["st", "nh"], w=["st"])
    k.stt(B1[N, :], x[N, :], st[N, 2:3], t["sc1"][N, :], ALU.mult, ALU.mult, r=["x", "st", "sc1"], w=B1W)
    k.tt("dve", h[N, :], B1[N, :], t["sh"][N, :], ALU.add, r=B1W + ["sh"], w=["h"])
    yield
    for kc in range(8):
        k.tr(pTv[:, kc, 0:n], h[N, kc * 128:(kc + 1) * 128], t["identb"][N, N], r=["h", "identb"], w=["pT"])
    k.cp("act", hT[:, :, 0:n], pTv[:, :, 0:n], r=["pT"], w=["hT"])
    yield

    def proj(bank, bname, dst, c0, ncol):
        for kc in range(8):
            k.mm(bank[N, dst:dst + ncol], hT[:, kc, 0:n], t["Win"][:, kc, c0:c0 + ncol], start=(kc == 0), stop=(kc == 7),
                 r=["hT"] + win_names(c0, ncol), w=[bname])

    def normrope(bank, bname, A, Bd, gain, dst_ap, dst_name):
        H = A * Bd
        HW = H * 64

        def v4(ap):
            return ap.rearrange("p (a b d) -> p a b d", a=A, b=Bd)

        k.cp("act", B2[N, 0:HW], bank[N, 0:HW], r=[bname], w=["B2"])
        k.act(B3[N, 0:HW], B2[N, 0:HW], AF.Square, r=["B2"], w=["B3"])
        k.red(st2[N, 0:H], v3(B3[N, 0:HW], H), ALU.add, r=["B3"], w=["st2"])
        k.ts("dve", st2[N, 8:8 + H], st2[N, 0:H], 1.0 / 64, EPS, ALU.mult, ALU.add, r=["st2"], w=["st2"])
        k.tt("pool", st2[N, 16:16 + H], st2[N, 8:8 + H], t["nh"][N, 0:H], ALU.pow, r=["st2", "nh"], w=["st2"])
        b2 = v4(B2[N, 0:HW])
        rs = st2[N, 16:16 + H].rearrange("p (a b) -> p a b", a=A).unsqueeze(3)
        k.tt("dve", b2, b2, bc(rs, [n, A, Bd, 64]), ALU.mult, r=["B2", "st2"], w=["B2"])
        k.tt("dve", b2, b2, bc(gain[N, :].unsqueeze(1).unsqueeze(1), [n, A, Bd, 64]), ALU.mult, r=["B2", "gains"], w=["B2"])
        cs = t["cs"]
        cosb = bc(cs[N, 0:32].unsqueeze(1).unsqueeze(1), [n, A, Bd, 32])
        sinb = bc(cs[N, 32:64].unsqueeze(1).unsqueeze(1), [n, A, Bd, 32])
        x1, x2 = b2[:, :, :, 0:32], b2[:, :, :, 32:64]
        r1 = t["R1"][N, 0:H * 32].rearrange("p (a b d) -> p a b d", a=A, b=Bd)
        r2 = t["R2"][N, 0:H * 32].rearrange("p (a b d) -> p a b d", a=A, b=Bd)
        k.tt("dve", r1, x1, cosb, ALU.mult, r=["B2", cs_name], w=["R1"])
        k.tt("dve", r2, x2, sinb, ALU.mult, r=["B2", cs_name], w=["R2"])
        k.tt("dve", dst_ap[:, :, :, 0:32], r1, r2, ALU.subtract, r=["R1", "R2"], w=[dst_name])
        k.tt("dve", r1, x2, cosb, ALU.mult, r=["B2", cs_name], w=["R1"])
        k.tt("dve", r2, x1, sinb, ALU.mult, r=["B2", cs_name], w=["R2"])
        k.tt("dve", dst_ap[:, :, :, 32:64], r1, r2, ALU.add, r=["R1", "R2"], w=[dst_name])

    proj(pA, "pA", 0, C_Q, 512)
    qdst = t["q_bf"][N, :].rearrange("p (r g d) -> p g r d", r=4, g=2)
    normrope(pA, "pA", 2, 4, t["qg"], qdst, "q_bf")
    yield
    proj(pB, "pB", 0, C_K, 384)
    normrope(pB, "pB", 6, 1, t["kg"], t["kf"][N, :].rearrange("p (a b d) -> p a b d", a=6, b=1), "kf")
    k.cp("act", t["k_bf"][N, :], t["kf"][N, :], r=["kf"], w=["k_bf"])
    yield
    proj(pA, "pA", 0, C_V, 384)
    proj(pA, "pA", 384, C_NSA, 24)
    k.cp("act", t["vf"][N, :], pA[N, 0:384], r=["pA"], w=["vf"])
    k.act(t["gwt"][N, :], pA[N, 384:408], AF.Tanh, r=["pA"], w=["gwt"], scale=0.5)
    k.ts("dve", gw[N, :], t["gwt"][N, :], 0.5, 0.5, ALU.mult, ALU.add, r=["gwt"], w=[n_gw])
    yield
    proj(pB, "pB", 0, C_ZA, 512)
    k.act(TH[N, :], pB[N, :], AF.Tanh, r=["pB"], w=["B1a"], scale=0.5)
    k.stt(za_s[N, :], TH[N, :], 1.0, pB[N, :], ALU.add, ALU.mult, r=["B1a", "pB"], w=[n_za])
    yield
    proj(pA, "pA", 0, C_VV, 512)
    k.cp("act", t["vn_f"][N, :], pA[N, :], r=["pA"], w=["vn_f"])
    P.add("dve", lambda e: e.bn_stats(out=st3[N, 0:6], in_=t["vn_f"][N, :]), ["vn_f"], ["st3"])
    P.add("dve", lambda e: e.bn_aggr(out=st3[N, 6:8], in_=st3[N, 0:6]), ["st3"], ["st3"])
    k.ts("dve", st3[N, 8:9], st3[N, 7:8], EPS, None, ALU.add, r=["st3"], w=["st3"])
    k.tt("pool", st3[N, 9:10], st3[N, 8:9], t["nh"][N, 0:1], ALU.pow, r=["st3", "nh"], w=["st3"])
    k.ts("dve", t["vn_f"][N, :], t["vn_f"][N, :], st3[N, 6:7], st3[N, 9:10], ALU.subtract, ALU.mult, r=["vn_f", "st3"], w=["vn_f"])
    k.tt("dve", t["vn_f"][N, :], t["vn_f"][N, :], t["vng"][N, :], ALU.mult, r=["vn_f", "gains"], w=["vn_f"])
    k.tt("dve", t["vn_f"][N, :], t["vn_f"][N, :], t["vnb"][N, :], ALU.add, r=["vn_f", "gains"], w=["vn_f"])
    yield
    proj(pB, "pB", 0, C_ZB, 512)
    k.act(TH[N, :], pB[N, :], AF.Tanh, r=["pB"], w=["B1a"], scale=0.5)
    k.stt(ZB[N, :], TH[N, :], 1.0, pB[N, :], ALU.add, ALU.mult, r=["B1a", "pB"], w=["B1b"])
    yield
    proj(pA, "pA", 0, C_U, 512)
    k.tt("dve", ZB[N, :], pA[N, :], ZB[N, :], ALU.mult, r=["pA", "B1b"], w=["B1b"])
    yield
    if is_prompt:
        k.cp("act", vn_bf[N, :], t["vn_f"][N, :], r=["vn_f"], w=["h"])
        for g in range(4):
            k.mm(pB[N, g * 128:(g + 1) * 128], t["wsT"][:, g, :], vn_bf[N, g * 128:(g + 1) * 128],
                 r=["wsT", "h"], w=["pB"])
        k.tt("dve", v3(B2[N, :], 4), v3(pB[N, :], 4), bc(t["bsT"][N, :].unsqueeze(2), [n, 4, 128]), ALU.add,
             r=["pB", "bsT"], w=["B2"])
    else:
        k.tt("dve", v3(B2[N, :], 4), v3(t["vn_f"][N, :], 4), bc(t["w00"][N, :].unsqueeze(2), [n, 4, 128]), ALU.mult,
             r=["vn_f", "w00"], w=["B2"])
        k.tt("dve", v3(B2[N, :], 4), v3(B2[N, :], 4), bc(t["b0"][N, :].unsqueeze(2), [n, 4, 128]), ALU.add,
             r=["B2", "w00"], w=["B2"])
    k.tt("dve", bmix[N, :], B2[N, :], ZB[N, :], ALU.mult, r=["B2", "B1b"], w=[n_bm])
    yield
    banks = [(pB, "pB"), (pA, "pA")]
    for j in range(4):
        bank, bname = banks[j % 2]
        proj(bank, bname, 0, C_GA + j * 512, 512)
        k.act(tg[N, j * 512:(j + 1) * 512], bank[N, :], AF.Tanh, r=[bname], w=[n_tg], scale=0.5)
        yield


def tail(k, t, n, y_dst, x_src, sfx, gate=None):
    N = slice(0, n)
    pA, pB, pT = t["pA"], t["pB"], t["pT"]
    pTv8 = pT[:].rearrange("p (a b) -> p a b", a=8)
    tg, za_s, bmix = t["tg" + sfx], t["za_s" + sfx], t["bmix" + sfx]
    n_tg, n_za, n_bm = "tg" + sfx, "za_s" + sfx, "bmix" + sfx
    mc = t["mc"]
    B1W = ["B1a", "B1b"]
    ozd = t["oz"][N, :].rearrange("p (r g d) -> p g r d", r=4, g=2)
    k.tt("dve", ozd, t["o_a"][N, :].rearrange("p (g r d) -> p g r d", g=2, r=4),
         za_s[N, :].rearrange("p (g r d) -> p g r d", g=2, r=4), ALU.mult, r=["o_a", n_za], w=["oz"])
    for r in range(4):
        k.tr(pTv8[:, r, 0:n], t["oz"][N, r * 128:(r + 1) * 128], t["identb"][N, N], r=["oz", "identb"], w=["pT"])
    for c in range(4):
        k.tr(pTv8[:, 4 + c, 0:n], bmix[N, c * 128:(c + 1) * 128], t["identb"][N, N], r=[n_bm, "identb"], w=["pT"])
    k.cp("act", t["ozT"][:, :, 0:n], pTv8[:, :, 0:n], r=["pT"], w=["ozT"])
    yield
    for half in range(2):
        cs = slice(half * 512, (half + 1) * 512)
        for r in range(4):
            k.mm(pA[N, :], t["ozT"][:, r, 0:n], t["WA"][:, r, cs], start=(r == 0), stop=(r == 3), r=["ozT", "WA"], w=["pA"])
        for c in range(4):
            k.mm(pB[N, :], t["ozT"][:, 4 + c, 0:n], t["WB"][:, c, cs], start=(c == 0), stop=(c == 3), r=["ozT", "WB"], w=["pB"])
        k.stt(t["B2"][N, :], tg[N, half * 512:(half + 1) * 512], 1.0, pA[N, :], ALU.add, ALU.mult,
              r=[n_tg, "pA"], w=["B2"])
        k.stt(t["B3"][N, :], tg[N, 1024 + half * 512:1024 + (half + 1) * 512], 1.0, pB[N, :], ALU.add, ALU.mult,
              r=[n_tg, "pB"], w=["B3"])
        k.tt("dve", mc[N, cs], t["B2"][N, :], t["B3"][N, :], ALU.add, r=["B2", "B3"], w=["mc"])
        yield
    for kc in range(8):
        k.tr(pTv8[:, kc, 0:n], mc[N, kc * 128:(kc + 1) * 128], t["identb"][N, N], r=["mc", "identb"], w=["pT"])
    k.cp("act", t["hT"][:, :, 0:n], pTv8[:, :, 0:n], r=["pT"], w=["hT"])
    yield
    banks = [(pA, "pA"), (pB, "pB")]
    for half in range(2):
        bank, bname = banks[half]
        cs = slice(half * 512, (half + 1) * 512)
        for kc in range(8):
            k.mm(bank[N, :], t["hT"][:, kc, 0:n], t["Wout"][:, kc, cs], start=(kc == 0), stop=(kc == 7),
                 r=["hT", "Wout"], w=[bname])
        if gate is None:
            gap, gname = t["gq"][N, cs], "gq"
        else:
            gap, gname = gate[half][0][N, 0:512], gate[half][1]
        k.tt("dve", t["B1"][N, cs], bank[N, :], gap, ALU.mult, r=[bname, gname], w=["B1a" if half == 0 else "B1b"])
        yield
    k.dma("sp", t["x"][N, :], x_src, w=["x"])
    k.tt("dve", t["x"][N, :], t["B1"][N, :], t["x"][N, :], ALU.add, r=B1W + ["x"], w=["x"])
    k.dma("sp", y_dst, t["x"][N, :], r=["x"], is_out=True)


def mod_pass(k, t, d, n, cT, cT_name, bcast):
    N = slice(0, n)
    stg = [t["G0"], t["G1"]]
    banks = [(t["pA"], "pA"), (t["pB"], "pB")]
    wada = d["w_ada"].rearrange("(kc p) n -> p kc n", p=128)
    for j in range(12):
        sg, sname = stg[j % 2], "G%d" % (j % 2)
        bank, bname = banks[j % 2]
        sgv = sg[:].rearrange("p (kc n) -> p kc n", kc=8)
        k.dma("sp", sgv, wada[:, :, j * 256:(j + 1) * 256], w=[sname])
        k.dma("sp", t["bada"][0:1, :], d["b_ada"][0:1, j * 256:(j + 1) * 256], w=["bada"])
        if not bcast:
            for kc in range(8):
                k.mm(bank[N, 0:256], cT[:, kc, 0:n], sgv[:, kc, :], start=(kc == 0), stop=False, r=[cT_name, sname], w=[bname])
            k.mm(bank[N, 0:256], t["ones0"][:, 0:n], t["bada"][:, :], start=False, stop=True, r=["ones0", "bada"], w=[bname])
            src = bank[N, 0:256]
        else:
            for kc in range(8):
                k.mm(bank[0:1, 0:256], cT[:, kc:kc + 1], sgv[:, kc, :], start=(kc == 0), stop=False, r=[cT_name, sname], w=[bname])
            k.mm(bank[0:1, 0:256], t["ones0"][:, 0:1], t["bada"][:, :], start=False, stop=True, r=["ones0", "bada"], w=[bname])
            k.cp("act", t["modrow"][0:1, :], bank[0:1, 0:256], r=[bname], w=["modrow"])
            k.mm(bank[N, 256:512], t["onesf"][0:1, 0:n], t["modrow"][0:1, :], r=["onesf", "modrow"], w=[bname])
            src = bank[N, 256:512]
        cs = slice((j % 4) * 256, (j % 4 + 1) * 256)
        if j < 4:
            k.cp("act", t["sh"][N, cs], src, r=[bname], w=["sh"])
        elif j < 8:
            k.stt(t["sc1"][N, cs], src, 1.0, t["normg"][N, cs], ALU.add, ALU.mult, r=[bname, "normg"], w=["sc1"])
        else:
            k.act(t["gq"][N, cs], src, AF.Copy, r=[bname], w=["gq"], scale=0.25)


def build_program():
    nc = bass.Bass("TRN2", target_bir_lowering=False)
    d = {}

    def din(name, shape, dt=F32):
        d[name] = nc.dram_tensor(name, shape, dt, kind="ExternalInput").ap()

    def dout(name, shape):
        d[name] = nc.dram_tensor(name, shape, F32, kind="ExternalOutput").ap()

    din("xp", [S, D]); din("xs", [NSMP, D]); din("cpv", [D]); din("csv", [NSMP, D])
    for nm in ("kcmp", "vcmp", "kslc", "vslc"):
        din(nm, [2560 * 8, 2048])
    din("kwin", [NSMP, 512, 128]); din("vwin", [NSMP, 512, 128])
    din("ptrep", [128, NSMP], I32)
    din("w_ada", [D, 3 * D]); din("b_ada", [1, 3 * D]); din("norm_g", [1, D]); din("w_in", [D, DIN])
    din("qg", [1, 64]); din("kg", [1, 64]); din("pek", [32, 64]); din("pev", [32, 64])
    din("wck", [64, 64]); din("wcv", [64, 64]); din("vng", [1, 512]); din("vnb", [1, 512])
    din("ws", [4, 128, 128]); din("bs", [4, 128]); din("wbra", [512, D]); din("wbrb", [512, D]); din("wout", [D, D])
    din("rope", [S + 1, 64]); din("tri_le", [128, 128]); din("tri_gt", [128, 128]); din("identf_c", [128, 128])
    din("pool4", [128, 4]); din("pair", [64, 32]); din("e2", [64, 2048]); din("cmpbase", [128, 512]); din("trineg_le", [128, 512]); din("trineg_gt", [128, 512])
    din("impbias", [NT, 128, 32]); din("pool2", [128, 64]); din("e4", [32, 128]); din("rmod8", [128, 1])
    d["gqs"] = nc.dram_tensor("gqs", [NSMP, D], F32, kind="Internal").ap()
    dout("yp", [S, D]); dout("ys", [NSMP, D])
    for nm in ("pk_cmp", "pv_cmp", "pk_slc", "pv_slc"):
        dout(nm, [S, 128])
    dout("pk_win", [512, 128]); dout("pv_win", [512, 128])
    for nm in ("sk_cmp", "sv_cmp", "sk_slc", "sv_slc"):
        dout(nm, [NSMP, 128])
    dout("sk_win", [NSMP, 512, 128]); dout("sv_win", [NSMP, 512, 128]); dout("svch", [NSMP, 512])

    with contextlib.ExitStack() as es:
        t = {}

        def sb(name, shape, dt, scope=es):
            t[name] = scope.enter_context(nc.sbuf_tensor("sb_" + name, shape, dt))
            return t[name]

        def ps(name, shape, dt, scope=es):
            t[name] = scope.enter_context(nc.psum_tensor("ps_" + name, shape, dt))
            return t[name]

        sb("Win", [128, 8, DIN], BF16)
        sb("sc1", [128, D], F32); sb("sh", [128, D], F32); sb("gq", [128, D], F32)
        sb("identb", [128, 128], BF16); sb("identf", [128, 128], F32); sb("tri_le", [128, 128], BF16); sb("tri_gt", [128, 128], BF16)
        sb("qg", [128, 64], F32); sb("kg", [128, 64], F32); sb("vng", [128, 512], F32); sb("vnb", [128, 512], F32)
        sb("nh", [128, 8], F32); sb("onesf", [128, 128], F32)
        sb("wsT", [128, 4, 128], BF16); sb("bsT", [128, 4], F32)
        sb("Wbdk", [128, 128], BF16); sb("Wbdv", [128, 128], BF16); sb("pebar", [128, 2], F32)
        sb("pool4", [128, 4], BF16); sb("e2", [128, 2048], BF16)
        sb("x", [128, D], F32); sb("B1", [128, D], F32); sb("B2", [128, 512], F32); sb("B3", [128, 512], F32)
        sb("h", [128, D], BF16); sb("hT", [128, 8, 128], BF16); sb("R1", [128, 256], F32); sb("R2", [128, 256], F32)
        sb("st", [128, 4], F32); sb("st2", [128, 24], F32); sb("st3", [128, 12], F32)
        sb("q_bf", [128, 512], BF16); sb("kf", [128, 384], F32); sb("vf", [128, 384], F32); sb("k_bf", [128, 384], BF16)
        sb("gwt", [128, 24], F32); sb("gw0", [128, 24], F32); sb("gw1", [128, 24], F32); sb("za_s0", [128, 512], BF16); sb("za_s1", [128, 512], BF16)
        sb("vn_f", [128, 512], F32); sb("tg0", [128, 2048], BF16); sb("tg1", [128, 2048], BF16)
        sb("bmix0", [128, 512], BF16); sb("bmix1", [128, 512], BF16); sb("o_a", [128, 512], F32); sb("oz", [128, 512], BF16); sb("ozT", [128, 8, 128], BF16)
        sb("cs", [128, 64], F32); sb("mc", [128, D], BF16)
        sb("rden", [128, 8], F32); sb("cx", [128, 8], F32); t["tmpO"] = t["B3"]
        ps("pA", [128, 512], F32); ps("pB", [128, 512], F32); ps("pT", [128, 1024], BF16)
        ps("pM", [128, 512], F32); ps("pS0", [128, 512], F32); ps("pS1", [128, 512], F32)
        ps("pO0", [128, 512], F32); ps("pO1", [128, 512], F32)

        with contextlib.ExitStack() as s1:
            P = Prog(nc, "a")
            k = K(P)
            sb("normg", [128, D], F32, s1)
            sb("G0", [128, 2048], F32, s1); sb("G1", [128, 2048], F32, s1)
            sb("Gw0", [128, 512], F32, s1); sb("Gw1", [128, 512], F32, s1)
            sb("Gb0", [128, 2048], BF16, s1); sb("Gb1", [128, 2048], BF16, s1); sb("pool2b", [128, 64], BF16, s1); sb("Gwb0", [128, 512], BF16, s1); sb("Gwb1", [128, 512], BF16, s1)
            sb("PTb", [128, 160], BF16, s1); sb("vfb", [128, 256], BF16, s1); sb("Enewb", [128, 2 * NSMP * 8], BF16, s1)
            sb("bada", [128, 256], F32, s1); sb("modrow", [1, 256], F32, s1); sb("ones0", [128, 128], F32, s1)
            sb("cTs", [128, 8, NSMP], F32, s1)
            sb("cp8", [128, 8], F32, s1)
            sb("w00", [128, 4], F32, s1); sb("b0", [128, 4], F32, s1)
            sb("pe2", [32, 128], F32, s1); sb("o32", [32, 2], F32, s1)
            sb("ptrep", [128, NSMP], I32, s1); sb("rmod8", [128, 1], F32, s1); sb("idx", [128, NSMP], I32, s1)
            sb("pool2", [128, 64], F32, s1); sb("pairf", [64, 32], F32, s1); sb("e4", [32, 128], BF16, s1)
            sb("QTs", [128, 2, 4 * NSMP], BF16, s1); sb("enr", [NSMP, 16], F32, s1); sb("en", [NSMP, 16], F32, s1); sb("Enew", [128, 2 * NSMP * 8], F32, s1)
            sb("pTk", [128, 64], BF16, s1); sb("pTv", [128, 64], BF16, s1); sb("KcTs", [128, 64], BF16, s1); sb("Vcs", [64, 128], F32, s1)
            sb("PcT", [64, NSMP * 8], F32, s1); sb("KTs", [128, 20, 128], BF16, s1)
            sb("PTs", [128, 160], F32, s1); sb("PTsum", [128, 16], F32, s1)
            sb("rDc", [32, 128], F32, s1); sb("impn", [32, 128], F32, s1); sb("impT", [32, 32], F32, s1)
            sb("impS", [32, 32], F32, s1); sb("bias0", [32, 32], F32, s1); sb("m8s", [32, 8], F32, s1)
            sb("selS", [32, 32], BF16, s1); sb("selTs", [32, 32], BF16, s1); sb("Msk", [128, 32], F32, s1)
            t["OTn"] = t["B1"][:, 0:384]; t["rDall"] = t["B1"][:, 384:768]; t["OT1"] = t["B2"][0:64, 0:384]; t["wsl"] = t["B3"][:, 0:128]

            k.dma("pool", t["identb"][:], d["identf_c"], w=["identb"])
            k.dma("sp", t["identf"][:], d["identf_c"], w=["identf"])
            k.dma("pool", t["tri_le"][:], d["tri_le"], w=["tri_le"])
            k.dma("pool", t["tri_gt"][:], d["tri_gt"], w=["tri_gt"])
            k.dma("pool", t["pool4"][:], d["pool4"], w=["pool4"])
            k.memset("pool", t["e2"][64:128, :], 0.0, w=["e2"])
            k.dma("pool", t["e2"][0:64, :], d["e2"], w=["e2"])
            k.dma("pool", t["e4"][:], d["e4"], w=["e4"])
            k.dma("sp", t["pool2"][:], d["pool2"], w=["pool2"])
            k.dma("pool", t["pool2b"][:], d["pool2"], w=["pool2b"])
            k.dma("sp", t["pairf"][:], d["pair"], w=["pairf"])
            k.dma("sp", t["rmod8"][:], d["rmod8"], w=["rmod8"])
            k.dma("sp", t["ptrep"][:], d["ptrep"], w=["ptrep"])
            k.dma("sp", t["qg"][:], d["qg"].partition_broadcast(128), w=["gains"])
            k.dma("sp", t["kg"][:], d["kg"].partition_broadcast(128), w=["gains"])
            k.dma("sp", t["vng"][:], d["vng"].partition_broadcast(128), w=["gains"])
            k.dma("sp", t["vnb"][:], d["vnb"].partition_broadcast(128), w=["gains"])
            k.dma("sp", t["normg"][:], d["norm_g"].partition_broadcast(128), w=["normg"])
            k.dma("sp", t["bsT"][:], d["bs"].rearrange("g i -> i g"), w=["bsT"], slow=True)
            k.dma("sp", t["w00"][0:NSMP, :], d["ws"][:, 0, 0:1].rearrange("g o -> o g").partition_broadcast(NSMP), w=["w00"], slow=True)
            k.dma("sp", t["b0"][0:NSMP, :], d["bs"][:, 0:1].rearrange("g o -> o g").partition_broadcast(NSMP), w=["w00"], slow=True)
            k.memset("pool", t["nh"][:], -0.5, w=["nh"])
            k.memset("pool", t["Enew"][:], 0.0, w=["Enew"])
            k.memset("pool", t["bada"][:], 0.0, w=["bada"])
            k.memset("pool", t["ones0"][:], 0.0, w=["ones0"])
            k.memset("pool", t["ones0"][0:1, :], 1.0, w=["ones0"])
            k.memset("pool", t["QTs"][:], 0.0, w=["QTs"])
            k.memset("pool", t["vf"][:], 0.0, w=["vf"])
            k.memset("pool", t["Enewb"][:], 0.0, w=["Enewb"])
            k.memset("pool", t["onesf"][:], 1.0, w=["onesf"])
            k.memset("pool", t["o32"][:], 1.0 / 32, w=["o32"])
            k.memset("pool", t["Wbdk"][:], 0.0, w=["Wbdk"])
            k.memset("pool", t["Wbdv"][:], 0.0, w=["Wbdv"])
            k.memset("pool", t["bias0"][:], 0.0, w=["bias0"])
            k.memset("pool", t["bias0"][:, 0:1], 1.0e4, w=["bias0"])
            for g in range(2):
                gs = slice(g * 64, (g + 1) * 64)
                k.dma("pool", t["Wbdk"][gs, gs], d["wck"], w=["Wbdk"])
                k.dma("pool", t["Wbdv"][gs, gs], d["wcv"], w=["Wbdv"])
                k.dma("sp", t["pe2"][:, gs], d["pek"], w=["pe2k"])
            k.mm(t["pM"][:, 0:1], t["pe2"][:, :], t["o32"][:, 0:1], r=["pe2k", "o32"], w=["pM"])
            k.cp("dve", t["pebar"][:, 0:1], t["pM"][:, 0:1], r=["pM"], w=["pebar"])
            for g in range(2):
                gs = slice(g * 64, (g + 1) * 64)
                k.dma("sp", t["pe2"][:, gs], d["pev"], r=[], w=["pe2k"])
            k.mm(t["pM"][:, 0:1], t["pe2"][:, :], t["o32"][:, 0:1], r=["pe2k", "o32"], w=["pM"])
            k.cp("dve", t["pebar"][:, 1:2], t["pM"][:, 0:1], r=["pM"], w=["pebar"])
            for g in range(4):
                k.dma("sp", t["wsl"], d["ws"][g], w=["B3"])
                k.tr(t["pM"][:, 0:128], t["wsl"], t["identf"][:], r=["B3", "identf"], w=["pM"])
                k.tt("dve", t["wsT"][:, g, :], t["pM"][:, 0:128], t["tri_le"][:], ALU.mult, r=["pM", "tri_le"], w=["wsT"])
            winv = d["w_in"].rearrange("(kc p) n -> p kc n", p=128)
            for j in range(11):
                c0, c1 = j * 512, min(DIN, (j + 1) * 512)
                k.dma("pool", t["Win"][:, :, c0:c1], winv[:, :, c0:c1], w=["win%d" % j])
            k.dma("sp", t["cp8"][:], d["cpv"].rearrange("(kc p) -> p kc", p=128), w=["cp8"], slow=True)
            k.dma("sp", t["x"][0:NSMP, :], d["csv"], w=["x"])
            pMv = t["pM"][:, 0:8 * NSMP].rearrange("p (a b) -> p a b", a=8)
            for kc in range(8):
                k.tr(pMv[:, kc, :], t["x"][0:NSMP, kc * 128:(kc + 1) * 128], t["identf"][0:NSMP, 0:NSMP],
                     r=["x", "identf"], w=["pM"])
            k.cp("dve", t["cTs"][:], pMv, r=["pM"], w=["cTs"])
            mod_pass(k, t, d, NSMP, t["cTs"], "cTs", False)
            k.ts("dve", t["idx"][:], t["ptrep"][:], 8.0, t["rmod8"][:, 0:1], ALU.mult, ALU.add, r=["ptrep", "rmod8"], w=["idx"])
            k.dma("sp", t["cs"][0:NSMP, :], d["rope"][S:S + 1, :].partition_broadcast(NSMP), w=["cs"])
            for _ in front(k, t, NSMP, d["xs"], "cs", "0", False):
                pass
            n = NSMP
            N = slice(0, n)
            k.dma("sp", d["sk_cmp"], t["kf"][N, 0:128], r=["kf"], is_out=True)
            k.dma("sp", d["sk_slc"], t["kf"][N, 128:256], r=["kf"], is_out=True)
            k.dma("sp", d["sv_cmp"], t["vf"][N, 0:128], r=["vf"], is_out=True)
            k.dma("sp", d["sv_slc"], t["vf"][N, 128:256], r=["vf"], w=["d_svslc"], is_out=True)
            k.dma("sp", d["svch"], t["vn_f"][N, :], r=["vn_f"], is_out=True)
            k.dma("sp", d["sk_win"][:, 0:511, :], d["kwin"][:, 1:512, :], is_out=True)
            k.dma("sp", d["sv_win"][:, 0:511, :], d["vwin"][:, 1:512, :], is_out=True)
            k.dma("sp", d["sk_win"][:, 511, :], t["kf"][N, 256:384], r=["kf"], is_out=True)
            k.dma("sp", d["sv_win"][:, 511, :], t["vf"][N, 256:384], r=["vf"], w=["d_svwin"], is_out=True)
            pTv8 = t["pT"][:].rearrange("p (a b) -> p a b", a=8)
            for r in range(4):
                k.tr(pTv8[:, r, 0:n], t["q_bf"][N, r * 128:(r + 1) * 128], t["identb"][N, N], r=["q_bf", "identb"], w=["pT"])
            k.cp("act", t["QTs"][0:64, 0, :].rearrange("p (r s) -> p r s", r=4), pTv8[0:64, 0:4, 0:n], r=["pT"], w=["QTs"])
            k.cp("act", t["QTs"][64:128, 1, :].rearrange("p (r s) -> p r s", r=4), pTv8[64:128, 0:4, 0:n], r=["pT"], w=["QTs"])
            Env = t["Enew"][:].rearrange("p (x s h) -> p x s h", x=2, s=NSMP)
            Env16 = t["Enew"][0:NSMP, :].rearrange("p (x s h) -> p x s h", x=2, s=NSMP)
            for xx in range(2):
                kcol = t["k_bf"][N, 128 + xx * 128:256 + xx * 128].rearrange("p (g d) -> p g d", g=2).unsqueeze(2)
                k.tt("dve", t["B2"][N, :].rearrange("p (g r d) -> p g r d", g=2, r=4),
                     t["q_bf"][N, :].rearrange("p (r g d) -> p g r d", r=4, g=2), bc(kcol, [n, 2, 4, 64]), ALU.mult,
                     r=["q_bf", "k_bf"], w=["B2"])
                k.red(t["enr"][:, xx * 8:(xx + 1) * 8], v3(t["B2"][N, :], 8), ALU.add, r=["B2"], w=["enr"])
            k.act(t["en"][:], t["enr"][:], AF.Exp, r=["enr"], w=["en"], scale=SCL)
            for xx in range(2):
                k.tt("dve", Env16[:, xx, :, :], bc(t["identf"][N, N].unsqueeze(2), [n, NSMP, 8]),
                     bc(t["en"][:, xx * 8:(xx + 1) * 8].unsqueeze(1), [n, NSMP, 8]), ALU.mult, r=["identf", "en"], w=["Enew"])
            k.cp("act", t["vfb"][:], t["vf"][:, 128:384], r=["vf"], w=["vfb"])
            k.cp("act", t["Enewb"][0:NSMP, :], t["Enew"][0:NSMP, :], r=["Enew"], w=["Enewb"])
            Envb = t["Enewb"][:].rearrange("p (x s h) -> p x s h", x=2, s=NSMP)
            pS, pO, pD, pM = t["pS0"], t["pO0"], t["pO1"], t["pM"]
            pOv = pO[:, 0:384].rearrange("p (s x h) -> p s x h", s=NSMP, x=3)
            pDv = pD[:, 0:384].rearrange("p (s x h) -> p s x h", s=NSMP, x=3)
            PcTv = t["PcT"][:].rearrange("p (s h) -> p s h", s=NSMP)
            for s in range(NSMP):
                k.gather(t["G0"][:], d["kcmp"], t["idx"][:, s:s + 1], r=["idx"], w=["G0"])
                k.gather(t["G1"][:], d["vcmp"], t["idx"][:, s:s + 1], r=["idx"], w=["G1"])
                k.cp("act", t["Gb0"][:], t["G0"][:], r=["G0"], w=["Gb0"])
                k.cp("act", t["Gb1"][:], t["G1"][:], r=["G1"], w=["Gb1"])
                for tt_ in range(16):
                    k.mm(pM[:, 0:64], t["Gb0"][:, tt_ * 128:(tt_ + 1) * 128], t["pool2b"][:], start=(tt_ == 0), stop=(tt_ == 15),
                         r=["Gb0", "pool2b"], w=["pM"])
                for tt_ in range(16):
                    k.mm(pM[:, 64:128], t["Gb1"][:, tt_ * 128:(tt_ + 1) * 128], t["pool2b"][:], start=(tt_ == 0), stop=(tt_ == 15),
                         r=["Gb1", "pool2b"], w=["pM"])
                k.ts("dve", t["pTk"][:], pM[:, 0:64], t["pebar"][:, 0:1], None, ALU.add, r=["pM", "pebar"], w=["pTk"])
                k.ts("dve", t["pTv"][:], pM[:, 64:128], t["pebar"][:, 1:2], None, ALU.add, r=["pM", "pebar"], w=["pTv"])
                k.mm(pM[:, 128:192], t["Wbdk"][:], t["pTk"][:], r=["Wbdk", "pTk"], w=["pM"])
                k.mm(pM[0:64, 192:320], t["pTv"][:], t["Wbdv"][:], r=["Wbdv", "pTv"], w=["pM"])
                k.cp("act", t["KcTs"][:], pM[:, 128:192], r=["pM"], w=["KcTs"])
                k.cp("act", t["Vcs"][:], pM[0:64, 192:320], r=["pM"], w=["Vcs"])
                for g in range(2):
                    gs = slice(g * 64, (g + 1) * 64)
                    k.mm(pS[0:64, s * 8 + g * 4:s * 8 + g * 4 + 4], t["KcTs"][:, :],
                         t["QTs"][:, g, :].rearrange("p (r s) -> p r s", r=4)[:, :, s], r=["KcTs", "QTs"], w=["pS0"])
                k.act(PcTv[:, s, :], pS[0:64, s * 8:(s + 1) * 8], AF.Exp, r=["pS0"], w=["PcT"], scale=SCL)
                k.mm(pOv[:, s, 0, :], t["Vcs"][:], PcTv[:, s, :], r=["Vcs", "PcT"], w=["pO0"])
                k.mm(pDv[:, s, 0, :], t["onesf"][0:64, :], PcTv[:, s, :], r=["onesf", "PcT"], w=["pO1"])
                k.mm(pS[0:32, 128 + s * 8:128 + (s + 1) * 8], t["pairf"][:], PcTv[:, s, :], r=["pairf", "PcT"], w=["pS0"])
            k.P.add("dve", lambda e: e.reciprocal(out=t["rDc"][:].rearrange("p (s h) -> p s h", s=NSMP), in_=pDv[0:32, :, 0, :]),
                    ["pO1"], ["rDc"])
            k.tt("dve", t["impn"][:], pS[0:32, 128:256], t["rDc"][:], ALU.mult, r=["pS0", "rDc"], w=["impn"])
            k.red(t["impT"][:], t["impn"][:].rearrange("p (a r) -> p a r", r=4), ALU.add, r=["impn"], w=["impT"])
            k.tr(pS[0:32, 256:288], t["impT"][:], t["identf"][0:32, 0:32], r=["impT", "identf"], w=["pS0"])
            k.tt("dve", t["impS"][:], pS[0:32, 256:288], t["bias0"][:], ALU.add, r=["pS0", "bias0"], w=["impS"])
            k.P.add("dve", lambda e: e.max(out=t["m8s"][:], in_=t["impS"][:]), ["impS"], ["m8s"])
            k.ts("dve", t["selS"][:], t["impS"][:], t["m8s"][:, 6:7], None, ALU.is_ge, r=["impS", "m8s"], w=["selS"])
            k.tr(t["pT"][0:32, 0:32], t["selS"][:], t["identb"][0:32, 0:32], r=["selS", "identb"], w=["pT"])
            k.cp("act", t["selTs"][:], t["pT"][0:32, 0:32], r=["pT"], w=["selTs"])
            k.mm(pS[:, 320:352], t["e4"][:], t["selTs"][:], r=["e4", "selTs"], w=["pS0"])
            k.cp("act", t["Msk"][:], pS[:, 320:352], r=["pS0"], w=["Msk"])
            pS = t["pS1"]
            banks = [(t["pA"], "pA"), (t["pB"], "pB")]
            for s in range(NSMP):
                k.gather(t["G0"][:], d["kslc"], t["idx"][:, s:s + 1], r=["idx"], w=["G0"])
                k.gather(t["G1"][:], d["vslc"], t["idx"][:, s:s + 1], r=["idx"], w=["G1"])
                k.dma("sp", t["Gw0"][:].rearrange("p (a c) -> p a c", a=4), d["kwin"][s].rearrange("(p a) c -> p a c", a=4), w=["Gw0"])
                k.dma("sp", t["Gw1"][:].rearrange("p (a c) -> p a c", a=4), d["vwin"][s].rearrange("(p a) c -> p a c", a=4), w=["Gw1"])
                k.cp("act", t["Gb0"][:], t["G0"][:], r=["G0"], w=["Gb0"])
                k.cp("act", t["Gwb0"][:], t["Gw0"][:], r=["Gw0"], w=["Gwb0"])
                k.cp("act", t["Gb1"][:], t["G1"][:], r=["G1"], w=["Gb1"])
                k.cp("act", t["Gwb1"][:], t["Gw1"][:], r=["Gw1"], w=["Gwb1"])
                tbanks = [(t["pT"][:].rearrange("p (a c) -> p a c", a=8), "pT"),
                          (t["pA"][:].bitcast(BF16).rearrange("p (a c) -> p a c", a=8), "pA"),
                          (t["pB"][:].bitcast(BF16).rearrange("p (a c) -> p a c", a=8), "pB")]
                for q8 in range(3):
                    pTk8, tbn = tbanks[q8]
                    nt_ = 8 if q8 < 2 else 4
                    for a in range(nt_):
                        tix = q8 * 8 + a
                        if tix < 16:
                            src, sname, col = t["Gb0"], "Gb0", tix * 128
                        else:
                            src, sname, col = t["Gwb0"], "Gwb0", (tix - 16) * 128
                        k.tr(pTk8[:, a, :], src[:, col:col + 128], t["identb"][:], r=[sname, "identb"], w=[tbn])
                    k.cp("act", t["KTs"][:, q8 * 8:q8 * 8 + nt_, :], pTk8[:, 0:nt_, :], r=[tbn], w=["KTs"])
                for tt_ in range(20):
                    for g in range(2):
                        gs = slice(g * 64, (g + 1) * 64)
                        k.mm(pS[:, tt_ * 8 + g * 4:tt_ * 8 + g * 4 + 4], t["KTs"][:, tt_, :],
                             t["QTs"][:, g, :].rearrange("p (r s) -> p r s", r=4)[:, :, s], r=["KTs", "QTs"], w=["pS1"])
                k.act(t["PTs"][:], pS[:, 0:160], AF.Exp, r=["pS1"], w=["PTs"], scale=SCL)
                k.tt("dve", t["PTs"][:, 0:128].rearrange("p (a g r) -> p a g r", a=16, g=2),
                     t["PTs"][:, 0:128].rearrange("p (a g r) -> p a g r", a=16, g=2),
                     bc(t["Msk"][:, s * 2:(s + 1) * 2].unsqueeze(1).unsqueeze(3), [128, 16, 2, 4]), ALU.mult,
                     r=["PTs", "Msk"], w=["PTs"])
                k.memset("dve", t["PTs"][0:1, 128:136], 0.0, w=["PTs"])
                k.red(t["PTsum"][:, 0:8], t["PTs"][:, 0:128].rearrange("p (a h) -> p h a", a=16), ALU.add, r=["PTs"], w=["PTsum"])
                k.red(t["PTsum"][:, 8:16], t["PTs"][:, 128:160].rearrange("p (a h) -> p h a", a=4), ALU.add, r=["PTs"], w=["PTsum"])
                k.cp("act", t["PTb"][:], t["PTs"][:], r=["PTs"], w=["PTb"])
                for tt_ in range(16):
                    k.mm(pOv[:, s, 1, :], t["Gb1"][:, tt_ * 128:(tt_ + 1) * 128], t["PTb"][:, tt_ * 8:(tt_ + 1) * 8],
                         start=(tt_ == 0), stop=False, r=["Gb1", "PTb"], w=["pO0"])
                k.mm(pOv[:, s, 1, :], t["vfb"][:, 0:128], Envb[:, 0, s, :], start=False, stop=True,
                     r=["vfb", "Enewb"], w=["pO0"])
                for tt_ in range(4):
                    k.mm(pOv[:, s, 2, :], t["Gwb1"][:, tt_ * 128:(tt_ + 1) * 128], t["PTb"][:, 128 + tt_ * 8:128 + (tt_ + 1) * 8],
                         start=(tt_ == 0), stop=False, r=["Gwb1", "PTb"], w=["pO0"])
                k.mm(pOv[:, s, 2, :], t["vfb"][:, 128:256], Envb[:, 1, s, :], start=False, stop=True,
                     r=["vfb", "Enewb"], w=["pO0"])
                for xx in range(2):
                    k.mm(pDv[:, s, 1 + xx, :], t["onesf"][:, :], t["PTsum"][:, xx * 8:(xx + 1) * 8], start=True, stop=False,
                         r=["onesf", "PTsum"], w=["pO1"])
                    k.mm(pDv[:, s, 1 + xx, :], t["onesf"][:, :], Env[:, xx, s, :], start=False, stop=True,
                         r=["onesf", "Enew"], w=["pO1"])
            k.P.add("dve", lambda e: e.reciprocal(out=t["rDall"], in_=pD[:, 0:384]), ["pO1"], ["B1a", "B1b"])
            k.tt("dve", t["OTn"], pO[:, 0:384], t["rDall"], ALU.mult, r=["pO0", "B1a", "B1b"], w=["B1a", "B1b"])
            k.cp("dve", t["OT1"], t["OTn"][64:128, :], r=["B1a", "B1b"], w=["B2"])
            gwv = t["gw0"][N, :].rearrange("p (h x) -> p x h", x=3)
            for xx in range(3):
                bank, bname = banks[xx % 2]
                for hh in range(8):
                    src = t["OTn"] if hh < 4 else t["OT1"]
                    sname = "B1a" if hh < 4 else "B2"
                    inap = src[0:64, :].rearrange("p (s c) -> p s c", s=NSMP)[:, :, xx * 8 + hh]
                    k.tr(bank[0:n, hh * 64:(hh + 1) * 64], inap, t["identf"][0:64, 0:64], r=[sname, "identf"], w=[bname])
                if xx == 0:
                    k.tt("dve", v3(t["o_a"][N, :], 8), v3(bank[N, :], 8), bc(gwv[:, xx, :].unsqueeze(2), [n, 8, 64]), ALU.mult,
                         r=[bname, "gw0"], w=["o_a"])
                else:
                    k.tt("dve", v3(t["tmpO"][N, :], 8), v3(bank[N, :], 8), bc(gwv[:, xx, :].unsqueeze(2), [n, 8, 64]), ALU.mult,
                         r=[bname, "gw0"], w=["B3"])
                    k.tt("pool", t["o_a"][N, :], t["o_a"][N, :], t["tmpO"][N, :], ALU.add, r=["o_a", "B3"], w=["o_a"])
            k.dma("sp", d["gqs"], t["gq"][N, :], r=["gq"], w=["d_gqs"])
            mod_pass(k, t, d, 128, t["cp8"], "cp8", True)
            P.emit(es)

        with contextlib.ExitStack() as s2:
            P = Prog(nc, "b")
            k = K(P)
            sb("Wout", [128, 8, D], BF16, s2); sb("WA", [128, 4, D], BF16, s2); sb("WB", [128, 4, D], BF16, s2)
            k.dma("pool", t["Wout"][:], d["wout"].rearrange("(kc p) n -> p kc n", p=128), w=["Wout"])
            for g in range(2):
                k.dma("pool", t["WA"][g * 64:(g + 1) * 64, :, :],
                      d["wbra"][g * 256:(g + 1) * 256, :].rearrange("(r dd) n -> dd r n", dd=64), w=["WA"])
            k.dma("pool", t["WB"][:], d["wbrb"].rearrange("(c p) n -> p c n", p=128), w=["WB"])
            k.dma("sp", t["B1"][0:NSMP, :], d["gqs"], w=["B1a", "B1b"])
            for _ in tail(k, t, NSMP, d["ys"], d["xs"], "0", gate=[(t["B1"][:, 0:512], "B1a"), (t["B1"][:, 512:1024], "B1b")]):
                pass
            sb("KsT", [128, S], BF16, s2); sb("KwT", [128, 5 * 128], BF16, s2)
            sb("Vs", [128, NT, 2, 65], BF16, s2); sb("Vw", [128, 5, 2, 65], BF16, s2)
            sb("QT", [128, 2, 512], BF16, s2); sb("vcb", [128, 128], BF16, s2)
            sb("pTk2", [128, 64], BF16, s2); sb("pTv2", [128, 64], BF16, s2); sb("KcT", [128, 64], BF16, s2)
            sb("Vc", [64, 2, 97], BF16, s2)
            sb("PT0", [128, 512], BF16, s2); sb("PT1", [128, 512], BF16, s2)
            sb("nsel4", [128, 2, 512], BF16, s2); sb("imp", [128, 64], F32, s2)
            sb("sel", [128, 64], BF16, s2); sb("m8", [128, 16], F32, s2); sb("mctneg", [128, 512], BF16, s2); sb("ibias", [128, 32], F32, s2)
            sb("tnle", [128, 512], BF16, s2); sb("tngt", [128, 512], BF16, s2)
            k.dma("pool", t["tnle"][:], d["trineg_le"], w=["tnle"])
            k.dma("pool", t["tngt"][:], d["trineg_gt"], w=["tngt"])
            n = 128
            N = slice(0, 128)
            k.memset("pool", t["Vs"][:], 1.0, w=["Vs"])
            k.memset("pool", t["Vw"][:], 1.0, w=["Vw"])
            k.memset("pool", t["Vc"][:], 1.0, w=["Vc"])
            k.memset("pool", t["pTk2"][:], 0.0, w=["pTk2"])
            k.memset("pool", t["pTv2"][:], 0.0, w=["pTv2"])
            k.memset("pool", t["KcT"][:], 0.0, w=["KcT"])
            k.memset("pool", t["QT"][:], 0.0, w=["QT"])
            k.memset("pool", t["nsel4"][:], 0.0, w=["nsel4"])
            k.dma("pool", t["mctneg"][:], d["cmpbase"], w=["mctneg"])
            for g in range(2):
                k.dma("pool", t["Vc"][:, g, 65:97], d["pair"], w=["Vc"])
            pS = [(t["pS0"], "pS0"), (t["pS1"], "pS1")]
            pO = [(t["pO0"], "pO0"), (t["pO1"], "pO1")]
            PT = [(t["PT0"], "PT0"), (t["PT1"], "PT1")]
            pM = t["pM"]
            pTv8 = t["pT"][:].rearrange("p (a b) -> p a b", a=8)
            cnt = {"s": 0, "p": 0}
            cur = {}
            sfx_of = lambda ii: str((ii + 1) % 2)

            def branch_finish(xx):
                for g in range(2):
                    bank, bname = pO[g]
                    ov = bank[:, 0:388].rearrange("p (r c) -> p r c", r=4)
                    k.ts("dve", t["rden"][:, g * 4:(g + 1) * 4], ov[:, :, 64], 1e-30, None, ALU.max, r=[bname], w=["rden"])
                k.P.add("dve", lambda e: e.reciprocal(out=t["rden"][:], in_=t["rden"][:]), ["rden"], ["rden"])
                k.tt("dve", t["cx"][:], t["rden"][:], cur["gwv"][:, xx, :], ALU.mult, r=["rden", cur["gwn"]], w=["cx"])
                for g in range(2):
                    bank, bname = pO[g]
                    ov = bank[:, 0:388].rearrange("p (r c) -> p r c", r=4)
                    for r in range(4):
                        hs = slice((g * 4 + r) * 64, (g * 4 + r + 1) * 64)
                        k.stt(t["o_a"][:, hs], ov[:, r, 0:64], t["cx"][:, g * 4 + r:g * 4 + r + 1], t["o_a"][:, hs], ALU.mult, ALU.add,
                              r=[bname, "cx", "o_a"], w=["o_a"])

            k.dma("sp", t["cs"][:], d["rope"][0:128, :], w=["cs"])
            for _ in front(k, t, 128, d["xp"][0:128, :], "cs", sfx_of(0), True):
                pass
            tgen = {"g": None}

            def tail_hook(nit):
                if tgen["g"] is not None:
                    next(tgen["g"], None)

            for i in range(NT):
                rows = slice(i * 128, (i + 1) * 128)
                sfx = sfx_of(i)
                cur["gwn"] = "gw" + sfx
                cur["gwv"] = t["gw" + sfx][:, :].rearrange("p (h x) -> p x h", x=3)
                k.dma("sp", d["pk_cmp"][rows, :], t["kf"][:, 0:128], r=["kf"], is_out=True)
                k.dma("sp", d["pk_slc"][rows, :], t["kf"][:, 128:256], r=["kf"], is_out=True)
                k.dma("sp", d["pv_cmp"][rows, :], t["vf"][:, 0:128], r=["vf"], is_out=True)
                k.dma("sp", d["pv_slc"][rows, :], t["vf"][:, 128:256], r=["vf"], is_out=True)
                if i >= NT - 4:
                    wr = slice((i - (NT - 4)) * 128, (i - (NT - 4) + 1) * 128)
                    k.dma("sp", d["pk_win"][wr, :], t["kf"][:, 256:384], r=["kf"], is_out=True)
                    k.dma("sp", d["pv_win"][wr, :], t["vf"][:, 256:384], r=["vf"], is_out=True)
                slot = i % 5
                k.cp("pool", t["Vs"][:, i, :, 0:64], v3(t["vf"][:, 128:256], 2), r=["vf"], w=["Vs"])
                k.cp("pool", t["Vw"][:, slot, :, 0:64], v3(t["vf"][:, 256:384], 2), r=["vf"], w=["Vw"])
                k.cp("pool", t["vcb"][:], t["vf"][:, 0:128], r=["vf"], w=["vcb"])
                for r in range(4):
                    k.tr(pTv8[:, r, :], t["q_bf"][:, r * 128:(r + 1) * 128], t["identb"][:], r=["q_bf", "identb"], w=["pT"])
                k.tr(pTv8[:, 4, :], t["k_bf"][:, 128:256], t["identb"][:], r=["k_bf", "identb"], w=["pT"])
                k.tr(pTv8[:, 5, :], t["k_bf"][:, 256:384], t["identb"][:], r=["k_bf", "identb"], w=["pT"])
                k.cp("act", t["QT"][0:64, 0, :].rearrange("p (r q) -> p r q", r=4), pTv8[0:64, 0:4, :], r=["pT"], w=["QT"])
                k.cp("act", t["QT"][64:128, 1, :].rearrange("p (r q) -> p r q", r=4), pTv8[64:128, 0:4, :], r=["pT"], w=["QT"])
                k.cp("act", t["KsT"][:, rows], pTv8[:, 4, :], r=["pT"], w=["KsT"])
                k.cp("act", t["KwT"][:, slot * 128:(slot + 1) * 128], pTv8[:, 5, :], r=["pT"], w=["KwT"])
                cc = slice(4 * i, 4 * i + 4)
                k.mm(pM[:, 0:4], t["k_bf"][:, 0:128], t["pool4"][:], r=["k_bf", "pool4"], w=["pM"])
                k.mm(pM[:, 4:8], t["vcb"][:], t["pool4"][:], r=["vcb", "pool4"], w=["pM"])
                k.ts("dve", t["pTk2"][:, cc], pM[:, 0:4], t["pebar"][:, 0:1], None, ALU.add, r=["pM", "pebar"], w=["pTk2"])
                k.ts("dve", t["pTv2"][:, cc], pM[:, 4:8], t["pebar"][:, 1:2], None, ALU.add, r=["pM", "pebar"], w=["pTv2"])
                k.mm(pM[:, 8:12], t["Wbdk"][:], t["pTk2"][:, cc], r=["Wbdk", "pTk2"], w=["pM"])
                k.mm(pM[0:64, 16:144], t["pTv2"][:], t["Wbdv"][:], r=["Wbdv", "pTv2"], w=["pM"])
                k.cp("act", t["KcT"][:, cc], pM[:, 8:12], r=["pM"], w=["KcT"])
                k.cp("act", t["Vc"][:, :, 0:64], v3(pM[0:64, 16:144], 2), r=["pM"], w=["Vc"])

                k.dma("sp", t["ibias"][:], d["impbias"][i], w=["ibias"])
                gen = None
                if i + 1 < NT:
                    nrows = slice((i + 1) * 128, (i + 2) * 128)
                    k.dma("sp", t["cs"][:], d["rope"][nrows, :], w=["cs"])
                    gen = front(k, t, 128, d["xp"][nrows, :], "cs", sfx_of(i + 1), True)
                    next(gen, None)
                qt = {g: t["QT"][:, g, :] for g in range(2)}

                def run_branch(items, hook=None):
                    recs = []

                    def s_stage(it):
                        g, kp, mms, v_ap, v_name, first, last = it
                        sbank, sname = pS[cnt["s"] % 2]; cnt["s"] += 1
                        pt, pname = PT[cnt["p"] % 2]; cnt["p"] += 1
                        for mi, (lh, rh, nm) in enumerate(mms):
                            k.mm(sbank[0:kp, :], lh, rh, start=(mi == 0), stop=(mi == len(mms) - 1), r=nm, w=[sname])
                        recs.append((sbank, sname, pt, pname))

                    def e_stage(idx):
                        g, kp, mms, v_ap, v_name, first, last = items[idx]
                        sbank, sname, pt, pname = recs[idx]
                        k.act(pt[0:kp, :], sbank[0:kp, :], AF.Exp, r=[sname], w=[pname], scale=SCL)
                        obank, oname = pO[g]
                        ov = obank[:, 0:388].rearrange("p (r c) -> p r c", r=4)
                        nv = v_ap.shape[-1]
                        for r in range(4):
                            k.mm(ov[:, r, 0:nv], pt[0:kp, r * 128:(r + 1) * 128], v_ap, start=(first and r == 0), stop=(last and r == 3),
                                 r=[pname, v_name], w=[oname])

                    s_stage(items[0])
                    for idx in range(len(items)):
                        if idx + 1 < len(items):
                            s_stage(items[idx + 1])
                        e_stage(idx)
                        if hook is not None:
                            hook(len(items))

                items = []
                for g in range(2):
                    gs = slice(g * 64, (g + 1) * 64)
                    items.append((g, 64, [(t["KcT"][:, :], qt[g], ["KcT", "QT"]),
                                          (t["identb"][:, 64 - 4 * i:128 - 4 * i], t["mctneg"][:, :], ["identb", "mctneg"])],
                                  t["Vc"][:, g, :], "Vc", True, True))
                run_branch(items)
                for g in range(2):
                    bank, bname = pO[g]
                    ov = bank[:, 0:388].rearrange("p (r c) -> p r c", r=4)
                    k.ts("dve", t["rden"][:, g * 4:(g + 1) * 4], ov[:, :, 64], 1e-30, None, ALU.max, r=[bname], w=["rden"])
                k.P.add("dve", lambda e: e.reciprocal(out=t["rden"][:], in_=t["rden"][:]), ["rden"], ["rden"])
                for g in range(2):
                    bank, bname = pO[g]
                    ov = bank[:, 0:388].rearrange("p (r c) -> p r c", r=4)
                    ig = t["imp"][:, g * 32:(g + 1) * 32]
                    k.stt(ig, ov[:, 0, 65:97], t["rden"][:, g * 4:g * 4 + 1], t["ibias"][:], ALU.mult, ALU.add,
                          r=[bname, "rden", "ibias"], w=["imp"])
                    for r in range(1, 4):
                        k.stt(ig, ov[:, r, 65:97], t["rden"][:, g * 4 + r:g * 4 + r + 1], ig, ALU.mult, ALU.add,
                              r=[bname, "rden", "imp"], w=["imp"])
                k.tt("dve", t["cx"][:], t["rden"][:], cur["gwv"][:, 0, :], ALU.mult, r=["rden", cur["gwn"]], w=["cx"])
                for g in range(2):
                    bank, bname = pO[g]
                    ov = bank[:, 0:388].rearrange("p (r c) -> p r c", r=4)
                    k.tt("dve", v3(t["o_a"][:, g * 256:(g + 1) * 256], 4), ov[:, :, 0:64],
                         bc(t["cx"][:, g * 4:(g + 1) * 4].unsqueeze(2), [128, 4, 64]), ALU.mult, r=[bname, "cx"], w=["o_a"])
                for g in range(2):
                    ig = t["imp"][:, g * 32:(g + 1) * 32]
                    k.P.add("dve", (lambda g: lambda e: e.max(out=t["m8"][:, g * 8:(g + 1) * 8], in_=t["imp"][:, g * 32:(g + 1) * 32]))(g),
                            ["imp"], ["m8"])
                    k.ts("dve", t["sel"][:, g * 32:(g + 1) * 32], ig, t["m8"][:, g * 8 + 7:g * 8 + 8], None, ALU.is_ge,
                         r=["imp", "m8"], w=["sel"])
                j0 = max(0, i - 4)
                items = []
                for g in range(2):
                    gs = slice(g * 64, (g + 1) * 64)
                    for j in range(j0, i + 1):
                        sl = j % 5
                        mms = [(t["KwT"][:, sl * 128:(sl + 1) * 128], qt[g], ["KwT", "QT"])]
                        if j == i:
                            mms.append((t["identb"][:, :], t["tnle"][:, :], ["identb", "tnle"]))
                        elif j == i - 4:
                            mms.append((t["identb"][:, :], t["tngt"][:, :], ["identb", "tngt"]))
                        items.append((g, 128, mms, t["Vw"][:, sl, g, :], "Vw", j == j0, j == i))
                run_branch(items, tail_hook)
                if tgen["g"] is not None:
                    for _ in tgen["g"]:
                        pass
                    tgen["g"] = None
                branch_finish(2)
                k.tr(t["pT"][0:64, 0:128], t["sel"][:], t["identb"][:], r=["sel", "identb"], w=["pT"])
                for g in range(2):
                    g32 = slice(g * 32, (g + 1) * 32)
                    k.ts("dve", v3(t["nsel4"][g32, g, :], 4), bc(t["pT"][g32, 0:128].unsqueeze(1), [32, 4, 128]), -1.0, 30000.0,
                         ALU.add, ALU.mult, r=["pT"], w=["nsel4"])
                items = []
                for g in range(2):
                    gs = slice(g * 64, (g + 1) * 64)
                    g32 = slice(g * 32, (g + 1) * 32)
                    for j in range(i + 1):
                        mms = [(t["KsT"][:, j * 128:(j + 1) * 128], qt[g], ["KsT", "QT"]),
                               (t["e2"][:, j * 128:(j + 1) * 128], t["nsel4"][:, g, :], ["e2", "nsel4"])]
                        if j == i:
                            mms.append((t["identb"][:, :], t["tnle"][:, :], ["identb", "tnle"]))
                        items.append((g, 128, mms, t["Vs"][:, j, g, :], "Vs", j == 0, j == i))
                acc = {"a": 0.0}

                def hook(nit):
                    if gen is None:
                        return
                    acc["a"] += 14.0 / nit
                    while acc["a"] >= 1.0:
                        acc["a"] -= 1.0
                        next(gen, None)

                run_branch(items, hook)
                if gen is not None:
                    for _ in gen:
                        pass
                branch_finish(1)
                tgen["g"] = tail(k, t, 128, d["yp"][rows, :], d["xp"][rows, :], sfx)
                next(tgen["g"], None)
            for _ in tgen["g"]:
                pass
            P.emit(es)
    return nc


def _consts():
    c = {}
    half = 32
    inv = (10000.0 ** (-np.arange(half, dtype=np.float32) * 2.0 / 64)).astype(np.float32)
    ang = np.arange(S + 1, dtype=np.float32)[:, None] * inv[None, :]
    c["rope"] = np.concatenate([np.cos(ang), np.sin(ang)], axis=1).astype(np.float32)
    kk = np.arange(128)[:, None]
    qq = np.arange(128)[None, :]
    c["tri_le"] = (kk <= qq).astype(np.float32)
    c["tri_gt"] = (kk > qq).astype(np.float32)
    c["identf_c"] = np.eye(128, dtype=np.float32)
    c["pool4"] = ((np.arange(128)[:, None] // 32) == np.arange(4)[None, :]).astype(np.float32) / 32.0
    c["pair"] = ((np.arange(64)[:, None] // 2) == np.arange(32)[None, :]).astype(np.float32)
    e = ((np.arange(2048)[None, :] // 64) == np.arange(32)[:, None]).astype(np.float32)
    c["e2"] = np.concatenate([e, e], axis=0)
    m = np.zeros((NT, 64, 128), np.float32)
    ib = np.zeros((NT, 128, 32), np.float32)
    for i in range(NT):
        pos = i * 128 + np.arange(128)
        cend = (np.arange(64) + 1) * 32 - 1
        m[i] = (cend[:, None] <= pos[None, :]).astype(np.float32)
        qblk = pos // 64
        blk = np.arange(32)
        forced = (blk[None, :] == 0) | (blk[None, :] == qblk[:, None])
        causal = blk[None, :] <= qblk[:, None]
        ib[i] = np.where(causal, 1.0e4 * forced, -1.0e30).astype(np.float32)
    base = np.zeros((128, 128), np.float32)
    base[68:, :] = -30000.0
    for u in range(64, 68):
        base[u, :] = np.where(np.arange(128) < 32 * (u - 64) + 31, -30000.0, 0.0)
    c["cmpbase"] = np.ascontiguousarray(np.tile(base[:, None, :], (1, 4, 1)).reshape(128, 512)).astype(np.float32)
    c["trineg_le"] = np.ascontiguousarray(np.tile(((1.0 - c["tri_le"]) * -30000.0)[:, None, :], (1, 4, 1)).reshape(128, 512)).astype(np.float32)
    c["trineg_gt"] = np.ascontiguousarray(np.tile(((1.0 - c["tri_gt"]) * -30000.0)[:, None, :], (1, 4, 1)).reshape(128, 512)).astype(np.float32)
    c["impbias"] = ib
    c["pool2"] = ((np.arange(128)[:, None] // 2) == np.arange(64)[None, :]).astype(np.float32) / 32.0
    c["e4"] = ((np.arange(128)[None, :] // 4) == np.arange(32)[:, None]).astype(np.float32)
    c["rmod8"] = (np.arange(128) % 8).astype(np.float32).reshape(128, 1)
    return c


_NC_CACHE = {}


def kernel(x_prompt, x_sample, cache_k_cmp, cache_v_cmp, cache_k_slc, cache_v_slc,
           cache_k_win, cache_v_win, page_table, c_prompt, c_sample,
           w_ada, b_ada, norm_g, w_in, q_norm_g, k_norm_g, cmp_pos_k, cmp_pos_v,
           w_cmp_k, w_cmp_v, vnorm_g, vnorm_b, w_s, b_s, w_br_a, w_br_b, w_out):
    f = lambda a: np.ascontiguousarray(np.asarray(a), dtype=np.float32)
    if "nc" not in _NC_CACHE:
        _NC_CACHE["nc"] = build_program()
    nc = _NC_CACHE["nc"]
    consts = _consts()
    pools = {nm: f(a).reshape(2560 * 8, 2048) for nm, a in
             (("kcmp", cache_k_cmp), ("vcmp", cache_v_cmp), ("kslc", cache_k_slc), ("vslc", cache_v_slc))}
    shared = dict(
        w_ada=f(w_ada)[0], b_ada=f(b_ada)[0].reshape(1, -1), norm_g=f(norm_g)[0].reshape(1, -1), w_in=f(w_in)[0],
        qg=f(q_norm_g)[0].reshape(1, -1), kg=f(k_norm_g)[0].reshape(1, -1), pek=f(cmp_pos_k)[0], pev=f(cmp_pos_v)[0],
        wck=f(w_cmp_k)[0], wcv=f(w_cmp_v)[0], vng=f(vnorm_g)[0].reshape(1, -1), vnb=f(vnorm_b)[0].reshape(1, -1),
        ws=f(w_s)[0], bs=f(b_s)[0], wbra=f(w_br_a)[0], wbrb=f(w_br_b)[0], wout=f(w_out)[0])
    shared.update(pools)
    shared.update(consts)
    xp, xs = f(x_prompt), f(x_sample)
    kw, vw = f(cache_k_win)[0], f(cache_v_win)[0]
    pt = np.asarray(page_table).astype(np.int32)
    cp, cs = f(c_prompt), f(c_sample)
    in_maps = []
    for c in range(8):
        sl = slice(c * NSMP, (c + 1) * NSMP)
        m = dict(shared)
        m["xp"] = xp[c]
        m["xs"] = np.ascontiguousarray(xs[sl, 0, :])
        m["cpv"] = np.ascontiguousarray(cp[c])
        m["csv"] = np.ascontiguousarray(cs[sl])
        m["kwin"] = np.ascontiguousarray(kw[sl].reshape(NSMP, 512, 128))
        m["vwin"] = np.ascontiguousarray(vw[sl].reshape(NSMP, 512, 128))
        m["ptrep"] = np.ascontiguousarray(np.repeat(pt[sl].T, 8, axis=0))
        in_maps.append(m)
    res = run_bass_kernel_spmd(nc, in_maps, core_ids=list(range(8)))
    R = res.results
    cat = lambda nm: np.stack([R[c][nm] for c in range(8)], axis=0)
    y_p = cat("yp")
    y_s = np.concatenate([R[c]["ys"] for c in range(8)], axis=0).reshape(128, 1, D)
    outs = [y_p, y_s]
    for nm in ("pk_cmp", "pv_cmp", "pk_slc", "pv_slc"):
        outs.append(cat(nm).reshape(1, 8, S, 2, 64))
    for nm in ("pk_win", "pv_win"):
        outs.append(cat(nm).reshape(1, 8, 512, 2, 64))
    for nm in ("sk_cmp", "sv_cmp", "sk_slc", "sv_slc"):
        outs.append(np.concatenate([R[c][nm] for c in range(8)], axis=0).reshape(1, 128, 1, 2, 64))
    for nm in ("sk_win", "sv_win"):
        outs.append(np.concatenate([R[c][nm] for c in range(8)], axis=0).reshape(1, 128, 512, 2, 64))
    outs.append(np.concatenate([R[c]["svch"] for c in range(8)], axis=0).reshape(1, 128, 1, 512))
    return tuple(o.astype(np.float32) for o in outs)
```

```python
import contextlib
import numpy as np
import concourse.bass as bass
import concourse.mybir as mybir
from concourse.bass_utils import run_bass_kernel_spmd

F32 = mybir.dt.float32
BF16 = mybir.dt.bfloat16
I32 = mybir.dt.int32
ALU = mybir.AluOpType
AF = mybir.ActivationFunctionType
AX = mybir.AxisListType

D = 1024
DIN = 5400
S = 2048
NT = 16
NSMP = 16
EPS = 1e-6
SCL = 0.125
C_Q, C_K, C_V, C_NSA, C_ZA, C_U, C_VV, C_ZB, C_GA, C_GB = 0, 512, 896, 1280, 1304, 1816, 2328, 2840, 3352, 4376


class _Op:
    __slots__ = ("eng", "fn", "deps", "idx", "signal", "is_dma", "sem", "target", "count")


class Prog:
    ENGS = ("pe", "act", "dve", "pool", "sp")
    DMA_POOL = {"sp": 16, "pool": 12, "act": 2}

    def __init__(self, nc, tag):
        self.nc = nc
        self.tag = tag
        self.q = {e: [] for e in self.ENGS}
        self.last_w = {}
        self.readers = {}
        self.dma_n = {e: 0 for e in self.DMA_POOL}
        self.out_dmas = []

    def add(self, eng, fn, r=(), w=(), dma=False, out=False):
        op = _Op()
        op.eng, op.fn, op.is_dma, op.signal = eng, fn, dma, False
        op.deps = set()
        op.sem = None
        op.target = 0
        op.count = 0
        for b in r:
            lw = self.last_w.get(b)
            if lw is not None:
                op.deps.add(lw)
        for b in w:
            lw = self.last_w.get(b)
            if lw is not None:
                op.deps.add(lw)
            for rd in self.readers.get(b, ()):
                op.deps.add(rd)
        for b in r:
            self.readers.setdefault(b, []).append(op)
        for b in w:
            self.last_w[b] = op
            self.readers[b] = []
        op.deps.discard(op)
        op.idx = len(self.q[eng])
        self.q[eng].append(op)
        if dma:
            j = self.dma_n[eng]
            self.dma_n[eng] += 1
            op.sem = (eng, j % self.DMA_POOL[eng])
            op.target = 16 * (j // self.DMA_POOL[eng] + 1)
            if out:
                self.out_dmas.append(op)
        return op

    def _needs_wait(self, op, dep):
        if dep.is_dma:
            return True
        if dep.eng == "pe" and op.eng == "pe" and not op.is_dma:
            return False
        return True

    def emit(self, es):
        nc = self.nc
        fin = self.add("sp", None)
        for o in self.out_dmas:
            fin.deps.add(o)
        for e in self.DMA_POOL:
            for op in self.q[e]:
                if op.is_dma:
                    fin.deps.add(op)
        for e in self.ENGS:
            for op in self.q[e]:
                for d in op.deps:
                    if (not d.is_dma) and self._needs_wait(op, d):
                        d.signal = True
        for e in self.ENGS:
            c = 0
            for op in self.q[e]:
                if (not op.is_dma) and op.signal:
                    c += 1
                    op.count = c
        esem = {e: es.enter_context(nc.semaphore("s%s_%s" % (self.tag, e))) for e in ("pe", "act", "dve", "pool")}
        dsem = {}
        for e, n in self.DMA_POOL.items():
            for i in range(n):
                dsem[(e, i)] = es.enter_context(nc.semaphore("d%s_%s_%d" % (self.tag, e, i)))
        prog = self
        with nc.Block() as block:
            def run_queue(ename, eng):
                waited = {}
                for op in prog.q[ename]:
                    waits = {}
                    for d in op.deps:
                        if not prog._needs_wait(op, d):
                            continue
                        if d.is_dma:
                            key, val = ("d", d.sem), d.target
                        else:
                            key, val = ("e", d.eng), d.count
                        if waits.get(key, 0) < val:
                            waits[key] = val
                    if op.is_dma and op.target > 16:
                        key = ("d", op.sem)
                        if waits.get(key, 0) < op.target - 16:
                            waits[key] = op.target - 16
                    for key, val in waits.items():
                        if waited.get(key, 0) >= val:
                            continue
                        waited[key] = val
                        s = dsem[key[1]] if key[0] == "d" else esem[key[1]]
                        eng.wait_ge(s, val)
                    if op.fn is None:
                        continue
                    ins = op.fn(eng)
                    if op.is_dma:
                        ins.then_inc(dsem[op.sem], 16)
                    elif op.signal:
                        ins.then_inc(esem[ename], 1)

            @block.sync
            def _(eng):
                run_queue("sp", eng)

            @block.tensor
            def _(eng):
                run_queue("pe", eng)

            @block.scalar
            def _(eng):
                run_queue("act", eng)

            @block.vector
            def _(eng):
                run_queue("dve", eng)

            @block.gpsimd
            def _(eng):
                run_queue("pool", eng)


class K:
    def __init__(self, P):
        self.P = P

    def mm(self, out, lhsT, rhs, start=True, stop=True, r=(), w=()):
        self.P.add("pe", lambda e: e.matmul(out, lhsT=lhsT, rhs=rhs, start=start, stop=stop, skip_group_check=True), r, w)

    def tr(self, out, in_, ident, r=(), w=()):
        self.P.add("pe", lambda e: e.transpose(out=out, in_=in_, identity=ident), r, w)

    def act(self, out, in_, func, r=(), w=(), scale=1.0, accum=None):
        if accum is None:
            self.P.add("act", lambda e: e.activation(out=out, in_=in_, func=func, scale=scale), r, w)
        else:
            self.P.add("act", lambda e: e.activation(out=out, in_=in_, func=func, scale=scale, accum_out=accum), r, w)

    def tt(self, eng, out, in0, in1, op, r=(), w=()):
        self.P.add(eng, lambda e: e.tensor_tensor(out=out, in0=in0, in1=in1, op=op), r, w)

    def ts(self, eng, out, in0, s1, s2, op0, op1=None, r=(), w=()):
        if op1 is None:
            self.P.add(eng, lambda e: e.tensor_scalar(out=out, in0=in0, scalar1=s1, scalar2=None, op0=op0), r, w)
        else:
            self.P.add(eng, lambda e: e.tensor_scalar(out=out, in0=in0, scalar1=s1, scalar2=s2, op0=op0, op1=op1), r, w)

    def stt(self, out, in0, scalar, in1, op0, op1, r=(), w=()):
        self.P.add("dve", lambda e: e.scalar_tensor_tensor(out=out, in0=in0, scalar=scalar, in1=in1, op0=op0, op1=op1), r, w)

    def cp(self, eng, out, in_, r=(), w=()):
        if eng == "act":
            self.P.add("act", lambda e: e.activation(out=out, in_=in_, func=AF.Copy), r, w)
        else:
            self.P.add(eng, lambda e: e.tensor_copy(out=out, in_=in_), r, w)

    def red(self, out, in_, op, r=(), w=()):
        self.P.add("dve", lambda e: e.tensor_reduce(out=out, in_=in_, axis=AX.X, op=op), r, w)

    def memset(self, eng, ap, val, w=()):
        self.P.add(eng, lambda e: e.memset(ap, val), (), w)

    def dma(self, q, out, in_, r=(), w=(), is_out=False, slow=False):
        if slow:
            self.P.add(q, lambda e: e.dma_start(out=out, in_=in_, allow_slow_non_contiguous=True), r, w, dma=True, out=is_out)
        else:
            self.P.add(q, lambda e: e.dma_start(out=out, in_=in_), r, w, dma=True, out=is_out)

    def gather(self, out, in_, idx, r=(), w=()):
        self.P.add("pool", lambda e: e.indirect_dma_start(out=out, out_offset=None, in_=in_,
                                                          in_offset=bass.IndirectOffsetOnAxis(ap=idx, axis=0)),
                   r, w, dma=True)


def v3(ap, a):
    return ap.rearrange("p (a b) -> p a b", a=a)


def bc(ap, shape):
    return ap.to_broadcast(shape)


def win_names(c0, n):
    return ["win%d" % j for j in range(c0 // 512, (c0 + n - 1) // 512 + 1)]


def front(k, t, n, x_src, cs_name, sfx, is_prompt):
    P = k.P
    N = slice(0, n)
    x, B1, B2, B3, h, hT = t["x"], t["B1"], t["B2"], t["B3"], t["h"], t["hT"]
    TH, ZB, vn_bf = B1[:, 0:512], B1[:, 512:1024], h[:, 0:512]
    tg, za_s, bmix, gw = t["tg" + sfx], t["za_s" + sfx], t["bmix" + sfx], t["gw" + sfx]
    n_tg, n_za, n_bm, n_gw = "tg" + sfx, "za_s" + sfx, "bmix" + sfx, "gw" + sfx
    B1W = ["B1a", "B1b"]
    st, st2, st3 = t["st"], t["st2"], t["st3"]
    pA, pB, pT = t["pA"], t["pB"], t["pT"]
    pTv = pT[:].rearrange("p (a b) -> p a b", a=8)
    k.dma("sp", x[N, :], x_src, w=["x"])
    k.act(B1[N, :], x[N, :], AF.Square, r=["x"], w=B1W + ["st"], accum=st[N, 0:1])
    k.ts("dve", st[N, 1:2], st[N, 0:1], 1.0 / D, EPS, ALU.mult, ALU.add, r=["st"], w=["st"])
    k.tt("pool", st[N, 2:3], st[N, 1:2], t["nh"][N, 0:1], ALU.pow, r=["st", "nh"], w=["st"])
    k.stt(B1[N, :], x[N, :], st[N, 2:3], t["sc1"][N, :], ALU.mult, ALU.mult, r=["x", "st", "sc1"], w=B1W)
    k.tt("dve", h[N, :], B1[N, :], t["sh"][N, :], ALU.add, r=B1W + ["sh"], w=["h"])
    yield
    for kc in range(8):
        k.tr(pTv[:, kc, 0:n], h[N, kc * 128:(kc + 1) * 128], t["identb"][N, N], r=["h", "identb"], w=["pT"])
    k.cp("act", hT[:, :, 0:n], pTv[:, :, 0:n], r=["pT"], w=["hT"])
    yield

    def proj(bank, bname, dst, c0, ncol):
        for kc in range(8):
            k.mm(bank[N, dst:dst + ncol], hT[:, kc, 0:n], t["Win"][:, kc, c0:c0 + ncol], start=(kc == 0), stop=(kc == 7),
                 r=["hT"] + win_names(c0, ncol), w=[bname])

    def normrope(bank, bname, A, Bd, gain, dst_ap, dst_name):
        H = A * Bd
        HW = H * 64

        def v4(ap):
            return ap.rearrange("p (a b d) -> p a b d", a=A, b=Bd)

        k.cp("act", B2[N, 0:HW], bank[N, 0:HW], r=[bname], w=["B2"])
        k.act(B3[N, 0:HW], B2[N, 0:HW], AF.Square, r=["B2"], w=["B3"])
        k.red(st2[N, 0:H], v3(B3[N, 0:HW], H), ALU.add, r=["B3"], w=["st2"])
        k.ts("dve", st2[N, 8:8 + H], st2[N, 0:H], 1.0 / 64, EPS, ALU.mult, ALU.add, r=["st2"], w=["st2"])
        k.tt("pool", st2[N, 16:16 + H], st2[N, 8:8 + H], t["nh"][N, 0:H], ALU.pow, r=["st2", "nh"], w=["st2"])
        b2 = v4(B2[N, 0:HW])
        rs = st2[N, 16:16 + H].rearrange("p (a b) -> p a b", a=A).unsqueeze(3)
        k.tt("dve", b2, b2, bc(rs, [n, A, Bd, 64]), ALU.mult, r=["B2", "st2"], w=["B2"])
        k.tt("dve", b2, b2, bc(gain[N, :].unsqueeze(1).unsqueeze(1), [n, A, Bd, 64]), ALU.mult, r=["B2", "gains"], w=["B2"])
        cs = t["cs"]
        cosb = bc(cs[N, 0:32].unsqueeze(1).unsqueeze(1), [n, A, Bd, 32])
        sinb = bc(cs[N, 32:64].unsqueeze(1).unsqueeze(1), [n, A, Bd, 32])
        x1, x2 = b2[:, :, :, 0:32], b2[:, :, :, 32:64]
        r1 = t["R1"][N, 0:H * 32].rearrange("p (a b d) -> p a b d", a=A, b=Bd)
        r2 = t["R2"][N, 0:H * 32].rearrange("p (a b d) -> p a b d", a=A, b=Bd)
        k.tt("dve", r1, x1, cosb, ALU.mult, r=["B2", cs_name], w=["R1"])
        k.tt("dve", r2, x2, sinb, ALU.mult, r=["B2", cs_name], w=["R2"])
        k.tt("dve", dst_ap[:, :, :, 0:32], r1, r2, ALU.subtract, r=["R1", "R2"], w=[dst_name])
        k.tt("dve", r1, x2, cosb, ALU.mult, r=["B2", cs_name], w=["R1"])
        k.tt("dve", r2, x1, sinb, ALU.mult, r=["B2", cs_name], w=["R2"])
        k.tt("dve", dst_ap[:, :, :, 32:64], r1, r2, ALU.add, r=["R1", "R2"], w=[dst_name])

    proj(pA, "pA", 0, C_Q, 512)
    qdst = t["q_bf"][N, :].rearrange("p (r g d) -> p g r d", r=4, g=2)
    normrope(pA, "pA", 2, 4, t["qg"], qdst, "q_bf")
    yield
    proj(pB, "pB", 0, C_K, 384)
    normrope(pB, "pB", 6, 1, t["kg"], t["kf"][N, :].rearrange("p (a b d) -> p a b d", a=6, b=1), "kf")
    k.cp("act", t["k_bf"][N, :], t["kf"][N, :], r=["kf"], w=["k_bf"])
    yield
    proj(pA, "pA", 0, C_V, 384)
    proj(pA, "pA", 384, C_NSA, 24)
    k.cp("act", t["vf"][N, :], pA[N, 0:384], r=["pA"], w=["vf"])
    k.act(t["gwt"][N, :], pA[N, 384:408], AF.Tanh, r=["pA"], w=["gwt"], scale=0.5)
    k.ts("dve", gw[N, :], t["gwt"][N, :], 0.5, 0.5, ALU.mult, ALU.add, r=["gwt"], w=[n_gw])
    yield
    proj(pB, "pB", 0, C_ZA, 512)
    k.act(TH[N, :], pB[N, :], AF.Tanh, r=["pB"], w=["B1a"], scale=0.5)
    k.stt(za_s[N, :], TH[N, :], 1.0, pB[N, :], ALU.add, ALU.mult, r=["B1a", "pB"], w=[n_za])
    yield
    proj(pA, "pA", 0, C_VV, 512)
    k.cp("act", t["vn_f"][N, :], pA[N, :], r=["pA"], w=["vn_f"])
    P.add("dve", lambda e: e.bn_stats(out=st3[N, 0:6], in_=t["vn_f"][N, :]), ["vn_f"], ["st3"])
    P.add("dve", lambda e: e.bn_aggr(out=st3[N, 6:8], in_=st3[N, 0:6]), ["st3"], ["st3"])
    k.ts("dve", st3[N, 8:9], st3[N, 7:8], EPS, None, ALU.add, r=["st3"], w=["st3"])
    k.tt("pool", st3[N, 9:10], st3[N, 8:9], t["nh"][N, 0:1], ALU.pow, r=["st3", "nh"], w=["st3"])
    k.ts("dve", t["vn_f"][N, :], t["vn_f"][N, :], st3[N, 6:7], st3[N, 9:10], ALU.subtract, ALU.mult, r=["vn_f", "st3"], w=["vn_f"])
    k.tt("dve", t["vn_f"][N, :], t["vn_f"][N, :], t["vng"][N, :], ALU.mult, r=["vn_f", "gains"], w=["vn_f"])
    k.tt("dve", t["vn_f"][N, :], t["vn_f"][N, :], t["vnb"][N, :], ALU.add, r=["vn_f", "gains"], w=["vn_f"])
    yield
    proj(pB, "pB", 0, C_ZB, 512)
    k.act(TH[N, :], pB[N, :], AF.Tanh, r=["pB"], w=["B1a"], scale=0.5)
    k.stt(ZB[N, :], TH[N, :], 1.0, pB[N, :], ALU.add, ALU.mult, r=["B1a", "pB"], w=["B1b"])
    yield
    proj(pA, "pA", 0, C_U, 512)
    k.tt("dve", ZB[N, :], pA[N, :], ZB[N, :], ALU.mult, r=["pA", "B1b"], w=["B1b"])
    yield
    if is_prompt:
        k.cp("act", vn_bf[N, :], t["vn_f"][N, :], r=["vn_f"], w=["h"])
        for g in range(4):
            k.mm(pB[N, g * 128:(g + 1) * 128], t["wsT"][:, g, :], vn_bf[N, g * 128:(g + 1) * 128],
                 r=["wsT", "h"], w=["pB"])
        k.tt("dve", v3(B2[N, :], 4), v3(pB[N, :], 4), bc(t["bsT"][N, :].unsqueeze(2), [n, 4, 128]), ALU.add,
             r=["pB", "bsT"], w=["B2"])
    else:
        k.tt("dve", v3(B2[N, :], 4), v3(t["vn_f"][N, :], 4), bc(t["w00"][N, :].unsqueeze(2), [n, 4, 128]), ALU.mult,
             r=["vn_f", "w00"], w=["B2"])
        k.tt("dve", v3(B2[N, :], 4), v3(B2[N, :], 4), bc(t["b0"][N, :].unsqueeze(2), [n, 4, 128]), ALU.add,
             r=["B2", "w00"], w=["B2"])
    k.tt("dve", bmix[N, :], B2[N, :], ZB[N, :], ALU.mult, r=["B2", "B1b"], w=[n_bm])
    yield
    banks = [(pB, "pB"), (pA, "pA")]
    for j in range(4):
        bank, bname = banks[j % 2]
        proj(bank, bname, 0, C_GA + j * 512, 512)
        k.act(tg[N, j * 512:(j + 1) * 512], bank[N, :], AF.Tanh, r=[bname], w=[n_tg], scale=0.5)
        yield


def tail(k, t, n, y_dst, x_src, sfx, gate=None):
    N = slice(0, n)
    pA, pB, pT = t["pA"], t["pB"], t["pT"]
    pTv8 = t["pM"][:].bitcast(BF16).rearrange("p (a b) -> p a b", a=8)
    tg, za_s, bmix = t["tg" + sfx], t["za_s" + sfx], t["bmix" + sfx]
    n_tg, n_za, n_bm = "tg" + sfx, "za_s" + sfx, "bmix" + sfx
    mc = t["mc"]
    B1W = ["B1a", "B1b"]
    ozd = t["oz"][N, :].rearrange("p (r g d) -> p g r d", r=4, g=2)
    k.tt("dve", ozd, t["o_a"][N, :].rearrange("p (g r d) -> p g r d", g=2, r=4),
         za_s[N, :].rearrange("p (g r d) -> p g r d", g=2, r=4), ALU.mult, r=["o_a", n_za], w=["oz"])
    for r in range(4):
        k.tr(pTv8[:, r, 0:n], t["oz"][N, r * 128:(r + 1) * 128], t["identb"][N, N], r=["oz", "identb"], w=["pM"])
    for c in range(4):
        k.tr(pTv8[:, 4 + c, 0:n], bmix[N, c * 128:(c + 1) * 128], t["identb"][N, N], r=[n_bm, "identb"], w=["pM"])
    k.cp("act", t["ozT"][:, :, 0:n], pTv8[:, :, 0:n], r=["pM"], w=["ozT"])
    yield
    for half in range(2):
        cs = slice(half * 512, (half + 1) * 512)
        for r in range(4):
            k.mm(pA[N, :], t["ozT"][:, r, 0:n], t["WA"][:, r, cs], start=(r == 0), stop=(r == 3), r=["ozT", "WA"], w=["pA"])
        for c in range(4):
            k.mm(pB[N, :], t["ozT"][:, 4 + c, 0:n], t["WB"][:, c, cs], start=(c == 0), stop=(c == 3), r=["ozT", "WB"], w=["pB"])
        k.stt(t["B2"][N, :], tg[N, half * 512:(half + 1) * 512], 1.0, pA[N, :], ALU.add, ALU.mult,
              r=[n_tg, "pA"], w=["B2"])
        k.stt(t["B3"][N, :], tg[N, 1024 + half * 512:1024 + (half + 1) * 512], 1.0, pB[N, :], ALU.add, ALU.mult,
              r=[n_tg, "pB"], w=["B3"])
        k.tt("dve", mc[N, cs], t["B2"][N, :], t["B3"][N, :], ALU.add, r=["B2", "B3"], w=["mc"])
        yield
    for kc in range(8):
        k.tr(pTv8[:, kc, 0:n], mc[N, kc * 128:(kc + 1) * 128], t["identb"][N, N], r=["mc", "identb"], w=["pM"])
    k.cp("act", t["hT"][:, :, 0:n], pTv8[:, :, 0:n], r=["pM"], w=["hT"])
    yield
    banks = [(pA, "pA"), (pB, "pB")]
    for half in range(2):
        bank, bname = banks[half]
        cs = slice(half * 512, (half + 1) * 512)
        for kc in range(8):
            k.mm(bank[N, :], t["hT"][:, kc, 0:n], t["Wout"][:, kc, cs], start=(kc == 0), stop=(kc == 7),
                 r=["hT", "Wout"], w=[bname])
        if gate is None:
            gap, gname = t["gq"][N, cs], "gq"
        else:
            gap, gname = gate[half][0][N, 0:512], gate[half][1]
        k.tt("dve", t["B1"][N, cs], bank[N, :], gap, ALU.mult, r=[bname, gname], w=["B1a" if half == 0 else "B1b"])
        yield
    k.dma("sp", t["x"][N, :], x_src, w=["x"])
    k.tt("dve", t["x"][N, :], t["B1"][N, :], t["x"][N, :], ALU.add, r=B1W + ["x"], w=["x"])
    k.dma("sp", y_dst, t["x"][N, :], r=["x"], is_out=True)


def mod_pass(k, t, d, n, cT, cT_name, bcast):
    N = slice(0, n)
    stg = [t["G0"], t["G1"]]
    banks = [(t["pA"], "pA"), (t["pB"], "pB")]
    wada = d["w_ada"].rearrange("(kc p) n -> p kc n", p=128)
    for j in range(12):
        sg, sname = stg[j % 2], "G%d" % (j % 2)
        bank, bname = banks[j % 2]
        sgv = sg[:].rearrange("p (kc n) -> p kc n", kc=8)
        k.dma("sp", sgv, wada[:, :, j * 256:(j + 1) * 256], w=[sname])
        k.dma("sp", t["bada"][0:1, :], d["b_ada"][0:1, j * 256:(j + 1) * 256], w=["bada"])
        if not bcast:
            for kc in range(8):
                k.mm(bank[N, 0:256], cT[:, kc, 0:n], sgv[:, kc, :], start=(kc == 0), stop=False, r=[cT_name, sname], w=[bname])
            k.mm(bank[N, 0:256], t["ones0"][:, 0:n], t["bada"][:, :], start=False, stop=True, r=["ones0", "bada"], w=[bname])
            src = bank[N, 0:256]
        else:
            for kc in range(8):
                k.mm(bank[0:1, 0:256], cT[:, kc:kc + 1], sgv[:, kc, :], start=(kc == 0), stop=False, r=[cT_name, sname], w=[bname])
            k.mm(bank[0:1, 0:256], t["ones0"][:, 0:1], t["bada"][:, :], start=False, stop=True, r=["ones0", "bada"], w=[bname])
            k.cp("act", t["modrow"][0:1, :], bank[0:1, 0:256], r=[bname], w=["modrow"])
            k.mm(bank[N, 256:512], t["onesf"][0:1, 0:n], t["modrow"][0:1, :], r=["onesf", "modrow"], w=[bname])
            src = bank[N, 256:512]
        cs = slice((j % 4) * 256, (j % 4 + 1) * 256)
        if j < 4:
            k.cp("act", t["sh"][N, cs], src, r=[bname], w=["sh"])
        elif j < 8:
            k.stt(t["sc1"][N, cs], src, 1.0, t["normg"][N, cs], ALU.add, ALU.mult, r=[bname, "normg"], w=["sc1"])
        else:
            k.act(t["gq"][N, cs], src, AF.Copy, r=[bname], w=["gq"], scale=0.25)


def build_program():
    nc = bass.Bass("TRN2", target_bir_lowering=False)
    d = {}

    def din(name, shape, dt=F32):
        d[name] = nc.dram_tensor(name, shape, dt, kind="ExternalInput").ap()

    def dout(name, shape):
        d[name] = nc.dram_tensor(name, shape, F32, kind="ExternalOutput").ap()

    din("xp", [S, D]); din("xs", [NSMP, D]); din("cpv", [D]); din("csv", [NSMP, D])
    for nm in ("kcmp", "vcmp", "kslc", "vslc"):
        din(nm, [2560 * 8, 2048])
    din("kwin", [NSMP, 512, 128]); din("vwin", [NSMP, 512, 128])
    din("ptrep", [128, NSMP], I32)
    din("w_ada", [D, 3 * D]); din("b_ada", [1, 3 * D]); din("norm_g", [1, D]); din("w_in", [D, DIN])
    din("qg", [1, 64]); din("kg", [1, 64]); din("pek", [32, 64]); din("pev", [32, 64])
    din("wck", [64, 64]); din("wcv", [64, 64]); din("vng", [1, 512]); din("vnb", [1, 512])
    din("ws", [4, 128, 128]); din("bs", [4, 128]); din("wbra", [512, D]); din("wbrb", [512, D]); din("wout", [D, D])
    din("rope", [S + 1, 64]); din("tri_le", [128, 128]); din("tri_gt", [128, 128]); din("identf_c", [128, 128])
    din("pool4", [128, 4]); din("pair", [64, 32]); din("e2", [64, 2048]); din("cmpbase", [128, 512]); din("trineg_le", [128, 512]); din("trineg_gt", [128, 512])
    din("impbias", [NT, 128, 32]); din("pool2", [128, 64]); din("e4", [32, 128]); din("rmod8", [128, 1])
    d["gqs"] = nc.dram_tensor("gqs", [NSMP, D], F32, kind="Internal").ap()
    dout("yp", [S, D]); dout("ys", [NSMP, D])
    for nm in ("pk_cmp", "pv_cmp", "pk_slc", "pv_slc"):
        dout(nm, [S, 128])
    dout("pk_win", [512, 128]); dout("pv_win", [512, 128])
    for nm in ("sk_cmp", "sv_cmp", "sk_slc", "sv_slc"):
        dout(nm, [NSMP, 128])
    dout("sk_win", [NSMP, 512, 128]); dout("sv_win", [NSMP, 512, 128]); dout("svch", [NSMP, 512])

    with contextlib.ExitStack() as es:
        t = {}

        def sb(name, shape, dt, scope=es):
            t[name] = scope.enter_context(nc.sbuf_tensor("sb_" + name, shape, dt))
            return t[name]

        def ps(name, shape, dt, scope=es):
            t[name] = scope.enter_context(nc.psum_tensor("ps_" + name, shape, dt))
            return t[name]

        sb("Win", [128, 8, DIN], BF16)
        sb("sc1", [128, D], F32); sb("sh", [128, D], F32); sb("gq", [128, D], F32)
        sb("identb", [128, 128], BF16); sb("identf", [128, 128], F32); sb("tri_le", [128, 128], BF16); sb("tri_gt", [128, 128], BF16)
        sb("qg", [128, 64], F32); sb("kg", [128, 64], F32); sb("vng", [128, 512], F32); sb("vnb", [128, 512], F32)
        sb("nh", [128, 8], F32); sb("onesf", [128, 128], F32)
        sb("wsT", [128, 4, 128], BF16); sb("bsT", [128, 4], F32)
        sb("Wbdk", [128, 128], BF16); sb("Wbdv", [128, 128], BF16); sb("pebar", [128, 2], F32)
        sb("pool4", [128, 4], BF16); sb("e2", [128, 2048], BF16)
        sb("x", [128, D], F32); sb("B1", [128, D], F32); sb("B2", [128, 512], F32); sb("B3", [128, 512], F32)
        sb("h", [128, D], BF16); sb("hT", [128, 8, 128], BF16); sb("R1", [128, 256], F32); sb("R2", [128, 256], F32)
        sb("st", [128, 4], F32); sb("st2", [128, 24], F32); sb("st3", [128, 12], F32)
        sb("q_bf", [128, 512], BF16); sb("kf", [128, 384], F32); sb("vf", [128, 384], F32); sb("k_bf", [128, 384], BF16)
        sb("gwt", [128, 24], F32); sb("gw0", [128, 24], F32); sb("gw1", [128, 24], F32); sb("za_s0", [128, 512], BF16); sb("za_s1", [128, 512], BF16)
        sb("vn_f", [128, 512], F32); sb("tg0", [128, 2048], BF16); sb("tg1", [128, 2048], BF16)
        sb("bmix0", [128, 512], BF16); sb("bmix1", [128, 512], BF16); sb("o_a", [128, 512], F32); sb("oz", [128, 512], BF16); sb("ozT", [128, 8, 128], BF16)
        sb("cs", [128, 64], F32); sb("mc", [128, D], BF16)
        sb("rden", [128, 8], F32); sb("cx", [128, 8], F32); t["tmpO"] = t["B3"]
        ps("pA", [128, 512], F32); ps("pB", [128, 512], F32); ps("pT", [128, 1024], BF16)
        ps("pM", [128, 512], F32); ps("pS0", [128, 512], F32); ps("pS1", [128, 512], F32)
        ps("pO0", [128, 512], F32); ps("pO1", [128, 512], F32)

        with contextlib.ExitStack() as s1:
            P = Prog(nc, "a")
            k = K(P)
            sb("normg", [128, D], F32, s1)
            sb("G0", [128, 2048], F32, s1); sb("G1", [128, 2048], F32, s1)
            sb("Gw0", [128, 512], F32, s1); sb("Gw1", [128, 512], F32, s1)
            sb("Gb0", [128, 2048], BF16, s1); sb("Gb1", [128, 2048], BF16, s1); sb("pool2b", [128, 64], BF16, s1); sb("Gwb0", [128, 512], BF16, s1); sb("Gwb1", [128, 512], BF16, s1)
            sb("PTb", [128, 160], BF16, s1); sb("vfb", [128, 256], BF16, s1); sb("Enewb", [128, 2 * NSMP * 8], BF16, s1)
            sb("bada", [128, 256], F32, s1); sb("modrow", [1, 256], F32, s1); sb("ones0", [128, 128], F32, s1)
            sb("cTs", [128, 8, NSMP], F32, s1)
            sb("cp8", [128, 8], F32, s1)
            sb("w00", [128, 4], F32, s1); sb("b0", [128, 4], F32, s1)
            sb("pe2", [32, 128], F32, s1); sb("o32", [32, 2], F32, s1)
            sb("ptrep", [128, NSMP], I32, s1); sb("rmod8", [128, 1], F32, s1); sb("idx", [128, NSMP], I32, s1)
            sb("pool2", [128, 64], F32, s1); sb("pairf", [64, 32], F32, s1); sb("e4", [32, 128], BF16, s1)
            sb("QTs", [128, 2, 4 * NSMP], BF16, s1); sb("enr", [NSMP, 16], F32, s1); sb("en", [NSMP, 16], F32, s1); sb("Enew", [128, 2 * NSMP * 8], F32, s1)
            sb("pTk", [128, 64], BF16, s1); sb("pTv", [128, 64], BF16, s1); sb("KcTs", [128, 64], BF16, s1); sb("Vcs", [64, 128], F32, s1)
            sb("PcT", [64, NSMP * 8], F32, s1); sb("KTs", [128, 20, 128], BF16, s1)
            sb("PTs", [128, 160], F32, s1); sb("PTsum", [128, 16], F32, s1)
            sb("rDc", [32, 128], F32, s1); sb("impn", [32, 128], F32, s1); sb("impT", [32, 32], F32, s1)
            sb("impS", [32, 32], F32, s1); sb("bias0", [32, 32], F32, s1); sb("m8s", [32, 8], F32, s1)
            sb("selS", [32, 32], BF16, s1); sb("selTs", [32, 32], BF16, s1); sb("Msk", [128, 32], F32, s1)
            t["OTn"] = t["B1"][:, 0:384]; t["rDall"] = t["B1"][:, 384:768]; t["OT1"] = t["B2"][0:64, 0:384]; t["wsl"] = t["B3"][:, 0:128]

            k.dma("pool", t["identb"][:], d["identf_c"], w=["identb"])
            k.dma("sp", t["identf"][:], d["identf_c"], w=["identf"])
            k.dma("pool", t["tri_le"][:], d["tri_le"], w=["tri_le"])
            k.dma("pool", t["tri_gt"][:], d["tri_gt"], w=["tri_gt"])
            k.dma("pool", t["pool4"][:], d["pool4"], w=["pool4"])
            k.memset("pool", t["e2"][64:128, :], 0.0, w=["e2"])
            k.dma("pool", t["e2"][0:64, :], d["e2"], w=["e2"])
            k.dma("pool", t["e4"][:], d["e4"], w=["e4"])
            k.dma("sp", t["pool2"][:], d["pool2"], w=["pool2"])
            k.dma("pool", t["pool2b"][:], d["pool2"], w=["pool2b"])
            k.dma("sp", t["pairf"][:], d["pair"], w=["pairf"])
            k.dma("sp", t["rmod8"][:], d["rmod8"], w=["rmod8"])
            k.dma("sp", t["ptrep"][:], d["ptrep"], w=["ptrep"])
            k.dma("sp", t["qg"][:], d["qg"].partition_broadcast(128), w=["gains"])
            k.dma("sp", t["kg"][:], d["kg"].partition_broadcast(128), w=["gains"])
            k.dma("sp", t["vng"][:], d["vng"].partition_broadcast(128), w=["gains"])
            k.dma("sp", t["vnb"][:], d["vnb"].partition_broadcast(128), w=["gains"])
            k.dma("sp", t["normg"][:], d["norm_g"].partition_broadcast(128), w=["normg"])
            k.dma("sp", t["bsT"][:], d["bs"].rearrange("g i -> i g"), w=["bsT"], slow=True)
            k.dma("sp", t["w00"][0:NSMP, :], d["ws"][:, 0, 0:1].rearrange("g o -> o g").partition_broadcast(NSMP), w=["w00"], slow=True)
            k.dma("sp", t["b0"][0:NSMP, :], d["bs"][:, 0:1].rearrange("g o -> o g").partition_broadcast(NSMP), w=["w00"], slow=True)
            k.memset("pool", t["nh"][:], -0.5, w=["nh"])
            k.memset("pool", t["Enew"][:], 0.0, w=["Enew"])
            k.memset("pool", t["bada"][:], 0.0, w=["bada"])
            k.memset("pool", t["ones0"][:], 0.0, w=["ones0"])
            k.memset("pool", t["ones0"][0:1, :], 1.0, w=["ones0"])
            k.memset("pool", t["QTs"][:], 0.0, w=["QTs"])
            k.memset("pool", t["vf"][:], 0.0, w=["vf"])
            k.memset("pool", t["Enewb"][:], 0.0, w=["Enewb"])
            k.memset("pool", t["onesf"][:], 1.0, w=["onesf"])
            k.memset("pool", t["o32"][:], 1.0 / 32, w=["o32"])
            k.memset("pool", t["Wbdk"][:], 0.0, w=["Wbdk"])
            k.memset("pool", t["Wbdv"][:], 0.0, w=["Wbdv"])
            k.memset("pool", t["bias0"][:], 0.0, w=["bias0"])
            k.memset("pool", t["bias0"][:, 0:1], 1.0e4, w=["bias0"])
            for g in range(2):
                gs = slice(g * 64, (g + 1) * 64)
                k.dma("pool", t["Wbdk"][gs, gs], d["wck"], w=["Wbdk"])
                k.dma("pool", t["Wbdv"][gs, gs], d["wcv"], w=["Wbdv"])
                k.dma("sp", t["pe2"][:, gs], d["pek"], w=["pe2k"])
            k.mm(t["pM"][:, 0:1], t["pe2"][:, :], t["o32"][:, 0:1], r=["pe2k", "o32"], w=["pM"])
            k.cp("dve", t["pebar"][:, 0:1], t["pM"][:, 0:1], r=["pM"], w=["pebar"])
            for g in range(2):
                gs = slice(g * 64, (g + 1) * 64)
                k.dma("sp", t["pe2"][:, gs], d["pev"], r=[], w=["pe2k"])
            k.mm(t["pM"][:, 0:1], t["pe2"][:, :], t["o32"][:, 0:1], r=["pe2k", "o32"], w=["pM"])
            k.cp("dve", t["pebar"][:, 1:2], t["pM"][:, 0:1], r=["pM"], w=["pebar"])
            for g in range(4):
                k.dma("sp", t["wsl"], d["ws"][g], w=["B3"])
                k.tr(t["pM"][:, 0:128], t["wsl"], t["identf"][:], r=["B3", "identf"], w=["pM"])
                k.tt("dve", t["wsT"][:, g, :], t["pM"][:, 0:128], t["tri_le"][:], ALU.mult, r=["pM", "tri_le"], w=["wsT"])
            winv = d["w_in"].rearrange("(kc p) n -> p kc n", p=128)
            for j in range(11):
                c0, c1 = j * 512, min(DIN, (j + 1) * 512)
                k.dma("pool", t["Win"][:, :, c0:c1], winv[:, :, c0:c1], w=["win%d" % j])
            k.dma("sp", t["cp8"][:], d["cpv"].rearrange("(kc p) -> p kc", p=128), w=["cp8"], slow=True)
            k.dma("sp", t["x"][0:NSMP, :], d["csv"], w=["x"])
            pMv = t["pM"][:, 0:8 * NSMP].rearrange("p (a b) -> p a b", a=8)
            for kc in range(8):
                k.tr(pMv[:, kc, :], t["x"][0:NSMP, kc * 128:(kc + 1) * 128], t["identf"][0:NSMP, 0:NSMP],
                     r=["x", "identf"], w=["pM"])
            k.cp("dve", t["cTs"][:], pMv, r=["pM"], w=["cTs"])
            mod_pass(k, t, d, NSMP, t["cTs"], "cTs", False)
            k.ts("dve", t["idx"][:], t["ptrep"][:], 8.0, t["rmod8"][:, 0:1], ALU.mult, ALU.add, r=["ptrep", "rmod8"], w=["idx"])
            k.dma("sp", t["cs"][0:NSMP, :], d["rope"][S:S + 1, :].partition_broadcast(NSMP), w=["cs"])
            for _ in front(k, t, NSMP, d["xs"], "cs", "0", False):
                pass
            n = NSMP
            N = slice(0, n)
            k.dma("sp", d["sk_cmp"], t["kf"][N, 0:128], r=["kf"], is_out=True)
            k.dma("sp", d["sk_slc"], t["kf"][N, 128:256], r=["kf"], is_out=True)
            k.dma("sp", d["sv_cmp"], t["vf"][N, 0:128], r=["vf"], is_out=True)
            k.dma("sp", d["sv_slc"], t["vf"][N, 128:256], r=["vf"], w=["d_svslc"], is_out=True)
            k.dma("sp", d["svch"], t["vn_f"][N, :], r=["vn_f"], is_out=True)
            k.dma("sp", d["sk_win"][:, 0:511, :], d["kwin"][:, 1:512, :], is_out=True)
            k.dma("sp", d["sv_win"][:, 0:511, :], d["vwin"][:, 1:512, :], is_out=True)
            k.dma("sp", d["sk_win"][:, 511, :], t["kf"][N, 256:384], r=["kf"], is_out=True)
            k.dma("sp", d["sv_win"][:, 511, :], t["vf"][N, 256:384], r=["vf"], w=["d_svwin"], is_out=True)
            pTv8 = t["pT"][:].rearrange("p (a b) -> p a b", a=8)
            for r in range(4):
                k.tr(pTv8[:, r, 0:n], t["q_bf"][N, r * 128:(r + 1) * 128], t["identb"][N, N], r=["q_bf", "identb"], w=["pT"])
            k.cp("act", t["QTs"][0:64, 0, :].rearrange("p (r s) -> p r s", r=4), pTv8[0:64, 0:4, 0:n], r=["pT"], w=["QTs"])
            k.cp("act", t["QTs"][64:128, 1, :].rearrange("p (r s) -> p r s", r=4), pTv8[64:128, 0:4, 0:n], r=["pT"], w=["QTs"])
            Env = t["Enew"][:].rearrange("p (x s h) -> p x s h", x=2, s=NSMP)
            Env16 = t["Enew"][0:NSMP, :].rearrange("p (x s h) -> p x s h", x=2, s=NSMP)
            for xx in range(2):
                kcol = t["k_bf"][N, 128 + xx * 128:256 + xx * 128].rearrange("p (g d) -> p g d", g=2).unsqueeze(2)
                k.tt("dve", t["B2"][N, :].rearrange("p (g r d) -> p g r d", g=2, r=4),
                     t["q_bf"][N, :].rearrange("p (r g d) -> p g r d", r=4, g=2), bc(kcol, [n, 2, 4, 64]), ALU.mult,
                     r=["q_bf", "k_bf"], w=["B2"])
                k.red(t["enr"][:, xx * 8:(xx + 1) * 8], v3(t["B2"][N, :], 8), ALU.add, r=["B2"], w=["enr"])
            k.act(t["en"][:], t["enr"][:], AF.Exp, r=["enr"], w=["en"], scale=SCL)
            for xx in range(2):
                k.tt("dve", Env16[:, xx, :, :], bc(t["identf"][N, N].unsqueeze(2), [n, NSMP, 8]),
                     bc(t["en"][:, xx * 8:(xx + 1) * 8].unsqueeze(1), [n, NSMP, 8]), ALU.mult, r=["identf", "en"], w=["Enew"])
            k.cp("act", t["vfb"][:], t["vf"][:, 128:384], r=["vf"], w=["vfb"])
            k.cp("act", t["Enewb"][0:NSMP, :], t["Enew"][0:NSMP, :], r=["Enew"], w=["Enewb"])
            Envb = t["Enewb"][:].rearrange("p (x s h) -> p x s h", x=2, s=NSMP)
            pS, pO, pD, pM = t["pS0"], t["pO0"], t["pO1"], t["pM"]
            pOv = pO[:, 0:384].rearrange("p (s x h) -> p s x h", s=NSMP, x=3)
            pDv = pD[:, 0:384].rearrange("p (s x h) -> p s x h", s=NSMP, x=3)
            PcTv = t["PcT"][:].rearrange("p (s h) -> p s h", s=NSMP)
            for s in range(NSMP):
                k.gather(t["G0"][:], d["kcmp"], t["idx"][:, s:s + 1], r=["idx"], w=["G0"])
                k.gather(t["G1"][:], d["vcmp"], t["idx"][:, s:s + 1], r=["idx"], w=["G1"])
                k.cp("act", t["Gb0"][:], t["G0"][:], r=["G0"], w=["Gb0"])
                k.cp("act", t["Gb1"][:], t["G1"][:], r=["G1"], w=["Gb1"])
                for tt_ in range(16):
                    k.mm(pM[:, 0:64], t["Gb0"][:, tt_ * 128:(tt_ + 1) * 128], t["pool2b"][:], start=(tt_ == 0), stop=(tt_ == 15),
                         r=["Gb0", "pool2b"], w=["pM"])
                for tt_ in range(16):
                    k.mm(pM[:, 64:128], t["Gb1"][:, tt_ * 128:(tt_ + 1) * 128], t["pool2b"][:], start=(tt_ == 0), stop=(tt_ == 15),
                         r=["Gb1", "pool2b"], w=["pM"])
                k.ts("dve", t["pTk"][:], pM[:, 0:64], t["pebar"][:, 0:1], None, ALU.add, r=["pM", "pebar"], w=["pTk"])
                k.ts("dve", t["pTv"][:], pM[:, 64:128], t["pebar"][:, 1:2], None, ALU.add, r=["pM", "pebar"], w=["pTv"])
                k.mm(pM[:, 128:192], t["Wbdk"][:], t["pTk"][:], r=["Wbdk", "pTk"], w=["pM"])
                k.mm(pM[0:64, 192:320], t["pTv"][:], t["Wbdv"][:], r=["Wbdv", "pTv"], w=["pM"])
                k.cp("act", t["KcTs"][:], pM[:, 128:192], r=["pM"], w=["KcTs"])
                k.cp("act", t["Vcs"][:], pM[0:64, 192:320], r=["pM"], w=["Vcs"])
                for g in range(2):
                    gs = slice(g * 64, (g + 1) * 64)
                    k.mm(pS[0:64, s * 8 + g * 4:s * 8 + g * 4 + 4], t["KcTs"][:, :],
                         t["QTs"][:, g, :].rearrange("p (r s) -> p r s", r=4)[:, :, s], r=["KcTs", "QTs"], w=["pS0"])
                k.act(PcTv[:, s, :], pS[0:64, s * 8:(s + 1) * 8], AF.Exp, r=["pS0"], w=["PcT"], scale=SCL)
                k.mm(pOv[:, s, 0, :], t["Vcs"][:], PcTv[:, s, :], r=["Vcs", "PcT"], w=["pO0"])
                k.mm(pDv[:, s, 0, :], t["onesf"][0:64, :], PcTv[:, s, :], r=["onesf", "PcT"], w=["pO1"])
                k.mm(pS[0:32, 128 + s * 8:128 + (s + 1) * 8], t["pairf"][:], PcTv[:, s, :], r=["pairf", "PcT"], w=["pS0"])
            k.P.add("dve", lambda e: e.reciprocal(out=t["rDc"][:].rearrange("p (s h) -> p s h", s=NSMP), in_=pDv[0:32, :, 0, :]),
                    ["pO1"], ["rDc"])
            k.tt("dve", t["impn"][:], pS[0:32, 128:256], t["rDc"][:], ALU.mult, r=["pS0", "rDc"], w=["impn"])
            k.red(t["impT"][:], t["impn"][:].rearrange("p (a r) -> p a r", r=4), ALU.add, r=["impn"], w=["impT"])
            k.tr(pS[0:32, 256:288], t["impT"][:], t["identf"][0:32, 0:32], r=["impT", "identf"], w=["pS0"])
            k.tt("dve", t["impS"][:], pS[0:32, 256:288], t["bias0"][:], ALU.add, r=["pS0", "bias0"], w=["impS"])
            k.P.add("dve", lambda e: e.max(out=t["m8s"][:], in_=t["impS"][:]), ["impS"], ["m8s"])
            k.ts("dve", t["selS"][:], t["impS"][:], t["m8s"][:, 6:7], None, ALU.is_ge, r=["impS", "m8s"], w=["selS"])
            k.tr(t["pT"][0:32, 0:32], t["selS"][:], t["identb"][0:32, 0:32], r=["selS", "identb"], w=["pT"])
            k.cp("act", t["selTs"][:], t["pT"][0:32, 0:32], r=["pT"], w=["selTs"])
            k.mm(pS[:, 320:352], t["e4"][:], t["selTs"][:], r=["e4", "selTs"], w=["pS0"])
            k.cp("act", t["Msk"][:], pS[:, 320:352], r=["pS0"], w=["Msk"])
            pS = t["pS1"]
            banks = [(t["pA"], "pA"), (t["pB"], "pB")]
            for s in range(NSMP):
                k.gather(t["G0"][:], d["kslc"], t["idx"][:, s:s + 1], r=["idx"], w=["G0"])
                k.gather(t["G1"][:], d["vslc"], t["idx"][:, s:s + 1], r=["idx"], w=["G1"])
                k.dma("sp", t["Gw0"][:].rearrange("p (a c) -> p a c", a=4), d["kwin"][s].rearrange("(p a) c -> p a c", a=4), w=["Gw0"])
                k.dma("sp", t["Gw1"][:].rearrange("p (a c) -> p a c", a=4), d["vwin"][s].rearrange("(p a) c -> p a c", a=4), w=["Gw1"])
                k.cp("act", t["Gb0"][:], t["G0"][:], r=["G0"], w=["Gb0"])
                k.cp("act", t["Gwb0"][:], t["Gw0"][:], r=["Gw0"], w=["Gwb0"])
                k.cp("act", t["Gb1"][:], t["G1"][:], r=["G1"], w=["Gb1"])
                k.cp("act", t["Gwb1"][:], t["Gw1"][:], r=["Gw1"], w=["Gwb1"])
                tbanks = [(t["pT"][:].rearrange("p (a c) -> p a c", a=8), "pT"),
                          (t["pA"][:].bitcast(BF16).rearrange("p (a c) -> p a c", a=8), "pA"),
                          (t["pB"][:].bitcast(BF16).rearrange("p (a c) -> p a c", a=8), "pB")]
                for q8 in range(3):
                    pTk8, tbn = tbanks[q8]
                    nt_ = 8 if q8 < 2 else 4
                    for a in range(nt_):
                        tix = q8 * 8 + a
                        if tix < 16:
                            src, sname, col = t["Gb0"], "Gb0", tix * 128
                        else:
                            src, sname, col = t["Gwb0"], "Gwb0", (tix - 16) * 128
                        k.tr(pTk8[:, a, :], src[:, col:col + 128], t["identb"][:], r=[sname, "identb"], w=[tbn])
                    k.cp("act", t["KTs"][:, q8 * 8:q8 * 8 + nt_, :], pTk8[:, 0:nt_, :], r=[tbn], w=["KTs"])
                for tt_ in range(20):
                    for g in range(2):
                        gs = slice(g * 64, (g + 1) * 64)
                        k.mm(pS[:, tt_ * 8 + g * 4:tt_ * 8 + g * 4 + 4], t["KTs"][:, tt_, :],
                             t["QTs"][:, g, :].rearrange("p (r s) -> p r s", r=4)[:, :, s], r=["KTs", "QTs"], w=["pS1"])
                k.act(t["PTs"][:], pS[:, 0:160], AF.Exp, r=["pS1"], w=["PTs"], scale=SCL)
                k.tt("dve", t["PTs"][:, 0:128].rearrange("p (a g r) -> p a g r", a=16, g=2),
                     t["PTs"][:, 0:128].rearrange("p (a g r) -> p a g r", a=16, g=2),
                     bc(t["Msk"][:, s * 2:(s + 1) * 2].unsqueeze(1).unsqueeze(3), [128, 16, 2, 4]), ALU.mult,
                     r=["PTs", "Msk"], w=["PTs"])
                k.memset("dve", t["PTs"][0:1, 128:136], 0.0, w=["PTs"])
                k.red(t["PTsum"][:, 0:8], t["PTs"][:, 0:128].rearrange("p (a h) -> p h a", a=16), ALU.add, r=["PTs"], w=["PTsum"])
                k.red(t["PTsum"][:, 8:16], t["PTs"][:, 128:160].rearrange("p (a h) -> p h a", a=4), ALU.add, r=["PTs"], w=["PTsum"])
                k.cp("act", t["PTb"][:], t["PTs"][:], r=["PTs"], w=["PTb"])
                for tt_ in range(16):
                    k.mm(pOv[:, s, 1, :], t["Gb1"][:, tt_ * 128:(tt_ + 1) * 128], t["PTb"][:, tt_ * 8:(tt_ + 1) * 8],
                         start=(tt_ == 0), stop=False, r=["Gb1", "PTb"], w=["pO0"])
                k.mm(pOv[:, s, 1, :], t["vfb"][:, 0:128], Envb[:, 0, s, :], start=False, stop=True,
                     r=["vfb", "Enewb"], w=["pO0"])
                for tt_ in range(4):
                    k.mm(pOv[:, s, 2, :], t["Gwb1"][:, tt_ * 128:(tt_ + 1) * 128], t["PTb"][:, 128 + tt_ * 8:128 + (tt_ + 1) * 8],
                         start=(tt_ == 0), stop=False, r=["Gwb1", "PTb"], w=["pO0"])
                k.mm(pOv[:, s, 2, :], t["vfb"][:, 128:256], Envb[:, 1, s, :], start=False, stop=True,
                     r=["vfb", "Enewb"], w=["pO0"])
                for xx in range(2):
                    k.mm(pDv[:, s, 1 + xx, :], t["onesf"][:, :], t["PTsum"][:, xx * 8:(xx + 1) * 8], start=True, stop=False,
                         r=["onesf", "PTsum"], w=["pO1"])
                    k.mm(pDv[:, s, 1 + xx, :], t["onesf"][:, :], Env[:, xx, s, :], start=False, stop=True,
                         r=["onesf", "Enew"], w=["pO1"])
            k.P.add("dve", lambda e: e.reciprocal(out=t["rDall"], in_=pD[:, 0:384]), ["pO1"], ["B1a", "B1b"])
            k.tt("dve", t["OTn"], pO[:, 0:384], t["rDall"], ALU.mult, r=["pO0", "B1a", "B1b"], w=["B1a", "B1b"])
            k.cp("dve", t["OT1"], t["OTn"][64:128, :], r=["B1a", "B1b"], w=["B2"])
            gwv = t["gw0"][N, :].rearrange("p (h x) -> p x h", x=3)
            for xx in range(3):
                bank, bname = banks[xx % 2]
                for hh in range(8):
                    src = t["OTn"] if hh < 4 else t["OT1"]
                    sname = "B1a" if hh < 4 else "B2"
                    inap = src[0:64, :].rearrange("p (s c) -> p s c", s=NSMP)[:, :, xx * 8 + hh]
                    k.tr(bank[0:n, hh * 64:(hh + 1) * 64], inap, t["identf"][0:64, 0:64], r=[sname, "identf"], w=[bname])
                if xx == 0:
                    k.tt("dve", v3(t["o_a"][N, :], 8), v3(bank[N, :], 8), bc(gwv[:, xx, :].unsqueeze(2), [n, 8, 64]), ALU.mult,
                         r=[bname, "gw0"], w=["o_a"])
                else:
                    k.tt("dve", v3(t["tmpO"][N, :], 8), v3(bank[N, :], 8), bc(gwv[:, xx, :].unsqueeze(2), [n, 8, 64]), ALU.mult,
                         r=[bname, "gw0"], w=["B3"])
                    k.tt("pool", t["o_a"][N, :], t["o_a"][N, :], t["tmpO"][N, :], ALU.add, r=["o_a", "B3"], w=["o_a"])
            k.dma("sp", d["gqs"], t["gq"][N, :], r=["gq"], w=["d_gqs"])
            mod_pass(k, t, d, 128, t["cp8"], "cp8", True)
            P.emit(es)

        with contextlib.ExitStack() as s2:
            P = Prog(nc, "b")
            k = K(P)
            sb("Wout", [128, 8, D], BF16, s2); sb("WA", [128, 4, D], BF16, s2); sb("WB", [128, 4, D], BF16, s2)
            k.dma("pool", t["Wout"][:], d["wout"].rearrange("(kc p) n -> p kc n", p=128), w=["Wout"])
            for g in range(2):
                k.dma("pool", t["WA"][g * 64:(g + 1) * 64, :, :],
                      d["wbra"][g * 256:(g + 1) * 256, :].rearrange("(r dd) n -> dd r n", dd=64), w=["WA"])
            k.dma("pool", t["WB"][:], d["wbrb"].rearrange("(c p) n -> p c n", p=128), w=["WB"])
            k.dma("sp", t["B1"][0:NSMP, :], d["gqs"], w=["B1a", "B1b"])
            for _ in tail(k, t, NSMP, d["ys"], d["xs"], "0", gate=[(t["B1"][:, 0:512], "B1a"), (t["B1"][:, 512:1024], "B1b")]):
                pass
            sb("KsT", [128, S], BF16, s2); sb("KwT", [128, 5 * 128], BF16, s2)
            sb("Vs", [128, NT, 2, 65], BF16, s2); sb("Vw", [128, 5, 2, 65], BF16, s2)
            sb("QT", [128, 2, 512], BF16, s2); sb("vcb", [128, 128], BF16, s2)
            sb("pTk2", [128, 64], BF16, s2); sb("pTv2", [128, 64], BF16, s2); sb("KcT", [128, 64], BF16, s2)
            sb("Vc", [64, 2, 97], BF16, s2)
            sb("PT0", [128, 512], BF16, s2); sb("PT1", [128, 512], BF16, s2)
            sb("nsel4", [128, 2, 512], BF16, s2); sb("imp", [128, 64], F32, s2)
            sb("sel", [128, 64], BF16, s2); sb("m8", [128, 16], F32, s2); sb("mctneg", [128, 512], BF16, s2); sb("ibias", [128, 32], F32, s2)
            sb("tnle", [128, 512], BF16, s2); sb("tngt", [128, 512], BF16, s2)
            k.dma("pool", t["tnle"][:], d["trineg_le"], w=["tnle"])
            k.dma("pool", t["tngt"][:], d["trineg_gt"], w=["tngt"])
            n = 128
            N = slice(0, 128)
            k.memset("pool", t["Vs"][:], 1.0, w=["Vs"])
            k.memset("pool", t["Vw"][:], 1.0, w=["Vw"])
            k.memset("pool", t["Vc"][:], 1.0, w=["Vc"])
            k.memset("pool", t["pTk2"][:], 0.0, w=["pTk2"])
            k.memset("pool", t["pTv2"][:], 0.0, w=["pTv2"])
            k.memset("pool", t["KcT"][:], 0.0, w=["KcT"])
            k.memset("pool", t["QT"][:], 0.0, w=["QT"])
            k.memset("pool", t["nsel4"][:], 0.0, w=["nsel4"])
            k.dma("pool", t["mctneg"][:], d["cmpbase"], w=["mctneg"])
            for g in range(2):
                k.dma("pool", t["Vc"][:, g, 65:97], d["pair"], w=["Vc"])
            pS = [(t["pS0"], "pS0"), (t["pS1"], "pS1")]
            pO = [(t["pO0"], "pO0"), (t["pO1"], "pO1")]
            PT = [(t["PT0"], "PT0"), (t["PT1"], "PT1")]
            pM = t["pM"]
            pTv8 = t["pT"][:].rearrange("p (a b) -> p a b", a=8)
            cnt = {"s": 0, "p": 0}
            cur = {}
            sfx_of = lambda ii: str((ii + 1) % 2)

            def branch_finish(xx):
                for g in range(2):
                    bank, bname = pO[g]
                    ov = bank[:, 0:388].rearrange("p (r c) -> p r c", r=4)
                    k.ts("dve", t["rden"][:, g * 4:(g + 1) * 4], ov[:, :, 64], 1e-30, None, ALU.max, r=[bname], w=["rden"])
                k.P.add("dve", lambda e: e.reciprocal(out=t["rden"][:], in_=t["rden"][:]), ["rden"], ["rden"])
                k.tt("dve", t["cx"][:], t["rden"][:], cur["gwv"][:, xx, :], ALU.mult, r=["rden", cur["gwn"]], w=["cx"])
                for g in range(2):
                    bank, bname = pO[g]
                    ov = bank[:, 0:388].rearrange("p (r c) -> p r c", r=4)
                    for r in range(4):
                        hs = slice((g * 4 + r) * 64, (g * 4 + r + 1) * 64)
                        k.stt(t["o_a"][:, hs], ov[:, r, 0:64], t["cx"][:, g * 4 + r:g * 4 + r + 1], t["o_a"][:, hs], ALU.mult, ALU.add,
                              r=[bname, "cx", "o_a"], w=["o_a"])

            k.dma("sp", t["cs"][:], d["rope"][0:128, :], w=["cs"])
            for _ in front(k, t, 128, d["xp"][0:128, :], "cs", sfx_of(0), True):
                pass
            tgen = {"g": None}

            def tail_hook(nit):
                if tgen["g"] is not None:
                    next(tgen["g"], None)

            for i in range(NT):
                rows = slice(i * 128, (i + 1) * 128)
                sfx = sfx_of(i)
                cur["gwn"] = "gw" + sfx
                cur["gwv"] = t["gw" + sfx][:, :].rearrange("p (h x) -> p x h", x=3)
                k.dma("sp", d["pk_cmp"][rows, :], t["kf"][:, 0:128], r=["kf"], is_out=True)
                k.dma("sp", d["pk_slc"][rows, :], t["kf"][:, 128:256], r=["kf"], is_out=True)
                k.dma("sp", d["pv_cmp"][rows, :], t["vf"][:, 0:128], r=["vf"], is_out=True)
                k.dma("sp", d["pv_slc"][rows, :], t["vf"][:, 128:256], r=["vf"], is_out=True)
                if i >= NT - 4:
                    wr = slice((i - (NT - 4)) * 128, (i - (NT - 4) + 1) * 128)
                    k.dma("sp", d["pk_win"][wr, :], t["kf"][:, 256:384], r=["kf"], is_out=True)
                    k.dma("sp", d["pv_win"][wr, :], t["vf"][:, 256:384], r=["vf"], is_out=True)
                slot = i % 5
                k.cp("pool", t["Vs"][:, i, :, 0:64], v3(t["vf"][:, 128:256], 2), r=["vf"], w=["Vs"])
                k.cp("pool", t["Vw"][:, slot, :, 0:64], v3(t["vf"][:, 256:384], 2), r=["vf"], w=["Vw"])
                k.cp("pool", t["vcb"][:], t["vf"][:, 0:128], r=["vf"], w=["vcb"])
                for r in range(4):
                    k.tr(pTv8[:, r, :], t["q_bf"][:, r * 128:(r + 1) * 128], t["identb"][:], r=["q_bf", "identb"], w=["pT"])
                k.tr(pTv8[:, 4, :], t["k_bf"][:, 128:256], t["identb"][:], r=["k_bf", "identb"], w=["pT"])
                k.tr(pTv8[:, 5, :], t["k_bf"][:, 256:384], t["identb"][:], r=["k_bf", "identb"], w=["pT"])
                k.cp("act", t["QT"][0:64, 0, :].rearrange("p (r q) -> p r q", r=4), pTv8[0:64, 0:4, :], r=["pT"], w=["QT"])
                k.cp("act", t["QT"][64:128, 1, :].rearrange("p (r q) -> p r q", r=4), pTv8[64:128, 0:4, :], r=["pT"], w=["QT"])
                k.cp("act", t["KsT"][:, rows], pTv8[:, 4, :], r=["pT"], w=["KsT"])
                k.cp("act", t["KwT"][:, slot * 128:(slot + 1) * 128], pTv8[:, 5, :], r=["pT"], w=["KwT"])
                cc = slice(4 * i, 4 * i + 4)
                k.mm(pM[:, 0:4], t["k_bf"][:, 0:128], t["pool4"][:], r=["k_bf", "pool4"], w=["pM"])
                k.mm(pM[:, 4:8], t["vcb"][:], t["pool4"][:], r=["vcb", "pool4"], w=["pM"])
                k.ts("dve", t["pTk2"][:, cc], pM[:, 0:4], t["pebar"][:, 0:1], None, ALU.add, r=["pM", "pebar"], w=["pTk2"])
                k.ts("dve", t["pTv2"][:, cc], pM[:, 4:8], t["pebar"][:, 1:2], None, ALU.add, r=["pM", "pebar"], w=["pTv2"])
                k.mm(pM[:, 8:12], t["Wbdk"][:], t["pTk2"][:, cc], r=["Wbdk", "pTk2"], w=["pM"])
                k.mm(pM[0:64, 16:144], t["pTv2"][:], t["Wbdv"][:], r=["Wbdv", "pTv2"], w=["pM"])
                k.cp("act", t["KcT"][:, cc], pM[:, 8:12], r=["pM"], w=["KcT"])
                k.cp("act", t["Vc"][:, :, 0:64], v3(pM[0:64, 16:144], 2), r=["pM"], w=["Vc"])

                k.dma("sp", t["ibias"][:], d["impbias"][i], w=["ibias"])
                gen = None
                if i + 1 < NT:
                    nrows = slice((i + 1) * 128, (i + 2) * 128)
                    k.dma("sp", t["cs"][:], d["rope"][nrows, :], w=["cs"])
                    gen = front(k, t, 128, d["xp"][nrows, :], "cs", sfx_of(i + 1), True)
                    next(gen, None)
                qt = {g: t["QT"][:, g, :] for g in range(2)}

                def run_branch(items, hook=None):
                    recs = []

                    def s_stage(it):
                        g, kp, mms, v_ap, v_name, first, last = it
                        sbank, sname = pS[cnt["s"] % 2]; cnt["s"] += 1
                        pt, pname = PT[cnt["p"] % 2]; cnt["p"] += 1
                        for mi, (lh, rh, nm) in enumerate(mms):
                            k.mm(sbank[0:kp, :], lh, rh, start=(mi == 0), stop=(mi == len(mms) - 1), r=nm, w=[sname])
                        recs.append((sbank, sname, pt, pname))

                    def e_stage(idx):
                        g, kp, mms, v_ap, v_name, first, last = items[idx]
                        sbank, sname, pt, pname = recs[idx]
                        k.act(pt[0:kp, :], sbank[0:kp, :], AF.Exp, r=[sname], w=[pname], scale=SCL)
                        obank, oname = pO[g]
                        ov = obank[:, 0:388].rearrange("p (r c) -> p r c", r=4)
                        nv = v_ap.shape[-1]
                        for r in range(4):
                            k.mm(ov[:, r, 0:nv], pt[0:kp, r * 128:(r + 1) * 128], v_ap, start=(first and r == 0), stop=(last and r == 3),
                                 r=[pname, v_name], w=[oname])

                    s_stage(items[0])
                    for idx in range(len(items)):
                        if idx + 1 < len(items):
                            s_stage(items[idx + 1])
                        e_stage(idx)
                        if hook is not None:
                            hook(len(items))

                items = []
                for g in range(2):
                    gs = slice(g * 64, (g + 1) * 64)
                    items.append((g, 64, [(t["KcT"][:, :], qt[g], ["KcT", "QT"]),
                                          (t["identb"][:, 64 - 4 * i:128 - 4 * i], t["mctneg"][:, :], ["identb", "mctneg"])],
                                  t["Vc"][:, g, :], "Vc", True, True))
                run_branch(items)
                for g in range(2):
                    bank, bname = pO[g]
                    ov = bank[:, 0:388].rearrange("p (r c) -> p r c", r=4)
                    k.ts("dve", t["rden"][:, g * 4:(g + 1) * 4], ov[:, :, 64], 1e-30, None, ALU.max, r=[bname], w=["rden"])
                k.P.add("dve", lambda e: e.reciprocal(out=t["rden"][:], in_=t["rden"][:]), ["rden"], ["rden"])
                for g in range(2):
                    bank, bname = pO[g]
                    ov = bank[:, 0:388].rearrange("p (r c) -> p r c", r=4)
                    ig = t["imp"][:, g * 32:(g + 1) * 32]
                    k.stt(ig, ov[:, 0, 65:97], t["rden"][:, g * 4:g * 4 + 1], t["ibias"][:], ALU.mult, ALU.add,
                          r=[bname, "rden", "ibias"], w=["imp"])
                    for r in range(1, 4):
                        k.stt(ig, ov[:, r, 65:97], t["rden"][:, g * 4 + r:g * 4 + r + 1], ig, ALU.mult, ALU.add,
                              r=[bname, "rden", "imp"], w=["imp"])
                k.tt("dve", t["cx"][:], t["rden"][:], cur["gwv"][:, 0, :], ALU.mult, r=["rden", cur["gwn"]], w=["cx"])
                for g in range(2):
                    bank, bname = pO[g]
                    ov = bank[:, 0:388].rearrange("p (r c) -> p r c", r=4)
                    k.tt("dve", v3(t["o_a"][:, g * 256:(g + 1) * 256], 4), ov[:, :, 0:64],
                         bc(t["cx"][:, g * 4:(g + 1) * 4].unsqueeze(2), [128, 4, 64]), ALU.mult, r=[bname, "cx"], w=["o_a"])
                for g in range(2):
                    ig = t["imp"][:, g * 32:(g + 1) * 32]
                    k.P.add("dve", (lambda g: lambda e: e.max(out=t["m8"][:, g * 8:(g + 1) * 8], in_=t["imp"][:, g * 32:(g + 1) * 32]))(g),
                            ["imp"], ["m8"])
                    k.ts("dve", t["sel"][:, g * 32:(g + 1) * 32], ig, t["m8"][:, g * 8 + 7:g * 8 + 8], None, ALU.is_ge,
                         r=["imp", "m8"], w=["sel"])
                j0 = max(0, i - 4)
                items = []
                for g in range(2):
                    gs = slice(g * 64, (g + 1) * 64)
                    for j in range(j0, i + 1):
                        sl = j % 5
                        mms = [(t["KwT"][:, sl * 128:(sl + 1) * 128], qt[g], ["KwT", "QT"])]
                        if j == i:
                            mms.append((t["identb"][:, :], t["tnle"][:, :], ["identb", "tnle"]))
                        elif j == i - 4:
                            mms.append((t["identb"][:, :], t["tngt"][:, :], ["identb", "tngt"]))
                        items.append((g, 128, mms, t["Vw"][:, sl, g, :], "Vw", j == j0, j == i))
                run_branch(items, tail_hook)
                if tgen["g"] is not None:
                    for _ in tgen["g"]:
                        pass
                    tgen["g"] = None
                branch_finish(2)
                k.tr(t["pT"][0:64, 0:128], t["sel"][:], t["identb"][:], r=["sel", "identb"], w=["pT"])
                for g in range(2):
                    g32 = slice(g * 32, (g + 1) * 32)
                    k.ts("dve", v3(t["nsel4"][g32, g, :], 4), bc(t["pT"][g32, 0:128].unsqueeze(1), [32, 4, 128]), -1.0, 30000.0,
                         ALU.add, ALU.mult, r=["pT"], w=["nsel4"])
                items = []
                for g in range(2):
                    gs = slice(g * 64, (g + 1) * 64)
                    g32 = slice(g * 32, (g + 1) * 32)
                    for j in range(i + 1):
                        mms = [(t["KsT"][:, j * 128:(j + 1) * 128], qt[g], ["KsT", "QT"]),
                               (t["e2"][:, j * 128:(j + 1) * 128], t["nsel4"][:, g, :], ["e2", "nsel4"])]
                        if j == i:
                            mms.append((t["identb"][:, :], t["tnle"][:, :], ["identb", "tnle"]))
                        items.append((g, 128, mms, t["Vs"][:, j, g, :], "Vs", j == 0, j == i))
                acc = {"a": 0.0}

                def hook(nit):
                    if gen is None:
                        return
                    acc["a"] += 14.0 / nit
                    while acc["a"] >= 1.0:
                        acc["a"] -= 1.0
                        next(gen, None)

                run_branch(items, hook)
                if gen is not None:
                    for _ in gen:
                        pass
                branch_finish(1)
                tgen["g"] = tail(k, t, 128, d["yp"][rows, :], d["xp"][rows, :], sfx)
                next(tgen["g"], None)
            for _ in tgen["g"]:
                pass
            P.emit(es)
    return nc


def _consts():
    c = {}
    half = 32
    inv = (10000.0 ** (-np.arange(half, dtype=np.float32) * 2.0 / 64)).astype(np.float32)
    ang = np.arange(S + 1, dtype=np.float32)[:, None] * inv[None, :]
    c["rope"] = np.concatenate([np.cos(ang), np.sin(ang)], axis=1).astype(np.float32)
    kk = np.arange(128)[:, None]
    qq = np.arange(128)[None, :]
    c["tri_le"] = (kk <= qq).astype(np.float32)
    c["tri_gt"] = (kk > qq).astype(np.float32)
    c["identf_c"] = np.eye(128, dtype=np.float32)
    c["pool4"] = ((np.arange(128)[:, None] // 32) == np.arange(4)[None, :]).astype(np.float32) / 32.0
    c["pair"] = ((np.arange(64)[:, None] // 2) == np.arange(32)[None, :]).astype(np.float32)
    e = ((np.arange(2048)[None, :] // 64) == np.arange(32)[:, None]).astype(np.float32)
    c["e2"] = np.concatenate([e, e], axis=0)
    m = np.zeros((NT, 64, 128), np.float32)
    ib = np.zeros((NT, 128, 32), np.float32)
    for i in range(NT):
        pos = i * 128 + np.arange(128)
        cend = (np.arange(64) + 1) * 32 - 1
        m[i] = (cend[:, None] <= pos[None, :]).astype(np.float32)
        qblk = pos // 64
        blk = np.arange(32)
        forced = (blk[None, :] == 0) | (blk[None, :] == qblk[:, None])
        causal = blk[None, :] <= qblk[:, None]
        ib[i] = np.where(causal, 1.0e4 * forced, -1.0e30).astype(np.float32)
    base = np.zeros((128, 128), np.float32)
    base[68:, :] = -30000.0
    for u in range(64, 68):
        base[u, :] = np.where(np.arange(128) < 32 * (u - 64) + 31, -30000.0, 0.0)
    c["cmpbase"] = np.ascontiguousarray(np.tile(base[:, None, :], (1, 4, 1)).reshape(128, 512)).astype(np.float32)
    c["trineg_le"] = np.ascontiguousarray(np.tile(((1.0 - c["tri_le"]) * -30000.0)[:, None, :], (1, 4, 1)).reshape(128, 512)).astype(np.float32)
    c["trineg_gt"] = np.ascontiguousarray(np.tile(((1.0 - c["tri_gt"]) * -30000.0)[:, None, :], (1, 4, 1)).reshape(128, 512)).astype(np.float32)
    c["impbias"] = ib
    c["pool2"] = ((np.arange(128)[:, None] // 2) == np.arange(64)[None, :]).astype(np.float32) / 32.0
    c["e4"] = ((np.arange(128)[None, :] // 4) == np.arange(32)[:, None]).astype(np.float32)
    c["rmod8"] = (np.arange(128) % 8).astype(np.float32).reshape(128, 1)
    return c


_NC_CACHE = {}


def kernel(x_prompt, x_sample, cache_k_cmp, cache_v_cmp, cache_k_slc, cache_v_slc,
           cache_k_win, cache_v_win, page_table, c_prompt, c_sample,
           w_ada, b_ada, norm_g, w_in, q_norm_g, k_norm_g, cmp_pos_k, cmp_pos_v,
           w_cmp_k, w_cmp_v, vnorm_g, vnorm_b, w_s, b_s, w_br_a, w_br_b, w_out):
    f = lambda a: np.ascontiguousarray(np.asarray(a), dtype=np.float32)
    if "nc" not in _NC_CACHE:
        _NC_CACHE["nc"] = build_program()
    nc = _NC_CACHE["nc"]
    consts = _consts()
    pools = {nm: f(a).reshape(2560 * 8, 2048) for nm, a in
             (("kcmp", cache_k_cmp), ("vcmp", cache_v_cmp), ("kslc", cache_k_slc), ("vslc", cache_v_slc))}
    shared = dict(
        w_ada=f(w_ada)[0], b_ada=f(b_ada)[0].reshape(1, -1), norm_g=f(norm_g)[0].reshape(1, -1), w_in=f(w_in)[0],
        qg=f(q_norm_g)[0].reshape(1, -1), kg=f(k_norm_g)[0].reshape(1, -1), pek=f(cmp_pos_k)[0], pev=f(cmp_pos_v)[0],
        wck=f(w_cmp_k)[0], wcv=f(w_cmp_v)[0], vng=f(vnorm_g)[0].reshape(1, -1), vnb=f(vnorm_b)[0].reshape(1, -1),
        ws=f(w_s)[0], bs=f(b_s)[0], wbra=f(w_br_a)[0], wbrb=f(w_br_b)[0], wout=f(w_out)[0])
    shared.update(pools)
    shared.update(consts)
    xp, xs = f(x_prompt), f(x_sample)
    kw, vw = f(cache_k_win)[0], f(cache_v_win)[0]
    pt = np.asarray(page_table).astype(np.int32)
    cp, cs = f(c_prompt), f(c_sample)
    in_maps = []
    for c in range(8):
        sl = slice(c * NSMP, (c + 1) * NSMP)
        m = dict(shared)
        m["xp"] = xp[c]
        m["xs"] = np.ascontiguousarray(xs[sl, 0, :])
        m["cpv"] = np.ascontiguousarray(cp[c])
        m["csv"] = np.ascontiguousarray(cs[sl])
        m["kwin"] = np.ascontiguousarray(kw[sl].reshape(NSMP, 512, 128))
        m["vwin"] = np.ascontiguousarray(vw[sl].reshape(NSMP, 512, 128))
        m["ptrep"] = np.ascontiguousarray(np.repeat(pt[sl].T, 8, axis=0))
        in_maps.append(m)
    res = run_bass_kernel_spmd(nc, in_maps, core_ids=list(range(8)))
    R = res.results
    cat = lambda nm: np.stack([R[c][nm] for c in range(8)], axis=0)
    y_p = cat("yp")
    y_s = np.concatenate([R[c]["ys"] for c in range(8)], axis=0).reshape(128, 1, D)
    outs = [y_p, y_s]
    for nm in ("pk_cmp", "pv_cmp", "pk_slc", "pv_slc"):
        outs.append(cat(nm).reshape(1, 8, S, 2, 64))
    for nm in ("pk_win", "pv_win"):
        outs.append(cat(nm).reshape(1, 8, 512, 2, 64))
    for nm in ("sk_cmp", "sv_cmp", "sk_slc", "sv_slc"):
        outs.append(np.concatenate([R[c][nm] for c in range(8)], axis=0).reshape(1, 128, 1, 2, 64))
    for nm in ("sk_win", "sv_win"):
        outs.append(np.concatenate([R[c][nm] for c in range(8)], axis=0).reshape(1, 128, 512, 2, 64))
    outs.append(np.concatenate([R[c]["svch"] for c in range(8)], axis=0).reshape(1, 128, 1, 512))
    return tuple(o.astype(np.float32) for o in outs)
```

```python
import contextlib
import numpy as np
import concourse.bass as bass
import concourse.mybir as mybir
from concourse.bass_utils import run_bass_kernel_spmd

F32 = mybir.dt.float32
BF16 = mybir.dt.bfloat16
I32 = mybir.dt.int32
ALU = mybir.AluOpType
AF = mybir.ActivationFunctionType
AX = mybir.AxisListType

D = 1024
DIN = 5400
S = 2048
NT = 16
NSMP = 16
EPS = 1e-6
SCL = 0.125
C_Q, C_K, C_V, C_NSA, C_ZA, C_U, C_VV, C_ZB, C_GA, C_GB = 0, 512, 896, 1280, 1304, 1816, 2328, 2840, 3352, 4376


class _Op:
    __slots__ = ("eng", "fn", "deps", "idx", "signal", "is_dma", "sem", "target", "count")


class Prog:
    ENGS = ("pe", "act", "dve", "pool", "sp")
    DMA_POOL = {"sp": 16, "pool": 12, "act": 2}

    def __init__(self, nc, tag):
        self.nc = nc
        self.tag = tag
        self.q = {e: [] for e in self.ENGS}
        self.last_w = {}
        self.readers = {}
        self.dma_n = {e: 0 for e in self.DMA_POOL}
        self.out_dmas = []

    def add(self, eng, fn, r=(), w=(), dma=False, out=False):
        op = _Op()
        op.eng, op.fn, op.is_dma, op.signal = eng, fn, dma, False
        op.deps = set()
        op.sem = None
        op.target = 0
        op.count = 0
        for b in r:
            lw = self.last_w.get(b)
            if lw is not None:
                op.deps.add(lw)
        for b in w:
            lw = self.last_w.get(b)
            if lw is not None:
                op.deps.add(lw)
            for rd in self.readers.get(b, ()):
                op.deps.add(rd)
        for b in r:
            self.readers.setdefault(b, []).append(op)
        for b in w:
            self.last_w[b] = op
            self.readers[b] = []
        op.deps.discard(op)
        op.idx = len(self.q[eng])
        self.q[eng].append(op)
        if dma:
            j = self.dma_n[eng]
            self.dma_n[eng] += 1
            op.sem = (eng, j % self.DMA_POOL[eng])
            op.target = 16 * (j // self.DMA_POOL[eng] + 1)
            if out:
                self.out_dmas.append(op)
        return op

    def _needs_wait(self, op, dep):
        if dep.is_dma:
            return True
        if dep.eng == "pe" and op.eng == "pe" and not op.is_dma:
            return False
        return True

    def emit(self, es):
        nc = self.nc
        fin = self.add("sp", None)
        for o in self.out_dmas:
            fin.deps.add(o)
        for e in self.DMA_POOL:
            for op in self.q[e]:
                if op.is_dma:
                    fin.deps.add(op)
        for e in self.ENGS:
            for op in self.q[e]:
                for d in op.deps:
                    if (not d.is_dma) and self._needs_wait(op, d):
                        d.signal = True
        for e in self.ENGS:
            c = 0
            for op in self.q[e]:
                if (not op.is_dma) and op.signal:
                    c += 1
                    op.count = c
        esem = {e: es.enter_context(nc.semaphore("s%s_%s" % (self.tag, e))) for e in ("pe", "act", "dve", "pool")}
        dsem = {}
        for e, n in self.DMA_POOL.items():
            for i in range(n):
                dsem[(e, i)] = es.enter_context(nc.semaphore("d%s_%s_%d" % (self.tag, e, i)))
        prog = self
        with nc.Block() as block:
            def run_queue(ename, eng):
                waited = {}
                for op in prog.q[ename]:
                    waits = {}
                    for d in op.deps:
                        if not prog._needs_wait(op, d):
                            continue
                        if d.is_dma:
                            key, val = ("d", d.sem), d.target
                        else:
                            key, val = ("e", d.eng), d.count
                        if waits.get(key, 0) < val:
                            waits[key] = val
                    if op.is_dma and op.target > 16:
                        key = ("d", op.sem)
                        if waits.get(key, 0) < op.target - 16:
                            waits[key] = op.target - 16
                    for key, val in waits.items():
                        if waited.get(key, 0) >= val:
                            continue
                        waited[key] = val
                        s = dsem[key[1]] if key[0] == "d" else esem[key[1]]
                        eng.wait_ge(s, val)
                    if op.fn is None:
                        continue
                    ins = op.fn(eng)
                    if op.is_dma:
                        ins.then_inc(dsem[op.sem], 16)
                    elif op.signal:
                        ins.then_inc(esem[ename], 1)

            @block.sync
            def _(eng):
                run_queue("sp", eng)

            @block.tensor
            def _(eng):
                run_queue("pe", eng)

            @block.scalar
            def _(eng):
                run_queue("act", eng)

            @block.vector
            def _(eng):
                run_queue("dve", eng)

            @block.gpsimd
            def _(eng):
                run_queue("pool", eng)


class K:
    def __init__(self, P):
        self.P = P

    def mm(self, out, lhsT, rhs, start=True, stop=True, r=(), w=()):
        self.P.add("pe", lambda e: e.matmul(out, lhsT=lhsT, rhs=rhs, start=start, stop=stop, skip_group_check=True), r, w)

    def tr(self, out, in_, ident, r=(), w=()):
        self.P.add("pe", lambda e: e.transpose(out=out, in_=in_, identity=ident), r, w)

    def act(self, out, in_, func, r=(), w=(), scale=1.0, accum=None):
        if accum is None:
            self.P.add("act", lambda e: e.activation(out=out, in_=in_, func=func, scale=scale), r, w)
        else:
            self.P.add("act", lambda e: e.activation(out=out, in_=in_, func=func, scale=scale, accum_out=accum), r, w)

    def tt(self, eng, out, in0, in1, op, r=(), w=()):
        self.P.add(eng, lambda e: e.tensor_tensor(out=out, in0=in0, in1=in1, op=op), r, w)

    def ts(self, eng, out, in0, s1, s2, op0, op1=None, r=(), w=()):
        if op1 is None:
            self.P.add(eng, lambda e: e.tensor_scalar(out=out, in0=in0, scalar1=s1, scalar2=None, op0=op0), r, w)
        else:
            self.P.add(eng, lambda e: e.tensor_scalar(out=out, in0=in0, scalar1=s1, scalar2=s2, op0=op0, op1=op1), r, w)

    def stt(self, out, in0, scalar, in1, op0, op1, r=(), w=()):
        self.P.add("dve", lambda e: e.scalar_tensor_tensor(out=out, in0=in0, scalar=scalar, in1=in1, op0=op0, op1=op1), r, w)

    def cp(self, eng, out, in_, r=(), w=()):
        if eng == "act":
            self.P.add("act", lambda e: e.activation(out=out, in_=in_, func=AF.Copy), r, w)
        else:
            self.P.add(eng, lambda e: e.tensor_copy(out=out, in_=in_), r, w)

    def red(self, out, in_, op, r=(), w=()):
        self.P.add("dve", lambda e: e.tensor_reduce(out=out, in_=in_, axis=AX.X, op=op), r, w)

    def memset(self, eng, ap, val, w=()):
        self.P.add(eng, lambda e: e.memset(ap, val), (), w)

    def dma(self, q, out, in_, r=(), w=(), is_out=False, slow=False):
        if slow:
            self.P.add(q, lambda e: e.dma_start(out=out, in_=in_, allow_slow_non_contiguous=True), r, w, dma=True, out=is_out)
        else:
            self.P.add(q, lambda e: e.dma_start(out=out, in_=in_), r, w, dma=True, out=is_out)

    def gather(self, out, in_, idx, r=(), w=()):
        self.P.add("pool", lambda e: e.indirect_dma_start(out=out, out_offset=None, in_=in_,
                                                          in_offset=bass.IndirectOffsetOnAxis(ap=idx, axis=0)),
                   r, w, dma=True)


def v3(ap, a):
    return ap.rearrange("p (a b) -> p a b", a=a)


def bc(ap, shape):
    return ap.to_broadcast(shape)


def win_names(c0, n):
    return ["win%d" % j for j in range(c0 // 512, (c0 + n - 1) // 512 + 1)]


def front(k, t, n, x_src, cs_name, sfx, is_prompt):
    P = k.P
    N = slice(0, n)
    x, B1, B2, B3, h, hT = t["x"], t["B1"], t["B2"], t["B3"], t["h"], t["hT"]
    TH, ZB, vn_bf = B1[:, 0:512], B1[:, 512:1024], h[:, 0:512]
    tg, za_s, bmix, gw = t["tg" + sfx], t["za_s" + sfx], t["bmix" + sfx], t["gw" + sfx]
    n_tg, n_za, n_bm, n_gw = "tg" + sfx, "za_s" + sfx, "bmix" + sfx, "gw" + sfx
    B1W = ["B1a", "B1b"]
    st, st2, st3 = t["st"], t["st2"], t["st3"]
    pA, pB, pT = t["pA"], t["pB"], t["pT"]
    pTv = pT[:].rearrange("p (a b) -> p a b", a=8)
    k.dma("sp", x[N, :], x_src, w=["x"])
    k.act(B1[N, :], x[N, :], AF.Square, r=["x"], w=B1W + ["st"], accum=st[N, 0:1])
    k.ts("dve", st[N, 1:2], st[N, 0:1], 1.0 / D, EPS, ALU.mult, ALU.add, r=["st"], w=["st"])
    k.tt("pool", st[N, 2:3], st[N, 1:2], t["nh"][N, 0:1], ALU.pow, r=["st", "nh"], w=["st"])
    k.stt(B1[N, :], x[N, :], st[N, 2:3], t["sc1"][N, :], ALU.mult, ALU.mult, r=["x", "st", "sc1"], w=B1W)
    k.tt("dve", h[N, :], B1[N, :], t["sh"][N, :], ALU.add, r=B1W + ["sh"], w=["h"])
    yield
    for kc in range(8):
        k.tr(pTv[:, kc, 0:n], h[N, kc * 128:(kc + 1) * 128], t["identb"][N, N], r=["h", "identb"], w=["pT"])
    k.cp("act", hT[:, :, 0:n], pTv[:, :, 0:n], r=["pT"], w=["hT"])
    yield

    def proj(bank, bname, dst, c0, ncol):
        for kc in range(8):
            k.mm(bank[N, dst:dst + ncol], hT[:, kc, 0:n], t["Win"][:, kc, c0:c0 + ncol], start=(kc == 0), stop=(kc == 7),
                 r=["hT"] + win_names(c0, ncol), w=[bname])

    def normrope(bank, bname, A, Bd, gain, dst_ap, dst_name):
        H = A * Bd
        HW = H * 64

        def v4(ap):
            return ap.rearrange("p (a b d) -> p a b d", a=A, b=Bd)

        k.cp("act", B2[N, 0:HW], bank[N, 0:HW], r=[bname], w=["B2"])
        k.act(B3[N, 0:HW], B2[N, 0:HW], AF.Square, r=["B2"], w=["B3"])
        k.red(st2[N, 0:H], v3(B3[N, 0:HW], H), ALU.add, r=["B3"], w=["st2"])
        k.ts("dve", st2[N, 8:8 + H], st2[N, 0:H], 1.0 / 64, EPS, ALU.mult, ALU.add, r=["st2"], w=["st2"])
        k.tt("pool", st2[N, 16:16 + H], st2[N, 8:8 + H], t["nh"][N, 0:H], ALU.pow, r=["st2", "nh"], w=["st2"])
        b2 = v4(B2[N, 0:HW])
        rs = st2[N, 16:16 + H].rearrange("p (a b) -> p a b", a=A).unsqueeze(3)
        k.tt("dve", b2, b2, bc(rs, [n, A, Bd, 64]), ALU.mult, r=["B2", "st2"], w=["B2"])
        k.tt("dve", b2, b2, bc(gain[N, :].unsqueeze(1).unsqueeze(1), [n, A, Bd, 64]), ALU.mult, r=["B2", "gains"], w=["B2"])
        cs = t["cs"]
        cosb = bc(cs[N, 0:32].unsqueeze(1).unsqueeze(1), [n, A, Bd, 32])
        sinb = bc(cs[N, 32:64].unsqueeze(1).unsqueeze(1), [n, A, Bd, 32])
        x1, x2 = b2[:, :, :, 0:32], b2[:, :, :, 32:64]
        r1 = t["R1"][N, 0:H * 32].rearrange("p (a b d) -> p a b d", a=A, b=Bd)
        r2 = t["R2"][N, 0:H * 32].rearrange("p (a b d) -> p a b d", a=A, b=Bd)
        k.tt("dve", r1, x1, cosb, ALU.mult, r=["B2", cs_name], w=["R1"])
        k.tt("dve", r2, x2, sinb, ALU.mult, r=["B2", cs_name], w=["R2"])
        k.tt("dve", dst_ap[:, :, :, 0:32], r1, r2, ALU.subtract, r=["R1", "R2"], w=[dst_name])
        k.tt("dve", r1, x2, cosb, ALU.mult, r=["B2", cs_name], w=["R1"])
        k.tt("dve", r2, x1, sinb, ALU.mult, r=["B2", cs_name], w=["R2"])
        k.tt("dve", dst_ap[:, :, :, 32:64], r1, r2, ALU.add, r=["R1", "R2"], w=[dst_name])

    qdst = t["q_bf"][N, :].rearrange("p (r g d) -> p g r d", r=4, g=2)
    stages = []

    def st_q_post():
        normrope(pA, "pA", 2, 4, t["qg"], qdst, "q_bf")
    stages.append((lambda: proj(pA, "pA", 0, C_Q, 512), st_q_post))

    def st_k_post():
        normrope(pB, "pB", 6, 1, t["kg"], t["kf"][N, :].rearrange("p (a b d) -> p a b d", a=6, b=1), "kf")
        k.cp("act", t["k_bf"][N, :], t["kf"][N, :], r=["kf"], w=["k_bf"])
    stages.append((lambda: proj(pB, "pB", 0, C_K, 384), st_k_post))

    def st_v_pe():
        proj(pA, "pA", 0, C_V, 384)
        proj(pA, "pA", 384, C_NSA, 24)

    def st_v_post():
        k.cp("act", t["vf"][N, :], pA[N, 0:384], r=["pA"], w=["vf"])
        k.act(t["gwt"][N, :], pA[N, 384:408], AF.Tanh, r=["pA"], w=["gwt"], scale=0.5)
        k.ts("dve", gw[N, :], t["gwt"][N, :], 0.5, 0.5, ALU.mult, ALU.add, r=["gwt"], w=[n_gw])
    stages.append((st_v_pe, st_v_post))

    def st_za_post():
        k.act(TH[N, :], pB[N, :], AF.Tanh, r=["pB"], w=["B1a"], scale=0.5)
        k.stt(za_s[N, :], TH[N, :], 1.0, pB[N, :], ALU.add, ALU.mult, r=["B1a", "pB"], w=[n_za])
    stages.append((lambda: proj(pB, "pB", 0, C_ZA, 512), st_za_post))

    def st_ln_post():
        k.cp("act", t["vn_f"][N, :], pA[N, :], r=["pA"], w=["vn_f"])
        P.add("dve", lambda e: e.bn_stats(out=st3[N, 0:6], in_=t["vn_f"][N, :]), ["vn_f"], ["st3"])
        P.add("dve", lambda e: e.bn_aggr(out=st3[N, 6:8], in_=st3[N, 0:6]), ["st3"], ["st3"])
        k.ts("dve", st3[N, 8:9], st3[N, 7:8], EPS, None, ALU.add, r=["st3"], w=["st3"])
        k.tt("pool", st3[N, 9:10], st3[N, 8:9], t["nh"][N, 0:1], ALU.pow, r=["st3", "nh"], w=["st3"])
        k.ts("dve", t["vn_f"][N, :], t["vn_f"][N, :], st3[N, 6:7], st3[N, 9:10], ALU.subtract, ALU.mult, r=["vn_f", "st3"], w=["vn_f"])
        k.tt("dve", t["vn_f"][N, :], t["vn_f"][N, :], t["vng"][N, :], ALU.mult, r=["vn_f", "gains"], w=["vn_f"])
        k.tt("dve", t["vn_f"][N, :], t["vn_f"][N, :], t["vnb"][N, :], ALU.add, r=["vn_f", "gains"], w=["vn_f"])
    stages.append((lambda: proj(pA, "pA", 0, C_VV, 512), st_ln_post))

    def st_zb_post():
        k.act(TH[N, :], pB[N, :], AF.Tanh, r=["pB"], w=["B1a"], scale=0.5)
        k.stt(ZB[N, :], TH[N, :], 1.0, pB[N, :], ALU.add, ALU.mult, r=["B1a", "pB"], w=["B1b"])
    stages.append((lambda: proj(pB, "pB", 0, C_ZB, 512), st_zb_post))

    def st_u_post():
        k.tt("dve", ZB[N, :], pA[N, :], ZB[N, :], ALU.mult, r=["pA", "B1b"], w=["B1b"])
    stages.append((lambda: proj(pA, "pA", 0, C_U, 512), st_u_post))

    def st_sp_pe():
        if is_prompt:
            k.cp("act", vn_bf[N, :], t["vn_f"][N, :], r=["vn_f"], w=["h"])
            for g in range(4):
                k.mm(pB[N, g * 128:(g + 1) * 128], t["wsT"][:, g, :], vn_bf[N, g * 128:(g + 1) * 128],
                     r=["wsT", "h"], w=["pB"])

    def st_sp_post():
        if is_prompt:
            k.tt("dve", v3(B2[N, :], 4), v3(pB[N, :], 4), bc(t["bsT"][N, :].unsqueeze(2), [n, 4, 128]), ALU.add,
                 r=["pB", "bsT"], w=["B2"])
        else:
            k.tt("dve", v3(B2[N, :], 4), v3(t["vn_f"][N, :], 4), bc(t["w00"][N, :].unsqueeze(2), [n, 4, 128]), ALU.mult,
                 r=["vn_f", "w00"], w=["B2"])
            k.tt("dve", v3(B2[N, :], 4), v3(B2[N, :], 4), bc(t["b0"][N, :].unsqueeze(2), [n, 4, 128]), ALU.add,
                 r=["B2", "w00"], w=["B2"])
        k.tt("dve", bmix[N, :], B2[N, :], ZB[N, :], ALU.mult, r=["B2", "B1b"], w=[n_bm])
    stages.append((st_sp_pe, st_sp_post))

    gbanks = [(pA, "pA"), (pB, "pB")]
    for j in range(4):
        bank, bname = gbanks[j % 2]
        stages.append(((lambda bank=bank, bname=bname, j=j: proj(bank, bname, 0, C_GA + j * 512, 512)),
                       (lambda bank=bank, bname=bname, j=j: k.act(tg[N, j * 512:(j + 1) * 512], bank[N, :], AF.Tanh,
                                                                  r=[bname], w=[n_tg], scale=0.5))))
    stages[0][0]()
    yield
    for si in range(len(stages)):
        if si + 1 < len(stages):
            stages[si + 1][0]()
        stages[si][1]()
        yield


def tail(k, t, n, y_dst, x_src, sfx, gate=None):
    N = slice(0, n)
    pA, pB, pT = t["pA"], t["pB"], t["pT"]
    pTv8 = t["pM"][:].bitcast(BF16).rearrange("p (a b) -> p a b", a=8)
    tg, za_s, bmix = t["tg" + sfx], t["za_s" + sfx], t["bmix" + sfx]
    n_tg, n_za, n_bm = "tg" + sfx, "za_s" + sfx, "bmix" + sfx
    mc = t["mc"]
    B1W = ["B1a", "B1b"]
    ozd = t["oz"][N, :].rearrange("p (r g d) -> p g r d", r=4, g=2)
    k.tt("dve", ozd, t["o_a"][N, :].rearrange("p (g r d) -> p g r d", g=2, r=4),
         za_s[N, :].rearrange("p (g r d) -> p g r d", g=2, r=4), ALU.mult, r=["o_a", n_za], w=["oz"])
    for r in range(4):
        k.tr(pTv8[:, r, 0:n], t["oz"][N, r * 128:(r + 1) * 128], t["identb"][N, N], r=["oz", "identb"], w=["pM"])
    for c in range(4):
        k.tr(pTv8[:, 4 + c, 0:n], bmix[N, c * 128:(c + 1) * 128], t["identb"][N, N], r=[n_bm, "identb"], w=["pM"])
    k.cp("act", t["ozT"][:, :, 0:n], pTv8[:, :, 0:n], r=["pM"], w=["ozT"])
    yield
    for half in range(2):
        cs = slice(half * 512, (half + 1) * 512)
        for r in range(4):
            k.mm(pA[N, :], t["ozT"][:, r, 0:n], t["WA"][:, r, cs], start=(r == 0), stop=(r == 3), r=["ozT", "WA"], w=["pA"])
        for c in range(4):
            k.mm(pB[N, :], t["ozT"][:, 4 + c, 0:n], t["WB"][:, c, cs], start=(c == 0), stop=(c == 3), r=["ozT", "WB"], w=["pB"])
        k.stt(t["B2"][N, :], tg[N, half * 512:(half + 1) * 512], 1.0, pA[N, :], ALU.add, ALU.mult,
              r=[n_tg, "pA"], w=["B2"])
        k.stt(t["B3"][N, :], tg[N, 1024 + half * 512:1024 + (half + 1) * 512], 1.0, pB[N, :], ALU.add, ALU.mult,
              r=[n_tg, "pB"], w=["B3"])
        k.tt("dve", mc[N, cs], t["B2"][N, :], t["B3"][N, :], ALU.add, r=["B2", "B3"], w=["mc"])
        yield
    for kc in range(8):
        k.tr(pTv8[:, kc, 0:n], mc[N, kc * 128:(kc + 1) * 128], t["identb"][N, N], r=["mc", "identb"], w=["pM"])
    k.cp("act", t["hT"][:, :, 0:n], pTv8[:, :, 0:n], r=["pM"], w=["hT"])
    yield
    banks = [(pA, "pA"), (pB, "pB")]
    for half in range(2):
        bank, bname = banks[half]
        cs = slice(half * 512, (half + 1) * 512)
        for kc in range(8):
            k.mm(bank[N, :], t["hT"][:, kc, 0:n], t["Wout"][:, kc, cs], start=(kc == 0), stop=(kc == 7),
                 r=["hT", "Wout"], w=[bname])
        if gate is None:
            gap, gname = t["gq"][N, cs], "gq"
        else:
            gap, gname = gate[half][0][N, 0:512], gate[half][1]
        k.tt("dve", t["B1"][N, cs], bank[N, :], gap, ALU.mult, r=[bname, gname], w=["B1a" if half == 0 else "B1b"])
        yield
    k.dma("sp", t["x"][N, :], x_src, w=["x"])
    k.tt("dve", t["x"][N, :], t["B1"][N, :], t["x"][N, :], ALU.add, r=B1W + ["x"], w=["x"])
    k.dma("sp", y_dst, t["x"][N, :], r=["x"], is_out=True)


def mod_pass(k, t, d, n, cT, cT_name, bcast):
    N = slice(0, n)
    stg = [t["G0"], t["G1"]]
    banks = [(t["pA"], "pA"), (t["pB"], "pB")]
    wada = d["w_ada"].rearrange("(kc p) n -> p kc n", p=128)
    for j in range(12):
        sg, sname = stg[j % 2], "G%d" % (j % 2)
        bank, bname = banks[j % 2]
        sgv = sg[:].rearrange("p (kc n) -> p kc n", kc=8)
        k.dma("sp", sgv, wada[:, :, j * 256:(j + 1) * 256], w=[sname])
        k.dma("sp", t["bada"][0:1, :], d["b_ada"][0:1, j * 256:(j + 1) * 256], w=["bada"])
        if not bcast:
            for kc in range(8):
                k.mm(bank[N, 0:256], cT[:, kc, 0:n], sgv[:, kc, :], start=(kc == 0), stop=False, r=[cT_name, sname], w=[bname])
            k.mm(bank[N, 0:256], t["ones0"][:, 0:n], t["bada"][:, :], start=False, stop=True, r=["ones0", "bada"], w=[bname])
            src = bank[N, 0:256]
        else:
            for kc in range(8):
                k.mm(bank[0:1, 0:256], cT[:, kc:kc + 1], sgv[:, kc, :], start=(kc == 0), stop=False, r=[cT_name, sname], w=[bname])
            k.mm(bank[0:1, 0:256], t["ones0"][:, 0:1], t["bada"][:, :], start=False, stop=True, r=["ones0", "bada"], w=[bname])
            k.cp("act", t["modrow"][0:1, :], bank[0:1, 0:256], r=[bname], w=["modrow"])
            k.mm(bank[N, 256:512], t["onesf"][0:1, 0:n], t["modrow"][0:1, :], r=["onesf", "modrow"], w=[bname])
            src = bank[N, 256:512]
        cs = slice((j % 4) * 256, (j % 4 + 1) * 256)
        if j < 4:
            k.cp("act", t["sh"][N, cs], src, r=[bname], w=["sh"])
        elif j < 8:
            k.stt(t["sc1"][N, cs], src, 1.0, t["normg"][N, cs], ALU.add, ALU.mult, r=[bname, "normg"], w=["sc1"])
        else:
            k.act(t["gq"][N, cs], src, AF.Copy, r=[bname], w=["gq"], scale=0.25)


def build_program():
    nc = bass.Bass("TRN2", target_bir_lowering=False)
    d = {}

    def din(name, shape, dt=F32):
        d[name] = nc.dram_tensor(name, shape, dt, kind="ExternalInput").ap()

    def dout(name, shape):
        d[name] = nc.dram_tensor(name, shape, F32, kind="ExternalOutput").ap()

    din("xp", [S, D]); din("xs", [NSMP, D]); din("cpv", [D]); din("csv", [NSMP, D])
    for nm in ("kcmp", "vcmp", "kslc", "vslc"):
        din(nm, [2560 * 8, 2048])
    din("kwin", [NSMP, 512, 128]); din("vwin", [NSMP, 512, 128])
    din("ptrep", [128, NSMP], I32)
    din("w_ada", [D, 3 * D]); din("b_ada", [1, 3 * D]); din("norm_g", [1, D]); din("w_in", [D, DIN])
    din("qg", [1, 64]); din("kg", [1, 64]); din("pek", [32, 64]); din("pev", [32, 64])
    din("wck", [64, 64]); din("wcv", [64, 64]); din("vng", [1, 512]); din("vnb", [1, 512])
    din("ws", [4, 128, 128]); din("bs", [4, 128]); din("wbra", [512, D]); din("wbrb", [512, D]); din("wout", [D, D])
    din("rope", [S + 1, 64]); din("tri_le", [128, 128]); din("tri_gt", [128, 128]); din("identf_c", [128, 128])
    din("pool4", [128, 4]); din("pair", [64, 32]); din("e2", [64, 2048]); din("cmpbase", [128, 512]); din("trineg_le", [128, 512]); din("trineg_gt", [128, 512])
    din("impbias", [NT, 128, 32]); din("pool2", [128, 64]); din("e4", [32, 128]); din("rmod8", [128, 1])
    d["gqs"] = nc.dram_tensor("gqs", [NSMP, D], F32, kind="Internal").ap()
    dout("yp", [S, D]); dout("ys", [NSMP, D])
    for nm in ("pk_cmp", "pv_cmp", "pk_slc", "pv_slc"):
        dout(nm, [S, 128])
    dout("pk_win", [512, 128]); dout("pv_win", [512, 128])
    for nm in ("sk_cmp", "sv_cmp", "sk_slc", "sv_slc"):
        dout(nm, [NSMP, 128])
    dout("sk_win", [NSMP, 512, 128]); dout("sv_win", [NSMP, 512, 128]); dout("svch", [NSMP, 512])

    with contextlib.ExitStack() as es:
        t = {}

        def sb(name, shape, dt, scope=es):
            t[name] = scope.enter_context(nc.sbuf_tensor("sb_" + name, shape, dt))
            return t[name]

        def ps(name, shape, dt, scope=es):
            t[name] = scope.enter_context(nc.psum_tensor("ps_" + name, shape, dt))
            return t[name]

        sb("Win", [128, 8, DIN], BF16)
        sb("sc1", [128, D], F32); sb("sh", [128, D], F32); sb("gq", [128, D], F32)
        sb("identb", [128, 128], BF16); sb("identf", [128, 128], F32); sb("tri_le", [128, 128], BF16); sb("tri_gt", [128, 128], BF16)
        sb("qg", [128, 64], F32); sb("kg", [128, 64], F32); sb("vng", [128, 512], F32); sb("vnb", [128, 512], F32)
        sb("nh", [128, 8], F32); sb("onesf", [128, 128], F32)
        sb("wsT", [128, 4, 128], BF16); sb("bsT", [128, 4], F32)
        sb("Wbdk", [128, 128], BF16); sb("Wbdv", [128, 128], BF16); sb("pebar", [128, 2], F32)
        sb("pool4", [128, 4], BF16); sb("e2", [128, 2048], BF16)
        sb("x", [128, D], F32); sb("B1", [128, D], F32); sb("B2", [128, 512], F32); sb("B3", [128, 512], F32)
        sb("h", [128, D], BF16); sb("hT", [128, 8, 128], BF16); sb("R1", [128, 256], F32); sb("R2", [128, 256], F32)
        sb("st", [128, 4], F32); sb("st2", [128, 24], F32); sb("st3", [128, 12], F32)
        sb("q_bf", [128, 512], BF16); sb("kf", [128, 384], F32); sb("vf", [128, 384], F32); sb("k_bf", [128, 384], BF16)
        sb("gwt", [128, 24], F32); sb("gw0", [128, 24], F32); sb("gw1", [128, 24], F32); sb("za_s0", [128, 512], BF16); sb("za_s1", [128, 512], BF16)
        sb("vn_f", [128, 512], F32); sb("tg0", [128, 2048], BF16); sb("tg1", [128, 2048], BF16)
        sb("bmix0", [128, 512], BF16); sb("bmix1", [128, 512], BF16); sb("o_a", [128, 512], F32); sb("oz", [128, 512], BF16); sb("ozT", [128, 8, 128], BF16)
        sb("cs", [128, 64], F32); sb("mc", [128, D], BF16)
        sb("rden", [128, 8], F32); sb("cx", [128, 8], F32); t["tmpO"] = t["B3"]
        ps("pA", [128, 512], F32); ps("pB", [128, 512], F32); ps("pT", [128, 1024], BF16)
        ps("pM", [128, 512], F32); ps("pS0", [128, 512], F32); ps("pS1", [128, 512], F32)
        ps("pO0", [128, 512], F32); ps("pO1", [128, 512], F32)

        with contextlib.ExitStack() as s1:
            P = Prog(nc, "a")
            k = K(P)
            sb("normg", [128, D], F32, s1)
            sb("G0", [128, 2048], F32, s1); sb("G1", [128, 2048], F32, s1)
            sb("Gw0", [128, 512], F32, s1); sb("Gw1", [128, 512], F32, s1)
            sb("Gb0", [128, 2048], BF16, s1); sb("Gb1", [128, 2048], BF16, s1); sb("pool2b", [128, 64], BF16, s1); sb("Gwb0", [128, 512], BF16, s1); sb("Gwb1", [128, 512], BF16, s1)
            sb("PTb", [128, 160], BF16, s1); sb("vfb", [128, 256], BF16, s1); sb("Enewb", [128, 2 * NSMP * 8], BF16, s1)
            sb("bada", [128, 256], F32, s1); sb("modrow", [1, 256], F32, s1); sb("ones0", [128, 128], F32, s1)
            sb("cTs", [128, 8, NSMP], F32, s1)
            sb("cp8", [128, 8], F32, s1)
            sb("w00", [128, 4], F32, s1); sb("b0", [128, 4], F32, s1)
            sb("pe2", [32, 128], F32, s1); sb("o32", [32, 2], F32, s1)
            sb("ptrep", [128, NSMP], I32, s1); sb("rmod8", [128, 1], F32, s1); sb("idx", [128, NSMP], I32, s1)
            sb("pool2", [128, 64], F32, s1); sb("pairf", [64, 32], F32, s1); sb("e4", [32, 128], BF16, s1)
            sb("QTs", [128, 2, 4 * NSMP], BF16, s1); sb("enr", [NSMP, 16], F32, s1); sb("en", [NSMP, 16], F32, s1); sb("Enew", [128, 2 * NSMP * 8], F32, s1)
            sb("pTk", [128, 64], BF16, s1); sb("pTv", [128, 64], BF16, s1); sb("KcTs", [128, 64], BF16, s1); sb("Vcs", [64, 128], F32, s1)
            sb("PcT", [64, NSMP * 8], F32, s1); sb("KTs", [128, 20, 128], BF16, s1)
            sb("PTs", [128, 160], F32, s1); sb("PTsum", [128, 16], F32, s1)
            sb("rDc", [32, 128], F32, s1); sb("impn", [32, 128], F32, s1); sb("impT", [32, 32], F32, s1)
            sb("impS", [32, 32], F32, s1); sb("bias0", [32, 32], F32, s1); sb("m8s", [32, 8], F32, s1)
            sb("selS", [32, 32], BF16, s1); sb("selTs", [32, 32], BF16, s1); sb("Msk", [128, 32], F32, s1)
            t["OTn"] = t["B1"][:, 0:384]; t["rDall"] = t["B1"][:, 384:768]; t["OT1"] = t["B2"][0:64, 0:384]; t["wsl"] = t["B3"][:, 0:128]

            k.dma("pool", t["identb"][:], d["identf_c"], w=["identb"])
            k.dma("sp", t["identf"][:], d["identf_c"], w=["identf"])
            k.dma("pool", t["tri_le"][:], d["tri_le"], w=["tri_le"])
            k.dma("pool", t["tri_gt"][:], d["tri_gt"], w=["tri_gt"])
            k.dma("pool", t["pool4"][:], d["pool4"], w=["pool4"])
            k.memset("pool", t["e2"][64:128, :], 0.0, w=["e2"])
            k.dma("pool", t["e2"][0:64, :], d["e2"], w=["e2"])
            k.dma("pool", t["e4"][:], d["e4"], w=["e4"])
            k.dma("sp", t["pool2"][:], d["pool2"], w=["pool2"])
            k.dma("pool", t["pool2b"][:], d["pool2"], w=["pool2b"])
            k.dma("sp", t["pairf"][:], d["pair"], w=["pairf"])
            k.dma("sp", t["rmod8"][:], d["rmod8"], w=["rmod8"])
            k.dma("sp", t["ptrep"][:], d["ptrep"], w=["ptrep"])
            k.dma("sp", t["qg"][:], d["qg"].partition_broadcast(128), w=["gains"])
            k.dma("sp", t["kg"][:], d["kg"].partition_broadcast(128), w=["gains"])
            k.dma("sp", t["vng"][:], d["vng"].partition_broadcast(128), w=["gains"])
            k.dma("sp", t["vnb"][:], d["vnb"].partition_broadcast(128), w=["gains"])
            k.dma("sp", t["normg"][:], d["norm_g"].partition_broadcast(128), w=["normg"])
            k.dma("sp", t["bsT"][:], d["bs"].rearrange("g i -> i g"), w=["bsT"], slow=True)
            k.dma("sp", t["w00"][0:NSMP, :], d["ws"][:, 0, 0:1].rearrange("g o -> o g").partition_broadcast(NSMP), w=["w00"], slow=True)
            k.dma("sp", t["b0"][0:NSMP, :], d["bs"][:, 0:1].rearrange("g o -> o g").partition_broadcast(NSMP), w=["w00"], slow=True)
            k.memset("pool", t["nh"][:], -0.5, w=["nh"])
            k.memset("pool", t["Enew"][:], 0.0, w=["Enew"])
            k.memset("pool", t["bada"][:], 0.0, w=["bada"])
            k.memset("pool", t["ones0"][:], 0.0, w=["ones0"])
            k.memset("pool", t["ones0"][0:1, :], 1.0, w=["ones0"])
            k.memset("pool", t["QTs"][:], 0.0, w=["QTs"])
            k.memset("pool", t["vf"][:], 0.0, w=["vf"])
            k.memset("pool", t["Enewb"][:], 0.0, w=["Enewb"])
            k.memset("pool", t["onesf"][:], 1.0, w=["onesf"])
            k.memset("pool", t["o32"][:], 1.0 / 32, w=["o32"])
            k.memset("pool", t["Wbdk"][:], 0.0, w=["Wbdk"])
            k.memset("pool", t["Wbdv"][:], 0.0, w=["Wbdv"])
            k.memset("pool", t["bias0"][:], 0.0, w=["bias0"])
            k.memset("pool", t["bias0"][:, 0:1], 1.0e4, w=["bias0"])
            for g in range(2):
                gs = slice(g * 64, (g + 1) * 64)
                k.dma("pool", t["Wbdk"][gs, gs], d["wck"], w=["Wbdk"])
                k.dma("pool", t["Wbdv"][gs, gs], d["wcv"], w=["Wbdv"])
                k.dma("sp", t["pe2"][:, gs], d["pek"], w=["pe2k"])
            k.mm(t["pM"][:, 0:1], t["pe2"][:, :], t["o32"][:, 0:1], r=["pe2k", "o32"], w=["pM"])
            k.cp("dve", t["pebar"][:, 0:1], t["pM"][:, 0:1], r=["pM"], w=["pebar"])
            for g in range(2):
                gs = slice(g * 64, (g + 1) * 64)
                k.dma("sp", t["pe2"][:, gs], d["pev"], r=[], w=["pe2k"])
            k.mm(t["pM"][:, 0:1], t["pe2"][:, :], t["o32"][:, 0:1], r=["pe2k", "o32"], w=["pM"])
            k.cp("dve", t["pebar"][:, 1:2], t["pM"][:, 0:1], r=["pM"], w=["pebar"])
            for g in range(4):
                k.dma("sp", t["wsl"], d["ws"][g], w=["B3"])
                k.tr(t["pM"][:, 0:128], t["wsl"], t["identf"][:], r=["B3", "identf"], w=["pM"])
                k.tt("dve", t["wsT"][:, g, :], t["pM"][:, 0:128], t["tri_le"][:], ALU.mult, r=["pM", "tri_le"], w=["wsT"])
            winv = d["w_in"].rearrange("(kc p) n -> p kc n", p=128)
            for j in range(11):
                c0, c1 = j * 512, min(DIN, (j + 1) * 512)
                k.dma("pool", t["Win"][:, :, c0:c1], winv[:, :, c0:c1], w=["win%d" % j])
            k.dma("sp", t["cp8"][:], d["cpv"].rearrange("(kc p) -> p kc", p=128), w=["cp8"], slow=True)
            k.dma("sp", t["x"][0:NSMP, :], d["csv"], w=["x"])
            pMv = t["pM"][:, 0:8 * NSMP].rearrange("p (a b) -> p a b", a=8)
            for kc in range(8):
                k.tr(pMv[:, kc, :], t["x"][0:NSMP, kc * 128:(kc + 1) * 128], t["identf"][0:NSMP, 0:NSMP],
                     r=["x", "identf"], w=["pM"])
            k.cp("dve", t["cTs"][:], pMv, r=["pM"], w=["cTs"])
            mod_pass(k, t, d, NSMP, t["cTs"], "cTs", False)
            k.ts("dve", t["idx"][:], t["ptrep"][:], 8.0, t["rmod8"][:, 0:1], ALU.mult, ALU.add, r=["ptrep", "rmod8"], w=["idx"])
            k.dma("sp", t["cs"][0:NSMP, :], d["rope"][S:S + 1, :].partition_broadcast(NSMP), w=["cs"])
            for _ in front(k, t, NSMP, d["xs"], "cs", "0", False):
                pass
            n = NSMP
            N = slice(0, n)
            k.dma("sp", d["sk_cmp"], t["kf"][N, 0:128], r=["kf"], is_out=True)
            k.dma("sp", d["sk_slc"], t["kf"][N, 128:256], r=["kf"], is_out=True)
            k.dma("sp", d["sv_cmp"], t["vf"][N, 0:128], r=["vf"], is_out=True)
            k.dma("sp", d["sv_slc"], t["vf"][N, 128:256], r=["vf"], w=["d_svslc"], is_out=True)
            k.dma("sp", d["svch"], t["vn_f"][N, :], r=["vn_f"], is_out=True)
            k.dma("sp", d["sk_win"][:, 0:511, :], d["kwin"][:, 1:512, :], is_out=True)
            k.dma("sp", d["sv_win"][:, 0:511, :], d["vwin"][:, 1:512, :], is_out=True)
            k.dma("sp", d["sk_win"][:, 511, :], t["kf"][N, 256:384], r=["kf"], is_out=True)
            k.dma("sp", d["sv_win"][:, 511, :], t["vf"][N, 256:384], r=["vf"], w=["d_svwin"], is_out=True)
            pTv8 = t["pT"][:].rearrange("p (a b) -> p a b", a=8)
            for r in range(4):
                k.tr(pTv8[:, r, 0:n], t["q_bf"][N, r * 128:(r + 1) * 128], t["identb"][N, N], r=["q_bf", "identb"], w=["pT"])
            k.cp("act", t["QTs"][0:64, 0, :].rearrange("p (r s) -> p r s", r=4), pTv8[0:64, 0:4, 0:n], r=["pT"], w=["QTs"])
            k.cp("act", t["QTs"][64:128, 1, :].rearrange("p (r s) -> p r s", r=4), pTv8[64:128, 0:4, 0:n], r=["pT"], w=["QTs"])
            Env = t["Enew"][:].rearrange("p (x s h) -> p x s h", x=2, s=NSMP)
            Env16 = t["Enew"][0:NSMP, :].rearrange("p (x s h) -> p x s h", x=2, s=NSMP)
            for xx in range(2):
                kcol = t["k_bf"][N, 128 + xx * 128:256 + xx * 128].rearrange("p (g d) -> p g d", g=2).unsqueeze(2)
                k.tt("dve", t["B2"][N, :].rearrange("p (g r d) -> p g r d", g=2, r=4),
                     t["q_bf"][N, :].rearrange("p (r g d) -> p g r d", r=4, g=2), bc(kcol, [n, 2, 4, 64]), ALU.mult,
                     r=["q_bf", "k_bf"], w=["B2"])
                k.red(t["enr"][:, xx * 8:(xx + 1) * 8], v3(t["B2"][N, :], 8), ALU.add, r=["B2"], w=["enr"])
            k.act(t["en"][:], t["enr"][:], AF.Exp, r=["enr"], w=["en"], scale=SCL)
            for xx in range(2):
                k.tt("dve", Env16[:, xx, :, :], bc(t["identf"][N, N].unsqueeze(2), [n, NSMP, 8]),
                     bc(t["en"][:, xx * 8:(xx + 1) * 8].unsqueeze(1), [n, NSMP, 8]), ALU.mult, r=["identf", "en"], w=["Enew"])
            k.cp("act", t["vfb"][:], t["vf"][:, 128:384], r=["vf"], w=["vfb"])
            k.cp("act", t["Enewb"][0:NSMP, :], t["Enew"][0:NSMP, :], r=["Enew"], w=["Enewb"])
            Envb = t["Enewb"][:].rearrange("p (x s h) -> p x s h", x=2, s=NSMP)
            pS, pO, pD, pM = t["pS0"], t["pO0"], t["pO1"], t["pM"]
            pOv = pO[:, 0:384].rearrange("p (s x h) -> p s x h", s=NSMP, x=3)
            pDv = pD[:, 0:384].rearrange("p (s x h) -> p s x h", s=NSMP, x=3)
            PcTv = t["PcT"][:].rearrange("p (s h) -> p s h", s=NSMP)
            for s in range(NSMP):
                k.gather(t["G0"][:], d["kcmp"], t["idx"][:, s:s + 1], r=["idx"], w=["G0"])
                k.gather(t["G1"][:], d["vcmp"], t["idx"][:, s:s + 1], r=["idx"], w=["G1"])
                k.cp("act", t["Gb0"][:], t["G0"][:], r=["G0"], w=["Gb0"])
                k.cp("act", t["Gb1"][:], t["G1"][:], r=["G1"], w=["Gb1"])
                for tt_ in range(16):
                    k.mm(pM[:, 0:64], t["Gb0"][:, tt_ * 128:(tt_ + 1) * 128], t["pool2b"][:], start=(tt_ == 0), stop=(tt_ == 15),
                         r=["Gb0", "pool2b"], w=["pM"])
                for tt_ in range(16):
                    k.mm(pM[:, 64:128], t["Gb1"][:, tt_ * 128:(tt_ + 1) * 128], t["pool2b"][:], start=(tt_ == 0), stop=(tt_ == 15),
                         r=["Gb1", "pool2b"], w=["pM"])
                k.ts("dve", t["pTk"][:], pM[:, 0:64], t["pebar"][:, 0:1], None, ALU.add, r=["pM", "pebar"], w=["pTk"])
                k.ts("dve", t["pTv"][:], pM[:, 64:128], t["pebar"][:, 1:2], None, ALU.add, r=["pM", "pebar"], w=["pTv"])
                k.mm(pM[:, 128:192], t["Wbdk"][:], t["pTk"][:], r=["Wbdk", "pTk"], w=["pM"])
                k.mm(pM[0:64, 192:320], t["pTv"][:], t["Wbdv"][:], r=["Wbdv", "pTv"], w=["pM"])
                k.cp("act", t["KcTs"][:], pM[:, 128:192], r=["pM"], w=["KcTs"])
                k.cp("act", t["Vcs"][:], pM[0:64, 192:320], r=["pM"], w=["Vcs"])
                for g in range(2):
                    gs = slice(g * 64, (g + 1) * 64)
                    k.mm(pS[0:64, s * 8 + g * 4:s * 8 + g * 4 + 4], t["KcTs"][:, :],
                         t["QTs"][:, g, :].rearrange("p (r s) -> p r s", r=4)[:, :, s], r=["KcTs", "QTs"], w=["pS0"])
                k.act(PcTv[:, s, :], pS[0:64, s * 8:(s + 1) * 8], AF.Exp, r=["pS0"], w=["PcT"], scale=SCL)
                k.mm(pOv[:, s, 0, :], t["Vcs"][:], PcTv[:, s, :], r=["Vcs", "PcT"], w=["pO0"])
                k.mm(pDv[:, s, 0, :], t["onesf"][0:64, :], PcTv[:, s, :], r=["onesf", "PcT"], w=["pO1"])
                k.mm(pS[0:32, 128 + s * 8:128 + (s + 1) * 8], t["pairf"][:], PcTv[:, s, :], r=["pairf", "PcT"], w=["pS0"])
            k.P.add("dve", lambda e: e.reciprocal(out=t["rDc"][:].rearrange("p (s h) -> p s h", s=NSMP), in_=pDv[0:32, :, 0, :]),
                    ["pO1"], ["rDc"])
            k.tt("dve", t["impn"][:], pS[0:32, 128:256], t["rDc"][:], ALU.mult, r=["pS0", "rDc"], w=["impn"])
            k.red(t["impT"][:], t["impn"][:].rearrange("p (a r) -> p a r", r=4), ALU.add, r=["impn"], w=["impT"])
            k.tr(pS[0:32, 256:288], t["impT"][:], t["identf"][0:32, 0:32], r=["impT", "identf"], w=["pS0"])
            k.tt("dve", t["impS"][:], pS[0:32, 256:288], t["bias0"][:], ALU.add, r=["pS0", "bias0"], w=["impS"])
            k.P.add("dve", lambda e: e.max(out=t["m8s"][:], in_=t["impS"][:]), ["impS"], ["m8s"])
            k.ts("dve", t["selS"][:], t["impS"][:], t["m8s"][:, 6:7], None, ALU.is_ge, r=["impS", "m8s"], w=["selS"])
            k.tr(t["pT"][0:32, 0:32], t["selS"][:], t["identb"][0:32, 0:32], r=["selS", "identb"], w=["pT"])
            k.cp("act", t["selTs"][:], t["pT"][0:32, 0:32], r=["pT"], w=["selTs"])
            k.mm(pS[:, 320:352], t["e4"][:], t["selTs"][:], r=["e4", "selTs"], w=["pS0"])
            k.cp("act", t["Msk"][:], pS[:, 320:352], r=["pS0"], w=["Msk"])
            pS = t["pS1"]
            banks = [(t["pA"], "pA"), (t["pB"], "pB")]
            for s in range(NSMP):
                k.gather(t["G0"][:], d["kslc"], t["idx"][:, s:s + 1], r=["idx"], w=["G0"])
                k.gather(t["G1"][:], d["vslc"], t["idx"][:, s:s + 1], r=["idx"], w=["G1"])
                k.dma("sp", t["Gw0"][:].rearrange("p (a c) -> p a c", a=4), d["kwin"][s].rearrange("(p a) c -> p a c", a=4), w=["Gw0"])
                k.dma("sp", t["Gw1"][:].rearrange("p (a c) -> p a c", a=4), d["vwin"][s].rearrange("(p a) c -> p a c", a=4), w=["Gw1"])
                k.cp("act", t["Gb0"][:], t["G0"][:], r=["G0"], w=["Gb0"])
                k.cp("act", t["Gwb0"][:], t["Gw0"][:], r=["Gw0"], w=["Gwb0"])
                k.cp("act", t["Gb1"][:], t["G1"][:], r=["G1"], w=["Gb1"])
                k.cp("act", t["Gwb1"][:], t["Gw1"][:], r=["Gw1"], w=["Gwb1"])
                tbanks = [(t["pT"][:].rearrange("p (a c) -> p a c", a=8), "pT"),
                          (t["pA"][:].bitcast(BF16).rearrange("p (a c) -> p a c", a=8), "pA"),
                          (t["pB"][:].bitcast(BF16).rearrange("p (a c) -> p a c", a=8), "pB")]
                for q8 in range(3):
                    pTk8, tbn = tbanks[q8]
                    nt_ = 8 if q8 < 2 else 4
                    for a in range(nt_):
                        tix = q8 * 8 + a
                        if tix < 16:
                            src, sname, col = t["Gb0"], "Gb0", tix * 128
                        else:
                            src, sname, col = t["Gwb0"], "Gwb0", (tix - 16) * 128
                        k.tr(pTk8[:, a, :], src[:, col:col + 128], t["identb"][:], r=[sname, "identb"], w=[tbn])
                    k.cp("act", t["KTs"][:, q8 * 8:q8 * 8 + nt_, :], pTk8[:, 0:nt_, :], r=[tbn], w=["KTs"])
                for tt_ in range(20):
                    for g in range(2):
                        gs = slice(g * 64, (g + 1) * 64)
                        k.mm(pS[:, tt_ * 8 + g * 4:tt_ * 8 + g * 4 + 4], t["KTs"][:, tt_, :],
                             t["QTs"][:, g, :].rearrange("p (r s) -> p r s", r=4)[:, :, s], r=["KTs", "QTs"], w=["pS1"])
                k.act(t["PTs"][:], pS[:, 0:160], AF.Exp, r=["pS1"], w=["PTs"], scale=SCL)
                k.tt("dve", t["PTs"][:, 0:128].rearrange("p (a g r) -> p a g r", a=16, g=2),
                     t["PTs"][:, 0:128].rearrange("p (a g r) -> p a g r", a=16, g=2),
                     bc(t["Msk"][:, s * 2:(s + 1) * 2].unsqueeze(1).unsqueeze(3), [128, 16, 2, 4]), ALU.mult,
                     r=["PTs", "Msk"], w=["PTs"])
                k.memset("dve", t["PTs"][0:1, 128:136], 0.0, w=["PTs"])
                k.red(t["PTsum"][:, 0:8], t["PTs"][:, 0:128].rearrange("p (a h) -> p h a", a=16), ALU.add, r=["PTs"], w=["PTsum"])
                k.red(t["PTsum"][:, 8:16], t["PTs"][:, 128:160].rearrange("p (a h) -> p h a", a=4), ALU.add, r=["PTs"], w=["PTsum"])
                k.cp("act", t["PTb"][:], t["PTs"][:], r=["PTs"], w=["PTb"])
                for tt_ in range(16):
                    k.mm(pOv[:, s, 1, :], t["Gb1"][:, tt_ * 128:(tt_ + 1) * 128], t["PTb"][:, tt_ * 8:(tt_ + 1) * 8],
                         start=(tt_ == 0), stop=False, r=["Gb1", "PTb"], w=["pO0"])
                k.mm(pOv[:, s, 1, :], t["vfb"][:, 0:128], Envb[:, 0, s, :], start=False, stop=True,
                     r=["vfb", "Enewb"], w=["pO0"])
                for tt_ in range(4):
                    k.mm(pOv[:, s, 2, :], t["Gwb1"][:, tt_ * 128:(tt_ + 1) * 128], t["PTb"][:, 128 + tt_ * 8:128 + (tt_ + 1) * 8],
                         start=(tt_ == 0), stop=False, r=["Gwb1", "PTb"], w=["pO0"])
                k.mm(pOv[:, s, 2, :], t["vfb"][:, 128:256], Envb[:, 1, s, :], start=False, stop=True,
                     r=["vfb", "Enewb"], w=["pO0"])
                for xx in range(2):
                    k.mm(pDv[:, s, 1 + xx, :], t["onesf"][:, :], t["PTsum"][:, xx * 8:(xx + 1) * 8], start=True, stop=False,
                         r=["onesf", "PTsum"], w=["pO1"])
                    k.mm(pDv[:, s, 1 + xx, :], t["onesf"][:, :], Env[:, xx, s, :], start=False, stop=True,
                         r=["onesf", "Enew"], w=["pO1"])
            k.P.add("dve", lambda e: e.reciprocal(out=t["rDall"], in_=pD[:, 0:384]), ["pO1"], ["B1a", "B1b"])
            k.tt("dve", t["OTn"], pO[:, 0:384], t["rDall"], ALU.mult, r=["pO0", "B1a", "B1b"], w=["B1a", "B1b"])
            k.cp("dve", t["OT1"], t["OTn"][64:128, :], r=["B1a", "B1b"], w=["B2"])
            gwv = t["gw0"][N, :].rearrange("p (h x) -> p x h", x=3)
            for xx in range(3):
                bank, bname = banks[xx % 2]
                for hh in range(8):
                    src = t["OTn"] if hh < 4 else t["OT1"]
                    sname = "B1a" if hh < 4 else "B2"
                    inap = src[0:64, :].rearrange("p (s c) -> p s c", s=NSMP)[:, :, xx * 8 + hh]
                    k.tr(bank[0:n, hh * 64:(hh + 1) * 64], inap, t["identf"][0:64, 0:64], r=[sname, "identf"], w=[bname])
                if xx == 0:
                    k.tt("dve", v3(t["o_a"][N, :], 8), v3(bank[N, :], 8), bc(gwv[:, xx, :].unsqueeze(2), [n, 8, 64]), ALU.mult,
                         r=[bname, "gw0"], w=["o_a"])
                else:
                    k.tt("dve", v3(t["tmpO"][N, :], 8), v3(bank[N, :], 8), bc(gwv[:, xx, :].unsqueeze(2), [n, 8, 64]), ALU.mult,
                         r=[bname, "gw0"], w=["B3"])
                    k.tt("pool", t["o_a"][N, :], t["o_a"][N, :], t["tmpO"][N, :], ALU.add, r=["o_a", "B3"], w=["o_a"])
            k.dma("sp", d["gqs"], t["gq"][N, :], r=["gq"], w=["d_gqs"])
            mod_pass(k, t, d, 128, t["cp8"], "cp8", True)
            P.emit(es)

        with contextlib.ExitStack() as s2:
            P = Prog(nc, "b")
            k = K(P)
            sb("Wout", [128, 8, D], BF16, s2); sb("WA", [128, 4, D], BF16, s2); sb("WB", [128, 4, D], BF16, s2)
            k.dma("pool", t["Wout"][:], d["wout"].rearrange("(kc p) n -> p kc n", p=128), w=["Wout"])
            for g in range(2):
                k.dma("pool", t["WA"][g * 64:(g + 1) * 64, :, :],
                      d["wbra"][g * 256:(g + 1) * 256, :].rearrange("(r dd) n -> dd r n", dd=64), w=["WA"])
            k.dma("pool", t["WB"][:], d["wbrb"].rearrange("(c p) n -> p c n", p=128), w=["WB"])
            k.dma("sp", t["B1"][0:NSMP, :], d["gqs"], w=["B1a", "B1b"])
            for _ in tail(k, t, NSMP, d["ys"], d["xs"], "0", gate=[(t["B1"][:, 0:512], "B1a"), (t["B1"][:, 512:1024], "B1b")]):
                pass
            sb("KsT", [128, S], BF16, s2); sb("KwT", [128, 5 * 128], BF16, s2)
            sb("Vs", [128, NT, 2, 65], BF16, s2); sb("Vw", [128, 5, 2, 65], BF16, s2)
            sb("QT", [128, 2, 512], BF16, s2); sb("vcb", [128, 128], BF16, s2)
            sb("pTk2", [128, 64], BF16, s2); sb("pTv2", [128, 64], BF16, s2); sb("KcT", [128, 64], BF16, s2)
            sb("Vc", [64, 2, 97], BF16, s2)
            sb("PT0", [128, 512], BF16, s2); sb("PT1", [128, 512], BF16, s2)
            sb("nsel4", [128, 2, 512], BF16, s2); sb("imp", [128, 64], F32, s2)
            sb("sel", [128, 64], BF16, s2); sb("m8", [128, 16], F32, s2); sb("mctneg", [128, 512], BF16, s2); sb("ibias", [128, 32], F32, s2)
            sb("tnle", [128, 512], BF16, s2); sb("tngt", [128, 512], BF16, s2)
            k.dma("pool", t["tnle"][:], d["trineg_le"], w=["tnle"])
            k.dma("pool", t["tngt"][:], d["trineg_gt"], w=["tngt"])
            n = 128
            N = slice(0, 128)
            k.memset("pool", t["Vs"][:], 1.0, w=["Vs"])
            k.memset("pool", t["Vw"][:], 1.0, w=["Vw"])
            k.memset("pool", t["Vc"][:], 1.0, w=["Vc"])
            k.memset("pool", t["pTk2"][:], 0.0, w=["pTk2"])
            k.memset("pool", t["pTv2"][:], 0.0, w=["pTv2"])
            k.memset("pool", t["KcT"][:], 0.0, w=["KcT"])
            k.memset("pool", t["QT"][:], 0.0, w=["QT"])
            k.memset("pool", t["nsel4"][:], 0.0, w=["nsel4"])
            k.dma("pool", t["mctneg"][:], d["cmpbase"], w=["mctneg"])
            for g in range(2):
                k.dma("pool", t["Vc"][:, g, 65:97], d["pair"], w=["Vc"])
            pS = [(t["pS0"], "pS0"), (t["pS1"], "pS1")]
            pO = [(t["pO0"], "pO0"), (t["pO1"], "pO1")]
            PT = [(t["PT0"], "PT0"), (t["PT1"], "PT1")]
            pM = t["pM"]
            pTv8 = t["pT"][:].rearrange("p (a b) -> p a b", a=8)
            cnt = {"s": 0, "p": 0}
            cur = {}
            sfx_of = lambda ii: str((ii + 1) % 2)

            def branch_finish(xx):
                for g in range(2):
                    bank, bname = pO[g]
                    ov = bank[:, 0:388].rearrange("p (r c) -> p r c", r=4)
                    k.ts("dve", t["rden"][:, g * 4:(g + 1) * 4], ov[:, :, 64], 1e-30, None, ALU.max, r=[bname], w=["rden"])
                k.P.add("dve", lambda e: e.reciprocal(out=t["rden"][:], in_=t["rden"][:]), ["rden"], ["rden"])
                k.tt("dve", t["cx"][:], t["rden"][:], cur["gwv"][:, xx, :], ALU.mult, r=["rden", cur["gwn"]], w=["cx"])
                for g in range(2):
                    bank, bname = pO[g]
                    ov = bank[:, 0:388].rearrange("p (r c) -> p r c", r=4)
                    for r in range(4):
                        hs = slice((g * 4 + r) * 64, (g * 4 + r + 1) * 64)
                        k.stt(t["o_a"][:, hs], ov[:, r, 0:64], t["cx"][:, g * 4 + r:g * 4 + r + 1], t["o_a"][:, hs], ALU.mult, ALU.add,
                              r=[bname, "cx", "o_a"], w=["o_a"])

            k.dma("sp", t["cs"][:], d["rope"][0:128, :], w=["cs"])
            for _ in front(k, t, 128, d["xp"][0:128, :], "cs", sfx_of(0), True):
                pass
            tgen = {"g": None}

            def tail_hook(nit):
                if tgen["g"] is not None:
                    next(tgen["g"], None)

            for i in range(NT):
                rows = slice(i * 128, (i + 1) * 128)
                sfx = sfx_of(i)
                cur["gwn"] = "gw" + sfx
                cur["gwv"] = t["gw" + sfx][:, :].rearrange("p (h x) -> p x h", x=3)
                k.dma("sp", d["pk_cmp"][rows, :], t["kf"][:, 0:128], r=["kf"], is_out=True)
                k.dma("sp", d["pk_slc"][rows, :], t["kf"][:, 128:256], r=["kf"], is_out=True)
                k.dma("sp", d["pv_cmp"][rows, :], t["vf"][:, 0:128], r=["vf"], is_out=True)
                k.dma("sp", d["pv_slc"][rows, :], t["vf"][:, 128:256], r=["vf"], is_out=True)
                if i >= NT - 4:
                    wr = slice((i - (NT - 4)) * 128, (i - (NT - 4) + 1) * 128)
                    k.dma("sp", d["pk_win"][wr, :], t["kf"][:, 256:384], r=["kf"], is_out=True)
                    k.dma("sp", d["pv_win"][wr, :], t["vf"][:, 256:384], r=["vf"], is_out=True)
                slot = i % 5
                k.cp("pool", t["Vs"][:, i, :, 0:64], v3(t["vf"][:, 128:256], 2), r=["vf"], w=["Vs"])
                k.cp("pool", t["Vw"][:, slot, :, 0:64], v3(t["vf"][:, 256:384], 2), r=["vf"], w=["Vw"])
                k.cp("pool", t["vcb"][:], t["vf"][:, 0:128], r=["vf"], w=["vcb"])
                for r in range(4):
                    k.tr(pTv8[:, r, :], t["q_bf"][:, r * 128:(r + 1) * 128], t["identb"][:], r=["q_bf", "identb"], w=["pT"])
                k.tr(pTv8[:, 4, :], t["k_bf"][:, 128:256], t["identb"][:], r=["k_bf", "identb"], w=["pT"])
                k.tr(pTv8[:, 5, :], t["k_bf"][:, 256:384], t["identb"][:], r=["k_bf", "identb"], w=["pT"])
                k.cp("act", t["QT"][0:64, 0, :].rearrange("p (r q) -> p r q", r=4), pTv8[0:64, 0:4, :], r=["pT"], w=["QT"])
                k.cp("act", t["QT"][64:128, 1, :].rearrange("p (r q) -> p r q", r=4), pTv8[64:128, 0:4, :], r=["pT"], w=["QT"])
                k.cp("act", t["KsT"][:, rows], pTv8[:, 4, :], r=["pT"], w=["KsT"])
                k.cp("act", t["KwT"][:, slot * 128:(slot + 1) * 128], pTv8[:, 5, :], r=["pT"], w=["KwT"])
                cc = slice(4 * i, 4 * i + 4)
                k.mm(pM[:, 0:4], t["k_bf"][:, 0:128], t["pool4"][:], r=["k_bf", "pool4"], w=["pM"])
                k.mm(pM[:, 4:8], t["vcb"][:], t["pool4"][:], r=["vcb", "pool4"], w=["pM"])
                k.ts("dve", t["pTk2"][:, cc], pM[:, 0:4], t["pebar"][:, 0:1], None, ALU.add, r=["pM", "pebar"], w=["pTk2"])
                k.ts("dve", t["pTv2"][:, cc], pM[:, 4:8], t["pebar"][:, 1:2], None, ALU.add, r=["pM", "pebar"], w=["pTv2"])
                k.mm(pM[:, 8:12], t["Wbdk"][:], t["pTk2"][:, cc], r=["Wbdk", "pTk2"], w=["pM"])
                k.mm(pM[0:64, 16:144], t["pTv2"][:], t["Wbdv"][:], r=["Wbdv", "pTv2"], w=["pM"])
                k.cp("act", t["KcT"][:, cc], pM[:, 8:12], r=["pM"], w=["KcT"])
                k.cp("act", t["Vc"][:, :, 0:64], v3(pM[0:64, 16:144], 2), r=["pM"], w=["Vc"])

                k.dma("sp", t["ibias"][:], d["impbias"][i], w=["ibias"])
                gen = None
                if i + 1 < NT:
                    nrows = slice((i + 1) * 128, (i + 2) * 128)
                    k.dma("sp", t["cs"][:], d["rope"][nrows, :], w=["cs"])
                    gen = front(k, t, 128, d["xp"][nrows, :], "cs", sfx_of(i + 1), True)
                    next(gen, None)
                qt = {g: t["QT"][:, g, :] for g in range(2)}

                def run_branch(items, hook=None):
                    recs = []

                    def s_stage(it):
                        g, kp, mms, v_ap, v_name, first, last = it
                        sbank, sname = pS[cnt["s"] % 2]; cnt["s"] += 1
                        pt, pname = PT[cnt["p"] % 2]; cnt["p"] += 1
                        for mi, (lh, rh, nm) in enumerate(mms):
                            k.mm(sbank[0:kp, :], lh, rh, start=(mi == 0), stop=(mi == len(mms) - 1), r=nm, w=[sname])
                        recs.append((sbank, sname, pt, pname))

                    def e_stage(idx):
                        g, kp, mms, v_ap, v_name, first, last = items[idx]
                        sbank, sname, pt, pname = recs[idx]
                        k.act(pt[0:kp, :], sbank[0:kp, :], AF.Exp, r=[sname], w=[pname], scale=SCL)
                        obank, oname = pO[g]
                        ov = obank[:, 0:388].rearrange("p (r c) -> p r c", r=4)
                        nv = v_ap.shape[-1]
                        for r in range(4):
                            k.mm(ov[:, r, 0:nv], pt[0:kp, r * 128:(r + 1) * 128], v_ap, start=(first and r == 0), stop=(last and r == 3),
                                 r=[pname, v_name], w=[oname])

                    s_stage(items[0])
                    for idx in range(len(items)):
                        if idx + 1 < len(items):
                            s_stage(items[idx + 1])
                        e_stage(idx)
                        if hook is not None:
                            hook(len(items))

                items = []
                for g in range(2):
                    gs = slice(g * 64, (g + 1) * 64)
                    items.append((g, 64, [(t["KcT"][:, :], qt[g], ["KcT", "QT"]),
                                          (t["identb"][:, 64 - 4 * i:128 - 4 * i], t["mctneg"][:, :], ["identb", "mctneg"])],
                                  t["Vc"][:, g, :], "Vc", True, True))
                run_branch(items)
                for g in range(2):
                    bank, bname = pO[g]
                    ov = bank[:, 0:388].rearrange("p (r c) -> p r c", r=4)
                    k.ts("dve", t["rden"][:, g * 4:(g + 1) * 4], ov[:, :, 64], 1e-30, None, ALU.max, r=[bname], w=["rden"])
                k.P.add("dve", lambda e: e.reciprocal(out=t["rden"][:], in_=t["rden"][:]), ["rden"], ["rden"])
                for g in range(2):
                    bank, bname = pO[g]
                    ov = bank[:, 0:388].rearrange("p (r c) -> p r c", r=4)
                    ig = t["imp"][:, g * 32:(g + 1) * 32]
                    k.stt(ig, ov[:, 0, 65:97], t["rden"][:, g * 4:g * 4 + 1], t["ibias"][:], ALU.mult, ALU.add,
                          r=[bname, "rden", "ibias"], w=["imp"])
                    for r in range(1, 4):
                        k.stt(ig, ov[:, r, 65:97], t["rden"][:, g * 4 + r:g * 4 + r + 1], ig, ALU.mult, ALU.add,
                              r=[bname, "rden", "imp"], w=["imp"])
                k.tt("dve", t["cx"][:], t["rden"][:], cur["gwv"][:, 0, :], ALU.mult, r=["rden", cur["gwn"]], w=["cx"])
                for g in range(2):
                    bank, bname = pO[g]
                    ov = bank[:, 0:388].rearrange("p (r c) -> p r c", r=4)
                    k.tt("dve", v3(t["o_a"][:, g * 256:(g + 1) * 256], 4), ov[:, :, 0:64],
                         bc(t["cx"][:, g * 4:(g + 1) * 4].unsqueeze(2), [128, 4, 64]), ALU.mult, r=[bname, "cx"], w=["o_a"])
                for g in range(2):
                    ig = t["imp"][:, g * 32:(g + 1) * 32]
                    k.P.add("dve", (lambda g: lambda e: e.max(out=t["m8"][:, g * 8:(g + 1) * 8], in_=t["imp"][:, g * 32:(g + 1) * 32]))(g),
                            ["imp"], ["m8"])
                    k.ts("dve", t["sel"][:, g * 32:(g + 1) * 32], ig, t["m8"][:, g * 8 + 7:g * 8 + 8], None, ALU.is_ge,
                         r=["imp", "m8"], w=["sel"])
                j0 = max(0, i - 4)
                items = []
                for g in range(2):
                    gs = slice(g * 64, (g + 1) * 64)
                    for j in range(j0, i + 1):
                        sl = j % 5
                        mms = [(t["KwT"][:, sl * 128:(sl + 1) * 128], qt[g], ["KwT", "QT"])]
                        if j == i:
                            mms.append((t["identb"][:, :], t["tnle"][:, :], ["identb", "tnle"]))
                        elif j == i - 4:
                            mms.append((t["identb"][:, :], t["tngt"][:, :], ["identb", "tngt"]))
                        items.append((g, 128, mms, t["Vw"][:, sl, g, :], "Vw", j == j0, j == i))
                run_branch(items, tail_hook)
                if tgen["g"] is not None:
                    for _ in tgen["g"]:
                        pass
                    tgen["g"] = None
                branch_finish(2)
                k.tr(t["pT"][0:64, 0:128], t["sel"][:], t["identb"][:], r=["sel", "identb"], w=["pT"])
                for g in range(2):
                    g32 = slice(g * 32, (g + 1) * 32)
                    k.ts("dve", v3(t["nsel4"][g32, g, :], 4), bc(t["pT"][g32, 0:128].unsqueeze(1), [32, 4, 128]), -1.0, 30000.0,
                         ALU.add, ALU.mult, r=["pT"], w=["nsel4"])
                items = []
                for g in range(2):
                    gs = slice(g * 64, (g + 1) * 64)
                    g32 = slice(g * 32, (g + 1) * 32)
                    for j in range(i + 1):
                        mms = [(t["KsT"][:, j * 128:(j + 1) * 128], qt[g], ["KsT", "QT"]),
                               (t["e2"][:, j * 128:(j + 1) * 128], t["nsel4"][:, g, :], ["e2", "nsel4"])]
                        if j == i:
                            mms.append((t["identb"][:, :], t["tnle"][:, :], ["identb", "tnle"]))
                        items.append((g, 128, mms, t["Vs"][:, j, g, :], "Vs", j == 0, j == i))
                acc = {"a": 0.0}

                def hook(nit):
                    if gen is None:
                        return
                    acc["a"] += 14.0 / nit
                    while acc["a"] >= 1.0:
                        acc["a"] -= 1.0
                        next(gen, None)

                run_branch(items, hook)
                if gen is not None:
                    for _ in gen:
                        pass
                branch_finish(1)
                tgen["g"] = tail(k, t, 128, d["yp"][rows, :], d["xp"][rows, :], sfx)
                next(tgen["g"], None)
            for _ in tgen["g"]:
                pass
            P.emit(es)
    return nc


def _consts():
    c = {}
    half = 32
    inv = (10000.0 ** (-np.arange(half, dtype=np.float32) * 2.0 / 64)).astype(np.float32)
    ang = np.arange(S + 1, dtype=np.float32)[:, None] * inv[None, :]
    c["rope"] = np.concatenate([np.cos(ang), np.sin(ang)], axis=1).astype(np.float32)
    kk = np.arange(128)[:, None]
    qq = np.arange(128)[None, :]
    c["tri_le"] = (kk <= qq).astype(np.float32)
    c["tri_gt"] = (kk > qq).astype(np.float32)
    c["identf_c"] = np.eye(128, dtype=np.float32)
    c["pool4"] = ((np.arange(128)[:, None] // 32) == np.arange(4)[None, :]).astype(np.float32) / 32.0
    c["pair"] = ((np.arange(64)[:, None] // 2) == np.arange(32)[None, :]).astype(np.float32)
    e = ((np.arange(2048)[None, :] // 64) == np.arange(32)[:, None]).astype(np.float32)
    c["e2"] = np.concatenate([e, e], axis=0)
    m = np.zeros((NT, 64, 128), np.float32)
    ib = np.zeros((NT, 128, 32), np.float32)
    for i in range(NT):
        pos = i * 128 + np.arange(128)
        cend = (np.arange(64) + 1) * 32 - 1
        m[i] = (cend[:, None] <= pos[None, :]).astype(np.float32)
        qblk = pos // 64
        blk = np.arange(32)
        forced = (blk[None, :] == 0) | (blk[None, :] == qblk[:, None])
        causal = blk[None, :] <= qblk[:, None]
        ib[i] = np.where(causal, 1.0e4 * forced, -1.0e30).astype(np.float32)
    base = np.zeros((128, 128), np.float32)
    base[68:, :] = -30000.0
    for u in range(64, 68):
        base[u, :] = np.where(np.arange(128) < 32 * (u - 64) + 31, -30000.0, 0.0)
    c["cmpbase"] = np.ascontiguousarray(np.tile(base[:, None, :], (1, 4, 1)).reshape(128, 512)).astype(np.float32)
    c["trineg_le"] = np.ascontiguousarray(np.tile(((1.0 - c["tri_le"]) * -30000.0)[:, None, :], (1, 4, 1)).reshape(128, 512)).astype(np.float32)
    c["trineg_gt"] = np.ascontiguousarray(np.tile(((1.0 - c["tri_gt"]) * -30000.0)[:, None, :], (1, 4, 1)).reshape(128, 512)).astype(np.float32)
    c["impbias"] = ib
    c["pool2"] = ((np.arange(128)[:, None] // 2) == np.arange(64)[None, :]).astype(np.float32) / 32.0
    c["e4"] = ((np.arange(128)[None, :] // 4) == np.arange(32)[:, None]).astype(np.float32)
    c["rmod8"] = (np.arange(128) % 8).astype(np.float32).reshape(128, 1)
    return c


_NC_CACHE = {}


def kernel(x_prompt, x_sample, cache_k_cmp, cache_v_cmp, cache_k_slc, cache_v_slc,
           cache_k_win, cache_v_win, page_table, c_prompt, c_sample,
           w_ada, b_ada, norm_g, w_in, q_norm_g, k_norm_g, cmp_pos_k, cmp_pos_v,
           w_cmp_k, w_cmp_v, vnorm_g, vnorm_b, w_s, b_s, w_br_a, w_br_b, w_out):
    f = lambda a: np.ascontiguousarray(np.asarray(a), dtype=np.float32)
    if "nc" not in _NC_CACHE:
        _NC_CACHE["nc"] = build_program()
    nc = _NC_CACHE["nc"]
    consts = _consts()
    pools = {nm: f(a).reshape(2560 * 8, 2048) for nm, a in
             (("kcmp", cache_k_cmp), ("vcmp", cache_v_cmp), ("kslc", cache_k_slc), ("vslc", cache_v_slc))}
    shared = dict(
        w_ada=f(w_ada)[0], b_ada=f(b_ada)[0].reshape(1, -1), norm_g=f(norm_g)[0].reshape(1, -1), w_in=f(w_in)[0],
        qg=f(q_norm_g)[0].reshape(1, -1), kg=f(k_norm_g)[0].reshape(1, -1), pek=f(cmp_pos_k)[0], pev=f(cmp_pos_v)[0],
        wck=f(w_cmp_k)[0], wcv=f(w_cmp_v)[0], vng=f(vnorm_g)[0].reshape(1, -1), vnb=f(vnorm_b)[0].reshape(1, -1),
        ws=f(w_s)[0], bs=f(b_s)[0], wbra=f(w_br_a)[0], wbrb=f(w_br_b)[0], wout=f(w_out)[0])
    shared.update(pools)
    shared.update(consts)
    xp, xs = f(x_prompt), f(x_sample)
    kw, vw = f(cache_k_win)[0], f(cache_v_win)[0]
    pt = np.asarray(page_table).astype(np.int32)
    cp, cs = f(c_prompt), f(c_sample)
    in_maps = []
    for c in range(8):
        sl = slice(c * NSMP, (c + 1) * NSMP)
        m = dict(shared)
        m["xp"] = xp[c]
        m["xs"] = np.ascontiguousarray(xs[sl, 0, :])
        m["cpv"] = np.ascontiguousarray(cp[c])
        m["csv"] = np.ascontiguousarray(cs[sl])
        m["kwin"] = np.ascontiguousarray(kw[sl].reshape(NSMP, 512, 128))
        m["vwin"] = np.ascontiguousarray(vw[sl].reshape(NSMP, 512, 128))
        m["ptrep"] = np.ascontiguousarray(np.repeat(pt[sl].T, 8, axis=0))
        in_maps.append(m)
    res = run_bass_kernel_spmd(nc, in_maps, core_ids=list(range(8)))
    R = res.results
    cat = lambda nm: np.stack([R[c][nm] for c in range(8)], axis=0)
    y_p = cat("yp")
    y_s = np.concatenate([R[c]["ys"] for c in range(8)], axis=0).reshape(128, 1, D)
    outs = [y_p, y_s]
    for nm in ("pk_cmp", "pv_cmp", "pk_slc", "pv_slc"):
        outs.append(cat(nm).reshape(1, 8, S, 2, 64))
    for nm in ("pk_win", "pv_win"):
        outs.append(cat(nm).reshape(1, 8, 512, 2, 64))
    for nm in ("sk_cmp", "sv_cmp", "sk_slc", "sv_slc"):
        outs.append(np.concatenate([R[c][nm] for c in range(8)], axis=0).reshape(1, 128, 1, 2, 64))
    for nm in ("sk_win", "sv_win"):
        outs.append(np.concatenate([R[c][nm] for c in range(8)], axis=0).reshape(1, 128, 512, 2, 64))
    outs.append(np.concatenate([R[c]["svch"] for c in range(8)], axis=0).reshape(1, 128, 1, 512))
    return tuple(o.astype(np.float32) for o in outs)
```

```python
import contextlib
import numpy as np
import concourse.bass as bass
import concourse.mybir as mybir
from concourse.bass_utils import run_bass_kernel_spmd

F32 = mybir.dt.float32
BF16 = mybir.dt.bfloat16
I32 = mybir.dt.int32
ALU = mybir.AluOpType
AF = mybir.ActivationFunctionType
AX = mybir.AxisListType

D = 1024
DIN = 5400
S = 2048
NT = 16
NSMP = 16
EPS = 1e-6
SCL = 0.125
C_Q, C_K, C_V, C_NSA, C_ZA, C_U, C_VV, C_ZB, C_GA, C_GB = 0, 512, 896, 1280, 1304, 1816, 2328, 2840, 3352, 4376


class _Op:
    __slots__ = ("eng", "fn", "deps", "idx", "signal", "is_dma", "sem", "target", "count")


class Prog:
    ENGS = ("pe", "act", "dve", "pool", "sp")
    DMA_POOL = {"sp": 16, "pool": 12, "act": 2}

    def __init__(self, nc, tag):
        self.nc = nc
        self.tag = tag
        self.q = {e: [] for e in self.ENGS}
        self.last_w = {}
        self.readers = {}
        self.dma_n = {e: 0 for e in self.DMA_POOL}
        self.out_dmas = []

    def add(self, eng, fn, r=(), w=(), dma=False, out=False):
        op = _Op()
        op.eng, op.fn, op.is_dma, op.signal = eng, fn, dma, False
        op.deps = set()
        op.sem = None
        op.target = 0
        op.count = 0
        for b in r:
            lw = self.last_w.get(b)
            if lw is not None:
                op.deps.add(lw)
        for b in w:
            lw = self.last_w.get(b)
            if lw is not None:
                op.deps.add(lw)
            for rd in self.readers.get(b, ()):
                op.deps.add(rd)
        for b in r:
            self.readers.setdefault(b, []).append(op)
        for b in w:
            self.last_w[b] = op
            self.readers[b] = []
        op.deps.discard(op)
        op.idx = len(self.q[eng])
        self.q[eng].append(op)
        if dma:
            j = self.dma_n[eng]
            self.dma_n[eng] += 1
            op.sem = (eng, j % self.DMA_POOL[eng])
            op.target = 16 * (j // self.DMA_POOL[eng] + 1)
            if out:
                self.out_dmas.append(op)
        return op

    def _needs_wait(self, op, dep):
        if dep.is_dma:
            return True
        if dep.eng == "pe" and op.eng == "pe" and not op.is_dma:
            return False
        return True

    def emit(self, es):
        nc = self.nc
        fin = self.add("sp", None)
        for o in self.out_dmas:
            fin.deps.add(o)
        for e in self.DMA_POOL:
            for op in self.q[e]:
                if op.is_dma:
                    fin.deps.add(op)
        for e in self.ENGS:
            for op in self.q[e]:
                for d in op.deps:
                    if (not d.is_dma) and self._needs_wait(op, d):
                        d.signal = True
        for e in self.ENGS:
            c = 0
            for op in self.q[e]:
                if (not op.is_dma) and op.signal:
                    c += 1
                    op.count = c
        esem = {e: es.enter_context(nc.semaphore("s%s_%s" % (self.tag, e))) for e in ("pe", "act", "dve", "pool")}
        dsem = {}
        for e, n in self.DMA_POOL.items():
            for i in range(n):
                dsem[(e, i)] = es.enter_context(nc.semaphore("d%s_%s_%d" % (self.tag, e, i)))
        prog = self
        with nc.Block() as block:
            def run_queue(ename, eng):
                waited = {}
                for op in prog.q[ename]:
                    waits = {}
                    for d in op.deps:
                        if not prog._needs_wait(op, d):
                            continue
                        if d.is_dma:
                            key, val = ("d", d.sem), d.target
                        else:
                            key, val = ("e", d.eng), d.count
                        if waits.get(key, 0) < val:
                            waits[key] = val
                    if op.is_dma and op.target > 16:
                        key = ("d", op.sem)
                        if waits.get(key, 0) < op.target - 16:
                            waits[key] = op.target - 16
                    for key, val in waits.items():
                        if waited.get(key, 0) >= val:
                            continue
                        waited[key] = val
                        s = dsem[key[1]] if key[0] == "d" else esem[key[1]]
                        eng.wait_ge(s, val)
                    if op.fn is None:
                        continue
                    ins = op.fn(eng)
                    if op.is_dma:
                        ins.then_inc(dsem[op.sem], 16)
                    elif op.signal:
                        ins.then_inc(esem[ename], 1)

            @block.sync
            def _(eng):
                run_queue("sp", eng)

            @block.tensor
            def _(eng):
                run_queue("pe", eng)

            @block.scalar
            def _(eng):
                run_queue("act", eng)

            @block.vector
            def _(eng):
                run_queue("dve", eng)

            @block.gpsimd
            def _(eng):
                run_queue("pool", eng)


class K:
    def __init__(self, P):
        self.P = P

    def mm(self, out, lhsT, rhs, start=True, stop=True, r=(), w=()):
        self.P.add("pe", lambda e: e.matmul(out, lhsT=lhsT, rhs=rhs, start=start, stop=stop, skip_group_check=True), r, w)

    def tr(self, out, in_, ident, r=(), w=()):
        self.P.add("pe", lambda e: e.transpose(out=out, in_=in_, identity=ident), r, w)

    def act(self, out, in_, func, r=(), w=(), scale=1.0, accum=None):
        if accum is None:
            self.P.add("act", lambda e: e.activation(out=out, in_=in_, func=func, scale=scale), r, w)
        else:
            self.P.add("act", lambda e: e.activation(out=out, in_=in_, func=func, scale=scale, accum_out=accum), r, w)

    def tt(self, eng, out, in0, in1, op, r=(), w=()):
        self.P.add(eng, lambda e: e.tensor_tensor(out=out, in0=in0, in1=in1, op=op), r, w)

    def ts(self, eng, out, in0, s1, s2, op0, op1=None, r=(), w=()):
        if op1 is None:
            self.P.add(eng, lambda e: e.tensor_scalar(out=out, in0=in0, scalar1=s1, scalar2=None, op0=op0), r, w)
        else:
            self.P.add(eng, lambda e: e.tensor_scalar(out=out, in0=in0, scalar1=s1, scalar2=s2, op0=op0, op1=op1), r, w)

    def stt(self, out, in0, scalar, in1, op0, op1, r=(), w=()):
        self.P.add("dve", lambda e: e.scalar_tensor_tensor(out=out, in0=in0, scalar=scalar, in1=in1, op0=op0, op1=op1), r, w)

    def cp(self, eng, out, in_, r=(), w=()):
        if eng == "act":
            self.P.add("act", lambda e: e.activation(out=out, in_=in_, func=AF.Copy), r, w)
        else:
            self.P.add(eng, lambda e: e.tensor_copy(out=out, in_=in_), r, w)

    def red(self, out, in_, op, r=(), w=()):
        self.P.add("dve", lambda e: e.tensor_reduce(out=out, in_=in_, axis=AX.X, op=op), r, w)

    def memset(self, eng, ap, val, w=()):
        self.P.add(eng, lambda e: e.memset(ap, val), (), w)

    def dma(self, q, out, in_, r=(), w=(), is_out=False, slow=False):
        if slow:
            self.P.add(q, lambda e: e.dma_start(out=out, in_=in_, allow_slow_non_contiguous=True), r, w, dma=True, out=is_out)
        else:
            self.P.add(q, lambda e: e.dma_start(out=out, in_=in_), r, w, dma=True, out=is_out)

    def gather(self, out, in_, idx, r=(), w=()):
        self.P.add("pool", lambda e: e.indirect_dma_start(out=out, out_offset=None, in_=in_,
                                                          in_offset=bass.IndirectOffsetOnAxis(ap=idx, axis=0)),
                   r, w, dma=True)


def v3(ap, a):
    return ap.rearrange("p (a b) -> p a b", a=a)


def bc(ap, shape):
    return ap.to_broadcast(shape)


def win_names(c0, n):
    return ["win%d" % j for j in range(c0 // 512, (c0 + n - 1) // 512 + 1)]


def front(k, t, n, x_src, cs_name, sfx, is_prompt):
    P = k.P
    N = slice(0, n)
    x, B1, B2, B3, h, hT = t["x"], t["B1"], t["B2"], t["B3"], t["h"], t["hT"]
    TH, ZB, vn_bf = B1[:, 0:512], B1[:, 512:1024], h[:, 0:512]
    tg, za_s, bmix, gw = t["tg" + sfx], t["za_s" + sfx], t["bmix" + sfx], t["gw" + sfx]
    n_tg, n_za, n_bm, n_gw = "tg" + sfx, "za_s" + sfx, "bmix" + sfx, "gw" + sfx
    B1W = ["B1a", "B1b"]
    st, st2, st3 = t["st"], t["st2"], t["st3"]
    pA, pB, pT = t["pA"], t["pB"], t["pT"]
    pTv = pT[:].rearrange("p (a b) -> p a b", a=8)
    k.dma("sp", x[N, :], x_src, w=["x"])
    k.act(B1[N, :], x[N, :], AF.Square, r=["x"], w=B1W + ["st"], accum=st[N, 0:1])
    k.ts("dve", st[N, 1:2], st[N, 0:1], 1.0 / D, EPS, ALU.mult, ALU.add, r=["st"], w=["st"])
    k.tt("pool", st[N, 2:3], st[N, 1:2], t["nh"][N, 0:1], ALU.pow, r=["st", "nh"], w=["st"])
    k.stt(B1[N, :], x[N, :], st[N, 2:3], t["sc1"][N, :], ALU.mult, ALU.mult, r=["x", "st", "sc1"], w=B1W)
    k.tt("dve", h[N, :], B1[N, :], t["sh"][N, :], ALU.add, r=B1W + ["sh"], w=["h"])
    yield
    for kc in range(8):
        k.tr(pTv[:, kc, 0:n], h[N, kc * 128:(kc + 1) * 128], t["identb"][N, N], r=["h", "identb"], w=["pT"])
    k.cp("act", hT[:, :, 0:n], pTv[:, :, 0:n], r=["pT"], w=["hT"])
    yield

    def proj(bank, bname, dst, c0, ncol):
        for kc in range(8):
            k.mm(bank[N, dst:dst + ncol], hT[:, kc, 0:n], t["Win"][:, kc, c0:c0 + ncol], start=(kc == 0), stop=(kc == 7),
                 r=["hT"] + win_names(c0, ncol), w=[bname])

    def normrope(bank, bname, A, Bd, gain, dst_ap, dst_name):
        H = A * Bd
        HW = H * 64

        def v4(ap):
            return ap.rearrange("p (a b d) -> p a b d", a=A, b=Bd)

        k.cp("act", B2[N, 0:HW], bank[N, 0:HW], r=[bname], w=["B2"])
        k.act(B3[N, 0:HW], B2[N, 0:HW], AF.Square, r=["B2"], w=["B3"])
        k.red(st2[N, 0:H], v3(B3[N, 0:HW], H), ALU.add, r=["B3"], w=["st2"])
        k.ts("dve", st2[N, 8:8 + H], st2[N, 0:H], 1.0 / 64, EPS, ALU.mult, ALU.add, r=["st2"], w=["st2"])
        k.tt("pool", st2[N, 16:16 + H], st2[N, 8:8 + H], t["nh"][N, 0:H], ALU.pow, r=["st2", "nh"], w=["st2"])
        b2 = v4(B2[N, 0:HW])
        rs = st2[N, 16:16 + H].rearrange("p (a b) -> p a b", a=A).unsqueeze(3)
        k.tt("dve", b2, b2, bc(rs, [n, A, Bd, 64]), ALU.mult, r=["B2", "st2"], w=["B2"])
        k.tt("dve", b2, b2, bc(gain[N, :].unsqueeze(1).unsqueeze(1), [n, A, Bd, 64]), ALU.mult, r=["B2", "gains"], w=["B2"])
        cs = t["cs"]
        cosb = bc(cs[N, 0:32].unsqueeze(1).unsqueeze(1), [n, A, Bd, 32])
        sinb = bc(cs[N, 32:64].unsqueeze(1).unsqueeze(1), [n, A, Bd, 32])
        x1, x2 = b2[:, :, :, 0:32], b2[:, :, :, 32:64]
        r1 = t["R1"][N, 0:H * 32].rearrange("p (a b d) -> p a b d", a=A, b=Bd)
        r2 = t["R2"][N, 0:H * 32].rearrange("p (a b d) -> p a b d", a=A, b=Bd)
        k.tt("dve", r1, x1, cosb, ALU.mult, r=["B2", cs_name], w=["R1"])
        k.tt("dve", r2, x2, sinb, ALU.mult, r=["B2", cs_name], w=["R2"])
        k.tt("dve", dst_ap[:, :, :, 0:32], r1, r2, ALU.subtract, r=["R1", "R2"], w=[dst_name])
        k.tt("dve", r1, x2, cosb, ALU.mult, r=["B2", cs_name], w=["R1"])
        k.tt("dve", r2, x1, sinb, ALU.mult, r=["B2", cs_name], w=["R2"])
        k.tt("dve", dst_ap[:, :, :, 32:64], r1, r2, ALU.add, r=["R1", "R2"], w=[dst_name])

    qdst = t["q_bf"][N, :].rearrange("p (r g d) -> p g r d", r=4, g=2)
    stages = []

    def st_q_post():
        normrope(pA, "pA", 2, 4, t["qg"], qdst, "q_bf")
    stages.append((lambda: proj(pA, "pA", 0, C_Q, 512), st_q_post))

    def st_k_post():
        normrope(pB, "pB", 6, 1, t["kg"], t["kf"][N, :].rearrange("p (a b d) -> p a b d", a=6, b=1), "kf")
        k.cp("act", t["k_bf"][N, :], t["kf"][N, :], r=["kf"], w=["k_bf"])
    stages.append((lambda: proj(pB, "pB", 0, C_K, 384), st_k_post))

    def st_v_pe():
        proj(pA, "pA", 0, C_V, 384)
        proj(pA, "pA", 384, C_NSA, 24)

    def st_v_post():
        k.cp("act", t["vf"][N, :], pA[N, 0:384], r=["pA"], w=["vf"])
        k.act(t["gwt"][N, :], pA[N, 384:408], AF.Tanh, r=["pA"], w=["gwt"], scale=0.5)
        k.ts("dve", gw[N, :], t["gwt"][N, :], 0.5, 0.5, ALU.mult, ALU.add, r=["gwt"], w=[n_gw])
    stages.append((st_v_pe, st_v_post))

    def st_za_post():
        k.act(TH[N, :], pB[N, :], AF.Tanh, r=["pB"], w=["B1a"], scale=0.5)
        k.stt(za_s[N, :], TH[N, :], 1.0, pB[N, :], ALU.add, ALU.mult, r=["B1a", "pB"], w=[n_za])
    stages.append((lambda: proj(pB, "pB", 0, C_ZA, 512), st_za_post))

    def st_ln_post():
        k.cp("act", t["vn_f"][N, :], pA[N, :], r=["pA"], w=["vn_f"])
        P.add("dve", lambda e: e.bn_stats(out=st3[N, 0:6], in_=t["vn_f"][N, :]), ["vn_f"], ["st3"])
        P.add("dve", lambda e: e.bn_aggr(out=st3[N, 6:8], in_=st3[N, 0:6]), ["st3"], ["st3"])
        k.ts("dve", st3[N, 8:9], st3[N, 7:8], EPS, None, ALU.add, r=["st3"], w=["st3"])
        k.tt("pool", st3[N, 9:10], st3[N, 8:9], t["nh"][N, 0:1], ALU.pow, r=["st3", "nh"], w=["st3"])
        k.ts("dve", t["vn_f"][N, :], t["vn_f"][N, :], st3[N, 6:7], st3[N, 9:10], ALU.subtract, ALU.mult, r=["vn_f", "st3"], w=["vn_f"])
        k.tt("dve", t["vn_f"][N, :], t["vn_f"][N, :], t["vng"][N, :], ALU.mult, r=["vn_f", "gains"], w=["vn_f"])
        k.tt("dve", t["vn_f"][N, :], t["vn_f"][N, :], t["vnb"][N, :], ALU.add, r=["vn_f", "gains"], w=["vn_f"])
    stages.append((lambda: proj(pA, "pA", 0, C_VV, 512), st_ln_post))

    def st_zb_post():
        k.act(TH[N, :], pB[N, :], AF.Tanh, r=["pB"], w=["B1a"], scale=0.5)
        k.stt(ZB[N, :], TH[N, :], 1.0, pB[N, :], ALU.add, ALU.mult, r=["B1a", "pB"], w=["B1b"])
    stages.append((lambda: proj(pB, "pB", 0, C_ZB, 512), st_zb_post))

    def st_u_post():
        k.tt("dve", ZB[N, :], pA[N, :], ZB[N, :], ALU.mult, r=["pA", "B1b"], w=["B1b"])
    stages.append((lambda: proj(pA, "pA", 0, C_U, 512), st_u_post))

    def st_sp_pe():
        if is_prompt:
            k.cp("act", vn_bf[N, :], t["vn_f"][N, :], r=["vn_f"], w=["h"])
            for g in range(4):
                k.mm(pB[N, g * 128:(g + 1) * 128], t["wsT"][:, g, :], vn_bf[N, g * 128:(g + 1) * 128],
                     r=["wsT", "h"], w=["pB"])

    def st_sp_post():
        if is_prompt:
            k.tt("dve", v3(B2[N, :], 4), v3(pB[N, :], 4), bc(t["bsT"][N, :].unsqueeze(2), [n, 4, 128]), ALU.add,
                 r=["pB", "bsT"], w=["B2"])
        else:
            k.tt("dve", v3(B2[N, :], 4), v3(t["vn_f"][N, :], 4), bc(t["w00"][N, :].unsqueeze(2), [n, 4, 128]), ALU.mult,
                 r=["vn_f", "w00"], w=["B2"])
            k.tt("dve", v3(B2[N, :], 4), v3(B2[N, :], 4), bc(t["b0"][N, :].unsqueeze(2), [n, 4, 128]), ALU.add,
                 r=["B2", "w00"], w=["B2"])
        k.tt("dve", bmix[N, :], B2[N, :], ZB[N, :], ALU.mult, r=["B2", "B1b"], w=[n_bm])
    stages.append((st_sp_pe, st_sp_post))

    gbanks = [(pA, "pA"), (pB, "pB")]
    for j in range(4):
        bank, bname = gbanks[j % 2]
        stages.append(((lambda bank=bank, bname=bname, j=j: proj(bank, bname, 0, C_GA + j * 512, 512)),
                       (lambda bank=bank, bname=bname, j=j: k.act(tg[N, j * 512:(j + 1) * 512], bank[N, :], AF.Tanh,
                                                                  r=[bname], w=[n_tg], scale=0.5))))
    stages[0][0]()
    yield
    for si in range(len(stages)):
        if si + 1 < len(stages):
            stages[si + 1][0]()
        stages[si][1]()
        yield


def tail(k, t, n, y_dst, x_src, sfx, gate=None):
    N = slice(0, n)
    pA, pB, pT = t["pA"], t["pB"], t["pT"]
    pTv8 = t["pM"][:].bitcast(BF16).rearrange("p (a b) -> p a b", a=8)
    tg, za_s, bmix = t["tg" + sfx], t["za_s" + sfx], t["bmix" + sfx]
    n_tg, n_za, n_bm = "tg" + sfx, "za_s" + sfx, "bmix" + sfx
    mc = t["mc"]
    B1W = ["B1a", "B1b"]
    ozd = t["oz"][N, :].rearrange("p (r g d) -> p g r d", r=4, g=2)
    k.tt("dve", ozd, t["o_a"][N, :].rearrange("p (g r d) -> p g r d", g=2, r=4),
         za_s[N, :].rearrange("p (g r d) -> p g r d", g=2, r=4), ALU.mult, r=["o_a", n_za], w=["oz"])
    for r in range(4):
        k.tr(pTv8[:, r, 0:n], t["oz"][N, r * 128:(r + 1) * 128], t["identb"][N, N], r=["oz", "identb"], w=["pM"])
    for c in range(4):
        k.tr(pTv8[:, 4 + c, 0:n], bmix[N, c * 128:(c + 1) * 128], t["identb"][N, N], r=[n_bm, "identb"], w=["pM"])
    k.cp("dve", t["ozT"][:, :, 0:n], pTv8[:, :, 0:n], r=["pM"], w=["ozT"])
    yield
    for half in range(2):
        cs = slice(half * 512, (half + 1) * 512)
        for r in range(4):
            k.mm(pA[N, :], t["ozT"][:, r, 0:n], t["WA"][:, r, cs], start=(r == 0), stop=(r == 3), r=["ozT", "WA"], w=["pA"])
        for c in range(4):
            k.mm(pB[N, :], t["ozT"][:, 4 + c, 0:n], t["WB"][:, c, cs], start=(c == 0), stop=(c == 3), r=["ozT", "WB"], w=["pB"])
        k.stt(t["B2"][N, :], tg[N, half * 512:(half + 1) * 512], 1.0, pA[N, :], ALU.add, ALU.mult,
              r=[n_tg, "pA"], w=["B2"])
        k.stt(t["B3"][N, :], tg[N, 1024 + half * 512:1024 + (half + 1) * 512], 1.0, pB[N, :], ALU.add, ALU.mult,
              r=[n_tg, "pB"], w=["B3"])
        k.tt("dve", mc[N, cs], t["B2"][N, :], t["B3"][N, :], ALU.add, r=["B2", "B3"], w=["mc"])
        yield
    for kc in range(8):
        k.tr(pTv8[:, kc, 0:n], mc[N, kc * 128:(kc + 1) * 128], t["identb"][N, N], r=["mc", "identb"], w=["pM"])
    k.cp("dve", t["hT"][:, :, 0:n], pTv8[:, :, 0:n], r=["pM"], w=["hT"])
    yield
    banks = [(pA, "pA"), (pB, "pB")]
    for half in range(2):
        bank, bname = banks[half]
        cs = slice(half * 512, (half + 1) * 512)
        for kc in range(8):
            k.mm(bank[N, :], t["hT"][:, kc, 0:n], t["Wout"][:, kc, cs], start=(kc == 0), stop=(kc == 7),
                 r=["hT", "Wout"], w=[bname])
        if gate is None:
            gap, gname = t["gq"][N, cs], "gq"
        else:
            gap, gname = gate[half][0][N, 0:512], gate[half][1]
        k.tt("dve", t["B1"][N, cs], bank[N, :], gap, ALU.mult, r=[bname, gname], w=["B1a" if half == 0 else "B1b"])
        yield
    k.dma("sp", t["x"][N, :], x_src, w=["x"])
    k.tt("dve", t["x"][N, :], t["B1"][N, :], t["x"][N, :], ALU.add, r=B1W + ["x"], w=["x"])
    k.dma("sp", y_dst, t["x"][N, :], r=["x"], is_out=True)


def mod_pass(k, t, d, n, cT, cT_name, bcast):
    N = slice(0, n)
    stg = [t["G0"], t["G1"]]
    banks = [(t["pA"], "pA"), (t["pB"], "pB")]
    wada = d["w_ada"].rearrange("(kc p) n -> p kc n", p=128)
    for j in range(12):
        sg, sname = stg[j % 2], "G%d" % (j % 2)
        bank, bname = banks[j % 2]
        sgv = sg[:].rearrange("p (kc n) -> p kc n", kc=8)
        k.dma("sp", sgv, wada[:, :, j * 256:(j + 1) * 256], w=[sname])
        k.dma("sp", t["bada"][0:1, :], d["b_ada"][0:1, j * 256:(j + 1) * 256], w=["bada"])
        if not bcast:
            for kc in range(8):
                k.mm(bank[N, 0:256], cT[:, kc, 0:n], sgv[:, kc, :], start=(kc == 0), stop=False, r=[cT_name, sname], w=[bname])
            k.mm(bank[N, 0:256], t["ones0"][:, 0:n], t["bada"][:, :], start=False, stop=True, r=["ones0", "bada"], w=[bname])
            src = bank[N, 0:256]
        else:
            for kc in range(8):
                k.mm(bank[0:1, 0:256], cT[:, kc:kc + 1], sgv[:, kc, :], start=(kc == 0), stop=False, r=[cT_name, sname], w=[bname])
            k.mm(bank[0:1, 0:256], t["ones0"][:, 0:1], t["bada"][:, :], start=False, stop=True, r=["ones0", "bada"], w=[bname])
            k.cp("act", t["modrow"][0:1, :], bank[0:1, 0:256], r=[bname], w=["modrow"])
            k.mm(bank[N, 256:512], t["onesf"][0:1, 0:n], t["modrow"][0:1, :], r=["onesf", "modrow"], w=[bname])
            src = bank[N, 256:512]
        cs = slice((j % 4) * 256, (j % 4 + 1) * 256)
        if j < 4:
            k.cp("act", t["sh"][N, cs], src, r=[bname], w=["sh"])
        elif j < 8:
            k.stt(t["sc1"][N, cs], src, 1.0, t["normg"][N, cs], ALU.add, ALU.mult, r=[bname, "normg"], w=["sc1"])
        else:
            k.act(t["gq"][N, cs], src, AF.Copy, r=[bname], w=["gq"], scale=0.25)


def build_program():
    nc = bass.Bass("TRN2", target_bir_lowering=False)
    d = {}

    def din(name, shape, dt=F32):
        d[name] = nc.dram_tensor(name, shape, dt, kind="ExternalInput").ap()

    def dout(name, shape):
        d[name] = nc.dram_tensor(name, shape, F32, kind="ExternalOutput").ap()

    din("xp", [S, D]); din("xs", [NSMP, D]); din("cpv", [D]); din("csv", [NSMP, D])
    for nm in ("kcmp", "vcmp", "kslc", "vslc"):
        din(nm, [2560 * 8, 2048])
    din("kwin", [NSMP, 512, 128]); din("vwin", [NSMP, 512, 128])
    din("ptrep", [128, NSMP], I32)
    din("w_ada", [D, 3 * D]); din("b_ada", [1, 3 * D]); din("norm_g", [1, D]); din("w_in", [D, DIN])
    din("qg", [1, 64]); din("kg", [1, 64]); din("pek", [32, 64]); din("pev", [32, 64])
    din("wck", [64, 64]); din("wcv", [64, 64]); din("vng", [1, 512]); din("vnb", [1, 512])
    din("ws", [4, 128, 128]); din("bs", [4, 128]); din("wbra", [512, D]); din("wbrb", [512, D]); din("wout", [D, D])
    din("rope", [S + 1, 64]); din("tri_le", [128, 128]); din("tri_gt", [128, 128]); din("identf_c", [128, 128])
    din("pool4", [128, 4]); din("pair", [64, 32]); din("e2", [64, 2048]); din("cmpbase", [128, 512]); din("trineg_le", [128, 512]); din("trineg_gt", [128, 512])
    din("impbias", [NT, 128, 32]); din("pool2", [128, 64]); din("e4", [32, 128]); din("rmod8", [128, 1])
    d["gqs"] = nc.dram_tensor("gqs", [NSMP, D], F32, kind="Internal").ap()
    dout("yp", [S, D]); dout("ys", [NSMP, D])
    for nm in ("pk_cmp", "pv_cmp", "pk_slc", "pv_slc"):
        dout(nm, [S, 128])
    dout("pk_win", [512, 128]); dout("pv_win", [512, 128])
    for nm in ("sk_cmp", "sv_cmp", "sk_slc", "sv_slc"):
        dout(nm, [NSMP, 128])
    dout("sk_win", [NSMP, 512, 128]); dout("sv_win", [NSMP, 512, 128]); dout("svch", [NSMP, 512])

    with contextlib.ExitStack() as es:
        t = {}

        def sb(name, shape, dt, scope=es):
            t[name] = scope.enter_context(nc.sbuf_tensor("sb_" + name, shape, dt))
            return t[name]

        def ps(name, shape, dt, scope=es):
            t[name] = scope.enter_context(nc.psum_tensor("ps_" + name, shape, dt))
            return t[name]

        sb("Win", [128, 8, DIN], BF16)
        sb("sc1", [128, D], F32); sb("sh", [128, D], F32); sb("gq", [128, D], F32)
        sb("identb", [128, 128], BF16); sb("identf", [128, 128], F32); sb("tri_le", [128, 128], BF16); sb("tri_gt", [128, 128], BF16)
        sb("qg", [128, 64], F32); sb("kg", [128, 64], F32); sb("vng", [128, 512], F32); sb("vnb", [128, 512], F32)
        sb("nh", [128, 8], F32); sb("onesf", [128, 128], F32)
        sb("wsT", [128, 4, 128], BF16); sb("bsT", [128, 4], F32)
        sb("Wbdk", [128, 128], BF16); sb("Wbdv", [128, 128], BF16); sb("pebar", [128, 2], F32)
        sb("pool4", [128, 4], BF16); sb("e2", [128, 2048], BF16)
        sb("x", [128, D], F32); sb("B1", [128, D], F32); sb("B2", [128, 512], F32); sb("B3", [128, 512], F32)
        sb("h", [128, D], BF16); sb("hT", [128, 8, 128], BF16); sb("R1", [128, 256], F32); sb("R2", [128, 256], F32)
        sb("st", [128, 4], F32); sb("st2", [128, 24], F32); sb("st3", [128, 12], F32)
        sb("q_bf", [128, 512], BF16); sb("kf", [128, 384], F32); sb("vf", [128, 384], F32); sb("k_bf", [128, 384], BF16)
        sb("gwt", [128, 24], F32); sb("gw0", [128, 24], F32); sb("gw1", [128, 24], F32); sb("za_s0", [128, 512], BF16); sb("za_s1", [128, 512], BF16)
        sb("vn_f", [128, 512], F32); sb("tg0", [128, 2048], BF16); sb("tg1", [128, 2048], BF16)
        sb("bmix0", [128, 512], BF16); sb("bmix1", [128, 512], BF16); sb("o_a", [128, 512], F32); sb("oz", [128, 512], BF16); sb("ozT", [128, 8, 128], BF16)
        sb("cs", [128, 64], F32); sb("mc", [128, D], BF16)
        sb("rden", [128, 8], F32); sb("cx", [128, 8], F32); t["tmpO"] = t["B3"]
        ps("pA", [128, 512], F32); ps("pB", [128, 512], F32); ps("pT", [128, 1024], BF16)
        ps("pM", [128, 512], F32); ps("pS0", [128, 512], F32); ps("pS1", [128, 512], F32)
        ps("pO0", [128, 512], F32); ps("pO1", [128, 512], F32)

        with contextlib.ExitStack() as s1:
            P = Prog(nc, "a")
            k = K(P)
            sb("normg", [128, D], F32, s1)
            sb("G0", [128, 2048], F32, s1); sb("G1", [128, 2048], F32, s1)
            sb("Gw0", [128, 512], F32, s1); sb("Gw1", [128, 512], F32, s1)
            sb("Gb0", [128, 2048], BF16, s1); sb("Gb1", [128, 2048], BF16, s1); sb("pool2b", [128, 64], BF16, s1); sb("Gwb0", [128, 512], BF16, s1); sb("Gwb1", [128, 512], BF16, s1)
            sb("PTb", [128, 160], BF16, s1); sb("vfb", [128, 256], BF16, s1); sb("Enewb", [128, 2 * NSMP * 8], BF16, s1)
            sb("bada", [128, 256], F32, s1); sb("modrow", [1, 256], F32, s1); sb("ones0", [128, 128], F32, s1)
            sb("cTs", [128, 8, NSMP], F32, s1)
            sb("cp8", [128, 8], F32, s1)
            sb("w00", [128, 4], F32, s1); sb("b0", [128, 4], F32, s1)
            sb("pe2", [32, 128], F32, s1); sb("o32", [32, 2], F32, s1)
            sb("ptrep", [128, NSMP], I32, s1); sb("rmod8", [128, 1], F32, s1); sb("idx", [128, NSMP], I32, s1)
            sb("pool2", [128, 64], F32, s1); sb("pairf", [64, 32], F32, s1); sb("e4", [32, 128], BF16, s1)
            sb("QTs", [128, 2, 4 * NSMP], BF16, s1); sb("enr", [NSMP, 16], F32, s1); sb("en", [NSMP, 16], F32, s1); sb("Enew", [128, 2 * NSMP * 8], F32, s1)
            sb("pTk", [128, 64], BF16, s1); sb("pTv", [128, 64], BF16, s1); sb("KcTs", [128, 64], BF16, s1); sb("Vcs", [64, 128], F32, s1)
            sb("PcT", [64, NSMP * 8], F32, s1); sb("KTs", [128, 20, 128], BF16, s1)
            sb("PTs", [128, 160], F32, s1); sb("PTsum", [128, 16], F32, s1)
            sb("rDc", [32, 128], F32, s1); sb("impn", [32, 128], F32, s1); sb("impT", [32, 32], F32, s1)
            sb("impS", [32, 32], F32, s1); sb("bias0", [32, 32], F32, s1); sb("m8s", [32, 8], F32, s1)
            sb("selS", [32, 32], BF16, s1); sb("selTs", [32, 32], BF16, s1); sb("Msk", [128, 32], F32, s1)
            t["OTn"] = t["B1"][:, 0:384]; t["rDall"] = t["B1"][:, 384:768]; t["OT1"] = t["B2"][0:64, 0:384]; t["wsl"] = t["B3"][:, 0:128]

            k.dma("pool", t["identb"][:], d["identf_c"], w=["identb"])
            k.dma("sp", t["identf"][:], d["identf_c"], w=["identf"])
            k.dma("pool", t["tri_le"][:], d["tri_le"], w=["tri_le"])
            k.dma("pool", t["tri_gt"][:], d["tri_gt"], w=["tri_gt"])
            k.dma("pool", t["pool4"][:], d["pool4"], w=["pool4"])
            k.memset("pool", t["e2"][64:128, :], 0.0, w=["e2"])
            k.dma("pool", t["e2"][0:64, :], d["e2"], w=["e2"])
            k.dma("pool", t["e4"][:], d["e4"], w=["e4"])
            k.dma("sp", t["pool2"][:], d["pool2"], w=["pool2"])
            k.dma("pool", t["pool2b"][:], d["pool2"], w=["pool2b"])
            k.dma("sp", t["pairf"][:], d["pair"], w=["pairf"])
            k.dma("sp", t["rmod8"][:], d["rmod8"], w=["rmod8"])
            k.dma("sp", t["ptrep"][:], d["ptrep"], w=["ptrep"])
            k.dma("sp", t["qg"][:], d["qg"].partition_broadcast(128), w=["gains"])
            k.dma("sp", t["kg"][:], d["kg"].partition_broadcast(128), w=["gains"])
            k.dma("sp", t["vng"][:], d["vng"].partition_broadcast(128), w=["gains"])
            k.dma("sp", t["vnb"][:], d["vnb"].partition_broadcast(128), w=["gains"])
            k.dma("sp", t["normg"][:], d["norm_g"].partition_broadcast(128), w=["normg"])
            k.dma("sp", t["bsT"][:], d["bs"].rearrange("g i -> i g"), w=["bsT"], slow=True)
            k.dma("sp", t["w00"][0:NSMP, :], d["ws"][:, 0, 0:1].rearrange("g o -> o g").partition_broadcast(NSMP), w=["w00"], slow=True)
            k.dma("sp", t["b0"][0:NSMP, :], d["bs"][:, 0:1].rearrange("g o -> o g").partition_broadcast(NSMP), w=["w00"], slow=True)
            k.memset("pool", t["nh"][:], -0.5, w=["nh"])
            k.memset("pool", t["Enew"][:], 0.0, w=["Enew"])
            k.memset("pool", t["bada"][:], 0.0, w=["bada"])
            k.memset("pool", t["ones0"][:], 0.0, w=["ones0"])
            k.memset("pool", t["ones0"][0:1, :], 1.0, w=["ones0"])
            k.memset("pool", t["QTs"][:], 0.0, w=["QTs"])
            k.memset("pool", t["vf"][:], 0.0, w=["vf"])
            k.memset("pool", t["Enewb"][:], 0.0, w=["Enewb"])
            k.memset("pool", t["onesf"][:], 1.0, w=["onesf"])
            k.memset("pool", t["o32"][:], 1.0 / 32, w=["o32"])
            k.memset("pool", t["Wbdk"][:], 0.0, w=["Wbdk"])
            k.memset("pool", t["Wbdv"][:], 0.0, w=["Wbdv"])
            k.memset("pool", t["bias0"][:], 0.0, w=["bias0"])
            k.memset("pool", t["bias0"][:, 0:1], 1.0e4, w=["bias0"])
            for g in range(2):
                gs = slice(g * 64, (g + 1) * 64)
                k.dma("pool", t["Wbdk"][gs, gs], d["wck"], w=["Wbdk"])
                k.dma("pool", t["Wbdv"][gs, gs], d["wcv"], w=["Wbdv"])
                k.dma("sp", t["pe2"][:, gs], d["pek"], w=["pe2k"])
            k.mm(t["pM"][:, 0:1], t["pe2"][:, :], t["o32"][:, 0:1], r=["pe2k", "o32"], w=["pM"])
            k.cp("dve", t["pebar"][:, 0:1], t["pM"][:, 0:1], r=["pM"], w=["pebar"])
            for g in range(2):
                gs = slice(g * 64, (g + 1) * 64)
                k.dma("sp", t["pe2"][:, gs], d["pev"], r=[], w=["pe2k"])
            k.mm(t["pM"][:, 0:1], t["pe2"][:, :], t["o32"][:, 0:1], r=["pe2k", "o32"], w=["pM"])
            k.cp("dve", t["pebar"][:, 1:2], t["pM"][:, 0:1], r=["pM"], w=["pebar"])
            for g in range(4):
                k.dma("sp", t["wsl"], d["ws"][g], w=["B3"])
                k.tr(t["pM"][:, 0:128], t["wsl"], t["identf"][:], r=["B3", "identf"], w=["pM"])
                k.tt("dve", t["wsT"][:, g, :], t["pM"][:, 0:128], t["tri_le"][:], ALU.mult, r=["pM", "tri_le"], w=["wsT"])
            winv = d["w_in"].rearrange("(kc p) n -> p kc n", p=128)
            for j in range(11):
                c0, c1 = j * 512, min(DIN, (j + 1) * 512)
                k.dma("pool", t["Win"][:, :, c0:c1], winv[:, :, c0:c1], w=["win%d" % j])
            k.dma("sp", t["cp8"][:], d["cpv"].rearrange("(kc p) -> p kc", p=128), w=["cp8"], slow=True)
            k.dma("sp", t["x"][0:NSMP, :], d["csv"], w=["x"])
            pMv = t["pM"][:, 0:8 * NSMP].rearrange("p (a b) -> p a b", a=8)
            for kc in range(8):
                k.tr(pMv[:, kc, :], t["x"][0:NSMP, kc * 128:(kc + 1) * 128], t["identf"][0:NSMP, 0:NSMP],
                     r=["x", "identf"], w=["pM"])
            k.cp("dve", t["cTs"][:], pMv, r=["pM"], w=["cTs"])
            mod_pass(k, t, d, NSMP, t["cTs"], "cTs", False)
            k.ts("dve", t["idx"][:], t["ptrep"][:], 8.0, t["rmod8"][:, 0:1], ALU.mult, ALU.add, r=["ptrep", "rmod8"], w=["idx"])
            k.dma("sp", t["cs"][0:NSMP, :], d["rope"][S:S + 1, :].partition_broadcast(NSMP), w=["cs"])
            for _ in front(k, t, NSMP, d["xs"], "cs", "0", False):
                pass
            n = NSMP
            N = slice(0, n)
            k.dma("sp", d["sk_cmp"], t["kf"][N, 0:128], r=["kf"], is_out=True)
            k.dma("sp", d["sk_slc"], t["kf"][N, 128:256], r=["kf"], is_out=True)
            k.dma("sp", d["sv_cmp"], t["vf"][N, 0:128], r=["vf"], is_out=True)
            k.dma("sp", d["sv_slc"], t["vf"][N, 128:256], r=["vf"], w=["d_svslc"], is_out=True)
            k.dma("sp", d["svch"], t["vn_f"][N, :], r=["vn_f"], is_out=True)
            k.dma("sp", d["sk_win"][:, 0:511, :], d["kwin"][:, 1:512, :], is_out=True)
            k.dma("sp", d["sv_win"][:, 0:511, :], d["vwin"][:, 1:512, :], is_out=True)
            k.dma("sp", d["sk_win"][:, 511, :], t["kf"][N, 256:384], r=["kf"], is_out=True)
            k.dma("sp", d["sv_win"][:, 511, :], t["vf"][N, 256:384], r=["vf"], w=["d_svwin"], is_out=True)
            pTv8 = t["pT"][:].rearrange("p (a b) -> p a b", a=8)
            for r in range(4):
                k.tr(pTv8[:, r, 0:n], t["q_bf"][N, r * 128:(r + 1) * 128], t["identb"][N, N], r=["q_bf", "identb"], w=["pT"])
            k.cp("act", t["QTs"][0:64, 0, :].rearrange("p (r s) -> p r s", r=4), pTv8[0:64, 0:4, 0:n], r=["pT"], w=["QTs"])
            k.cp("act", t["QTs"][64:128, 1, :].rearrange("p (r s) -> p r s", r=4), pTv8[64:128, 0:4, 0:n], r=["pT"], w=["QTs"])
            Env = t["Enew"][:].rearrange("p (x s h) -> p x s h", x=2, s=NSMP)
            Env16 = t["Enew"][0:NSMP, :].rearrange("p (x s h) -> p x s h", x=2, s=NSMP)
            for xx in range(2):
                kcol = t["k_bf"][N, 128 + xx * 128:256 + xx * 128].rearrange("p (g d) -> p g d", g=2).unsqueeze(2)
                k.tt("dve", t["B2"][N, :].rearrange("p (g r d) -> p g r d", g=2, r=4),
                     t["q_bf"][N, :].rearrange("p (r g d) -> p g r d", r=4, g=2), bc(kcol, [n, 2, 4, 64]), ALU.mult,
                     r=["q_bf", "k_bf"], w=["B2"])
                k.red(t["enr"][:, xx * 8:(xx + 1) * 8], v3(t["B2"][N, :], 8), ALU.add, r=["B2"], w=["enr"])
            k.act(t["en"][:], t["enr"][:], AF.Exp, r=["enr"], w=["en"], scale=SCL)
            for xx in range(2):
                k.tt("dve", Env16[:, xx, :, :], bc(t["identf"][N, N].unsqueeze(2), [n, NSMP, 8]),
                     bc(t["en"][:, xx * 8:(xx + 1) * 8].unsqueeze(1), [n, NSMP, 8]), ALU.mult, r=["identf", "en"], w=["Enew"])
            k.cp("act", t["vfb"][:], t["vf"][:, 128:384], r=["vf"], w=["vfb"])
            k.cp("act", t["Enewb"][0:NSMP, :], t["Enew"][0:NSMP, :], r=["Enew"], w=["Enewb"])
            Envb = t["Enewb"][:].rearrange("p (x s h) -> p x s h", x=2, s=NSMP)
            pS, pO, pD, pM = t["pS0"], t["pO0"], t["pO1"], t["pM"]
            pOv = pO[:, 0:384].rearrange("p (s x h) -> p s x h", s=NSMP, x=3)
            pDv = pD[:, 0:384].rearrange("p (s x h) -> p s x h", s=NSMP, x=3)
            PcTv = t["PcT"][:].rearrange("p (s h) -> p s h", s=NSMP)
            for s in range(NSMP):
                k.gather(t["G0"][:], d["kcmp"], t["idx"][:, s:s + 1], r=["idx"], w=["G0"])
                k.gather(t["G1"][:], d["vcmp"], t["idx"][:, s:s + 1], r=["idx"], w=["G1"])
                k.cp("act", t["Gb0"][:], t["G0"][:], r=["G0"], w=["Gb0"])
                k.cp("act", t["Gb1"][:], t["G1"][:], r=["G1"], w=["Gb1"])
                for tt_ in range(16):
                    k.mm(pM[:, 0:64], t["Gb0"][:, tt_ * 128:(tt_ + 1) * 128], t["pool2b"][:], start=(tt_ == 0), stop=(tt_ == 15),
                         r=["Gb0", "pool2b"], w=["pM"])
                for tt_ in range(16):
                    k.mm(pM[:, 64:128], t["Gb1"][:, tt_ * 128:(tt_ + 1) * 128], t["pool2b"][:], start=(tt_ == 0), stop=(tt_ == 15),
                         r=["Gb1", "pool2b"], w=["pM"])
                k.ts("dve", t["pTk"][:], pM[:, 0:64], t["pebar"][:, 0:1], None, ALU.add, r=["pM", "pebar"], w=["pTk"])
                k.ts("dve", t["pTv"][:], pM[:, 64:128], t["pebar"][:, 1:2], None, ALU.add, r=["pM", "pebar"], w=["pTv"])
                k.mm(pM[:, 128:192], t["Wbdk"][:], t["pTk"][:], r=["Wbdk", "pTk"], w=["pM"])
                k.mm(pM[0:64, 192:320], t["pTv"][:], t["Wbdv"][:], r=["Wbdv", "pTv"], w=["pM"])
                k.cp("act", t["KcTs"][:], pM[:, 128:192], r=["pM"], w=["KcTs"])
                k.cp("act", t["Vcs"][:], pM[0:64, 192:320], r=["pM"], w=["Vcs"])
                for g in range(2):
                    gs = slice(g * 64, (g + 1) * 64)
                    k.mm(pS[0:64, s * 8 + g * 4:s * 8 + g * 4 + 4], t["KcTs"][:, :],
                         t["QTs"][:, g, :].rearrange("p (r s) -> p r s", r=4)[:, :, s], r=["KcTs", "QTs"], w=["pS0"])
                k.act(PcTv[:, s, :], pS[0:64, s * 8:(s + 1) * 8], AF.Exp, r=["pS0"], w=["PcT"], scale=SCL)
                k.mm(pOv[:, s, 0, :], t["Vcs"][:], PcTv[:, s, :], r=["Vcs", "PcT"], w=["pO0"])
                k.mm(pDv[:, s, 0, :], t["onesf"][0:64, :], PcTv[:, s, :], r=["onesf", "PcT"], w=["pO1"])
                k.mm(pS[0:32, 128 + s * 8:128 + (s + 1) * 8], t["pairf"][:], PcTv[:, s, :], r=["pairf", "PcT"], w=["pS0"])
            k.P.add("dve", lambda e: e.reciprocal(out=t["rDc"][:].rearrange("p (s h) -> p s h", s=NSMP), in_=pDv[0:32, :, 0, :]),
                    ["pO1"], ["rDc"])
            k.tt("dve", t["impn"][:], pS[0:32, 128:256], t["rDc"][:], ALU.mult, r=["pS0", "rDc"], w=["impn"])
            k.red(t["impT"][:], t["impn"][:].rearrange("p (a r) -> p a r", r=4), ALU.add, r=["impn"], w=["impT"])
            k.tr(pS[0:32, 256:288], t["impT"][:], t["identf"][0:32, 0:32], r=["impT", "identf"], w=["pS0"])
            k.tt("dve", t["impS"][:], pS[0:32, 256:288], t["bias0"][:], ALU.add, r=["pS0", "bias0"], w=["impS"])
            k.P.add("dve", lambda e: e.max(out=t["m8s"][:], in_=t["impS"][:]), ["impS"], ["m8s"])
            k.ts("dve", t["selS"][:], t["impS"][:], t["m8s"][:, 6:7], None, ALU.is_ge, r=["impS", "m8s"], w=["selS"])
            k.tr(t["pT"][0:32, 0:32], t["selS"][:], t["identb"][0:32, 0:32], r=["selS", "identb"], w=["pT"])
            k.cp("act", t["selTs"][:], t["pT"][0:32, 0:32], r=["pT"], w=["selTs"])
            k.mm(pS[:, 320:352], t["e4"][:], t["selTs"][:], r=["e4", "selTs"], w=["pS0"])
            k.cp("act", t["Msk"][:], pS[:, 320:352], r=["pS0"], w=["Msk"])
            pS = t["pS1"]
            banks = [(t["pA"], "pA"), (t["pB"], "pB")]
            for s in range(NSMP):
                k.gather(t["G0"][:], d["kslc"], t["idx"][:, s:s + 1], r=["idx"], w=["G0"])
                k.gather(t["G1"][:], d["vslc"], t["idx"][:, s:s + 1], r=["idx"], w=["G1"])
                k.dma("sp", t["Gw0"][:].rearrange("p (a c) -> p a c", a=4), d["kwin"][s].rearrange("(p a) c -> p a c", a=4), w=["Gw0"])
                k.dma("sp", t["Gw1"][:].rearrange("p (a c) -> p a c", a=4), d["vwin"][s].rearrange("(p a) c -> p a c", a=4), w=["Gw1"])
                k.cp("act", t["Gb0"][:], t["G0"][:], r=["G0"], w=["Gb0"])
                k.cp("act", t["Gwb0"][:], t["Gw0"][:], r=["Gw0"], w=["Gwb0"])
                k.cp("act", t["Gb1"][:], t["G1"][:], r=["G1"], w=["Gb1"])
                k.cp("act", t["Gwb1"][:], t["Gw1"][:], r=["Gw1"], w=["Gwb1"])
                tbanks = [(t["pT"][:].rearrange("p (a c) -> p a c", a=8), "pT"),
                          (t["pA"][:].bitcast(BF16).rearrange("p (a c) -> p a c", a=8), "pA"),
                          (t["pB"][:].bitcast(BF16).rearrange("p (a c) -> p a c", a=8), "pB")]
                for q8 in range(3):
                    pTk8, tbn = tbanks[q8]
                    nt_ = 8 if q8 < 2 else 4
                    for a in range(nt_):
                        tix = q8 * 8 + a
                        if tix < 16:
                            src, sname, col = t["Gb0"], "Gb0", tix * 128
                        else:
                            src, sname, col = t["Gwb0"], "Gwb0", (tix - 16) * 128
                        k.tr(pTk8[:, a, :], src[:, col:col + 128], t["identb"][:], r=[sname, "identb"], w=[tbn])
                    k.cp("act", t["KTs"][:, q8 * 8:q8 * 8 + nt_, :], pTk8[:, 0:nt_, :], r=[tbn], w=["KTs"])
                for tt_ in range(20):
                    for g in range(2):
                        gs = slice(g * 64, (g + 1) * 64)
                        k.mm(pS[:, tt_ * 8 + g * 4:tt_ * 8 + g * 4 + 4], t["KTs"][:, tt_, :],
                             t["QTs"][:, g, :].rearrange("p (r s) -> p r s", r=4)[:, :, s], r=["KTs", "QTs"], w=["pS1"])
                k.act(t["PTs"][:], pS[:, 0:160], AF.Exp, r=["pS1"], w=["PTs"], scale=SCL)
                k.tt("dve", t["PTs"][:, 0:128].rearrange("p (a g r) -> p a g r", a=16, g=2),
                     t["PTs"][:, 0:128].rearrange("p (a g r) -> p a g r", a=16, g=2),
                     bc(t["Msk"][:, s * 2:(s + 1) * 2].unsqueeze(1).unsqueeze(3), [128, 16, 2, 4]), ALU.mult,
                     r=["PTs", "Msk"], w=["PTs"])
                k.memset("dve", t["PTs"][0:1, 128:136], 0.0, w=["PTs"])
                k.red(t["PTsum"][:, 0:8], t["PTs"][:, 0:128].rearrange("p (a h) -> p h a", a=16), ALU.add, r=["PTs"], w=["PTsum"])
                k.red(t["PTsum"][:, 8:16], t["PTs"][:, 128:160].rearrange("p (a h) -> p h a", a=4), ALU.add, r=["PTs"], w=["PTsum"])
                k.cp("act", t["PTb"][:], t["PTs"][:], r=["PTs"], w=["PTb"])
                for tt_ in range(16):
                    k.mm(pOv[:, s, 1, :], t["Gb1"][:, tt_ * 128:(tt_ + 1) * 128], t["PTb"][:, tt_ * 8:(tt_ + 1) * 8],
                         start=(tt_ == 0), stop=False, r=["Gb1", "PTb"], w=["pO0"])
                k.mm(pOv[:, s, 1, :], t["vfb"][:, 0:128], Envb[:, 0, s, :], start=False, stop=True,
                     r=["vfb", "Enewb"], w=["pO0"])
                for tt_ in range(4):
                    k.mm(pOv[:, s, 2, :], t["Gwb1"][:, tt_ * 128:(tt_ + 1) * 128], t["PTb"][:, 128 + tt_ * 8:128 + (tt_ + 1) * 8],
                         start=(tt_ == 0), stop=False, r=["Gwb1", "PTb"], w=["pO0"])
                k.mm(pOv[:, s, 2, :], t["vfb"][:, 128:256], Envb[:, 1, s, :], start=False, stop=True,
                     r=["vfb", "Enewb"], w=["pO0"])
                for xx in range(2):
                    k.mm(pDv[:, s, 1 + xx, :], t["onesf"][:, :], t["PTsum"][:, xx * 8:(xx + 1) * 8], start=True, stop=False,
                         r=["onesf", "PTsum"], w=["pO1"])
                    k.mm(pDv[:, s, 1 + xx, :], t["onesf"][:, :], Env[:, xx, s, :], start=False, stop=True,
                         r=["onesf", "Enew"], w=["pO1"])
            k.P.add("dve", lambda e: e.reciprocal(out=t["rDall"], in_=pD[:, 0:384]), ["pO1"], ["B1a", "B1b"])
            k.tt("dve", t["OTn"], pO[:, 0:384], t["rDall"], ALU.mult, r=["pO0", "B1a", "B1b"], w=["B1a", "B1b"])
            k.cp("dve", t["OT1"], t["OTn"][64:128, :], r=["B1a", "B1b"], w=["B2"])
            gwv = t["gw0"][N, :].rearrange("p (h x) -> p x h", x=3)
            for xx in range(3):
                bank, bname = banks[xx % 2]
                for hh in range(8):
                    src = t["OTn"] if hh < 4 else t["OT1"]
                    sname = "B1a" if hh < 4 else "B2"
                    inap = src[0:64, :].rearrange("p (s c) -> p s c", s=NSMP)[:, :, xx * 8 + hh]
                    k.tr(bank[0:n, hh * 64:(hh + 1) * 64], inap, t["identf"][0:64, 0:64], r=[sname, "identf"], w=[bname])
                if xx == 0:
                    k.tt("dve", v3(t["o_a"][N, :], 8), v3(bank[N, :], 8), bc(gwv[:, xx, :].unsqueeze(2), [n, 8, 64]), ALU.mult,
                         r=[bname, "gw0"], w=["o_a"])
                else:
                    k.tt("dve", v3(t["tmpO"][N, :], 8), v3(bank[N, :], 8), bc(gwv[:, xx, :].unsqueeze(2), [n, 8, 64]), ALU.mult,
                         r=[bname, "gw0"], w=["B3"])
                    k.tt("pool", t["o_a"][N, :], t["o_a"][N, :], t["tmpO"][N, :], ALU.add, r=["o_a", "B3"], w=["o_a"])
            k.dma("sp", d["gqs"], t["gq"][N, :], r=["gq"], w=["d_gqs"])
            mod_pass(k, t, d, 128, t["cp8"], "cp8", True)
            P.emit(es)

        with contextlib.ExitStack() as s2:
            P = Prog(nc, "b")
            k = K(P)
            sb("Wout", [128, 8, D], BF16, s2); sb("WA", [128, 4, D], BF16, s2); sb("WB", [128, 4, D], BF16, s2)
            k.dma("pool", t["Wout"][:], d["wout"].rearrange("(kc p) n -> p kc n", p=128), w=["Wout"])
            for g in range(2):
                k.dma("pool", t["WA"][g * 64:(g + 1) * 64, :, :],
                      d["wbra"][g * 256:(g + 1) * 256, :].rearrange("(r dd) n -> dd r n", dd=64), w=["WA"])
            k.dma("pool", t["WB"][:], d["wbrb"].rearrange("(c p) n -> p c n", p=128), w=["WB"])
            k.dma("sp", t["B1"][0:NSMP, :], d["gqs"], w=["B1a", "B1b"])
            for _ in tail(k, t, NSMP, d["ys"], d["xs"], "0", gate=[(t["B1"][:, 0:512], "B1a"), (t["B1"][:, 512:1024], "B1b")]):
                pass
            sb("KsT", [128, S], BF16, s2); sb("KwT", [128, 5 * 128], BF16, s2)
            sb("Vs", [128, NT, 2, 65], BF16, s2); sb("Vw", [128, 5, 2, 65], BF16, s2)
            sb("QT", [128, 2, 512], BF16, s2); sb("vcb", [128, 128], BF16, s2)
            sb("pTk2", [128, 64], BF16, s2); sb("pTv2", [128, 64], BF16, s2); sb("KcT", [128, 64], BF16, s2)
            sb("Vc", [64, 2, 97], BF16, s2)
            sb("PT0", [128, 512], BF16, s2); sb("PT1", [128, 512], BF16, s2)
            sb("nsel4", [128, 2, 512], BF16, s2); sb("imp", [128, 64], F32, s2)
            sb("sel", [128, 64], BF16, s2); sb("m8", [128, 16], F32, s2); sb("mctneg", [128, 512], BF16, s2); sb("ibias", [128, 32], F32, s2)
            sb("tnle", [128, 512], BF16, s2); sb("tngt", [128, 512], BF16, s2)
            k.dma("pool", t["tnle"][:], d["trineg_le"], w=["tnle"])
            k.dma("pool", t["tngt"][:], d["trineg_gt"], w=["tngt"])
            n = 128
            N = slice(0, 128)
            k.memset("pool", t["Vs"][:], 1.0, w=["Vs"])
            k.memset("pool", t["Vw"][:], 1.0, w=["Vw"])
            k.memset("pool", t["Vc"][:], 1.0, w=["Vc"])
            k.memset("pool", t["pTk2"][:], 0.0, w=["pTk2"])
            k.memset("pool", t["pTv2"][:], 0.0, w=["pTv2"])
            k.memset("pool", t["KcT"][:], 0.0, w=["KcT"])
            k.memset("pool", t["QT"][:], 0.0, w=["QT"])
            k.memset("pool", t["nsel4"][:], 0.0, w=["nsel4"])
            k.dma("pool", t["mctneg"][:], d["cmpbase"], w=["mctneg"])
            for g in range(2):
                k.dma("pool", t["Vc"][:, g, 65:97], d["pair"], w=["Vc"])
            pS = [(t["pS0"], "pS0"), (t["pS1"], "pS1")]
            pO = [(t["pO0"], "pO0"), (t["pO1"], "pO1")]
            PT = [(t["PT0"], "PT0"), (t["PT1"], "PT1")]
            pM = t["pM"]
            pTv8 = t["pT"][:].rearrange("p (a b) -> p a b", a=8)
            cnt = {"s": 0, "p": 0}
            cur = {}
            sfx_of = lambda ii: str((ii + 1) % 2)

            def branch_finish(xx):
                for g in range(2):
                    bank, bname = pO[g]
                    ov = bank[:, 0:388].rearrange("p (r c) -> p r c", r=4)
                    k.ts("dve", t["rden"][:, g * 4:(g + 1) * 4], ov[:, :, 64], 1e-30, None, ALU.max, r=[bname], w=["rden"])
                k.P.add("dve", lambda e: e.reciprocal(out=t["rden"][:], in_=t["rden"][:]), ["rden"], ["rden"])
                k.tt("dve", t["cx"][:], t["rden"][:], cur["gwv"][:, xx, :], ALU.mult, r=["rden", cur["gwn"]], w=["cx"])
                for g in range(2):
                    bank, bname = pO[g]
                    ov = bank[:, 0:388].rearrange("p (r c) -> p r c", r=4)
                    for r in range(4):
                        hs = slice((g * 4 + r) * 64, (g * 4 + r + 1) * 64)
                        k.stt(t["o_a"][:, hs], ov[:, r, 0:64], t["cx"][:, g * 4 + r:g * 4 + r + 1], t["o_a"][:, hs], ALU.mult, ALU.add,
                              r=[bname, "cx", "o_a"], w=["o_a"])

            k.dma("sp", t["cs"][:], d["rope"][0:128, :], w=["cs"])
            for _ in front(k, t, 128, d["xp"][0:128, :], "cs", sfx_of(0), True):
                pass
            tgen = {"g": None}

            def tail_hook(nit):
                if tgen["g"] is not None:
                    next(tgen["g"], None)

            for i in range(NT):
                rows = slice(i * 128, (i + 1) * 128)
                sfx = sfx_of(i)
                cur["gwn"] = "gw" + sfx
                cur["gwv"] = t["gw" + sfx][:, :].rearrange("p (h x) -> p x h", x=3)
                k.dma("sp", d["pk_cmp"][rows, :], t["kf"][:, 0:128], r=["kf"], is_out=True)
                k.dma("sp", d["pk_slc"][rows, :], t["kf"][:, 128:256], r=["kf"], is_out=True)
                k.dma("sp", d["pv_cmp"][rows, :], t["vf"][:, 0:128], r=["vf"], is_out=True)
                k.dma("sp", d["pv_slc"][rows, :], t["vf"][:, 128:256], r=["vf"], is_out=True)
                if i >= NT - 4:
                    wr = slice((i - (NT - 4)) * 128, (i - (NT - 4) + 1) * 128)
                    k.dma("sp", d["pk_win"][wr, :], t["kf"][:, 256:384], r=["kf"], is_out=True)
                    k.dma("sp", d["pv_win"][wr, :], t["vf"][:, 256:384], r=["vf"], is_out=True)
                slot = i % 5
                k.cp("pool", t["Vs"][:, i, :, 0:64], v3(t["vf"][:, 128:256], 2), r=["vf"], w=["Vs"])
                k.cp("pool", t["Vw"][:, slot, :, 0:64], v3(t["vf"][:, 256:384], 2), r=["vf"], w=["Vw"])
                k.cp("pool", t["vcb"][:], t["vf"][:, 0:128], r=["vf"], w=["vcb"])
                for r in range(4):
                    k.tr(pTv8[:, r, :], t["q_bf"][:, r * 128:(r + 1) * 128], t["identb"][:], r=["q_bf", "identb"], w=["pT"])
                k.tr(pTv8[:, 4, :], t["k_bf"][:, 128:256], t["identb"][:], r=["k_bf", "identb"], w=["pT"])
                k.tr(pTv8[:, 5, :], t["k_bf"][:, 256:384], t["identb"][:], r=["k_bf", "identb"], w=["pT"])
                k.cp("act", t["QT"][0:64, 0, :].rearrange("p (r q) -> p r q", r=4), pTv8[0:64, 0:4, :], r=["pT"], w=["QT"])
                k.cp("act", t["QT"][64:128, 1, :].rearrange("p (r q) -> p r q", r=4), pTv8[64:128, 0:4, :], r=["pT"], w=["QT"])
                k.cp("act", t["KsT"][:, rows], pTv8[:, 4, :], r=["pT"], w=["KsT"])
                k.cp("act", t["KwT"][:, slot * 128:(slot + 1) * 128], pTv8[:, 5, :], r=["pT"], w=["KwT"])
                cc = slice(4 * i, 4 * i + 4)
                k.mm(pM[:, 0:4], t["k_bf"][:, 0:128], t["pool4"][:], r=["k_bf", "pool4"], w=["pM"])
                k.mm(pM[:, 4:8], t["vcb"][:], t["pool4"][:], r=["vcb", "pool4"], w=["pM"])
                k.ts("dve", t["pTk2"][:, cc], pM[:, 0:4], t["pebar"][:, 0:1], None, ALU.add, r=["pM", "pebar"], w=["pTk2"])
                k.ts("dve", t["pTv2"][:, cc], pM[:, 4:8], t["pebar"][:, 1:2], None, ALU.add, r=["pM", "pebar"], w=["pTv2"])
                k.mm(pM[:, 8:12], t["Wbdk"][:], t["pTk2"][:, cc], r=["Wbdk", "pTk2"], w=["pM"])
                k.mm(pM[0:64, 16:144], t["pTv2"][:], t["Wbdv"][:], r=["Wbdv", "pTv2"], w=["pM"])
                k.cp("act", t["KcT"][:, cc], pM[:, 8:12], r=["pM"], w=["KcT"])
                k.cp("act", t["Vc"][:, :, 0:64], v3(pM[0:64, 16:144], 2), r=["pM"], w=["Vc"])

                k.dma("sp", t["ibias"][:], d["impbias"][i], w=["ibias"])
                gen = None
                if i + 1 < NT:
                    nrows = slice((i + 1) * 128, (i + 2) * 128)
                    k.dma("sp", t["cs"][:], d["rope"][nrows, :], w=["cs"])
                    gen = front(k, t, 128, d["xp"][nrows, :], "cs", sfx_of(i + 1), True)
                    next(gen, None)
                qt = {g: t["QT"][:, g, :] for g in range(2)}

                def run_branch(items, hook=None):
                    recs = []

                    def s_stage(it):
                        g, kp, mms, v_ap, v_name, first, last = it
                        sbank, sname = pS[cnt["s"] % 2]; cnt["s"] += 1
                        pt, pname = PT[cnt["p"] % 2]; cnt["p"] += 1
                        for mi, (lh, rh, nm) in enumerate(mms):
                            k.mm(sbank[0:kp, :], lh, rh, start=(mi == 0), stop=(mi == len(mms) - 1), r=nm, w=[sname])
                        recs.append((sbank, sname, pt, pname))

                    def e_stage(idx):
                        g, kp, mms, v_ap, v_name, first, last = items[idx]
                        sbank, sname, pt, pname = recs[idx]
                        k.act(pt[0:kp, :], sbank[0:kp, :], AF.Exp, r=[sname], w=[pname], scale=SCL)
                        obank, oname = pO[g]
                        ov = obank[:, 0:388].rearrange("p (r c) -> p r c", r=4)
                        nv = v_ap.shape[-1]
                        for r in range(4):
                            k.mm(ov[:, r, 0:nv], pt[0:kp, r * 128:(r + 1) * 128], v_ap, start=(first and r == 0), stop=(last and r == 3),
                                 r=[pname, v_name], w=[oname])

                    s_stage(items[0])
                    for idx in range(len(items)):
                        if idx + 1 < len(items):
                            s_stage(items[idx + 1])
                        e_stage(idx)
                        if hook is not None:
                            hook(len(items))

                items = []
                for g in range(2):
                    gs = slice(g * 64, (g + 1) * 64)
                    items.append((g, 64, [(t["KcT"][:, :], qt[g], ["KcT", "QT"]),
                                          (t["identb"][:, 64 - 4 * i:128 - 4 * i], t["mctneg"][:, :], ["identb", "mctneg"])],
                                  t["Vc"][:, g, :], "Vc", True, True))
                run_branch(items)
                for g in range(2):
                    bank, bname = pO[g]
                    ov = bank[:, 0:388].rearrange("p (r c) -> p r c", r=4)
                    k.ts("dve", t["rden"][:, g * 4:(g + 1) * 4], ov[:, :, 64], 1e-30, None, ALU.max, r=[bname], w=["rden"])
                k.P.add("dve", lambda e: e.reciprocal(out=t["rden"][:], in_=t["rden"][:]), ["rden"], ["rden"])
                for g in range(2):
                    bank, bname = pO[g]
                    ov = bank[:, 0:388].rearrange("p (r c) -> p r c", r=4)
                    ig = t["imp"][:, g * 32:(g + 1) * 32]
                    k.stt(ig, ov[:, 0, 65:97], t["rden"][:, g * 4:g * 4 + 1], t["ibias"][:], ALU.mult, ALU.add,
                          r=[bname, "rden", "ibias"], w=["imp"])
                    for r in range(1, 4):
                        k.stt(ig, ov[:, r, 65:97], t["rden"][:, g * 4 + r:g * 4 + r + 1], ig, ALU.mult, ALU.add,
                              r=[bname, "rden", "imp"], w=["imp"])
                k.tt("dve", t["cx"][:], t["rden"][:], cur["gwv"][:, 0, :], ALU.mult, r=["rden", cur["gwn"]], w=["cx"])
                for g in range(2):
                    bank, bname = pO[g]
                    ov = bank[:, 0:388].rearrange("p (r c) -> p r c", r=4)
                    k.tt("dve", v3(t["o_a"][:, g * 256:(g + 1) * 256], 4), ov[:, :, 0:64],
                         bc(t["cx"][:, g * 4:(g + 1) * 4].unsqueeze(2), [128, 4, 64]), ALU.mult, r=[bname, "cx"], w=["o_a"])
                for g in range(2):
                    ig = t["imp"][:, g * 32:(g + 1) * 32]
                    k.P.add("dve", (lambda g: lambda e: e.max(out=t["m8"][:, g * 8:(g + 1) * 8], in_=t["imp"][:, g * 32:(g + 1) * 32]))(g),
                            ["imp"], ["m8"])
                    k.ts("dve", t["sel"][:, g * 32:(g + 1) * 32], ig, t["m8"][:, g * 8 + 7:g * 8 + 8], None, ALU.is_ge,
                         r=["imp", "m8"], w=["sel"])
                j0 = max(0, i - 4)
                items = []
                for g in range(2):
                    gs = slice(g * 64, (g + 1) * 64)
                    for j in range(j0, i + 1):
                        sl = j % 5
                        mms = [(t["KwT"][:, sl * 128:(sl + 1) * 128], qt[g], ["KwT", "QT"])]
                        if j == i:
                            mms.append((t["identb"][:, :], t["tnle"][:, :], ["identb", "tnle"]))
                        elif j == i - 4:
                            mms.append((t["identb"][:, :], t["tngt"][:, :], ["identb", "tngt"]))
                        items.append((g, 128, mms, t["Vw"][:, sl, g, :], "Vw", j == j0, j == i))
                run_branch(items, tail_hook)
                if tgen["g"] is not None:
                    for _ in tgen["g"]:
                        pass
                    tgen["g"] = None
                branch_finish(2)
                k.tr(t["pT"][0:64, 0:128], t["sel"][:], t["identb"][:], r=["sel", "identb"], w=["pT"])
                for g in range(2):
                    g32 = slice(g * 32, (g + 1) * 32)
                    k.ts("dve", v3(t["nsel4"][g32, g, :], 4), bc(t["pT"][g32, 0:128].unsqueeze(1), [32, 4, 128]), -1.0, 30000.0,
                         ALU.add, ALU.mult, r=["pT"], w=["nsel4"])
                items = []
                for g in range(2):
                    gs = slice(g * 64, (g + 1) * 64)
                    g32 = slice(g * 32, (g + 1) * 32)
                    for j in range(i + 1):
                        mms = [(t["KsT"][:, j * 128:(j + 1) * 128], qt[g], ["KsT", "QT"]),
                               (t["e2"][:, j * 128:(j + 1) * 128], t["nsel4"][:, g, :], ["e2", "nsel4"])]
                        if j == i:
                            mms.append((t["identb"][:, :], t["tnle"][:, :], ["identb", "tnle"]))
                        items.append((g, 128, mms, t["Vs"][:, j, g, :], "Vs", j == 0, j == i))
                acc = {"a": 0.0}

                def hook(nit):
                    if gen is None:
                        return
                    acc["a"] += 14.0 / nit
                    while acc["a"] >= 1.0:
                        acc["a"] -= 1.0
                        next(gen, None)

                run_branch(items, hook)
                if gen is not None:
                    for _ in gen:
                        pass
                branch_finish(1)
                tgen["g"] = tail(k, t, 128, d["yp"][rows, :], d["xp"][rows, :], sfx)
                next(tgen["g"], None)
            for _ in tgen["g"]:
                pass
            P.emit(es)
    return nc


def _consts():
    c = {}
    half = 32
    inv = (10000.0 ** (-np.arange(half, dtype=np.float32) * 2.0 / 64)).astype(np.float32)
    ang = np.arange(S + 1, dtype=np.float32)[:, None] * inv[None, :]
    c["rope"] = np.concatenate([np.cos(ang), np.sin(ang)], axis=1).astype(np.float32)
    kk = np.arange(128)[:, None]
    qq = np.arange(128)[None, :]
    c["tri_le"] = (kk <= qq).astype(np.float32)
    c["tri_gt"] = (kk > qq).astype(np.float32)
    c["identf_c"] = np.eye(128, dtype=np.float32)
    c["pool4"] = ((np.arange(128)[:, None] // 32) == np.arange(4)[None, :]).astype(np.float32) / 32.0
    c["pair"] = ((np.arange(64)[:, None] // 2) == np.arange(32)[None, :]).astype(np.float32)
    e = ((np.arange(2048)[None, :] // 64) == np.arange(32)[:, None]).astype(np.float32)
    c["e2"] = np.concatenate([e, e], axis=0)
    m = np.zeros((NT, 64, 128), np.float32)
    ib = np.zeros((NT, 128, 32), np.float32)
    for i in range(NT):
        pos = i * 128 + np.arange(128)
        cend = (np.arange(64) + 1) * 32 - 1
        m[i] = (cend[:, None] <= pos[None, :]).astype(np.float32)
        qblk = pos // 64
        blk = np.arange(32)
        forced = (blk[None, :] == 0) | (blk[None, :] == qblk[:, None])
        causal = blk[None, :] <= qblk[:, None]
        ib[i] = np.where(causal, 1.0e4 * forced, -1.0e30).astype(np.float32)
    base = np.zeros((128, 128), np.float32)
    base[68:, :] = -30000.0
    for u in range(64, 68):
        base[u, :] = np.where(np.arange(128) < 32 * (u - 64) + 31, -30000.0, 0.0)
    c["cmpbase"] = np.ascontiguousarray(np.tile(base[:, None, :], (1, 4, 1)).reshape(128, 512)).astype(np.float32)
    c["trineg_le"] = np.ascontiguousarray(np.tile(((1.0 - c["tri_le"]) * -30000.0)[:, None, :], (1, 4, 1)).reshape(128, 512)).astype(np.float32)
    c["trineg_gt"] = np.ascontiguousarray(np.tile(((1.0 - c["tri_gt"]) * -30000.0)[:, None, :], (1, 4, 1)).reshape(128, 512)).astype(np.float32)
    c["impbias"] = ib
    c["pool2"] = ((np.arange(128)[:, None] // 2) == np.arange(64)[None, :]).astype(np.float32) / 32.0
    c["e4"] = ((np.arange(128)[None, :] // 4) == np.arange(32)[:, None]).astype(np.float32)
    c["rmod8"] = (np.arange(128) % 8).astype(np.float32).reshape(128, 1)
    return c


_NC_CACHE = {}


def kernel(x_prompt, x_sample, cache_k_cmp, cache_v_cmp, cache_k_slc, cache_v_slc,
           cache_k_win, cache_v_win, page_table, c_prompt, c_sample,
           w_ada, b_ada, norm_g, w_in, q_norm_g, k_norm_g, cmp_pos_k, cmp_pos_v,
           w_cmp_k, w_cmp_v, vnorm_g, vnorm_b, w_s, b_s, w_br_a, w_br_b, w_out):
    f = lambda a: np.ascontiguousarray(np.asarray(a), dtype=np.float32)
    if "nc" not in _NC_CACHE:
        _NC_CACHE["nc"] = build_program()
    nc = _NC_CACHE["nc"]
    consts = _consts()
    pools = {nm: f(a).reshape(2560 * 8, 2048) for nm, a in
             (("kcmp", cache_k_cmp), ("vcmp", cache_v_cmp), ("kslc", cache_k_slc), ("vslc", cache_v_slc))}
    shared = dict(
        w_ada=f(w_ada)[0], b_ada=f(b_ada)[0].reshape(1, -1), norm_g=f(norm_g)[0].reshape(1, -1), w_in=f(w_in)[0],
        qg=f(q_norm_g)[0].reshape(1, -1), kg=f(k_norm_g)[0].reshape(1, -1), pek=f(cmp_pos_k)[0], pev=f(cmp_pos_v)[0],
        wck=f(w_cmp_k)[0], wcv=f(w_cmp_v)[0], vng=f(vnorm_g)[0].reshape(1, -1), vnb=f(vnorm_b)[0].reshape(1, -1),
        ws=f(w_s)[0], bs=f(b_s)[0], wbra=f(w_br_a)[0], wbrb=f(w_br_b)[0], wout=f(w_out)[0])
    shared.update(pools)
    shared.update(consts)
    xp, xs = f(x_prompt), f(x_sample)
    kw, vw = f(cache_k_win)[0], f(cache_v_win)[0]
    pt = np.asarray(page_table).astype(np.int32)
    cp, cs = f(c_prompt), f(c_sample)
    in_maps = []
    for c in range(8):
        sl = slice(c * NSMP, (c + 1) * NSMP)
        m = dict(shared)
        m["xp"] = xp[c]
        m["xs"] = np.ascontiguousarray(xs[sl, 0, :])
        m["cpv"] = np.ascontiguousarray(cp[c])
        m["csv"] = np.ascontiguousarray(cs[sl])
        m["kwin"] = np.ascontiguousarray(kw[sl].reshape(NSMP, 512, 128))
        m["vwin"] = np.ascontiguousarray(vw[sl].reshape(NSMP, 512, 128))
        m["ptrep"] = np.ascontiguousarray(np.repeat(pt[sl].T, 8, axis=0))
        in_maps.append(m)
    res = run_bass_kernel_spmd(nc, in_maps, core_ids=list(range(8)))
    R = res.results
    cat = lambda nm: np.stack([R[c][nm] for c in range(8)], axis=0)
    y_p = cat("yp")
    y_s = np.concatenate([R[c]["ys"] for c in range(8)], axis=0).reshape(128, 1, D)
    outs = [y_p, y_s]
    for nm in ("pk_cmp", "pv_cmp", "pk_slc", "pv_slc"):
        outs.append(cat(nm).reshape(1, 8, S, 2, 64))
    for nm in ("pk_win", "pv_win"):
        outs.append(cat(nm).reshape(1, 8, 512, 2, 64))
    for nm in ("sk_cmp", "sv_cmp", "sk_slc", "sv_slc"):
        outs.append(np.concatenate([R[c][nm] for c in range(8)], axis=0).reshape(1, 128, 1, 2, 64))
    for nm in ("sk_win", "sv_win"):
        outs.append(np.concatenate([R[c][nm] for c in range(8)], axis=0).reshape(1, 128, 512, 2, 64))
    outs.append(np.concatenate([R[c]["svch"] for c in range(8)], axis=0).reshape(1, 128, 1, 512))
    return tuple(o.astype(np.float32) for o in outs)
```

```python
import contextlib
import numpy as np
import concourse.bass as bass
import concourse.mybir as mybir
from concourse.bass_utils import run_bass_kernel_spmd

F32 = mybir.dt.float32
BF16 = mybir.dt.bfloat16
I32 = mybir.dt.int32
ALU = mybir.AluOpType
AF = mybir.ActivationFunctionType
AX = mybir.AxisListType

D = 1024
DIN = 5400
S = 2048
NT = 16
NSMP = 16
EPS = 1e-6
SCL = 0.125
C_Q, C_K, C_V, C_NSA, C_ZA, C_U, C_VV, C_ZB, C_GA, C_GB = 0, 512, 896, 1280, 1304, 1816, 2328, 2840, 3352, 4376


class _Op:
    __slots__ = ("eng", "fn", "deps", "idx", "signal", "is_dma", "sem", "target", "count")


class Prog:
    ENGS = ("pe", "act", "dve", "pool", "sp")
    DMA_POOL = {"sp": 16, "pool": 12, "act": 2}

    def __init__(self, nc, tag):
        self.nc = nc
        self.tag = tag
        self.q = {e: [] for e in self.ENGS}
        self.last_w = {}
        self.readers = {}
        self.dma_n = {e: 0 for e in self.DMA_POOL}
        self.out_dmas = []

    def add(self, eng, fn, r=(), w=(), dma=False, out=False):
        op = _Op()
        op.eng, op.fn, op.is_dma, op.signal = eng, fn, dma, False
        op.deps = set()
        op.sem = None
        op.target = 0
        op.count = 0
        for b in r:
            lw = self.last_w.get(b)
            if lw is not None:
                op.deps.add(lw)
        for b in w:
            lw = self.last_w.get(b)
            if lw is not None:
                op.deps.add(lw)
            for rd in self.readers.get(b, ()):
                op.deps.add(rd)
        for b in r:
            self.readers.setdefault(b, []).append(op)
        for b in w:
            self.last_w[b] = op
            self.readers[b] = []
        op.deps.discard(op)
        op.idx = len(self.q[eng])
        self.q[eng].append(op)
        if dma:
            j = self.dma_n[eng]
            self.dma_n[eng] += 1
            op.sem = (eng, j % self.DMA_POOL[eng])
            op.target = 16 * (j // self.DMA_POOL[eng] + 1)
            if out:
                self.out_dmas.append(op)
        return op

    def _needs_wait(self, op, dep):
        if dep.is_dma:
            return True
        if dep.eng == "pe" and op.eng == "pe" and not op.is_dma:
            return False
        return True

    def emit(self, es):
        nc = self.nc
        fin = self.add("sp", None)
        for o in self.out_dmas:
            fin.deps.add(o)
        for e in self.DMA_POOL:
            for op in self.q[e]:
                if op.is_dma:
                    fin.deps.add(op)
        for e in self.ENGS:
            for op in self.q[e]:
                for d in op.deps:
                    if (not d.is_dma) and self._needs_wait(op, d):
                        d.signal = True
        for e in self.ENGS:
            c = 0
            for op in self.q[e]:
                if (not op.is_dma) and op.signal:
                    c += 1
                    op.count = c
        esem = {e: es.enter_context(nc.semaphore("s%s_%s" % (self.tag, e))) for e in ("pe", "act", "dve", "pool")}
        dsem = {}
        for e, n in self.DMA_POOL.items():
            for i in range(n):
                dsem[(e, i)] = es.enter_context(nc.semaphore("d%s_%s_%d" % (self.tag, e, i)))
        prog = self
        with nc.Block() as block:
            def run_queue(ename, eng):
                waited = {}
                for op in prog.q[ename]:
                    waits = {}
                    for d in op.deps:
                        if not prog._needs_wait(op, d):
                            continue
                        if d.is_dma:
                            key, val = ("d", d.sem), d.target
                        else:
                            key, val = ("e", d.eng), d.count
                        if waits.get(key, 0) < val:
                            waits[key] = val
                    if op.is_dma and op.target > 16:
                        key = ("d", op.sem)
                        if waits.get(key, 0) < op.target - 16:
                            waits[key] = op.target - 16
                    for key, val in waits.items():
                        if waited.get(key, 0) >= val:
                            continue
                        waited[key] = val
                        s = dsem[key[1]] if key[0] == "d" else esem[key[1]]
                        eng.wait_ge(s, val)
                    if op.fn is None:
                        continue
                    ins = op.fn(eng)
                    if op.is_dma:
                        ins.then_inc(dsem[op.sem], 16)
                    elif op.signal:
                        ins.then_inc(esem[ename], 1)

            @block.sync
            def _(eng):
                run_queue("sp", eng)

            @block.tensor
            def _(eng):
                run_queue("pe", eng)

            @block.scalar
            def _(eng):
                run_queue("act", eng)

            @block.vector
            def _(eng):
                run_queue("dve", eng)

            @block.gpsimd
            def _(eng):
                run_queue("pool", eng)


class K:
    def __init__(self, P):
        self.P = P

    def mm(self, out, lhsT, rhs, start=True, stop=True, r=(), w=()):
        self.P.add("pe", lambda e: e.matmul(out, lhsT=lhsT, rhs=rhs, start=start, stop=stop, skip_group_check=True), r, w)

    def tr(self, out, in_, ident, r=(), w=()):
        self.P.add("pe", lambda e: e.transpose(out=out, in_=in_, identity=ident), r, w)

    def act(self, out, in_, func, r=(), w=(), scale=1.0, accum=None):
        if accum is None:
            self.P.add("act", lambda e: e.activation(out=out, in_=in_, func=func, scale=scale), r, w)
        else:
            self.P.add("act", lambda e: e.activation(out=out, in_=in_, func=func, scale=scale, accum_out=accum), r, w)

    def tt(self, eng, out, in0, in1, op, r=(), w=()):
        self.P.add(eng, lambda e: e.tensor_tensor(out=out, in0=in0, in1=in1, op=op), r, w)

    def ts(self, eng, out, in0, s1, s2, op0, op1=None, r=(), w=()):
        if op1 is None:
            self.P.add(eng, lambda e: e.tensor_scalar(out=out, in0=in0, scalar1=s1, scalar2=None, op0=op0), r, w)
        else:
            self.P.add(eng, lambda e: e.tensor_scalar(out=out, in0=in0, scalar1=s1, scalar2=s2, op0=op0, op1=op1), r, w)

    def stt(self, out, in0, scalar, in1, op0, op1, r=(), w=()):
        self.P.add("dve", lambda e: e.scalar_tensor_tensor(out=out, in0=in0, scalar=scalar, in1=in1, op0=op0, op1=op1), r, w)

    def cp(self, eng, out, in_, r=(), w=()):
        if eng == "act":
            self.P.add("act", lambda e: e.activation(out=out, in_=in_, func=AF.Copy), r, w)
        else:
            self.P.add(eng, lambda e: e.tensor_copy(out=out, in_=in_), r, w)

    def red(self, out, in_, op, r=(), w=()):
        self.P.add("dve", lambda e: e.tensor_reduce(out=out, in_=in_, axis=AX.X, op=op), r, w)

    def memset(self, eng, ap, val, w=()):
        self.P.add(eng, lambda e: e.memset(ap, val), (), w)

    def dma(self, q, out, in_, r=(), w=(), is_out=False, slow=False):
        if slow:
            self.P.add(q, lambda e: e.dma_start(out=out, in_=in_, allow_slow_non_contiguous=True), r, w, dma=True, out=is_out)
        else:
            self.P.add(q, lambda e: e.dma_start(out=out, in_=in_), r, w, dma=True, out=is_out)

    def gather(self, out, in_, idx, r=(), w=()):
        self.P.add("pool", lambda e: e.indirect_dma_start(out=out, out_offset=None, in_=in_,
                                                          in_offset=bass.IndirectOffsetOnAxis(ap=idx, axis=0)),
                   r, w, dma=True)


def v3(ap, a):
    return ap.rearrange("p (a b) -> p a b", a=a)


def bc(ap, shape):
    return ap.to_broadcast(shape)


def win_names(c0, n):
    return ["win%d" % j for j in range(c0 // 512, (c0 + n - 1) // 512 + 1)]


def front(k, t, n, x_src, cs_name, sfx, is_prompt):
    P = k.P
    N = slice(0, n)
    x, B1, B2, B3, h, hT = t["x"], t["B1"], t["B2"], t["B3"], t["h"], t["hT"]
    TH, ZB, vn_bf = B1[:, 0:512], B1[:, 512:1024], h[:, 0:512]
    tg, za_s, bmix, gw = t["tg" + sfx], t["za_s" + sfx], t["bmix" + sfx], t["gw" + sfx]
    n_tg, n_za, n_bm, n_gw = "tg" + sfx, "za_s" + sfx, "bmix" + sfx, "gw" + sfx
    B1W = ["B1a", "B1b"]
    st, st2, st3 = t["st"], t["st2"], t["st3"]
    pA, pB, pT = t["pA"], t["pB"], t["pT"]
    pTv = pT[:].rearrange("p (a b) -> p a b", a=8)
    k.dma("sp", x[N, :], x_src, w=["x"])
    k.act(B1[N, :], x[N, :], AF.Square, r=["x"], w=B1W + ["st"], accum=st[N, 0:1])
    k.ts("dve", st[N, 1:2], st[N, 0:1], 1.0 / D, EPS, ALU.mult, ALU.add, r=["st"], w=["st"])
    k.tt("pool", st[N, 2:3], st[N, 1:2], t["nh"][N, 0:1], ALU.pow, r=["st", "nh"], w=["st"])
    k.stt(B1[N, :], x[N, :], st[N, 2:3], t["sc1"][N, :], ALU.mult, ALU.mult, r=["x", "st", "sc1"], w=B1W)
    k.tt("dve", h[N, :], B1[N, :], t["sh"][N, :], ALU.add, r=B1W + ["sh"], w=["h"])
    yield
    for kc in range(8):
        k.tr(pTv[:, kc, 0:n], h[N, kc * 128:(kc + 1) * 128], t["identb"][N, N], r=["h", "identb"], w=["pT"])
    k.cp("act", hT[:, :, 0:n], pTv[:, :, 0:n], r=["pT"], w=["hT"])
    yield

    def proj(bank, bname, dst, c0, ncol):
        for kc in range(8):
            k.mm(bank[N, dst:dst + ncol], hT[:, kc, 0:n], t["Win"][:, kc, c0:c0 + ncol], start=(kc == 0), stop=(kc == 7),
                 r=["hT"] + win_names(c0, ncol), w=[bname])

    def normrope(bank, bname, A, Bd, gain, dst_ap, dst_name):
        H = A * Bd
        HW = H * 64

        def v4(ap):
            return ap.rearrange("p (a b d) -> p a b d", a=A, b=Bd)

        k.cp("act", B2[N, 0:HW], bank[N, 0:HW], r=[bname], w=["B2"])
        k.act(B3[N, 0:HW], B2[N, 0:HW], AF.Square, r=["B2"], w=["B3"])
        k.red(st2[N, 0:H], v3(B3[N, 0:HW], H), ALU.add, r=["B3"], w=["st2"])
        k.ts("dve", st2[N, 8:8 + H], st2[N, 0:H], 1.0 / 64, EPS, ALU.mult, ALU.add, r=["st2"], w=["st2"])
        k.tt("pool", st2[N, 16:16 + H], st2[N, 8:8 + H], t["nh"][N, 0:H], ALU.pow, r=["st2", "nh"], w=["st2"])
        b2 = v4(B2[N, 0:HW])
        rs = st2[N, 16:16 + H].rearrange("p (a b) -> p a b", a=A).unsqueeze(3)
        k.tt("dve", b2, b2, bc(rs, [n, A, Bd, 64]), ALU.mult, r=["B2", "st2"], w=["B2"])
        k.tt("dve", b2, b2, bc(gain[N, :].unsqueeze(1).unsqueeze(1), [n, A, Bd, 64]), ALU.mult, r=["B2", "gains"], w=["B2"])
        cs = t["cs"]
        cosb = bc(cs[N, 0:32].unsqueeze(1).unsqueeze(1), [n, A, Bd, 32])
        sinb = bc(cs[N, 32:64].unsqueeze(1).unsqueeze(1), [n, A, Bd, 32])
        x1, x2 = b2[:, :, :, 0:32], b2[:, :, :, 32:64]
        r1 = t["R1"][N, 0:H * 32].rearrange("p (a b d) -> p a b d", a=A, b=Bd)
        r2 = t["R2"][N, 0:H * 32].rearrange("p (a b d) -> p a b d", a=A, b=Bd)
        k.tt("dve", r1, x1, cosb, ALU.mult, r=["B2", cs_name], w=["R1"])
        k.tt("dve", r2, x2, sinb, ALU.mult, r=["B2", cs_name], w=["R2"])
        k.tt("dve", dst_ap[:, :, :, 0:32], r1, r2, ALU.subtract, r=["R1", "R2"], w=[dst_name])
        k.tt("dve", r1, x2, cosb, ALU.mult, r=["B2", cs_name], w=["R1"])
        k.tt("dve", r2, x1, sinb, ALU.mult, r=["B2", cs_name], w=["R2"])
        k.tt("dve", dst_ap[:, :, :, 32:64], r1, r2, ALU.add, r=["R1", "R2"], w=[dst_name])

    qdst = t["q_bf"][N, :].rearrange("p (r g d) -> p g r d", r=4, g=2)
    stages = []

    def st_q_post():
        normrope(pA, "pA", 2, 4, t["qg"], qdst, "q_bf")
    stages.append((lambda: proj(pA, "pA", 0, C_Q, 512), st_q_post))

    def st_k_post():
        normrope(pB, "pB", 6, 1, t["kg"], t["kf"][N, :].rearrange("p (a b d) -> p a b d", a=6, b=1), "kf")
        k.cp("act", t["k_bf"][N, :], t["kf"][N, :], r=["kf"], w=["k_bf"])
    stages.append((lambda: proj(pB, "pB", 0, C_K, 384), st_k_post))

    def st_v_pe():
        proj(pA, "pA", 0, C_V, 384)
        proj(pA, "pA", 384, C_NSA, 24)

    def st_v_post():
        k.cp("act", t["vf"][N, :], pA[N, 0:384], r=["pA"], w=["vf"])
        k.act(t["gwt"][N, :], pA[N, 384:408], AF.Tanh, r=["pA"], w=["gwt"], scale=0.5)
        k.ts("dve", gw[N, :], t["gwt"][N, :], 0.5, 0.5, ALU.mult, ALU.add, r=["gwt"], w=[n_gw])
    stages.append((st_v_pe, st_v_post))

    def st_za_post():
        k.act(TH[N, :], pB[N, :], AF.Tanh, r=["pB"], w=["B1a"], scale=0.5)
        k.stt(za_s[N, :], TH[N, :], 1.0, pB[N, :], ALU.add, ALU.mult, r=["B1a", "pB"], w=[n_za])
    stages.append((lambda: proj(pB, "pB", 0, C_ZA, 512), st_za_post))

    def st_ln_post():
        k.cp("act", t["vn_f"][N, :], pA[N, :], r=["pA"], w=["vn_f"])
        P.add("dve", lambda e: e.bn_stats(out=st3[N, 0:6], in_=t["vn_f"][N, :]), ["vn_f"], ["st3"])
        P.add("dve", lambda e: e.bn_aggr(out=st3[N, 6:8], in_=st3[N, 0:6]), ["st3"], ["st3"])
        k.ts("dve", st3[N, 8:9], st3[N, 7:8], EPS, None, ALU.add, r=["st3"], w=["st3"])
        k.tt("pool", st3[N, 9:10], st3[N, 8:9], t["nh"][N, 0:1], ALU.pow, r=["st3", "nh"], w=["st3"])
        k.ts("dve", t["vn_f"][N, :], t["vn_f"][N, :], st3[N, 6:7], st3[N, 9:10], ALU.subtract, ALU.mult, r=["vn_f", "st3"], w=["vn_f"])
        k.tt("dve", t["vn_f"][N, :], t["vn_f"][N, :], t["vng"][N, :], ALU.mult, r=["vn_f", "gains"], w=["vn_f"])
        k.tt("dve", t["vn_f"][N, :], t["vn_f"][N, :], t["vnb"][N, :], ALU.add, r=["vn_f", "gains"], w=["vn_f"])
    stages.append((lambda: proj(pA, "pA", 0, C_VV, 512), st_ln_post))

    def st_zb_post():
        k.act(TH[N, :], pB[N, :], AF.Tanh, r=["pB"], w=["B1a"], scale=0.5)
        k.stt(ZB[N, :], TH[N, :], 1.0, pB[N, :], ALU.add, ALU.mult, r=["B1a", "pB"], w=["B1b"])
    stages.append((lambda: proj(pB, "pB", 0, C_ZB, 512), st_zb_post))

    def st_u_post():
        k.tt("dve", ZB[N, :], pA[N, :], ZB[N, :], ALU.mult, r=["pA", "B1b"], w=["B1b"])
    stages.append((lambda: proj(pA, "pA", 0, C_U, 512), st_u_post))

    def st_sp_pe():
        if is_prompt:
            k.cp("act", vn_bf[N, :], t["vn_f"][N, :], r=["vn_f"], w=["h"])
            for g in range(4):
                k.mm(pB[N, g * 128:(g + 1) * 128], t["wsT"][:, g, :], vn_bf[N, g * 128:(g + 1) * 128],
                     r=["wsT", "h"], w=["pB"])

    def st_sp_post():
        if is_prompt:
            k.tt("dve", v3(B2[N, :], 4), v3(pB[N, :], 4), bc(t["bsT"][N, :].unsqueeze(2), [n, 4, 128]), ALU.add,
                 r=["pB", "bsT"], w=["B2"])
        else:
            k.tt("dve", v3(B2[N, :], 4), v3(t["vn_f"][N, :], 4), bc(t["w00"][N, :].unsqueeze(2), [n, 4, 128]), ALU.mult,
                 r=["vn_f", "w00"], w=["B2"])
            k.tt("dve", v3(B2[N, :], 4), v3(B2[N, :], 4), bc(t["b0"][N, :].unsqueeze(2), [n, 4, 128]), ALU.add,
                 r=["B2", "w00"], w=["B2"])
        k.tt("dve", bmix[N, :], B2[N, :], ZB[N, :], ALU.mult, r=["B2", "B1b"], w=[n_bm])
    stages.append((st_sp_pe, st_sp_post))

    gbanks = [(pA, "pA"), (pB, "pB")]
    for j in range(4):
        bank, bname = gbanks[j % 2]
        stages.append(((lambda bank=bank, bname=bname, j=j: proj(bank, bname, 0, C_GA + j * 512, 512)),
                       (lambda bank=bank, bname=bname, j=j: k.act(tg[N, j * 512:(j + 1) * 512], bank[N, :], AF.Tanh,
                                                                  r=[bname], w=[n_tg], scale=0.5))))
    stages[0][0]()
    yield
    for si in range(len(stages)):
        if si + 1 < len(stages):
            stages[si + 1][0]()
        stages[si][1]()
        yield


def tail(k, t, n, y_dst, x_src, sfx, gate=None):
    N = slice(0, n)
    pA, pB, pT = t["pA"], t["pB"], t["pT"]
    pTv8 = t["pM"][:].bitcast(BF16).rearrange("p (a b) -> p a b", a=8)
    tg, za_s, bmix = t["tg" + sfx], t["za_s" + sfx], t["bmix" + sfx]
    n_tg, n_za, n_bm = "tg" + sfx, "za_s" + sfx, "bmix" + sfx
    mc = t["mc"]
    B1W = ["B1a", "B1b"]
    ozd = t["oz"][N, :].rearrange("p (r g d) -> p g r d", r=4, g=2)
    k.tt("dve", ozd, t["o_a"][N, :].rearrange("p (g r d) -> p g r d", g=2, r=4),
         za_s[N, :].rearrange("p (g r d) -> p g r d", g=2, r=4), ALU.mult, r=["o_a", n_za], w=["oz"])
    for r in range(4):
        k.tr(pTv8[:, r, 0:n], t["oz"][N, r * 128:(r + 1) * 128], t["identb"][N, N], r=["oz", "identb"], w=["pM"])
    for c in range(4):
        k.tr(pTv8[:, 4 + c, 0:n], bmix[N, c * 128:(c + 1) * 128], t["identb"][N, N], r=[n_bm, "identb"], w=["pM"])
    k.cp("dve", t["ozT"][:, :, 0:n], pTv8[:, :, 0:n], r=["pM"], w=["ozT"])
    yield
    for half in range(2):
        cs = slice(half * 512, (half + 1) * 512)
        for r in range(4):
            k.mm(pA[N, :], t["ozT"][:, r, 0:n], t["WA"][:, r, cs], start=(r == 0), stop=(r == 3), r=["ozT", "WA"], w=["pA"])
        for c in range(4):
            k.mm(pB[N, :], t["ozT"][:, 4 + c, 0:n], t["WB"][:, c, cs], start=(c == 0), stop=(c == 3), r=["ozT", "WB"], w=["pB"])
        k.stt(t["B2"][N, :], tg[N, half * 512:(half + 1) * 512], 1.0, pA[N, :], ALU.add, ALU.mult,
              r=[n_tg, "pA"], w=["B2"])
        k.stt(t["B3"][N, :], tg[N, 1024 + half * 512:1024 + (half + 1) * 512], 1.0, pB[N, :], ALU.add, ALU.mult,
              r=[n_tg, "pB"], w=["B3"])
        k.tt("dve", mc[N, cs], t["B2"][N, :], t["B3"][N, :], ALU.add, r=["B2", "B3"], w=["mc"])
        yield
    for kc in range(8):
        k.tr(pTv8[:, kc, 0:n], mc[N, kc * 128:(kc + 1) * 128], t["identb"][N, N], r=["mc", "identb"], w=["pM"])
    k.cp("dve", t["hT"][:, :, 0:n], pTv8[:, :, 0:n], r=["pM"], w=["hT"])
    yield
    banks = [(pA, "pA"), (pB, "pB")]
    for half in range(2):
        bank, bname = banks[half]
        cs = slice(half * 512, (half + 1) * 512)
        for kc in range(8):
            k.mm(bank[N, :], t["hT"][:, kc, 0:n], t["Wout"][:, kc, cs], start=(kc == 0), stop=(kc == 7),
                 r=["hT", "Wout"], w=[bname])
        if gate is None:
            gap, gname = t["gq"][N, cs], "gq"
        else:
            gap, gname = gate[half][0][N, 0:512], gate[half][1]
        k.tt("dve", t["B1"][N, cs], bank[N, :], gap, ALU.mult, r=[bname, gname], w=["B1a" if half == 0 else "B1b"])
        yield
    k.dma("sp", t["x"][N, :], x_src, w=["x"])
    k.tt("dve", t["x"][N, :], t["B1"][N, :], t["x"][N, :], ALU.add, r=B1W + ["x"], w=["x"])
    k.dma("sp", y_dst, t["x"][N, :], r=["x"], is_out=True)


def mod_pass(k, t, d, n, cT, cT_name, bcast):
    N = slice(0, n)
    stg = [t["G0"], t["G1"]]
    banks = [(t["pA"], "pA"), (t["pB"], "pB")]
    wada = d["w_ada"].rearrange("(kc p) n -> p kc n", p=128)
    for j in range(12):
        sg, sname = stg[j % 2], "G%d" % (j % 2)
        bank, bname = banks[j % 2]
        sgv = sg[:].rearrange("p (kc n) -> p kc n", kc=8)
        k.dma("sp", sgv, wada[:, :, j * 256:(j + 1) * 256], w=[sname])
        k.dma("sp", t["bada"][0:1, :], d["b_ada"][0:1, j * 256:(j + 1) * 256], w=["bada"])
        if not bcast:
            for kc in range(8):
                k.mm(bank[N, 0:256], cT[:, kc, 0:n], sgv[:, kc, :], start=(kc == 0), stop=False, r=[cT_name, sname], w=[bname])
            k.mm(bank[N, 0:256], t["ones0"][:, 0:n], t["bada"][:, :], start=False, stop=True, r=["ones0", "bada"], w=[bname])
            src = bank[N, 0:256]
        else:
            for kc in range(8):
                k.mm(bank[0:1, 0:256], cT[:, kc:kc + 1], sgv[:, kc, :], start=(kc == 0), stop=False, r=[cT_name, sname], w=[bname])
            k.mm(bank[0:1, 0:256], t["ones0"][:, 0:1], t["bada"][:, :], start=False, stop=True, r=["ones0", "bada"], w=[bname])
            k.cp("act", t["modrow"][0:1, :], bank[0:1, 0:256], r=[bname], w=["modrow"])
            k.mm(bank[N, 256:512], t["onesf"][0:1, 0:n], t["modrow"][0:1, :], r=["onesf", "modrow"], w=[bname])
            src = bank[N, 256:512]
        cs = slice((j % 4) * 256, (j % 4 + 1) * 256)
        if j < 4:
            k.cp("act", t["sh"][N, cs], src, r=[bname], w=["sh"])
        elif j < 8:
            k.stt(t["sc1"][N, cs], src, 1.0, t["normg"][N, cs], ALU.add, ALU.mult, r=[bname, "normg"], w=["sc1"])
        else:
            k.act(t["gq"][N, cs], src, AF.Copy, r=[bname], w=["gq"], scale=0.25)


def build_program():
    nc = bass.Bass("TRN2", target_bir_lowering=False)
    d = {}

    def din(name, shape, dt=F32):
        d[name] = nc.dram_tensor(name, shape, dt, kind="ExternalInput").ap()

    def dout(name, shape):
        d[name] = nc.dram_tensor(name, shape, F32, kind="ExternalOutput").ap()

    din("xp", [S, D]); din("xs", [NSMP, D]); din("cpv", [D]); din("csv", [NSMP, D])
    for nm in ("kcmp", "vcmp", "kslc", "vslc"):
        din(nm, [2560 * 8, 2048])
    din("kwin", [NSMP, 512, 128]); din("vwin", [NSMP, 512, 128])
    din("ptrep", [128, NSMP], I32)
    din("w_ada", [D, 3 * D]); din("b_ada", [1, 3 * D]); din("norm_g", [1, D]); din("w_in", [D, DIN])
    din("qg", [1, 64]); din("kg", [1, 64]); din("pek", [32, 64]); din("pev", [32, 64])
    din("wck", [64, 64]); din("wcv", [64, 64]); din("vng", [1, 512]); din("vnb", [1, 512])
    din("ws", [4, 128, 128]); din("bs", [4, 128]); din("wbra", [512, D]); din("wbrb", [512, D]); din("wout", [D, D])
    din("rope", [S + 1, 64]); din("tri_le", [128, 128]); din("tri_gt", [128, 128]); din("identf_c", [128, 128])
    din("pool4", [128, 4]); din("pair", [64, 32]); din("e2", [64, 2048]); din("cmpbase", [128, 512]); din("trineg_le", [128, 512]); din("trineg_gt", [128, 512])
    din("impbias", [NT, 128, 32]); din("pool2", [128, 64]); din("e4", [32, 128]); din("rmod8", [128, 1])
    d["gqs"] = nc.dram_tensor("gqs", [NSMP, D], F32, kind="Internal").ap()
    dout("yp", [S, D]); dout("ys", [NSMP, D])
    for nm in ("pk_cmp", "pv_cmp", "pk_slc", "pv_slc"):
        dout(nm, [S, 128])
    dout("pk_win", [512, 128]); dout("pv_win", [512, 128])
    for nm in ("sk_cmp", "sv_cmp", "sk_slc", "sv_slc"):
        dout(nm, [NSMP, 128])
    dout("sk_win", [NSMP, 512, 128]); dout("sv_win", [NSMP, 512, 128]); dout("svch", [NSMP, 512])

    with contextlib.ExitStack() as es:
        t = {}

        def sb(name, shape, dt, scope=es):
            t[name] = scope.enter_context(nc.sbuf_tensor("sb_" + name, shape, dt))
            return t[name]

        def ps(name, shape, dt, scope=es):
            t[name] = scope.enter_context(nc.psum_tensor("ps_" + name, shape, dt))
            return t[name]

        sb("Win", [128, 8, DIN], BF16)
        sb("sc1", [128, D], F32); sb("sh", [128, D], F32); sb("gq", [128, D], F32)
        sb("identb", [128, 128], BF16); sb("identf", [128, 128], F32); sb("tri_le", [128, 128], BF16); sb("tri_gt", [128, 128], BF16)
        sb("qg", [128, 64], F32); sb("kg", [128, 64], F32); sb("vng", [128, 512], F32); sb("vnb", [128, 512], F32)
        sb("nh", [128, 8], F32); sb("onesf", [128, 128], F32)
        sb("wsT", [128, 4, 128], BF16); sb("bsT", [128, 4], F32)
        sb("Wbdk", [128, 128], BF16); sb("Wbdv", [128, 128], BF16); sb("pebar", [128, 2], F32)
        sb("pool4", [128, 4], BF16); sb("e2", [128, 2048], BF16)
        sb("x", [128, D], F32); sb("B1", [128, D], F32); sb("B2", [128, 512], F32); sb("B3", [128, 512], F32)
        sb("h", [128, D], BF16); sb("hT", [128, 8, 128], BF16); sb("R1", [128, 256], F32); sb("R2", [128, 256], F32)
        sb("st", [128, 4], F32); sb("st2", [128, 24], F32); sb("st3", [128, 12], F32)
        sb("q_bf", [128, 512], BF16); sb("kf", [128, 384], F32); sb("vf", [128, 384], F32); sb("k_bf", [128, 384], BF16)
        sb("gwt", [128, 24], F32); sb("gw0", [128, 24], F32); sb("gw1", [128, 24], F32); sb("za_s0", [128, 512], BF16); sb("za_s1", [128, 512], BF16)
        sb("vn_f", [128, 512], F32); sb("tg0", [128, 2048], BF16); sb("tg1", [128, 2048], BF16)
        sb("bmix0", [128, 512], BF16); sb("bmix1", [128, 512], BF16); sb("o_a", [128, 512], F32); sb("oz", [128, 512], BF16); sb("ozT", [128, 8, 128], BF16)
        sb("cs", [128, 64], F32); sb("mc", [128, D], BF16)
        sb("rden", [128, 8], F32); sb("cx", [128, 8], F32); t["tmpO"] = t["B3"]
        ps("pA", [128, 512], F32); ps("pB", [128, 512], F32); ps("pT", [128, 1024], BF16)
        ps("pM", [128, 512], F32); ps("pS0", [128, 512], F32); ps("pS1", [128, 512], F32)
        ps("pO0", [128, 512], F32); ps("pO1", [128, 512], F32)

        with contextlib.ExitStack() as s1:
            P = Prog(nc, "a")
            k = K(P)
            sb("normg", [128, D], F32, s1)
            sb("G0", [128, 2048], F32, s1); sb("G1", [128, 2048], F32, s1)
            sb("Gw0", [128, 512], F32, s1); sb("Gw1", [128, 512], F32, s1)
            sb("Gb0", [128, 2048], BF16, s1); sb("Gb1", [128, 2048], BF16, s1); sb("pool2b", [128, 64], BF16, s1); sb("Gwb0", [128, 512], BF16, s1); sb("Gwb1", [128, 512], BF16, s1)
            sb("PTb", [128, 160], BF16, s1); sb("vfb", [128, 256], BF16, s1); sb("Enewb", [128, 2 * NSMP * 8], BF16, s1)
            sb("bada", [128, 256], F32, s1); sb("modrow", [1, 256], F32, s1); sb("ones0", [128, 128], F32, s1)
            sb("cTs", [128, 8, NSMP], F32, s1)
            sb("cp8", [128, 8], F32, s1)
            sb("w00", [128, 4], F32, s1); sb("b0", [128, 4], F32, s1)
            sb("pe2", [32, 128], F32, s1); sb("o32", [32, 2], F32, s1)
            sb("ptrep", [128, NSMP], I32, s1); sb("rmod8", [128, 1], F32, s1); sb("idx", [128, NSMP], I32, s1)
            sb("pool2", [128, 64], F32, s1); sb("pairf", [64, 32], F32, s1); sb("e4", [32, 128], BF16, s1)
            sb("QTs", [128, 2, 4 * NSMP], BF16, s1); sb("enr", [NSMP, 16], F32, s1); sb("en", [NSMP, 16], F32, s1); sb("Enew", [128, 2 * NSMP * 8], F32, s1)
            sb("pTk", [128, 64], BF16, s1); sb("pTv", [128, 64], BF16, s1); sb("KcTs", [128, 64], BF16, s1); sb("Vcs", [64, 128], F32, s1)
            sb("PcT", [64, NSMP * 8], F32, s1); sb("KTs", [128, 20, 128], BF16, s1)
            sb("PTs", [128, 160], F32, s1); sb("PTsum", [128, 16], F32, s1)
            sb("rDc", [32, 128], F32, s1); sb("impn", [32, 128], F32, s1); sb("impT", [32, 32], F32, s1)
            sb("impS", [32, 32], F32, s1); sb("bias0", [32, 32], F32, s1); sb("m8s", [32, 8], F32, s1)
            sb("selS", [32, 32], BF16, s1); sb("selTs", [32, 32], BF16, s1); sb("Msk", [128, 32], F32, s1)
            t["OTn"] = t["B1"][:, 0:384]; t["rDall"] = t["B1"][:, 384:768]; t["OT1"] = t["B2"][0:64, 0:384]; t["wsl"] = t["B3"][:, 0:128]

            k.dma("pool", t["identb"][:], d["identf_c"], w=["identb"])
            k.dma("sp", t["identf"][:], d["identf_c"], w=["identf"])
            k.dma("pool", t["tri_le"][:], d["tri_le"], w=["tri_le"])
            k.dma("pool", t["tri_gt"][:], d["tri_gt"], w=["tri_gt"])
            k.dma("pool", t["pool4"][:], d["pool4"], w=["pool4"])
            k.memset("pool", t["e2"][64:128, :], 0.0, w=["e2"])
            k.dma("pool", t["e2"][0:64, :], d["e2"], w=["e2"])
            k.dma("pool", t["e4"][:], d["e4"], w=["e4"])
            k.dma("sp", t["pool2"][:], d["pool2"], w=["pool2"])
            k.dma("pool", t["pool2b"][:], d["pool2"], w=["pool2b"])
            k.dma("sp", t["pairf"][:], d["pair"], w=["pairf"])
            k.dma("sp", t["rmod8"][:], d["rmod8"], w=["rmod8"])
            k.dma("sp", t["ptrep"][:], d["ptrep"], w=["ptrep"])
            k.dma("sp", t["qg"][:], d["qg"].partition_broadcast(128), w=["gains"])
            k.dma("sp", t["kg"][:], d["kg"].partition_broadcast(128), w=["gains"])
            k.dma("sp", t["vng"][:], d["vng"].partition_broadcast(128), w=["gains"])
            k.dma("sp", t["vnb"][:], d["vnb"].partition_broadcast(128), w=["gains"])
            k.dma("sp", t["normg"][:], d["norm_g"].partition_broadcast(128), w=["normg"])
            k.dma("sp", t["bsT"][:], d["bs"].rearrange("g i -> i g"), w=["bsT"], slow=True)
            k.dma("sp", t["w00"][0:NSMP, :], d["ws"][:, 0, 0:1].rearrange("g o -> o g").partition_broadcast(NSMP), w=["w00"], slow=True)
            k.dma("sp", t["b0"][0:NSMP, :], d["bs"][:, 0:1].rearrange("g o -> o g").partition_broadcast(NSMP), w=["w00"], slow=True)
            k.memset("pool", t["nh"][:], -0.5, w=["nh"])
            k.memset("pool", t["Enew"][:], 0.0, w=["Enew"])
            k.memset("pool", t["bada"][:], 0.0, w=["bada"])
            k.memset("pool", t["ones0"][:], 0.0, w=["ones0"])
            k.memset("pool", t["ones0"][0:1, :], 1.0, w=["ones0"])
            k.memset("pool", t["QTs"][:], 0.0, w=["QTs"])
            k.memset("pool", t["vf"][:], 0.0, w=["vf"])
            k.memset("pool", t["Enewb"][:], 0.0, w=["Enewb"])
            k.memset("pool", t["onesf"][:], 1.0, w=["onesf"])
            k.memset("pool", t["o32"][:], 1.0 / 32, w=["o32"])
            k.memset("pool", t["Wbdk"][:], 0.0, w=["Wbdk"])
            k.memset("pool", t["Wbdv"][:], 0.0, w=["Wbdv"])
            k.memset("pool", t["bias0"][:], 0.0, w=["bias0"])
            k.memset("pool", t["bias0"][:, 0:1], 1.0e4, w=["bias0"])
            for g in range(2):
                gs = slice(g * 64, (g + 1) * 64)
                k.dma("pool", t["Wbdk"][gs, gs], d["wck"], w=["Wbdk"])
                k.dma("pool", t["Wbdv"][gs, gs], d["wcv"], w=["Wbdv"])
                k.dma("sp", t["pe2"][:, gs], d["pek"], w=["pe2k"])
            k.mm(t["pM"][:, 0:1], t["pe2"][:, :], t["o32"][:, 0:1], r=["pe2k", "o32"], w=["pM"])
            k.cp("dve", t["pebar"][:, 0:1], t["pM"][:, 0:1], r=["pM"], w=["pebar"])
            for g in range(2):
                gs = slice(g * 64, (g + 1) * 64)
                k.dma("sp", t["pe2"][:, gs], d["pev"], r=[], w=["pe2k"])
            k.mm(t["pM"][:, 0:1], t["pe2"][:, :], t["o32"][:, 0:1], r=["pe2k", "o32"], w=["pM"])
            k.cp("dve", t["pebar"][:, 1:2], t["pM"][:, 0:1], r=["pM"], w=["pebar"])
            for g in range(4):
                k.dma("sp", t["wsl"], d["ws"][g], w=["B3"])
                k.tr(t["pM"][:, 0:128], t["wsl"], t["identf"][:], r=["B3", "identf"], w=["pM"])
                k.tt("dve", t["wsT"][:, g, :], t["pM"][:, 0:128], t["tri_le"][:], ALU.mult, r=["pM", "tri_le"], w=["wsT"])
            winv = d["w_in"].rearrange("(kc p) n -> p kc n", p=128)
            for j in range(11):
                c0, c1 = j * 512, min(DIN, (j + 1) * 512)
                k.dma("pool", t["Win"][:, :, c0:c1], winv[:, :, c0:c1], w=["win%d" % j])
            k.dma("sp", t["cp8"][:], d["cpv"].rearrange("(kc p) -> p kc", p=128), w=["cp8"], slow=True)
            k.dma("sp", t["x"][0:NSMP, :], d["csv"], w=["x"])
            pMv = t["pM"][:, 0:8 * NSMP].rearrange("p (a b) -> p a b", a=8)
            for kc in range(8):
                k.tr(pMv[:, kc, :], t["x"][0:NSMP, kc * 128:(kc + 1) * 128], t["identf"][0:NSMP, 0:NSMP],
                     r=["x", "identf"], w=["pM"])
            k.cp("dve", t["cTs"][:], pMv, r=["pM"], w=["cTs"])
            mod_pass(k, t, d, NSMP, t["cTs"], "cTs", False)
            k.ts("dve", t["idx"][:], t["ptrep"][:], 8.0, t["rmod8"][:, 0:1], ALU.mult, ALU.add, r=["ptrep", "rmod8"], w=["idx"])
            k.dma("sp", t["cs"][0:NSMP, :], d["rope"][S:S + 1, :].partition_broadcast(NSMP), w=["cs"])
            for _ in front(k, t, NSMP, d["xs"], "cs", "0", False):
                pass
            n = NSMP
            N = slice(0, n)
            k.dma("sp", d["sk_cmp"], t["kf"][N, 0:128], r=["kf"], is_out=True)
            k.dma("sp", d["sk_slc"], t["kf"][N, 128:256], r=["kf"], is_out=True)
            k.dma("sp", d["sv_cmp"], t["vf"][N, 0:128], r=["vf"], is_out=True)
            k.dma("sp", d["sv_slc"], t["vf"][N, 128:256], r=["vf"], w=["d_svslc"], is_out=True)
            k.dma("sp", d["svch"], t["vn_f"][N, :], r=["vn_f"], is_out=True)
            k.dma("sp", d["sk_win"][:, 0:511, :], d["kwin"][:, 1:512, :], is_out=True)
            k.dma("sp", d["sv_win"][:, 0:511, :], d["vwin"][:, 1:512, :], is_out=True)
            k.dma("sp", d["sk_win"][:, 511, :], t["kf"][N, 256:384], r=["kf"], is_out=True)
            k.dma("sp", d["sv_win"][:, 511, :], t["vf"][N, 256:384], r=["vf"], w=["d_svwin"], is_out=True)
            pTv8 = t["pT"][:].rearrange("p (a b) -> p a b", a=8)
            for r in range(4):
                k.tr(pTv8[:, r, 0:n], t["q_bf"][N, r * 128:(r + 1) * 128], t["identb"][N, N], r=["q_bf", "identb"], w=["pT"])
            k.cp("act", t["QTs"][0:64, 0, :].rearrange("p (r s) -> p r s", r=4), pTv8[0:64, 0:4, 0:n], r=["pT"], w=["QTs"])
            k.cp("act", t["QTs"][64:128, 1, :].rearrange("p (r s) -> p r s", r=4), pTv8[64:128, 0:4, 0:n], r=["pT"], w=["QTs"])
            Env = t["Enew"][:].rearrange("p (x s h) -> p x s h", x=2, s=NSMP)
            Env16 = t["Enew"][0:NSMP, :].rearrange("p (x s h) -> p x s h", x=2, s=NSMP)
            for xx in range(2):
                kcol = t["k_bf"][N, 128 + xx * 128:256 + xx * 128].rearrange("p (g d) -> p g d", g=2).unsqueeze(2)
                k.tt("dve", t["B2"][N, :].rearrange("p (g r d) -> p g r d", g=2, r=4),
                     t["q_bf"][N, :].rearrange("p (r g d) -> p g r d", r=4, g=2), bc(kcol, [n, 2, 4, 64]), ALU.mult,
                     r=["q_bf", "k_bf"], w=["B2"])
                k.red(t["enr"][:, xx * 8:(xx + 1) * 8], v3(t["B2"][N, :], 8), ALU.add, r=["B2"], w=["enr"])
            k.act(t["en"][:], t["enr"][:], AF.Exp, r=["enr"], w=["en"], scale=SCL)
            for xx in range(2):
                k.tt("dve", Env16[:, xx, :, :], bc(t["identf"][N, N].unsqueeze(2), [n, NSMP, 8]),
                     bc(t["en"][:, xx * 8:(xx + 1) * 8].unsqueeze(1), [n, NSMP, 8]), ALU.mult, r=["identf", "en"], w=["Enew"])
            k.cp("act", t["vfb"][:], t["vf"][:, 128:384], r=["vf"], w=["vfb"])
            k.cp("act", t["Enewb"][0:NSMP, :], t["Enew"][0:NSMP, :], r=["Enew"], w=["Enewb"])
            Envb = t["Enewb"][:].rearrange("p (x s h) -> p x s h", x=2, s=NSMP)
            pS, pO, pD, pM = t["pS0"], t["pO0"], t["pO1"], t["pM"]
            pOv = pO[:, 0:384].rearrange("p (s x h) -> p s x h", s=NSMP, x=3)
            pDv = pD[:, 0:384].rearrange("p (s x h) -> p s x h", s=NSMP, x=3)
            PcTv = t["PcT"][:].rearrange("p (s h) -> p s h", s=NSMP)
            for s in range(NSMP):
                k.gather(t["G0"][:], d["kcmp"], t["idx"][:, s:s + 1], r=["idx"], w=["G0"])
                k.gather(t["G1"][:], d["vcmp"], t["idx"][:, s:s + 1], r=["idx"], w=["G1"])
                k.cp("act", t["Gb0"][:], t["G0"][:], r=["G0"], w=["Gb0"])
                k.cp("act", t["Gb1"][:], t["G1"][:], r=["G1"], w=["Gb1"])
                for tt_ in range(16):
                    k.mm(pM[:, 0:64], t["Gb0"][:, tt_ * 128:(tt_ + 1) * 128], t["pool2b"][:], start=(tt_ == 0), stop=(tt_ == 15),
                         r=["Gb0", "pool2b"], w=["pM"])
                for tt_ in range(16):
                    k.mm(pM[:, 64:128], t["Gb1"][:, tt_ * 128:(tt_ + 1) * 128], t["pool2b"][:], start=(tt_ == 0), stop=(tt_ == 15),
                         r=["Gb1", "pool2b"], w=["pM"])
                k.ts("dve", t["pTk"][:], pM[:, 0:64], t["pebar"][:, 0:1], None, ALU.add, r=["pM", "pebar"], w=["pTk"])
                k.ts("dve", t["pTv"][:], pM[:, 64:128], t["pebar"][:, 1:2], None, ALU.add, r=["pM", "pebar"], w=["pTv"])
                k.mm(pM[:, 128:192], t["Wbdk"][:], t["pTk"][:], r=["Wbdk", "pTk"], w=["pM"])
                k.mm(pM[0:64, 192:320], t["pTv"][:], t["Wbdv"][:], r=["Wbdv", "pTv"], w=["pM"])
                k.cp("act", t["KcTs"][:], pM[:, 128:192], r=["pM"], w=["KcTs"])
                k.cp("act", t["Vcs"][:], pM[0:64, 192:320], r=["pM"], w=["Vcs"])
                for g in range(2):
                    gs = slice(g * 64, (g + 1) * 64)
                    k.mm(pS[0:64, s * 8 + g * 4:s * 8 + g * 4 + 4], t["KcTs"][:, :],
                         t["QTs"][:, g, :].rearrange("p (r s) -> p r s", r=4)[:, :, s], r=["KcTs", "QTs"], w=["pS0"])
                k.act(PcTv[:, s, :], pS[0:64, s * 8:(s + 1) * 8], AF.Exp, r=["pS0"], w=["PcT"], scale=SCL)
                k.mm(pOv[:, s, 0, :], t["Vcs"][:], PcTv[:, s, :], r=["Vcs", "PcT"], w=["pO0"])
                k.mm(pDv[:, s, 0, :], t["onesf"][0:64, :], PcTv[:, s, :], r=["onesf", "PcT"], w=["pO1"])
                k.mm(pS[0:32, 128 + s * 8:128 + (s + 1) * 8], t["pairf"][:], PcTv[:, s, :], r=["pairf", "PcT"], w=["pS0"])
            k.P.add("dve", lambda e: e.reciprocal(out=t["rDc"][:].rearrange("p (s h) -> p s h", s=NSMP), in_=pDv[0:32, :, 0, :]),
                    ["pO1"], ["rDc"])
            k.tt("dve", t["impn"][:], pS[0:32, 128:256], t["rDc"][:], ALU.mult, r=["pS0", "rDc"], w=["impn"])
            k.red(t["impT"][:], t["impn"][:].rearrange("p (a r) -> p a r", r=4), ALU.add, r=["impn"], w=["impT"])
            k.tr(pS[0:32, 256:288], t["impT"][:], t["identf"][0:32, 0:32], r=["impT", "identf"], w=["pS0"])
            k.tt("dve", t["impS"][:], pS[0:32, 256:288], t["bias0"][:], ALU.add, r=["pS0", "bias0"], w=["impS"])
            k.P.add("dve", lambda e: e.max(out=t["m8s"][:], in_=t["impS"][:]), ["impS"], ["m8s"])
            k.ts("dve", t["selS"][:], t["impS"][:], t["m8s"][:, 6:7], None, ALU.is_ge, r=["impS", "m8s"], w=["selS"])
            k.tr(t["pT"][0:32, 0:32], t["selS"][:], t["identb"][0:32, 0:32], r=["selS", "identb"], w=["pT"])
            k.cp("act", t["selTs"][:], t["pT"][0:32, 0:32], r=["pT"], w=["selTs"])
            k.mm(pS[:, 320:352], t["e4"][:], t["selTs"][:], r=["e4", "selTs"], w=["pS0"])
            k.cp("act", t["Msk"][:], pS[:, 320:352], r=["pS0"], w=["Msk"])
            pS = t["pS1"]
            banks = [(t["pA"], "pA"), (t["pB"], "pB")]
            for s in range(NSMP):
                k.gather(t["G0"][:], d["kslc"], t["idx"][:, s:s + 1], r=["idx"], w=["G0"])
                k.gather(t["G1"][:], d["vslc"], t["idx"][:, s:s + 1], r=["idx"], w=["G1"])
                k.dma("sp", t["Gw0"][:].rearrange("p (a c) -> p a c", a=4), d["kwin"][s].rearrange("(p a) c -> p a c", a=4), w=["Gw0"])
                k.dma("sp", t["Gw1"][:].rearrange("p (a c) -> p a c", a=4), d["vwin"][s].rearrange("(p a) c -> p a c", a=4), w=["Gw1"])
                k.cp("act", t["Gb0"][:], t["G0"][:], r=["G0"], w=["Gb0"])
                k.cp("act", t["Gwb0"][:], t["Gw0"][:], r=["Gw0"], w=["Gwb0"])
                k.cp("act", t["Gb1"][:], t["G1"][:], r=["G1"], w=["Gb1"])
                k.cp("act", t["Gwb1"][:], t["Gw1"][:], r=["Gw1"], w=["Gwb1"])
                tbanks = [(t["pT"][:].rearrange("p (a c) -> p a c", a=8), "pT"),
                          (t["pA"][:].bitcast(BF16).rearrange("p (a c) -> p a c", a=8), "pA"),
                          (t["pB"][:].bitcast(BF16).rearrange("p (a c) -> p a c", a=8), "pB")]
                for q8 in range(3):
                    pTk8, tbn = tbanks[q8]
                    nt_ = 8 if q8 < 2 else 4
                    for a in range(nt_):
                        tix = q8 * 8 + a
                        if tix < 16:
                            src, sname, col = t["Gb0"], "Gb0", tix * 128
                        else:
                            src, sname, col = t["Gwb0"], "Gwb0", (tix - 16) * 128
                        k.tr(pTk8[:, a, :], src[:, col:col + 128], t["identb"][:], r=[sname, "identb"], w=[tbn])
                    k.cp("act", t["KTs"][:, q8 * 8:q8 * 8 + nt_, :], pTk8[:, 0:nt_, :], r=[tbn], w=["KTs"])
                for tt_ in range(20):
                    for g in range(2):
                        gs = slice(g * 64, (g + 1) * 64)
                        k.mm(pS[:, tt_ * 8 + g * 4:tt_ * 8 + g * 4 + 4], t["KTs"][:, tt_, :],
                             t["QTs"][:, g, :].rearrange("p (r s) -> p r s", r=4)[:, :, s], r=["KTs", "QTs"], w=["pS1"])
                k.act(t["PTs"][:], pS[:, 0:160], AF.Exp, r=["pS1"], w=["PTs"], scale=SCL)
                k.tt("dve", t["PTs"][:, 0:128].rearrange("p (a g r) -> p a g r", a=16, g=2),
                     t["PTs"][:, 0:128].rearrange("p (a g r) -> p a g r", a=16, g=2),
                     bc(t["Msk"][:, s * 2:(s + 1) * 2].unsqueeze(1).unsqueeze(3), [128, 16, 2, 4]), ALU.mult,
                     r=["PTs", "Msk"], w=["PTs"])
                k.memset("dve", t["PTs"][0:1, 128:136], 0.0, w=["PTs"])
                k.red(t["PTsum"][:, 0:8], t["PTs"][:, 0:128].rearrange("p (a h) -> p h a", a=16), ALU.add, r=["PTs"], w=["PTsum"])
                k.red(t["PTsum"][:, 8:16], t["PTs"][:, 128:160].rearrange("p (a h) -> p h a", a=4), ALU.add, r=["PTs"], w=["PTsum"])
                k.cp("act", t["PTb"][:], t["PTs"][:], r=["PTs"], w=["PTb"])
                for tt_ in range(16):
                    k.mm(pOv[:, s, 1, :], t["Gb1"][:, tt_ * 128:(tt_ + 1) * 128], t["PTb"][:, tt_ * 8:(tt_ + 1) * 8],
                         start=(tt_ == 0), stop=False, r=["Gb1", "PTb"], w=["pO0"])
                k.mm(pOv[:, s, 1, :], t["vfb"][:, 0:128], Envb[:, 0, s, :], start=False, stop=True,
                     r=["vfb", "Enewb"], w=["pO0"])
                for tt_ in range(4):
                    k.mm(pOv[:, s, 2, :], t["Gwb1"][:, tt_ * 128:(tt_ + 1) * 128], t["PTb"][:, 128 + tt_ * 8:128 + (tt_ + 1) * 8],
                         start=(tt_ == 0), stop=False, r=["Gwb1", "PTb"], w=["pO0"])
                k.mm(pOv[:, s, 2, :], t["vfb"][:, 128:256], Envb[:, 1, s, :], start=False, stop=True,
                     r=["vfb", "Enewb"], w=["pO0"])
                for xx in range(2):
                    k.mm(pDv[:, s, 1 + xx, :], t["onesf"][:, :], t["PTsum"][:, xx * 8:(xx + 1) * 8], start=True, stop=False,
                         r=["onesf", "PTsum"], w=["pO1"])
                    k.mm(pDv[:, s, 1 + xx, :], t["onesf"][:, :], Env[:, xx, s, :], start=False, stop=True,
                         r=["onesf", "Enew"], w=["pO1"])
            k.P.add("dve", lambda e: e.reciprocal(out=t["rDall"], in_=pD[:, 0:384]), ["pO1"], ["B1a", "B1b"])
            k.tt("dve", t["OTn"], pO[:, 0:384], t["rDall"], ALU.mult, r=["pO0", "B1a", "B1b"], w=["B1a", "B1b"])
            k.cp("dve", t["OT1"], t["OTn"][64:128, :], r=["B1a", "B1b"], w=["B2"])
            gwv = t["gw0"][N, :].rearrange("p (h x) -> p x h", x=3)
            for xx in range(3):
                bank, bname = banks[xx % 2]
                for hh in range(8):
                    src = t["OTn"] if hh < 4 else t["OT1"]
                    sname = "B1a" if hh < 4 else "B2"
                    inap = src[0:64, :].rearrange("p (s c) -> p s c", s=NSMP)[:, :, xx * 8 + hh]
                    k.tr(bank[0:n, hh * 64:(hh + 1) * 64], inap, t["identf"][0:64, 0:64], r=[sname, "identf"], w=[bname])
                if xx == 0:
                    k.tt("dve", v3(t["o_a"][N, :], 8), v3(bank[N, :], 8), bc(gwv[:, xx, :].unsqueeze(2), [n, 8, 64]), ALU.mult,
                         r=[bname, "gw0"], w=["o_a"])
                else:
                    k.tt("dve", v3(t["tmpO"][N, :], 8), v3(bank[N, :], 8), bc(gwv[:, xx, :].unsqueeze(2), [n, 8, 64]), ALU.mult,
                         r=[bname, "gw0"], w=["B3"])
                    k.tt("pool", t["o_a"][N, :], t["o_a"][N, :], t["tmpO"][N, :], ALU.add, r=["o_a", "B3"], w=["o_a"])
            k.dma("sp", d["gqs"], t["gq"][N, :], r=["gq"], w=["d_gqs"])
            mod_pass(k, t, d, 128, t["cp8"], "cp8", True)
            P.emit(es)

        with contextlib.ExitStack() as s2:
            P = Prog(nc, "b")
            k = K(P)
            sb("Wout", [128, 8, D], BF16, s2); sb("WA", [128, 4, D], BF16, s2); sb("WB", [128, 4, D], BF16, s2)
            k.dma("pool", t["Wout"][:], d["wout"].rearrange("(kc p) n -> p kc n", p=128), w=["Wout"])
            for g in range(2):
                k.dma("pool", t["WA"][g * 64:(g + 1) * 64, :, :],
                      d["wbra"][g * 256:(g + 1) * 256, :].rearrange("(r dd) n -> dd r n", dd=64), w=["WA"])
            k.dma("pool", t["WB"][:], d["wbrb"].rearrange("(c p) n -> p c n", p=128), w=["WB"])
            k.dma("sp", t["B1"][0:NSMP, :], d["gqs"], w=["B1a", "B1b"])
            for _ in tail(k, t, NSMP, d["ys"], d["xs"], "0", gate=[(t["B1"][:, 0:512], "B1a"), (t["B1"][:, 512:1024], "B1b")]):
                pass
            sb("KsT", [128, S], BF16, s2); sb("KwT", [128, 5 * 128], BF16, s2)
            sb("Vs", [128, NT, 2, 65], BF16, s2); sb("Vw", [128, 5, 2, 65], BF16, s2)
            sb("QT", [128, 2, 512], BF16, s2); sb("vcb", [128, 128], BF16, s2)
            sb("pTk2", [128, 64], BF16, s2); sb("pTv2", [128, 64], BF16, s2); sb("KcT", [128, 64], BF16, s2)
            sb("Vc", [64, 2, 97], BF16, s2)
            sb("PT0", [128, 512], BF16, s2); sb("PT1", [128, 512], BF16, s2)
            sb("nsel4", [128, 2, 512], BF16, s2); sb("imp", [128, 64], F32, s2)
            sb("sel", [128, 64], BF16, s2); sb("m8", [128, 16], F32, s2); sb("mctneg", [128, 512], BF16, s2); sb("ibias", [128, 32], F32, s2)
            sb("tnle", [128, 512], BF16, s2); sb("tngt", [128, 512], BF16, s2)
            k.dma("pool", t["tnle"][:], d["trineg_le"], w=["tnle"])
            k.dma("pool", t["tngt"][:], d["trineg_gt"], w=["tngt"])
            n = 128
            N = slice(0, 128)
            k.memset("pool", t["Vs"][:], 1.0, w=["Vs"])
            k.memset("pool", t["Vw"][:], 1.0, w=["Vw"])
            k.memset("pool", t["Vc"][:], 1.0, w=["Vc"])
            k.memset("pool", t["pTk2"][:], 0.0, w=["pTk2"])
            k.memset("pool", t["pTv2"][:], 0.0, w=["pTv2"])
            k.memset("pool", t["KcT"][:], 0.0, w=["KcT"])
            k.memset("pool", t["QT"][:], 0.0, w=["QT"])
            k.memset("pool", t["nsel4"][:], 0.0, w=["nsel4"])
            k.dma("pool", t["mctneg"][:], d["cmpbase"], w=["mctneg"])
            for g in range(2):
                k.dma("pool", t["Vc"][:, g, 65:97], d["pair"], w=["Vc"])
            pS = [(t["pS0"], "pS0"), (t["pS1"], "pS1")]
            pO = [(t["pO0"], "pO0"), (t["pO1"], "pO1")]
            PT = [(t["PT0"], "PT0"), (t["PT1"], "PT1")]
            pM = t["pM"]
            pTv8 = t["pT"][:].rearrange("p (a b) -> p a b", a=8)
            cnt = {"s": 0, "p": 0}
            cur = {}
            sfx_of = lambda ii: str((ii + 1) % 2)

            def branch_finish(xx):
                for g in range(2):
                    bank, bname = pO[g]
                    ov = bank[:, 0:388].rearrange("p (r c) -> p r c", r=4)
                    k.ts("dve", t["rden"][:, g * 4:(g + 1) * 4], ov[:, :, 64], 1e-30, None, ALU.max, r=[bname], w=["rden"])
                k.P.add("dve", lambda e: e.reciprocal(out=t["rden"][:], in_=t["rden"][:]), ["rden"], ["rden"])
                k.tt("dve", t["cx"][:], t["rden"][:], cur["gwv"][:, xx, :], ALU.mult, r=["rden", cur["gwn"]], w=["cx"])
                for g in range(2):
                    bank, bname = pO[g]
                    ov = bank[:, 0:388].rearrange("p (r c) -> p r c", r=4)
                    for r in range(4):
                        hs = slice((g * 4 + r) * 64, (g * 4 + r + 1) * 64)
                        k.stt(t["o_a"][:, hs], ov[:, r, 0:64], t["cx"][:, g * 4 + r:g * 4 + r + 1], t["o_a"][:, hs], ALU.mult, ALU.add,
                              r=[bname, "cx", "o_a"], w=["o_a"])

            k.dma("sp", t["cs"][:], d["rope"][0:128, :], w=["cs"])
            for _ in front(k, t, 128, d["xp"][0:128, :], "cs", sfx_of(0), True):
                pass
            tgen = {"g": None}

            def tail_hook(nit):
                if tgen["g"] is not None:
                    next(tgen["g"], None)

            for i in range(NT):
                rows = slice(i * 128, (i + 1) * 128)
                sfx = sfx_of(i)
                cur["gwn"] = "gw" + sfx
                cur["gwv"] = t["gw" + sfx][:, :].rearrange("p (h x) -> p x h", x=3)
                k.dma("sp", d["pk_cmp"][rows, :], t["kf"][:, 0:128], r=["kf"], is_out=True)
                k.dma("sp", d["pk_slc"][rows, :], t["kf"][:, 128:256], r=["kf"], is_out=True)
                k.dma("sp", d["pv_cmp"][rows, :], t["vf"][:, 0:128], r=["vf"], is_out=True)
                k.dma("sp", d["pv_slc"][rows, :], t["vf"][:, 128:256], r=["vf"], is_out=True)
                if i >= NT - 4:
                    wr = slice((i - (NT - 4)) * 128, (i - (NT - 4) + 1) * 128)
                    k.dma("sp", d["pk_win"][wr, :], t["kf"][:, 256:384], r=["kf"], is_out=True)
                    k.dma("sp", d["pv_win"][wr, :], t["vf"][:, 256:384], r=["vf"], is_out=True)
                slot = i % 5
                k.cp("pool", t["Vs"][:, i, :, 0:64], v3(t["vf"][:, 128:256], 2), r=["vf"], w=["Vs"])
                k.cp("pool", t["Vw"][:, slot, :, 0:64], v3(t["vf"][:, 256:384], 2), r=["vf"], w=["Vw"])
                k.cp("pool", t["vcb"][:], t["vf"][:, 0:128], r=["vf"], w=["vcb"])
                for r in range(4):
                    k.tr(pTv8[:, r, :], t["q_bf"][:, r * 128:(r + 1) * 128], t["identb"][:], r=["q_bf", "identb"], w=["pT"])
                k.tr(pTv8[:, 4, :], t["k_bf"][:, 128:256], t["identb"][:], r=["k_bf", "identb"], w=["pT"])
                k.tr(pTv8[:, 5, :], t["k_bf"][:, 256:384], t["identb"][:], r=["k_bf", "identb"], w=["pT"])
                k.cp("act", t["QT"][0:64, 0, :].rearrange("p (r q) -> p r q", r=4), pTv8[0:64, 0:4, :], r=["pT"], w=["QT"])
                k.cp("act", t["QT"][64:128, 1, :].rearrange("p (r q) -> p r q", r=4), pTv8[64:128, 0:4, :], r=["pT"], w=["QT"])
                k.cp("act", t["KsT"][:, rows], pTv8[:, 4, :], r=["pT"], w=["KsT"])
                k.cp("act", t["KwT"][:, slot * 128:(slot + 1) * 128], pTv8[:, 5, :], r=["pT"], w=["KwT"])
                cc = slice(4 * i, 4 * i + 4)
                k.mm(pM[:, 0:4], t["k_bf"][:, 0:128], t["pool4"][:], r=["k_bf", "pool4"], w=["pM"])
                k.mm(pM[:, 4:8], t["vcb"][:], t["pool4"][:], r=["vcb", "pool4"], w=["pM"])
                k.ts("dve", t["pTk2"][:, cc], pM[:, 0:4], t["pebar"][:, 0:1], None, ALU.add, r=["pM", "pebar"], w=["pTk2"])
                k.ts("dve", t["pTv2"][:, cc], pM[:, 4:8], t["pebar"][:, 1:2], None, ALU.add, r=["pM", "pebar"], w=["pTv2"])
                k.mm(pM[:, 8:12], t["Wbdk"][:], t["pTk2"][:, cc], r=["Wbdk", "pTk2"], w=["pM"])
                k.mm(pM[0:64, 16:144], t["pTv2"][:], t["Wbdv"][:], r=["Wbdv", "pTv2"], w=["pM"])
                k.cp("act", t["KcT"][:, cc], pM[:, 8:12], r=["pM"], w=["KcT"])
                k.cp("act", t["Vc"][:, :, 0:64], v3(pM[0:64, 16:144], 2), r=["pM"], w=["Vc"])

                k.dma("sp", t["ibias"][:], d["impbias"][i], w=["ibias"])
                qt = {g: t["QT"][:, g, :] for g in range(2)}

                def run_branch(items, hook=None):
                    recs = []

                    def s_stage(it):
                        g, kp, mms, v_ap, v_name, first, last = it
                        sbank, sname = pS[cnt["s"] % 2]; cnt["s"] += 1
                        pt, pname = PT[cnt["p"] % 2]; cnt["p"] += 1
                        for mi, (lh, rh, nm) in enumerate(mms):
                            k.mm(sbank[0:kp, :], lh, rh, start=(mi == 0), stop=(mi == len(mms) - 1), r=nm, w=[sname])
                        recs.append((sbank, sname, pt, pname))

                    def e_stage(idx):
                        g, kp, mms, v_ap, v_name, first, last = items[idx]
                        sbank, sname, pt, pname = recs[idx]
                        k.act(pt[0:kp, :], sbank[0:kp, :], AF.Exp, r=[sname], w=[pname], scale=SCL)
                        obank, oname = pO[g]
                        ov = obank[:, 0:388].rearrange("p (r c) -> p r c", r=4)
                        nv = v_ap.shape[-1]
                        for r in range(4):
                            k.mm(ov[:, r, 0:nv], pt[0:kp, r * 128:(r + 1) * 128], v_ap, start=(first and r == 0), stop=(last and r == 3),
                                 r=[pname, v_name], w=[oname])

                    s_stage(items[0])
                    for idx in range(len(items)):
                        if idx + 1 < len(items):
                            s_stage(items[idx + 1])
                        e_stage(idx)
                        if hook is not None:
                            hook(len(items))

                items = []
                for g in range(2):
                    gs = slice(g * 64, (g + 1) * 64)
                    items.append((g, 64, [(t["KcT"][:, :], qt[g], ["KcT", "QT"]),
                                          (t["identb"][:, 64 - 4 * i:128 - 4 * i], t["mctneg"][:, :], ["identb", "mctneg"])],
                                  t["Vc"][:, g, :], "Vc", True, True))
                run_branch(items)
                for g in range(2):
                    bank, bname = pO[g]
                    ov = bank[:, 0:388].rearrange("p (r c) -> p r c", r=4)
                    k.ts("dve", t["rden"][:, g * 4:(g + 1) * 4], ov[:, :, 64], 1e-30, None, ALU.max, r=[bname], w=["rden"])
                k.P.add("dve", lambda e: e.reciprocal(out=t["rden"][:], in_=t["rden"][:]), ["rden"], ["rden"])
                for g in range(2):
                    bank, bname = pO[g]
                    ov = bank[:, 0:388].rearrange("p (r c) -> p r c", r=4)
                    ig = t["imp"][:, g * 32:(g + 1) * 32]
                    k.stt(ig, ov[:, 0, 65:97], t["rden"][:, g * 4:g * 4 + 1], t["ibias"][:], ALU.mult, ALU.add,
                          r=[bname, "rden", "ibias"], w=["imp"])
                    for r in range(1, 4):
                        k.stt(ig, ov[:, r, 65:97], t["rden"][:, g * 4 + r:g * 4 + r + 1], ig, ALU.mult, ALU.add,
                              r=[bname, "rden", "imp"], w=["imp"])
                k.tt("dve", t["cx"][:], t["rden"][:], cur["gwv"][:, 0, :], ALU.mult, r=["rden", cur["gwn"]], w=["cx"])
                for g in range(2):
                    bank, bname = pO[g]
                    ov = bank[:, 0:388].rearrange("p (r c) -> p r c", r=4)
                    k.tt("dve", v3(t["o_a"][:, g * 256:(g + 1) * 256], 4), ov[:, :, 0:64],
                         bc(t["cx"][:, g * 4:(g + 1) * 4].unsqueeze(2), [128, 4, 64]), ALU.mult, r=[bname, "cx"], w=["o_a"])
                for g in range(2):
                    ig = t["imp"][:, g * 32:(g + 1) * 32]
                    k.P.add("dve", (lambda g: lambda e: e.max(out=t["m8"][:, g * 8:(g + 1) * 8], in_=t["imp"][:, g * 32:(g + 1) * 32]))(g),
                            ["imp"], ["m8"])
                    k.ts("dve", t["sel"][:, g * 32:(g + 1) * 32], ig, t["m8"][:, g * 8 + 7:g * 8 + 8], None, ALU.is_ge,
                         r=["imp", "m8"], w=["sel"])
                gen = None
                if i + 1 < NT:
                    nrows = slice((i + 1) * 128, (i + 2) * 128)
                    k.dma("sp", t["cs"][:], d["rope"][nrows, :], w=["cs"])
                    gen = front(k, t, 128, d["xp"][nrows, :], "cs", sfx_of(i + 1), True)
                    next(gen, None)
                j0 = max(0, i - 4)
                items = []
                for g in range(2):
                    gs = slice(g * 64, (g + 1) * 64)
                    for j in range(j0, i + 1):
                        sl = j % 5
                        mms = [(t["KwT"][:, sl * 128:(sl + 1) * 128], qt[g], ["KwT", "QT"])]
                        if j == i:
                            mms.append((t["identb"][:, :], t["tnle"][:, :], ["identb", "tnle"]))
                        elif j == i - 4:
                            mms.append((t["identb"][:, :], t["tngt"][:, :], ["identb", "tngt"]))
                        items.append((g, 128, mms, t["Vw"][:, sl, g, :], "Vw", j == j0, j == i))
                run_branch(items, tail_hook)
                if tgen["g"] is not None:
                    for _ in tgen["g"]:
                        pass
                    tgen["g"] = None
                branch_finish(2)
                k.tr(t["pT"][0:64, 0:128], t["sel"][:], t["identb"][:], r=["sel", "identb"], w=["pT"])
                for g in range(2):
                    g32 = slice(g * 32, (g + 1) * 32)
                    k.ts("dve", v3(t["nsel4"][g32, g, :], 4), bc(t["pT"][g32, 0:128].unsqueeze(1), [32, 4, 128]), -1.0, 30000.0,
                         ALU.add, ALU.mult, r=["pT"], w=["nsel4"])
                items = []
                for g in range(2):
                    gs = slice(g * 64, (g + 1) * 64)
                    g32 = slice(g * 32, (g + 1) * 32)
                    for j in range(i + 1):
                        mms = [(t["KsT"][:, j * 128:(j + 1) * 128], qt[g], ["KsT", "QT"]),
                               (t["e2"][:, j * 128:(j + 1) * 128], t["nsel4"][:, g, :], ["e2", "nsel4"])]
                        if j == i:
                            mms.append((t["identb"][:, :], t["tnle"][:, :], ["identb", "tnle"]))
                        items.append((g, 128, mms, t["Vs"][:, j, g, :], "Vs", j == 0, j == i))
                acc = {"a": 0.0}

                def hook(nit):
                    if gen is None:
                        return
                    acc["a"] += 14.0 / nit
                    while acc["a"] >= 1.0:
                        acc["a"] -= 1.0
                        next(gen, None)

                run_branch(items, hook)
                if gen is not None:
                    for _ in gen:
                        pass
                branch_finish(1)
                tgen["g"] = tail(k, t, 128, d["yp"][rows, :], d["xp"][rows, :], sfx)
                next(tgen["g"], None)
            for _ in tgen["g"]:
                pass
            P.emit(es)
    return nc


def _consts():
    c = {}
    half = 32
    inv = (10000.0 ** (-np.arange(half, dtype=np.float32) * 2.0 / 64)).astype(np.float32)
    ang = np.arange(S + 1, dtype=np.float32)[:, None] * inv[None, :]
    c["rope"] = np.concatenate([np.cos(ang), np.sin(ang)], axis=1).astype(np.float32)
    kk = np.arange(128)[:, None]
    qq = np.arange(128)[None, :]
    c["tri_le"] = (kk <= qq).astype(np.float32)
    c["tri_gt"] = (kk > qq).astype(np.float32)
    c["identf_c"] = np.eye(128, dtype=np.float32)
    c["pool4"] = ((np.arange(128)[:, None] // 32) == np.arange(4)[None, :]).astype(np.float32) / 32.0
    c["pair"] = ((np.arange(64)[:, None] // 2) == np.arange(32)[None, :]).astype(np.float32)
    e = ((np.arange(2048)[None, :] // 64) == np.arange(32)[:, None]).astype(np.float32)
    c["e2"] = np.concatenate([e, e], axis=0)
    m = np.zeros((NT, 64, 128), np.float32)
    ib = np.zeros((NT, 128, 32), np.float32)
    for i in range(NT):
        pos = i * 128 + np.arange(128)
        cend = (np.arange(64) + 1) * 32 - 1
        m[i] = (cend[:, None] <= pos[None, :]).astype(np.float32)
        qblk = pos // 64
        blk = np.arange(32)
        forced = (blk[None, :] == 0) | (blk[None, :] == qblk[:, None])
        causal = blk[None, :] <= qblk[:, None]
        ib[i] = np.where(causal, 1.0e4 * forced, -1.0e30).astype(np.float32)
    base = np.zeros((128, 128), np.float32)
    base[68:, :] = -30000.0
    for u in range(64, 68):
        base[u, :] = np.where(np.arange(128) < 32 * (u - 64) + 31, -30000.0, 0.0)
    c["cmpbase"] = np.ascontiguousarray(np.tile(base[:, None, :], (1, 4, 1)).reshape(128, 512)).astype(np.float32)
    c["trineg_le"] = np.ascontiguousarray(np.tile(((1.0 - c["tri_le"]) * -30000.0)[:, None, :], (1, 4, 1)).reshape(128, 512)).astype(np.float32)
    c["trineg_gt"] = np.ascontiguousarray(np.tile(((1.0 - c["tri_gt"]) * -30000.0)[:, None, :], (1, 4, 1)).reshape(128, 512)).astype(np.float32)
    c["impbias"] = ib
    c["pool2"] = ((np.arange(128)[:, None] // 2) == np.arange(64)[None, :]).astype(np.float32) / 32.0
    c["e4"] = ((np.arange(128)[None, :] // 4) == np.arange(32)[:, None]).astype(np.float32)
    c["rmod8"] = (np.arange(128) % 8).astype(np.float32).reshape(128, 1)
    return c


_NC_CACHE = {}


def kernel(x_prompt, x_sample, cache_k_cmp, cache_v_cmp, cache_k_slc, cache_v_slc,
           cache_k_win, cache_v_win, page_table, c_prompt, c_sample,
           w_ada, b_ada, norm_g, w_in, q_norm_g, k_norm_g, cmp_pos_k, cmp_pos_v,
           w_cmp_k, w_cmp_v, vnorm_g, vnorm_b, w_s, b_s, w_br_a, w_br_b, w_out):
    f = lambda a: np.ascontiguousarray(np.asarray(a), dtype=np.float32)
    if "nc" not in _NC_CACHE:
        _NC_CACHE["nc"] = build_program()
    nc = _NC_CACHE["nc"]
    consts = _consts()
    pools = {nm: f(a).reshape(2560 * 8, 2048) for nm, a in
             (("kcmp", cache_k_cmp), ("vcmp", cache_v_cmp), ("kslc", cache_k_slc), ("vslc", cache_v_slc))}
    shared = dict(
        w_ada=f(w_ada)[0], b_ada=f(b_ada)[0].reshape(1, -1), norm_g=f(norm_g)[0].reshape(1, -1), w_in=f(w_in)[0],
        qg=f(q_norm_g)[0].reshape(1, -1), kg=f(k_norm_g)[0].reshape(1, -1), pek=f(cmp_pos_k)[0], pev=f(cmp_pos_v)[0],
        wck=f(w_cmp_k)[0], wcv=f(w_cmp_v)[0], vng=f(vnorm_g)[0].reshape(1, -1), vnb=f(vnorm_b)[0].reshape(1, -1),
        ws=f(w_s)[0], bs=f(b_s)[0], wbra=f(w_br_a)[0], wbrb=f(w_br_b)[0], wout=f(w_out)[0])
    shared.update(pools)
    shared.update(consts)
    xp, xs = f(x_prompt), f(x_sample)
    kw, vw = f(cache_k_win)[0], f(cache_v_win)[0]
    pt = np.asarray(page_table).astype(np.int32)
    cp, cs = f(c_prompt), f(c_sample)
    in_maps = []
    for c in range(8):
        sl = slice(c * NSMP, (c + 1) * NSMP)
        m = dict(shared)
        m["xp"] = xp[c]
        m["xs"] = np.ascontiguousarray(xs[sl, 0, :])
        m["cpv"] = np.ascontiguousarray(cp[c])
        m["csv"] = np.ascontiguousarray(cs[sl])
        m["kwin"] = np.ascontiguousarray(kw[sl].reshape(NSMP, 512, 128))
        m["vwin"] = np.ascontiguousarray(vw[sl].reshape(NSMP, 512, 128))
        m["ptrep"] = np.ascontiguousarray(np.repeat(pt[sl].T, 8, axis=0))
        in_maps.append(m)
    res = run_bass_kernel_spmd(nc, in_maps, core_ids=list(range(8)))
    R = res.results
    cat = lambda nm: np.stack([R[c][nm] for c in range(8)], axis=0)
    y_p = cat("yp")
    y_s = np.concatenate([R[c]["ys"] for c in range(8)], axis=0).reshape(128, 1, D)
    outs = [y_p, y_s]
    for nm in ("pk_cmp", "pv_cmp", "pk_slc", "pv_slc"):
        outs.append(cat(nm).reshape(1, 8, S, 2, 64))
    for nm in ("pk_win", "pv_win"):
        outs.append(cat(nm).reshape(1, 8, 512, 2, 64))
    for nm in ("sk_cmp", "sv_cmp", "sk_slc", "sv_slc"):
        outs.append(np.concatenate([R[c][nm] for c in range(8)], axis=0).reshape(1, 128, 1, 2, 64))
    for nm in ("sk_win", "sv_win"):
        outs.append(np.concatenate([R[c][nm] for c in range(8)], axis=0).reshape(1, 128, 512, 2, 64))
    outs.append(np.concatenate([R[c]["svch"] for c in range(8)], axis=0).reshape(1, 128, 1, 512))
    return tuple(o.astype(np.float32) for o in outs)
```

```python
import contextlib
import numpy as np
import concourse.bass as bass
import concourse.mybir as mybir
from concourse.bass_utils import run_bass_kernel_spmd

F32 = mybir.dt.float32
BF16 = mybir.dt.bfloat16
I32 = mybir.dt.int32
ALU = mybir.AluOpType
AF = mybir.ActivationFunctionType
AX = mybir.AxisListType

D = 1024
DIN = 5400
S = 2048
NT = 16
NSMP = 16
EPS = 1e-6
SCL = 0.125
C_Q, C_K, C_V, C_NSA, C_ZA, C_U, C_VV, C_ZB, C_GA, C_GB = 0, 512, 896, 1280, 1304, 1816, 2328, 2840, 3352, 4376


class _Op:
    __slots__ = ("eng", "fn", "deps", "idx", "signal", "is_dma", "sem", "target", "count")


class Prog:
    ENGS = ("pe", "act", "dve", "pool", "sp")
    DMA_POOL = {"sp": 16, "pool": 12, "act": 2}

    def __init__(self, nc, tag):
        self.nc = nc
        self.tag = tag
        self.q = {e: [] for e in self.ENGS}
        self.last_w = {}
        self.readers = {}
        self.dma_n = {e: 0 for e in self.DMA_POOL}
        self.out_dmas = []

    def add(self, eng, fn, r=(), w=(), dma=False, out=False):
        op = _Op()
        op.eng, op.fn, op.is_dma, op.signal = eng, fn, dma, False
        op.deps = set()
        op.sem = None
        op.target = 0
        op.count = 0
        for b in r:
            lw = self.last_w.get(b)
            if lw is not None:
                op.deps.add(lw)
        for b in w:
            lw = self.last_w.get(b)
            if lw is not None:
                op.deps.add(lw)
            for rd in self.readers.get(b, ()):
                op.deps.add(rd)
        for b in r:
            self.readers.setdefault(b, []).append(op)
        for b in w:
            self.last_w[b] = op
            self.readers[b] = []
        op.deps.discard(op)
        op.idx = len(self.q[eng])
        self.q[eng].append(op)
        if dma:
            j = self.dma_n[eng]
            self.dma_n[eng] += 1
            op.sem = (eng, j % self.DMA_POOL[eng])
            op.target = 16 * (j // self.DMA_POOL[eng] + 1)
            if out:
                self.out_dmas.append(op)
        return op

    def _needs_wait(self, op, dep):
        if dep.is_dma:
            return True
        if dep.eng == "pe" and op.eng == "pe" and not op.is_dma:
            return False
        return True

    def emit(self, es):
        nc = self.nc
        fin = self.add("sp", None)
        for o in self.out_dmas:
            fin.deps.add(o)
        for e in self.DMA_POOL:
            for op in self.q[e]:
                if op.is_dma:
                    fin.deps.add(op)
        for e in self.ENGS:
            for op in self.q[e]:
                for d in op.deps:
                    if (not d.is_dma) and self._needs_wait(op, d):
                        d.signal = True
        for e in self.ENGS:
            c = 0
            for op in self.q[e]:
                if (not op.is_dma) and op.signal:
                    c += 1
                    op.count = c
        esem = {e: es.enter_context(nc.semaphore("s%s_%s" % (self.tag, e))) for e in ("pe", "act", "dve", "pool")}
        dsem = {}
        for e, n in self.DMA_POOL.items():
            for i in range(n):
                dsem[(e, i)] = es.enter_context(nc.semaphore("d%s_%s_%d" % (self.tag, e, i)))
        prog = self
        with nc.Block() as block:
            def run_queue(ename, eng):
                waited = {}
                for op in prog.q[ename]:
                    waits = {}
                    for d in op.deps:
                        if not prog._needs_wait(op, d):
                            continue
                        if d.is_dma:
                            key, val = ("d", d.sem), d.target
                        else:
                            key, val = ("e", d.eng), d.count
                        if waits.get(key, 0) < val:
                            waits[key] = val
                    if op.is_dma and op.target > 16:
                        key = ("d", op.sem)
                        if waits.get(key, 0) < op.target - 16:
                            waits[key] = op.target - 16
                    for key, val in waits.items():
                        if waited.get(key, 0) >= val:
                            continue
                        waited[key] = val
                        s = dsem[key[1]] if key[0] == "d" else esem[key[1]]
                        eng.wait_ge(s, val)
                    if op.fn is None:
                        continue
                    ins = op.fn(eng)
                    if op.is_dma:
                        ins.then_inc(dsem[op.sem], 16)
                    elif op.signal:
                        ins.then_inc(esem[ename], 1)

            @block.sync
            def _(eng):
                run_queue("sp", eng)

            @block.tensor
            def _(eng):
                run_queue("pe", eng)

            @block.scalar
            def _(eng):
                run_queue("act", eng)

            @block.vector
            def _(eng):
                run_queue("dve", eng)

            @block.gpsimd
            def _(eng):
                run_queue("pool", eng)


class K:
    def __init__(self, P):
        self.P = P

    def mm(self, out, lhsT, rhs, start=True, stop=True, r=(), w=()):
        self.P.add("pe", lambda e: e.matmul(out, lhsT=lhsT, rhs=rhs, start=start, stop=stop, skip_group_check=True), r, w)

    def tr(self, out, in_, ident, r=(), w=()):
        self.P.add("pe", lambda e: e.transpose(out=out, in_=in_, identity=ident), r, w)

    def act(self, out, in_, func, r=(), w=(), scale=1.0, accum=None):
        if accum is None:
            self.P.add("act", lambda e: e.activation(out=out, in_=in_, func=func, scale=scale), r, w)
        else:
            self.P.add("act", lambda e: e.activation(out=out, in_=in_, func=func, scale=scale, accum_out=accum), r, w)

    def tt(self, eng, out, in0, in1, op, r=(), w=()):
        self.P.add(eng, lambda e: e.tensor_tensor(out=out, in0=in0, in1=in1, op=op), r, w)

    def ts(self, eng, out, in0, s1, s2, op0, op1=None, r=(), w=()):
        if op1 is None:
            self.P.add(eng, lambda e: e.tensor_scalar(out=out, in0=in0, scalar1=s1, scalar2=None, op0=op0), r, w)
        else:
            self.P.add(eng, lambda e: e.tensor_scalar(out=out, in0=in0, scalar1=s1, scalar2=s2, op0=op0, op1=op1), r, w)

    def stt(self, out, in0, scalar, in1, op0, op1, r=(), w=()):
        self.P.add("dve", lambda e: e.scalar_tensor_tensor(out=out, in0=in0, scalar=scalar, in1=in1, op0=op0, op1=op1), r, w)

    def cp(self, eng, out, in_, r=(), w=()):
        if eng == "act":
            self.P.add("act", lambda e: e.activation(out=out, in_=in_, func=AF.Copy), r, w)
        else:
            self.P.add(eng, lambda e: e.tensor_copy(out=out, in_=in_), r, w)

    def red(self, out, in_, op, r=(), w=()):
        self.P.add("dve", lambda e: e.tensor_reduce(out=out, in_=in_, axis=AX.X, op=op), r, w)

    def memset(self, eng, ap, val, w=()):
        self.P.add(eng, lambda e: e.memset(ap, val), (), w)

    def dma(self, q, out, in_, r=(), w=(), is_out=False, slow=False):
        if slow:
            self.P.add(q, lambda e: e.dma_start(out=out, in_=in_, allow_slow_non_contiguous=True), r, w, dma=True, out=is_out)
        else:
            self.P.add(q, lambda e: e.dma_start(out=out, in_=in_), r, w, dma=True, out=is_out)

    def gather(self, out, in_, idx, r=(), w=()):
        self.P.add("pool", lambda e: e.indirect_dma_start(out=out, out_offset=None, in_=in_,
                                                          in_offset=bass.IndirectOffsetOnAxis(ap=idx, axis=0)),
                   r, w, dma=True)


def v3(ap, a):
    return ap.rearrange("p (a b) -> p a b", a=a)


def bc(ap, shape):
    return ap.to_broadcast(shape)


def win_names(c0, n):
    return ["win%d" % j for j in range(c0 // 512, (c0 + n - 1) // 512 + 1)]


def front(k, t, n, x_src, cs_name, sfx, is_prompt):
    P = k.P
    N = slice(0, n)
    x, B1, B2, B3, h, hT = t["x"], t["B1"], t["B2"], t["B3"], t["h"], t["hT"]
    TH, ZB, vn_bf = B1[:, 0:512], B1[:, 512:1024], h[:, 0:512]
    tg, za_s, bmix, gw = t["tg" + sfx], t["za_s" + sfx], t["bmix" + sfx], t["gw" + sfx]
    n_tg, n_za, n_bm, n_gw = "tg" + sfx, "za_s" + sfx, "bmix" + sfx, "gw" + sfx
    B1W = ["B1a", "B1b"]
    st, st2, st3 = t["st"], t["st2"], t["st3"]
    pA, pB, pT = t["pA"], t["pB"], t["pT"]
    pTv = pT[:].rearrange("p (a b) -> p a b", a=8)
    k.dma("sp", x[N, :], x_src, w=["x"])
    k.act(B1[N, :], x[N, :], AF.Square, r=["x"], w=B1W + ["st"], accum=st[N, 0:1])
    k.ts("dve", st[N, 1:2], st[N, 0:1], 1.0 / D, EPS, ALU.mult, ALU.add, r=["st"], w=["st"])
    k.tt("pool", st[N, 2:3], st[N, 1:2], t["nh"][N, 0:1], ALU.pow, r=["st", "nh"], w=["st"])
    k.stt(B1[N, :], x[N, :], st[N, 2:3], t["sc1"][N, :], ALU.mult, ALU.mult, r=["x", "st", "sc1"], w=B1W)
    k.tt("dve", h[N, :], B1[N, :], t["sh"][N, :], ALU.add, r=B1W + ["sh"], w=["h"])
    yield
    for kc in range(8):
        k.tr(pTv[:, kc, 0:n], h[N, kc * 128:(kc + 1) * 128], t["identb"][N, N], r=["h", "identb"], w=["pT"])
    k.cp("act", hT[:, :, 0:n], pTv[:, :, 0:n], r=["pT"], w=["hT"])
    yield

    def proj(bank, bname, dst, c0, ncol):
        for kc in range(8):
            k.mm(bank[N, dst:dst + ncol], hT[:, kc, 0:n], t["Win"][:, kc, c0:c0 + ncol], start=(kc == 0), stop=(kc == 7),
                 r=["hT"] + win_names(c0, ncol), w=[bname])

    def normrope(bank, bname, A, Bd, gain, dst_ap, dst_name):
        H = A * Bd
        HW = H * 64

        def v4(ap):
            return ap.rearrange("p (a b d) -> p a b d", a=A, b=Bd)

        k.cp("act", B2[N, 0:HW], bank[N, 0:HW], r=[bname], w=["B2"])
        k.act(B3[N, 0:HW], B2[N, 0:HW], AF.Square, r=["B2"], w=["B3"])
        k.red(st2[N, 0:H], v3(B3[N, 0:HW], H), ALU.add, r=["B3"], w=["st2"])
        k.ts("dve", st2[N, 8:8 + H], st2[N, 0:H], 1.0 / 64, EPS, ALU.mult, ALU.add, r=["st2"], w=["st2"])
        k.tt("pool", st2[N, 16:16 + H], st2[N, 8:8 + H], t["nh"][N, 0:H], ALU.pow, r=["st2", "nh"], w=["st2"])
        b2 = v4(B2[N, 0:HW])
        rs = st2[N, 16:16 + H].rearrange("p (a b) -> p a b", a=A).unsqueeze(3)
        k.tt("dve", b2, b2, bc(rs, [n, A, Bd, 64]), ALU.mult, r=["B2", "st2"], w=["B2"])
        k.tt("dve", b2, b2, bc(gain[N, :].unsqueeze(1).unsqueeze(1), [n, A, Bd, 64]), ALU.mult, r=["B2", "gains"], w=["B2"])
        cs = t["cs"]
        cosb = bc(cs[N, 0:32].unsqueeze(1).unsqueeze(1), [n, A, Bd, 32])
        sinb = bc(cs[N, 32:64].unsqueeze(1).unsqueeze(1), [n, A, Bd, 32])
        x1, x2 = b2[:, :, :, 0:32], b2[:, :, :, 32:64]
        r1 = t["R1"][N, 0:H * 32].rearrange("p (a b d) -> p a b d", a=A, b=Bd)
        r2 = t["R2"][N, 0:H * 32].rearrange("p (a b d) -> p a b d", a=A, b=Bd)
        k.tt("dve", r1, x1, cosb, ALU.mult, r=["B2", cs_name], w=["R1"])
        k.tt("dve", r2, x2, sinb, ALU.mult, r=["B2", cs_name], w=["R2"])
        k.tt("dve", dst_ap[:, :, :, 0:32], r1, r2, ALU.subtract, r=["R1", "R2"], w=[dst_name])
        k.tt("dve", r1, x2, cosb, ALU.mult, r=["B2", cs_name], w=["R1"])
        k.tt("dve", r2, x1, sinb, ALU.mult, r=["B2", cs_name], w=["R2"])
        k.tt("dve", dst_ap[:, :, :, 32:64], r1, r2, ALU.add, r=["R1", "R2"], w=[dst_name])

    qdst = t["q_bf"][N, :].rearrange("p (r g d) -> p g r d", r=4, g=2)
    stages = []

    def st_q_post():
        normrope(pA, "pA", 2, 4, t["qg"], qdst, "q_bf")
    stages.append((lambda: proj(pA, "pA", 0, C_Q, 512), st_q_post))

    def st_k_post():
        normrope(pB, "pB", 6, 1, t["kg"], t["kf"][N, :].rearrange("p (a b d) -> p a b d", a=6, b=1), "kf")
        k.cp("act", t["k_bf"][N, :], t["kf"][N, :], r=["kf"], w=["k_bf"])
    stages.append((lambda: proj(pB, "pB", 0, C_K, 384), st_k_post))

    def st_v_pe():
        proj(pA, "pA", 0, C_V, 384)
        proj(pA, "pA", 384, C_NSA, 24)

    def st_v_post():
        k.cp("act", t["vf"][N, :], pA[N, 0:384], r=["pA"], w=["vf"])
        k.act(t["gwt"][N, :], pA[N, 384:408], AF.Tanh, r=["pA"], w=["gwt"], scale=0.5)
        k.ts("dve", gw[N, :], t["gwt"][N, :], 0.5, 0.5, ALU.mult, ALU.add, r=["gwt"], w=[n_gw])
    stages.append((st_v_pe, st_v_post))

    def st_za_post():
        k.act(TH[N, :], pB[N, :], AF.Tanh, r=["pB"], w=["B1a"], scale=0.5)
        k.stt(za_s[N, :], TH[N, :], 1.0, pB[N, :], ALU.add, ALU.mult, r=["B1a", "pB"], w=[n_za])
    stages.append((lambda: proj(pB, "pB", 0, C_ZA, 512), st_za_post))

    def st_ln_post():
        k.cp("act", t["vn_f"][N, :], pA[N, :], r=["pA"], w=["vn_f"])
        P.add("dve", lambda e: e.bn_stats(out=st3[N, 0:6], in_=t["vn_f"][N, :]), ["vn_f"], ["st3"])
        P.add("dve", lambda e: e.bn_aggr(out=st3[N, 6:8], in_=st3[N, 0:6]), ["st3"], ["st3"])
        k.ts("dve", st3[N, 8:9], st3[N, 7:8], EPS, None, ALU.add, r=["st3"], w=["st3"])
        k.tt("pool", st3[N, 9:10], st3[N, 8:9], t["nh"][N, 0:1], ALU.pow, r=["st3", "nh"], w=["st3"])
        k.ts("dve", t["vn_f"][N, :], t["vn_f"][N, :], st3[N, 6:7], st3[N, 9:10], ALU.subtract, ALU.mult, r=["vn_f", "st3"], w=["vn_f"])
        k.tt("dve", t["vn_f"][N, :], t["vn_f"][N, :], t["vng"][N, :], ALU.mult, r=["vn_f", "gains"], w=["vn_f"])
        k.tt("dve", t["vn_f"][N, :], t["vn_f"][N, :], t["vnb"][N, :], ALU.add, r=["vn_f", "gains"], w=["vn_f"])
    stages.append((lambda: proj(pA, "pA", 0, C_VV, 512), st_ln_post))

    def st_zb_post():
        k.act(TH[N, :], pB[N, :], AF.Tanh, r=["pB"], w=["B1a"], scale=0.5)
        k.stt(ZB[N, :], TH[N, :], 1.0, pB[N, :], ALU.add, ALU.mult, r=["B1a", "pB"], w=["B1b"])
    stages.append((lambda: proj(pB, "pB", 0, C_ZB, 512), st_zb_post))

    def st_u_post():
        k.tt("dve", ZB[N, :], pA[N, :], ZB[N, :], ALU.mult, r=["pA", "B1b"], w=["B1b"])
    stages.append((lambda: proj(pA, "pA", 0, C_U, 512), st_u_post))

    def st_sp_pe():
        if is_prompt:
            k.cp("act", vn_bf[N, :], t["vn_f"][N, :], r=["vn_f"], w=["h"])
            for g in range(4):
                k.mm(pB[N, g * 128:(g + 1) * 128], t["wsT"][:, g, :], vn_bf[N, g * 128:(g + 1) * 128],
                     r=["wsT", "h"], w=["pB"])

    def st_sp_post():
        if is_prompt:
            k.tt("dve", v3(B2[N, :], 4), v3(pB[N, :], 4), bc(t["bsT"][N, :].unsqueeze(2), [n, 4, 128]), ALU.add,
                 r=["pB", "bsT"], w=["B2"])
        else:
            k.tt("dve", v3(B2[N, :], 4), v3(t["vn_f"][N, :], 4), bc(t["w00"][N, :].unsqueeze(2), [n, 4, 128]), ALU.mult,
                 r=["vn_f", "w00"], w=["B2"])
            k.tt("dve", v3(B2[N, :], 4), v3(B2[N, :], 4), bc(t["b0"][N, :].unsqueeze(2), [n, 4, 128]), ALU.add,
                 r=["B2", "w00"], w=["B2"])
        k.tt("dve", bmix[N, :], B2[N, :], ZB[N, :], ALU.mult, r=["B2", "B1b"], w=[n_bm])
    stages.append((st_sp_pe, st_sp_post))

    gbanks = [(pA, "pA"), (pB, "pB")]
    for j in range(4):
        bank, bname = gbanks[j % 2]
        stages.append(((lambda bank=bank, bname=bname, j=j: proj(bank, bname, 0, C_GA + j * 512, 512)),
                       (lambda bank=bank, bname=bname, j=j: k.act(tg[N, j * 512:(j + 1) * 512], bank[N, :], AF.Tanh,
                                                                  r=[bname], w=[n_tg], scale=0.5))))
    stages[0][0]()
    yield
    for si in range(len(stages)):
        if si + 1 < len(stages):
            stages[si + 1][0]()
        stages[si][1]()
        yield


def tail(k, t, n, y_dst, x_src, sfx, gate=None):
    N = slice(0, n)
    pA, pB, pT = t["pA"], t["pB"], t["pT"]
    pTv8 = t["pM"][:].bitcast(BF16).rearrange("p (a b) -> p a b", a=8)
    tg, za_s, bmix = t["tg" + sfx], t["za_s" + sfx], t["bmix" + sfx]
    n_tg, n_za, n_bm = "tg" + sfx, "za_s" + sfx, "bmix" + sfx
    mc = t["mc"]
    B1W = ["B1a", "B1b"]
    ozd = t["oz"][N, :].rearrange("p (r g d) -> p g r d", r=4, g=2)
    k.tt("dve", ozd, t["o_a"][N, :].rearrange("p (g r d) -> p g r d", g=2, r=4),
         za_s[N, :].rearrange("p (g r d) -> p g r d", g=2, r=4), ALU.mult, r=["o_a", n_za], w=["oz"])
    for r in range(4):
        k.tr(pTv8[:, r, 0:n], t["oz"][N, r * 128:(r + 1) * 128], t["identb"][N, N], r=["oz", "identb"], w=["pM"])
    for c in range(4):
        k.tr(pTv8[:, 4 + c, 0:n], bmix[N, c * 128:(c + 1) * 128], t["identb"][N, N], r=[n_bm, "identb"], w=["pM"])
    k.cp("dve", t["ozT"][:, :, 0:n], pTv8[:, :, 0:n], r=["pM"], w=["ozT"])
    yield
    for half in range(2):
        cs = slice(half * 512, (half + 1) * 512)
        for r in range(4):
            k.mm(pA[N, :], t["ozT"][:, r, 0:n], t["WA"][:, r, cs], start=(r == 0), stop=(r == 3), r=["ozT", "WA"], w=["pA"])
        for c in range(4):
            k.mm(pB[N, :], t["ozT"][:, 4 + c, 0:n], t["WB"][:, c, cs], start=(c == 0), stop=(c == 3), r=["ozT", "WB"], w=["pB"])
        k.stt(t["B2"][N, :], tg[N, half * 512:(half + 1) * 512], 1.0, pA[N, :], ALU.add, ALU.mult,
              r=[n_tg, "pA"], w=["B2"])
        k.stt(t["B3"][N, :], tg[N, 1024 + half * 512:1024 + (half + 1) * 512], 1.0, pB[N, :], ALU.add, ALU.mult,
              r=[n_tg, "pB"], w=["B3"])
        k.tt("dve", mc[N, cs], t["B2"][N, :], t["B3"][N, :], ALU.add, r=["B2", "B3"], w=["mc"])
        yield
    for kc in range(8):
        k.tr(pTv8[:, kc, 0:n], mc[N, kc * 128:(kc + 1) * 128], t["identb"][N, N], r=["mc", "identb"], w=["pM"])
    k.cp("dve", t["hT"][:, :, 0:n], pTv8[:, :, 0:n], r=["pM"], w=["hT"])
    yield
    banks = [(pA, "pA"), (pB, "pB")]
    for half in range(2):
        bank, bname = banks[half]
        cs = slice(half * 512, (half + 1) * 512)
        for kc in range(8):
            k.mm(bank[N, :], t["hT"][:, kc, 0:n], t["Wout"][:, kc, cs], start=(kc == 0), stop=(kc == 7),
                 r=["hT", "Wout"], w=[bname])
        if gate is None:
            gap, gname = t["gq"][N, cs], "gq"
        else:
            gap, gname = gate[half][0][N, 0:512], gate[half][1]
        k.tt("dve", t["B1"][N, cs], bank[N, :], gap, ALU.mult, r=[bname, gname], w=["B1a" if half == 0 else "B1b"])
        yield
    k.dma("sp", t["x"][N, :], x_src, w=["x"])
    k.tt("dve", t["x"][N, :], t["B1"][N, :], t["x"][N, :], ALU.add, r=B1W + ["x"], w=["x"])
    k.dma("sp", y_dst, t["x"][N, :], r=["x"], is_out=True)


def mod_pass(k, t, d, n, cT, cT_name, bcast):
    N = slice(0, n)
    stg = [t["G0"], t["G1"]]
    banks = [(t["pA"], "pA"), (t["pB"], "pB")]
    wada = d["w_ada"].rearrange("(kc p) n -> p kc n", p=128)
    for j in range(12):
        sg, sname = stg[j % 2], "G%d" % (j % 2)
        bank, bname = banks[j % 2]
        sgv = sg[:].rearrange("p (kc n) -> p kc n", kc=8)
        k.dma("sp", sgv, wada[:, :, j * 256:(j + 1) * 256], w=[sname])
        k.dma("sp", t["bada"][0:1, :], d["b_ada"][0:1, j * 256:(j + 1) * 256], w=["bada"])
        if not bcast:
            for kc in range(8):
                k.mm(bank[N, 0:256], cT[:, kc, 0:n], sgv[:, kc, :], start=(kc == 0), stop=False, r=[cT_name, sname], w=[bname])
            k.mm(bank[N, 0:256], t["ones0"][:, 0:n], t["bada"][:, :], start=False, stop=True, r=["ones0", "bada"], w=[bname])
            src = bank[N, 0:256]
        else:
            for kc in range(8):
                k.mm(bank[0:1, 0:256], cT[:, kc:kc + 1], sgv[:, kc, :], start=(kc == 0), stop=False, r=[cT_name, sname], w=[bname])
            k.mm(bank[0:1, 0:256], t["ones0"][:, 0:1], t["bada"][:, :], start=False, stop=True, r=["ones0", "bada"], w=[bname])
            k.cp("act", t["modrow"][0:1, :], bank[0:1, 0:256], r=[bname], w=["modrow"])
            k.mm(bank[N, 256:512], t["onesf"][0:1, 0:n], t["modrow"][0:1, :], r=["onesf", "modrow"], w=[bname])
            src = bank[N, 256:512]
        cs = slice((j % 4) * 256, (j % 4 + 1) * 256)
        if j < 4:
            k.cp("act", t["sh"][N, cs], src, r=[bname], w=["sh"])
        elif j < 8:
            k.stt(t["sc1"][N, cs], src, 1.0, t["normg"][N, cs], ALU.add, ALU.mult, r=[bname, "normg"], w=["sc1"])
        else:
            k.act(t["gq"][N, cs], src, AF.Copy, r=[bname], w=["gq"], scale=0.25)


def build_program():
    nc = bass.Bass("TRN2", target_bir_lowering=False)
    d = {}

    def din(name, shape, dt=F32):
        d[name] = nc.dram_tensor(name, shape, dt, kind="ExternalInput").ap()

    def dout(name, shape):
        d[name] = nc.dram_tensor(name, shape, F32, kind="ExternalOutput").ap()

    din("xp", [S, D]); din("xs", [NSMP, D]); din("cpv", [D]); din("csv", [NSMP, D])
    for nm in ("kcmp", "vcmp", "kslc", "vslc"):
        din(nm, [2560 * 8, 2048])
    din("kwin", [NSMP, 512, 128]); din("vwin", [NSMP, 512, 128])
    din("ptrep", [128, NSMP], I32)
    din("w_ada", [D, 3 * D]); din("b_ada", [1, 3 * D]); din("norm_g", [1, D]); din("w_in", [D, DIN])
    din("qg", [1, 64]); din("kg", [1, 64]); din("pek", [32, 64]); din("pev", [32, 64])
    din("wck", [64, 64]); din("wcv", [64, 64]); din("vng", [1, 512]); din("vnb", [1, 512])
    din("ws", [4, 128, 128]); din("bs", [4, 128]); din("wbra", [512, D]); din("wbrb", [512, D]); din("wout", [D, D])
    din("rope", [S + 1, 64]); din("tri_le", [128, 128]); din("tri_gt", [128, 128]); din("identf_c", [128, 128])
    din("pool4", [128, 4]); din("pair", [64, 32]); din("e2", [64, 2048]); din("cmpbase", [128, 512]); din("trineg_le", [128, 512]); din("trineg_gt", [128, 512])
    din("impbias", [NT, 128, 32]); din("pool2", [128, 64]); din("e4", [32, 128]); din("rmod8", [128, 1])
    d["gqs"] = nc.dram_tensor("gqs", [NSMP, D], F32, kind="Internal").ap()
    dout("yp", [S, D]); dout("ys", [NSMP, D])
    for nm in ("pk_cmp", "pv_cmp", "pk_slc", "pv_slc"):
        dout(nm, [S, 128])
    dout("pk_win", [512, 128]); dout("pv_win", [512, 128])
    for nm in ("sk_cmp", "sv_cmp", "sk_slc", "sv_slc"):
        dout(nm, [NSMP, 128])
    dout("sk_win", [NSMP, 512, 128]); dout("sv_win", [NSMP, 512, 128]); dout("svch", [NSMP, 512])

    with contextlib.ExitStack() as es:
        t = {}

        def sb(name, shape, dt, scope=es):
            t[name] = scope.enter_context(nc.sbuf_tensor("sb_" + name, shape, dt))
            return t[name]

        def ps(name, shape, dt, scope=es):
            t[name] = scope.enter_context(nc.psum_tensor("ps_" + name, shape, dt))
            return t[name]

        sb("Win", [128, 8, DIN], BF16)
        sb("sc1", [128, D], F32); sb("sh", [128, D], F32); sb("gq", [128, D], F32)
        sb("identb", [128, 128], BF16); sb("identf", [128, 128], F32); sb("tri_le", [128, 128], BF16); sb("tri_gt", [128, 128], BF16)
        sb("qg", [128, 64], F32); sb("kg", [128, 64], F32); sb("vng", [128, 512], F32); sb("vnb", [128, 512], F32)
        sb("nh", [128, 8], F32); sb("onesf", [128, 128], F32)
        sb("wsT", [128, 4, 128], BF16); sb("bsT", [128, 4], F32)
        sb("Wbdk", [128, 128], BF16); sb("Wbdv", [128, 128], BF16); sb("pebar", [128, 2], F32)
        sb("pool4", [128, 4], BF16); sb("e2", [128, 2048], BF16)
        sb("x", [128, D], F32); sb("B1", [128, D], F32); sb("B2", [128, 512], F32); sb("B3", [128, 512], F32)
        sb("h", [128, D], BF16); sb("hT", [128, 8, 128], BF16); sb("R1", [128, 256], F32); sb("R2", [128, 256], F32)
        sb("st", [128, 4], F32); sb("st2", [128, 24], F32); sb("st3", [128, 12], F32)
        sb("q_bf", [128, 512], BF16); sb("kf", [128, 384], F32); sb("vf", [128, 384], F32); sb("k_bf", [128, 384], BF16)
        sb("gwt", [128, 24], F32); sb("gw0", [128, 24], F32); sb("gw1", [128, 24], F32); sb("za_s0", [128, 512], BF16); sb("za_s1", [128, 512], BF16)
        sb("vn_f", [128, 512], F32); sb("tg0", [128, 2048], BF16); sb("tg1", [128, 2048], BF16)
        sb("bmix0", [128, 512], BF16); sb("bmix1", [128, 512], BF16); sb("o_a", [128, 512], F32); sb("oz", [128, 512], BF16); sb("ozT", [128, 8, 128], BF16)
        sb("cs", [128, 64], F32); sb("mc", [128, D], BF16)
        sb("rden", [128, 8], F32); sb("cx", [128, 8], F32); t["tmpO"] = t["B3"]
        ps("pA", [128, 512], F32); ps("pB", [128, 512], F32); ps("pT", [128, 1024], BF16)
        ps("pM", [128, 512], F32); ps("pS0", [128, 512], F32); ps("pS1", [128, 512], F32)
        ps("pO0", [128, 512], F32); ps("pO1", [128, 512], F32)

        with contextlib.ExitStack() as s1:
            P = Prog(nc, "a")
            k = K(P)
            sb("normg", [128, D], F32, s1)
            sb("G0", [128, 2048], F32, s1); sb("G1", [128, 2048], F32, s1)
            sb("Gw0", [128, 512], F32, s1); sb("Gw1", [128, 512], F32, s1)
            sb("Gb0", [128, 2048], BF16, s1); sb("Gb1", [128, 2048], BF16, s1); sb("pool2b", [128, 64], BF16, s1); sb("Gwb0", [128, 512], BF16, s1); sb("Gwb1", [128, 512], BF16, s1)
            sb("PTb", [128, 160], BF16, s1); sb("vfb", [128, 256], BF16, s1); sb("Enewb", [128, 2 * NSMP * 8], BF16, s1)
            sb("bada", [128, 256], F32, s1); sb("modrow", [1, 256], F32, s1); sb("ones0", [128, 128], F32, s1)
            sb("cTs", [128, 8, NSMP], F32, s1)
            sb("cp8", [128, 8], F32, s1)
            sb("w00", [128, 4], F32, s1); sb("b0", [128, 4], F32, s1)
            sb("pe2", [32, 128], F32, s1); sb("o32", [32, 2], F32, s1)
            sb("ptrep", [128, NSMP], I32, s1); sb("rmod8", [128, 1], F32, s1); sb("idx", [128, NSMP], I32, s1)
            sb("pool2", [128, 64], F32, s1); sb("pairf", [64, 32], F32, s1); sb("e4", [32, 128], BF16, s1)
            sb("QTs", [128, 2, 4 * NSMP], BF16, s1); sb("enr", [NSMP, 16], F32, s1); sb("en", [NSMP, 16], F32, s1); sb("Enew", [128, 2 * NSMP * 8], F32, s1)
            sb("pTk", [128, 64], BF16, s1); sb("pTv", [128, 64], BF16, s1); sb("KcTs", [128, 64], BF16, s1); sb("Vcs", [64, 128], F32, s1)
            sb("PcT", [64, NSMP * 8], F32, s1); sb("KTs", [128, 20, 128], BF16, s1)
            sb("PTs", [128, 160], F32, s1); sb("PTsum", [128, 16], F32, s1)
            sb("rDc", [32, 128], F32, s1); sb("impn", [32, 128], F32, s1); sb("impT", [32, 32], F32, s1)
            sb("impS", [32, 32], F32, s1); sb("bias0", [32, 32], F32, s1); sb("m8s", [32, 8], F32, s1)
            sb("selS", [32, 32], BF16, s1); sb("selTs", [32, 32], BF16, s1); sb("Msk", [128, 32], F32, s1)
            t["OTn"] = t["B1"][:, 0:384]; t["rDall"] = t["B1"][:, 384:768]; t["OT1"] = t["B2"][0:64, 0:384]; t["wsl"] = t["B3"][:, 0:128]

            k.dma("pool", t["identb"][:], d["identf_c"], w=["identb"])
            k.dma("sp", t["identf"][:], d["identf_c"], w=["identf"])
            k.dma("pool", t["tri_le"][:], d["tri_le"], w=["tri_le"])
            k.dma("pool", t["tri_gt"][:], d["tri_gt"], w=["tri_gt"])
            k.dma("pool", t["pool4"][:], d["pool4"], w=["pool4"])
            k.memset("pool", t["e2"][64:128, :], 0.0, w=["e2"])
            k.dma("pool", t["e2"][0:64, :], d["e2"], w=["e2"])
            k.dma("pool", t["e4"][:], d["e4"], w=["e4"])
            k.dma("sp", t["pool2"][:], d["pool2"], w=["pool2"])
            k.dma("pool", t["pool2b"][:], d["pool2"], w=["pool2b"])
            k.dma("sp", t["pairf"][:], d["pair"], w=["pairf"])
            k.dma("sp", t["rmod8"][:], d["rmod8"], w=["rmod8"])
            k.dma("sp", t["ptrep"][:], d["ptrep"], w=["ptrep"])
            k.dma("sp", t["qg"][:], d["qg"].partition_broadcast(128), w=["gains"])
            k.dma("sp", t["kg"][:], d["kg"].partition_broadcast(128), w=["gains"])
            k.dma("sp", t["vng"][:], d["vng"].partition_broadcast(128), w=["gains"])
            k.dma("sp", t["vnb"][:], d["vnb"].partition_broadcast(128), w=["gains"])
            k.dma("sp", t["normg"][:], d["norm_g"].partition_broadcast(128), w=["normg"])
            k.dma("sp", t["bsT"][:], d["bs"].rearrange("g i -> i g"), w=["bsT"], slow=True)
            k.dma("sp", t["w00"][0:NSMP, :], d["ws"][:, 0, 0:1].rearrange("g o -> o g").partition_broadcast(NSMP), w=["w00"], slow=True)
            k.dma("sp", t["b0"][0:NSMP, :], d["bs"][:, 0:1].rearrange("g o -> o g").partition_broadcast(NSMP), w=["w00"], slow=True)
            k.memset("pool", t["nh"][:], -0.5, w=["nh"])
            k.memset("pool", t["Enew"][:], 0.0, w=["Enew"])
            k.memset("pool", t["bada"][:], 0.0, w=["bada"])
            k.memset("pool", t["ones0"][:], 0.0, w=["ones0"])
            k.memset("pool", t["ones0"][0:1, :], 1.0, w=["ones0"])
            k.memset("pool", t["QTs"][:], 0.0, w=["QTs"])
            k.memset("pool", t["vf"][:], 0.0, w=["vf"])
            k.memset("pool", t["Enewb"][:], 0.0, w=["Enewb"])
            k.memset("pool", t["onesf"][:], 1.0, w=["onesf"])
            k.memset("pool", t["o32"][:], 1.0 / 32, w=["o32"])
            k.memset("pool", t["Wbdk"][:], 0.0, w=["Wbdk"])
            k.memset("pool", t["Wbdv"][:], 0.0, w=["Wbdv"])
            k.memset("pool", t["bias0"][:], 0.0, w=["bias0"])
            k.memset("pool", t["bias0"][:, 0:1], 1.0e4, w=["bias0"])
            for g in range(2):
                gs = slice(g * 64, (g + 1) * 64)
                k.dma("pool", t["Wbdk"][gs, gs], d["wck"], w=["Wbdk"])
                k.dma("pool", t["Wbdv"][gs, gs], d["wcv"], w=["Wbdv"])
                k.dma("sp", t["pe2"][:, gs], d["pek"], w=["pe2k"])
            k.mm(t["pM"][:, 0:1], t["pe2"][:, :], t["o32"][:, 0:1], r=["pe2k", "o32"], w=["pM"])
            k.cp("dve", t["pebar"][:, 0:1], t["pM"][:, 0:1], r=["pM"], w=["pebar"])
            for g in range(2):
                gs = slice(g * 64, (g + 1) * 64)
                k.dma("sp", t["pe2"][:, gs], d["pev"], r=[], w=["pe2k"])
            k.mm(t["pM"][:, 0:1], t["pe2"][:, :], t["o32"][:, 0:1], r=["pe2k", "o32"], w=["pM"])
            k.cp("dve", t["pebar"][:, 1:2], t["pM"][:, 0:1], r=["pM"], w=["pebar"])
            for g in range(4):
                k.dma("sp", t["wsl"], d["ws"][g], w=["B3"])
                k.tr(t["pM"][:, 0:128], t["wsl"], t["identf"][:], r=["B3", "identf"], w=["pM"])
                k.tt("dve", t["wsT"][:, g, :], t["pM"][:, 0:128], t["tri_le"][:], ALU.mult, r=["pM", "tri_le"], w=["wsT"])
            winv = d["w_in"].rearrange("(kc p) n -> p kc n", p=128)
            for j in range(11):
                c0, c1 = j * 512, min(DIN, (j + 1) * 512)
                k.dma("pool", t["Win"][:, :, c0:c1], winv[:, :, c0:c1], w=["win%d" % j])
            k.dma("sp", t["cp8"][:], d["cpv"].rearrange("(kc p) -> p kc", p=128), w=["cp8"], slow=True)
            k.dma("sp", t["x"][0:NSMP, :], d["csv"], w=["x"])
            pMv = t["pM"][:, 0:8 * NSMP].rearrange("p (a b) -> p a b", a=8)
            for kc in range(8):
                k.tr(pMv[:, kc, :], t["x"][0:NSMP, kc * 128:(kc + 1) * 128], t["identf"][0:NSMP, 0:NSMP],
                     r=["x", "identf"], w=["pM"])
            k.cp("dve", t["cTs"][:], pMv, r=["pM"], w=["cTs"])
            mod_pass(k, t, d, NSMP, t["cTs"], "cTs", False)
            k.ts("dve", t["idx"][:], t["ptrep"][:], 8.0, t["rmod8"][:, 0:1], ALU.mult, ALU.add, r=["ptrep", "rmod8"], w=["idx"])
            k.dma("sp", t["cs"][0:NSMP, :], d["rope"][S:S + 1, :].partition_broadcast(NSMP), w=["cs"])
            for _ in front(k, t, NSMP, d["xs"], "cs", "0", False):
                pass
            n = NSMP
            N = slice(0, n)
            k.dma("sp", d["sk_cmp"], t["kf"][N, 0:128], r=["kf"], is_out=True)
            k.dma("sp", d["sk_slc"], t["kf"][N, 128:256], r=["kf"], is_out=True)
            k.dma("sp", d["sv_cmp"], t["vf"][N, 0:128], r=["vf"], is_out=True)
            k.dma("sp", d["sv_slc"], t["vf"][N, 128:256], r=["vf"], w=["d_svslc"], is_out=True)
            k.dma("sp", d["svch"], t["vn_f"][N, :], r=["vn_f"], is_out=True)
            k.dma("sp", d["sk_win"][:, 0:511, :], d["kwin"][:, 1:512, :], is_out=True)
            k.dma("sp", d["sv_win"][:, 0:511, :], d["vwin"][:, 1:512, :], is_out=True)
            k.dma("sp", d["sk_win"][:, 511, :], t["kf"][N, 256:384], r=["kf"], is_out=True)
            k.dma("sp", d["sv_win"][:, 511, :], t["vf"][N, 256:384], r=["vf"], w=["d_svwin"], is_out=True)
            pTv8 = t["pT"][:].rearrange("p (a b) -> p a b", a=8)
            for r in range(4):
                k.tr(pTv8[:, r, 0:n], t["q_bf"][N, r * 128:(r + 1) * 128], t["identb"][N, N], r=["q_bf", "identb"], w=["pT"])
            k.cp("act", t["QTs"][0:64, 0, :].rearrange("p (r s) -> p r s", r=4), pTv8[0:64, 0:4, 0:n], r=["pT"], w=["QTs"])
            k.cp("act", t["QTs"][64:128, 1, :].rearrange("p (r s) -> p r s", r=4), pTv8[64:128, 0:4, 0:n], r=["pT"], w=["QTs"])
            Env = t["Enew"][:].rearrange("p (x s h) -> p x s h", x=2, s=NSMP)
            Env16 = t["Enew"][0:NSMP, :].rearrange("p (x s h) -> p x s h", x=2, s=NSMP)
            for xx in range(2):
                kcol = t["k_bf"][N, 128 + xx * 128:256 + xx * 128].rearrange("p (g d) -> p g d", g=2).unsqueeze(2)
                k.tt("dve", t["B2"][N, :].rearrange("p (g r d) -> p g r d", g=2, r=4),
                     t["q_bf"][N, :].rearrange("p (r g d) -> p g r d", r=4, g=2), bc(kcol, [n, 2, 4, 64]), ALU.mult,
                     r=["q_bf", "k_bf"], w=["B2"])
                k.red(t["enr"][:, xx * 8:(xx + 1) * 8], v3(t["B2"][N, :], 8), ALU.add, r=["B2"], w=["enr"])
            k.act(t["en"][:], t["enr"][:], AF.Exp, r=["enr"], w=["en"], scale=SCL)
            for xx in range(2):
                k.tt("dve", Env16[:, xx, :, :], bc(t["identf"][N, N].unsqueeze(2), [n, NSMP, 8]),
                     bc(t["en"][:, xx * 8:(xx + 1) * 8].unsqueeze(1), [n, NSMP, 8]), ALU.mult, r=["identf", "en"], w=["Enew"])
            k.cp("act", t["vfb"][:], t["vf"][:, 128:384], r=["vf"], w=["vfb"])
            k.cp("act", t["Enewb"][0:NSMP, :], t["Enew"][0:NSMP, :], r=["Enew"], w=["Enewb"])
            Envb = t["Enewb"][:].rearrange("p (x s h) -> p x s h", x=2, s=NSMP)
            pS, pO, pD, pM = t["pS0"], t["pO0"], t["pO1"], t["pM"]
            pOv = pO[:, 0:384].rearrange("p (s x h) -> p s x h", s=NSMP, x=3)
            pDv = pD[:, 0:384].rearrange("p (s x h) -> p s x h", s=NSMP, x=3)
            PcTv = t["PcT"][:].rearrange("p (s h) -> p s h", s=NSMP)
            for s in range(NSMP):
                k.gather(t["G0"][:], d["kcmp"], t["idx"][:, s:s + 1], r=["idx"], w=["G0"])
                k.gather(t["G1"][:], d["vcmp"], t["idx"][:, s:s + 1], r=["idx"], w=["G1"])
                k.cp("act", t["Gb0"][:], t["G0"][:], r=["G0"], w=["Gb0"])
                k.cp("act", t["Gb1"][:], t["G1"][:], r=["G1"], w=["Gb1"])
                for tt_ in range(16):
                    k.mm(pM[:, 0:64], t["Gb0"][:, tt_ * 128:(tt_ + 1) * 128], t["pool2b"][:], start=(tt_ == 0), stop=(tt_ == 15),
                         r=["Gb0", "pool2b"], w=["pM"])
                for tt_ in range(16):
                    k.mm(pM[:, 64:128], t["Gb1"][:, tt_ * 128:(tt_ + 1) * 128], t["pool2b"][:], start=(tt_ == 0), stop=(tt_ == 15),
                         r=["Gb1", "pool2b"], w=["pM"])
                k.ts("dve", t["pTk"][:], pM[:, 0:64], t["pebar"][:, 0:1], None, ALU.add, r=["pM", "pebar"], w=["pTk"])
                k.ts("dve", t["pTv"][:], pM[:, 64:128], t["pebar"][:, 1:2], None, ALU.add, r=["pM", "pebar"], w=["pTv"])
                k.mm(pM[:, 128:192], t["Wbdk"][:], t["pTk"][:], r=["Wbdk", "pTk"], w=["pM"])
                k.mm(pM[0:64, 192:320], t["pTv"][:], t["Wbdv"][:], r=["Wbdv", "pTv"], w=["pM"])
                k.cp("act", t["KcTs"][:], pM[:, 128:192], r=["pM"], w=["KcTs"])
                k.cp("act", t["Vcs"][:], pM[0:64, 192:320], r=["pM"], w=["Vcs"])
                for g in range(2):
                    gs = slice(g * 64, (g + 1) * 64)
                    k.mm(pS[0:64, s * 8 + g * 4:s * 8 + g * 4 + 4], t["KcTs"][:, :],
                         t["QTs"][:, g, :].rearrange("p (r s) -> p r s", r=4)[:, :, s], r=["KcTs", "QTs"], w=["pS0"])
                k.act(PcTv[:, s, :], pS[0:64, s * 8:(s + 1) * 8], AF.Exp, r=["pS0"], w=["PcT"], scale=SCL)
                k.mm(pOv[:, s, 0, :], t["Vcs"][:], PcTv[:, s, :], r=["Vcs", "PcT"], w=["pO0"])
                k.mm(pDv[:, s, 0, :], t["onesf"][0:64, :], PcTv[:, s, :], r=["onesf", "PcT"], w=["pO1"])
                k.mm(pS[0:32, 128 + s * 8:128 + (s + 1) * 8], t["pairf"][:], PcTv[:, s, :], r=["pairf", "PcT"], w=["pS0"])
            k.P.add("dve", lambda e: e.reciprocal(out=t["rDc"][:].rearrange("p (s h) -> p s h", s=NSMP), in_=pDv[0:32, :, 0, :]),
                    ["pO1"], ["rDc"])
            k.tt("dve", t["impn"][:], pS[0:32, 128:256], t["rDc"][:], ALU.mult, r=["pS0", "rDc"], w=["impn"])
            k.red(t["impT"][:], t["impn"][:].rearrange("p (a r) -> p a r", r=4), ALU.add, r=["impn"], w=["impT"])
            k.tr(pS[0:32, 256:288], t["impT"][:], t["identf"][0:32, 0:32], r=["impT", "identf"], w=["pS0"])
            k.tt("dve", t["impS"][:], pS[0:32, 256:288], t["bias0"][:], ALU.add, r=["pS0", "bias0"], w=["impS"])
            k.P.add("dve", lambda e: e.max(out=t["m8s"][:], in_=t["impS"][:]), ["impS"], ["m8s"])
            k.ts("dve", t["selS"][:], t["impS"][:], t["m8s"][:, 6:7], None, ALU.is_ge, r=["impS", "m8s"], w=["selS"])
            k.tr(t["pT"][0:32, 0:32], t["selS"][:], t["identb"][0:32, 0:32], r=["selS", "identb"], w=["pT"])
            k.cp("act", t["selTs"][:], t["pT"][0:32, 0:32], r=["pT"], w=["selTs"])
            k.mm(pS[:, 320:352], t["e4"][:], t["selTs"][:], r=["e4", "selTs"], w=["pS0"])
            k.cp("act", t["Msk"][:], pS[:, 320:352], r=["pS0"], w=["Msk"])
            pS = t["pS1"]
            banks = [(t["pA"], "pA"), (t["pB"], "pB")]
            for s in range(NSMP):
                k.gather(t["G0"][:], d["kslc"], t["idx"][:, s:s + 1], r=["idx"], w=["G0"])
                k.gather(t["G1"][:], d["vslc"], t["idx"][:, s:s + 1], r=["idx"], w=["G1"])
                k.dma("sp", t["Gw0"][:].rearrange("p (a c) -> p a c", a=4), d["kwin"][s].rearrange("(p a) c -> p a c", a=4), w=["Gw0"])
                k.dma("sp", t["Gw1"][:].rearrange("p (a c) -> p a c", a=4), d["vwin"][s].rearrange("(p a) c -> p a c", a=4), w=["Gw1"])
                k.cp("act", t["Gb0"][:], t["G0"][:], r=["G0"], w=["Gb0"])
                k.cp("act", t["Gwb0"][:], t["Gw0"][:], r=["Gw0"], w=["Gwb0"])
                k.cp("act", t["Gb1"][:], t["G1"][:], r=["G1"], w=["Gb1"])
                k.cp("act", t["Gwb1"][:], t["Gw1"][:], r=["Gw1"], w=["Gwb1"])
                tbanks = [(t["pT"][:].rearrange("p (a c) -> p a c", a=8), "pT"),
                          (t["pA"][:].bitcast(BF16).rearrange("p (a c) -> p a c", a=8), "pA"),
                          (t["pB"][:].bitcast(BF16).rearrange("p (a c) -> p a c", a=8), "pB")]
                for q8 in range(3):
                    pTk8, tbn = tbanks[q8]
                    nt_ = 8 if q8 < 2 else 4
                    for a in range(nt_):
                        tix = q8 * 8 + a
                        if tix < 16:
                            src, sname, col = t["Gb0"], "Gb0", tix * 128
                        else:
                            src, sname, col = t["Gwb0"], "Gwb0", (tix - 16) * 128
                        k.tr(pTk8[:, a, :], src[:, col:col + 128], t["identb"][:], r=[sname, "identb"], w=[tbn])
                    k.cp("act", t["KTs"][:, q8 * 8:q8 * 8 + nt_, :], pTk8[:, 0:nt_, :], r=[tbn], w=["KTs"])
                for tt_ in range(20):
                    for g in range(2):
                        gs = slice(g * 64, (g + 1) * 64)
                        k.mm(pS[:, tt_ * 8 + g * 4:tt_ * 8 + g * 4 + 4], t["KTs"][:, tt_, :],
                             t["QTs"][:, g, :].rearrange("p (r s) -> p r s", r=4)[:, :, s], r=["KTs", "QTs"], w=["pS1"])
                k.act(t["PTs"][:], pS[:, 0:160], AF.Exp, r=["pS1"], w=["PTs"], scale=SCL)
                k.tt("dve", t["PTs"][:, 0:128].rearrange("p (a g r) -> p a g r", a=16, g=2),
                     t["PTs"][:, 0:128].rearrange("p (a g r) -> p a g r", a=16, g=2),
                     bc(t["Msk"][:, s * 2:(s + 1) * 2].unsqueeze(1).unsqueeze(3), [128, 16, 2, 4]), ALU.mult,
                     r=["PTs", "Msk"], w=["PTs"])
                k.memset("dve", t["PTs"][0:1, 128:136], 0.0, w=["PTs"])
                k.red(t["PTsum"][:, 0:8], t["PTs"][:, 0:128].rearrange("p (a h) -> p h a", a=16), ALU.add, r=["PTs"], w=["PTsum"])
                k.red(t["PTsum"][:, 8:16], t["PTs"][:, 128:160].rearrange("p (a h) -> p h a", a=4), ALU.add, r=["PTs"], w=["PTsum"])
                k.cp("act", t["PTb"][:], t["PTs"][:], r=["PTs"], w=["PTb"])
                for tt_ in range(16):
                    k.mm(pOv[:, s, 1, :], t["Gb1"][:, tt_ * 128:(tt_ + 1) * 128], t["PTb"][:, tt_ * 8:(tt_ + 1) * 8],
                         start=(tt_ == 0), stop=False, r=["Gb1", "PTb"], w=["pO0"])
                k.mm(pOv[:, s, 1, :], t["vfb"][:, 0:128], Envb[:, 0, s, :], start=False, stop=True,
                     r=["vfb", "Enewb"], w=["pO0"])
                for tt_ in range(4):
                    k.mm(pOv[:, s, 2, :], t["Gwb1"][:, tt_ * 128:(tt_ + 1) * 128], t["PTb"][:, 128 + tt_ * 8:128 + (tt_ + 1) * 8],
                         start=(tt_ == 0), stop=False, r=["Gwb1", "PTb"], w=["pO0"])
                k.mm(pOv[:, s, 2, :], t["vfb"][:, 128:256], Envb[:, 1, s, :], start=False, stop=True,
                     r=["vfb", "Enewb"], w=["pO0"])
                for xx in range(2):
                    k.mm(pDv[:, s, 1 + xx, :], t["onesf"][:, :], t["PTsum"][:, xx * 8:(xx + 1) * 8], start=True, stop=False,
                         r=["onesf", "PTsum"], w=["pO1"])
                    k.mm(pDv[:, s, 1 + xx, :], t["onesf"][:, :], Env[:, xx, s, :], start=False, stop=True,
                         r=["onesf", "Enew"], w=["pO1"])
            k.P.add("dve", lambda e: e.reciprocal(out=t["rDall"], in_=pD[:, 0:384]), ["pO1"], ["B1a", "B1b"])
            k.tt("dve", t["OTn"], pO[:, 0:384], t["rDall"], ALU.mult, r=["pO0", "B1a", "B1b"], w=["B1a", "B1b"])
            k.cp("dve", t["OT1"], t["OTn"][64:128, :], r=["B1a", "B1b"], w=["B2"])
            gwv = t["gw0"][N, :].rearrange("p (h x) -> p x h", x=3)
            for xx in range(3):
                bank, bname = banks[xx % 2]
                for hh in range(8):
                    src = t["OTn"] if hh < 4 else t["OT1"]
                    sname = "B1a" if hh < 4 else "B2"
                    inap = src[0:64, :].rearrange("p (s c) -> p s c", s=NSMP)[:, :, xx * 8 + hh]
                    k.tr(bank[0:n, hh * 64:(hh + 1) * 64], inap, t["identf"][0:64, 0:64], r=[sname, "identf"], w=[bname])
                if xx == 0:
                    k.tt("dve", v3(t["o_a"][N, :], 8), v3(bank[N, :], 8), bc(gwv[:, xx, :].unsqueeze(2), [n, 8, 64]), ALU.mult,
                         r=[bname, "gw0"], w=["o_a"])
                else:
                    k.tt("dve", v3(t["tmpO"][N, :], 8), v3(bank[N, :], 8), bc(gwv[:, xx, :].unsqueeze(2), [n, 8, 64]), ALU.mult,
                         r=[bname, "gw0"], w=["B3"])
                    k.tt("pool", t["o_a"][N, :], t["o_a"][N, :], t["tmpO"][N, :], ALU.add, r=["o_a", "B3"], w=["o_a"])
            k.dma("sp", d["gqs"], t["gq"][N, :], r=["gq"], w=["d_gqs"])
            mod_pass(k, t, d, 128, t["cp8"], "cp8", True)
            P.emit(es)

        with contextlib.ExitStack() as s2:
            P = Prog(nc, "b")
            k = K(P)
            sb("Wout", [128, 8, D], BF16, s2); sb("WA", [128, 4, D], BF16, s2); sb("WB", [128, 4, D], BF16, s2)
            k.dma("pool", t["Wout"][:], d["wout"].rearrange("(kc p) n -> p kc n", p=128), w=["Wout"])
            for g in range(2):
                k.dma("pool", t["WA"][g * 64:(g + 1) * 64, :, :],
                      d["wbra"][g * 256:(g + 1) * 256, :].rearrange("(r dd) n -> dd r n", dd=64), w=["WA"])
            k.dma("pool", t["WB"][:], d["wbrb"].rearrange("(c p) n -> p c n", p=128), w=["WB"])
            k.dma("sp", t["B1"][0:NSMP, :], d["gqs"], w=["B1a", "B1b"])
            for _ in tail(k, t, NSMP, d["ys"], d["xs"], "0", gate=[(t["B1"][:, 0:512], "B1a"), (t["B1"][:, 512:1024], "B1b")]):
                pass
            sb("KsT", [128, S], BF16, s2); sb("KwT", [128, 5 * 128], BF16, s2)
            sb("Vs", [128, NT, 2, 65], BF16, s2); sb("Vw", [128, 5, 2, 65], BF16, s2)
            sb("QT", [128, 2, 512], BF16, s2); sb("vcb", [128, 128], BF16, s2)
            sb("pTk2", [128, 64], BF16, s2); sb("pTv2", [128, 64], BF16, s2); sb("KcT", [128, 64], BF16, s2)
            sb("Vc", [64, 2, 97], BF16, s2)
            sb("PT0", [128, 512], BF16, s2); sb("PT1", [128, 512], BF16, s2)
            sb("nsel4", [128, 2, 512], BF16, s2); sb("imp", [128, 64], F32, s2)
            sb("sel", [128, 64], BF16, s2); sb("m8", [128, 16], F32, s2); sb("mctneg", [128, 512], BF16, s2); sb("ibias", [128, 32], F32, s2)
            sb("tnle", [128, 512], BF16, s2); sb("tngt", [128, 512], BF16, s2)
            k.dma("pool", t["tnle"][:], d["trineg_le"], w=["tnle"])
            k.dma("pool", t["tngt"][:], d["trineg_gt"], w=["tngt"])
            n = 128
            N = slice(0, 128)
            k.memset("pool", t["Vs"][:], 1.0, w=["Vs"])
            k.memset("pool", t["Vw"][:], 1.0, w=["Vw"])
            k.memset("pool", t["Vc"][:], 1.0, w=["Vc"])
            k.memset("pool", t["pTk2"][:], 0.0, w=["pTk2"])
            k.memset("pool", t["pTv2"][:], 0.0, w=["pTv2"])
            k.memset("pool", t["KcT"][:], 0.0, w=["KcT"])
            k.memset("pool", t["QT"][:], 0.0, w=["QT"])
            k.memset("pool", t["nsel4"][:], 0.0, w=["nsel4"])
            k.dma("pool", t["mctneg"][:], d["cmpbase"], w=["mctneg"])
            for g in range(2):
                k.dma("pool", t["Vc"][:, g, 65:97], d["pair"], w=["Vc"])
            pS = [(t["pS0"], "pS0"), (t["pS1"], "pS1")]
            pO = [(t["pO0"], "pO0"), (t["pO1"], "pO1")]
            PT = [(t["PT0"], "PT0"), (t["PT1"], "PT1")]
            pM = t["pM"]
            pTv8 = t["pT"][:].rearrange("p (a b) -> p a b", a=8)
            cnt = {"s": 0, "p": 0}
            cur = {}
            sfx_of = lambda ii: str((ii + 1) % 2)

            def branch_finish(xx):
                for g in range(2):
                    bank, bname = pO[g]
                    ov = bank[:, 0:388].rearrange("p (r c) -> p r c", r=4)
                    k.ts("dve", t["rden"][:, g * 4:(g + 1) * 4], ov[:, :, 64], 1e-30, None, ALU.max, r=[bname], w=["rden"])
                k.P.add("dve", lambda e: e.reciprocal(out=t["rden"][:], in_=t["rden"][:]), ["rden"], ["rden"])
                k.tt("dve", t["cx"][:], t["rden"][:], cur["gwv"][:, xx, :], ALU.mult, r=["rden", cur["gwn"]], w=["cx"])
                for g in range(2):
                    bank, bname = pO[g]
                    ov = bank[:, 0:388].rearrange("p (r c) -> p r c", r=4)
                    for r in range(4):
                        hs = slice((g * 4 + r) * 64, (g * 4 + r + 1) * 64)
                        k.stt(t["o_a"][:, hs], ov[:, r, 0:64], t["cx"][:, g * 4 + r:g * 4 + r + 1], t["o_a"][:, hs], ALU.mult, ALU.add,
                              r=[bname, "cx", "o_a"], w=["o_a"])

            k.dma("sp", t["cs"][:], d["rope"][0:128, :], w=["cs"])
            for _ in front(k, t, 128, d["xp"][0:128, :], "cs", sfx_of(0), True):
                pass
            tgen = {"g": None}

            def tail_hook(nit):
                if tgen["g"] is not None:
                    next(tgen["g"], None)

            for i in range(NT):
                rows = slice(i * 128, (i + 1) * 128)
                sfx = sfx_of(i)
                cur["gwn"] = "gw" + sfx
                cur["gwv"] = t["gw" + sfx][:, :].rearrange("p (h x) -> p x h", x=3)
                k.dma("sp", d["pk_cmp"][rows, :], t["kf"][:, 0:128], r=["kf"], is_out=True)
                k.dma("sp", d["pk_slc"][rows, :], t["kf"][:, 128:256], r=["kf"], is_out=True)
                k.dma("sp", d["pv_cmp"][rows, :], t["vf"][:, 0:128], r=["vf"], is_out=True)
                k.dma("sp", d["pv_slc"][rows, :], t["vf"][:, 128:256], r=["vf"], is_out=True)
                if i >= NT - 4:
                    wr = slice((i - (NT - 4)) * 128, (i - (NT - 4) + 1) * 128)
                    k.dma("sp", d["pk_win"][wr, :], t["kf"][:, 256:384], r=["kf"], is_out=True)
                    k.dma("sp", d["pv_win"][wr, :], t["vf"][:, 256:384], r=["vf"], is_out=True)
                slot = i % 5
                k.cp("pool", t["Vs"][:, i, :, 0:64], v3(t["vf"][:, 128:256], 2), r=["vf"], w=["Vs"])
                k.cp("pool", t["Vw"][:, slot, :, 0:64], v3(t["vf"][:, 256:384], 2), r=["vf"], w=["Vw"])
                k.cp("pool", t["vcb"][:], t["vf"][:, 0:128], r=["vf"], w=["vcb"])
                for r in range(4):
                    k.tr(pTv8[:, r, :], t["q_bf"][:, r * 128:(r + 1) * 128], t["identb"][:], r=["q_bf", "identb"], w=["pT"])
                k.tr(pTv8[:, 4, :], t["k_bf"][:, 128:256], t["identb"][:], r=["k_bf", "identb"], w=["pT"])
                k.tr(pTv8[:, 5, :], t["k_bf"][:, 256:384], t["identb"][:], r=["k_bf", "identb"], w=["pT"])
                k.cp("act", t["QT"][0:64, 0, :].rearrange("p (r q) -> p r q", r=4), pTv8[0:64, 0:4, :], r=["pT"], w=["QT"])
                k.cp("act", t["QT"][64:128, 1, :].rearrange("p (r q) -> p r q", r=4), pTv8[64:128, 0:4, :], r=["pT"], w=["QT"])
                k.cp("act", t["KsT"][:, rows], pTv8[:, 4, :], r=["pT"], w=["KsT"])
                k.cp("act", t["KwT"][:, slot * 128:(slot + 1) * 128], pTv8[:, 5, :], r=["pT"], w=["KwT"])
                cc = slice(4 * i, 4 * i + 4)
                k.mm(pM[:, 0:4], t["k_bf"][:, 0:128], t["pool4"][:], r=["k_bf", "pool4"], w=["pM"])
                k.mm(pM[:, 4:8], t["vcb"][:], t["pool4"][:], r=["vcb", "pool4"], w=["pM"])
                k.ts("dve", t["pTk2"][:, cc], pM[:, 0:4], t["pebar"][:, 0:1], None, ALU.add, r=["pM", "pebar"], w=["pTk2"])
                k.ts("dve", t["pTv2"][:, cc], pM[:, 4:8], t["pebar"][:, 1:2], None, ALU.add, r=["pM", "pebar"], w=["pTv2"])
                k.mm(pM[:, 8:12], t["Wbdk"][:], t["pTk2"][:, cc], r=["Wbdk", "pTk2"], w=["pM"])
                k.mm(pM[0:64, 16:144], t["pTv2"][:], t["Wbdv"][:], r=["Wbdv", "pTv2"], w=["pM"])
                k.cp("act", t["KcT"][:, cc], pM[:, 8:12], r=["pM"], w=["KcT"])
                k.cp("act", t["Vc"][:, :, 0:64], v3(pM[0:64, 16:144], 2), r=["pM"], w=["Vc"])

                k.dma("sp", t["ibias"][:], d["impbias"][i], w=["ibias"])
                qt = {g: t["QT"][:, g, :] for g in range(2)}

                def run_branch(items, hook=None):
                    recs = []

                    def s_stage(it):
                        g, kp, mms, v_ap, v_name, first, last = it
                        sbank, sname = pS[cnt["s"] % 2]; cnt["s"] += 1
                        pt, pname = PT[cnt["p"] % 2]; cnt["p"] += 1
                        for mi, (lh, rh, nm) in enumerate(mms):
                            k.mm(sbank[0:kp, :], lh, rh, start=(mi == 0), stop=(mi == len(mms) - 1), r=nm, w=[sname])
                        recs.append((sbank, sname, pt, pname))

                    def e_stage(idx):
                        g, kp, mms, v_ap, v_name, first, last = items[idx]
                        sbank, sname, pt, pname = recs[idx]
                        k.act(pt[0:kp, :], sbank[0:kp, :], AF.Exp, r=[sname], w=[pname], scale=SCL)
                        obank, oname = pO[g]
                        ov = obank[:, 0:388].rearrange("p (r c) -> p r c", r=4)
                        nv = v_ap.shape[-1]
                        for r in range(4):
                            k.mm(ov[:, r, 0:nv], pt[0:kp, r * 128:(r + 1) * 128], v_ap, start=(first and r == 0), stop=(last and r == 3),
                                 r=[pname, v_name], w=[oname])

                    s_stage(items[0])
                    for idx in range(len(items)):
                        if idx + 1 < len(items):
                            s_stage(items[idx + 1])
                        e_stage(idx)
                        if hook is not None:
                            hook(len(items))

                if tgen["g"] is not None and tgen.get("fresh"):
                    next(tgen["g"], None)
                    tgen["fresh"] = False
                items = []
                for g in range(2):
                    gs = slice(g * 64, (g + 1) * 64)
                    items.append((g, 64, [(t["KcT"][:, :], qt[g], ["KcT", "QT"]),
                                          (t["identb"][:, 64 - 4 * i:128 - 4 * i], t["mctneg"][:, :], ["identb", "mctneg"])],
                                  t["Vc"][:, g, :], "Vc", True, True))
                run_branch(items)
                for g in range(2):
                    bank, bname = pO[g]
                    ov = bank[:, 0:388].rearrange("p (r c) -> p r c", r=4)
                    k.ts("dve", t["rden"][:, g * 4:(g + 1) * 4], ov[:, :, 64], 1e-30, None, ALU.max, r=[bname], w=["rden"])
                k.P.add("dve", lambda e: e.reciprocal(out=t["rden"][:], in_=t["rden"][:]), ["rden"], ["rden"])
                for g in range(2):
                    bank, bname = pO[g]
                    ov = bank[:, 0:388].rearrange("p (r c) -> p r c", r=4)
                    ig = t["imp"][:, g * 32:(g + 1) * 32]
                    k.stt(ig, ov[:, 0, 65:97], t["rden"][:, g * 4:g * 4 + 1], t["ibias"][:], ALU.mult, ALU.add,
                          r=[bname, "rden", "ibias"], w=["imp"])
                    for r in range(1, 4):
                        k.stt(ig, ov[:, r, 65:97], t["rden"][:, g * 4 + r:g * 4 + r + 1], ig, ALU.mult, ALU.add,
                              r=[bname, "rden", "imp"], w=["imp"])
                k.tt("dve", t["cx"][:], t["rden"][:], cur["gwv"][:, 0, :], ALU.mult, r=["rden", cur["gwn"]], w=["cx"])
                for g in range(2):
                    bank, bname = pO[g]
                    ov = bank[:, 0:388].rearrange("p (r c) -> p r c", r=4)
                    k.tt("dve", v3(t["o_a"][:, g * 256:(g + 1) * 256], 4), ov[:, :, 0:64],
                         bc(t["cx"][:, g * 4:(g + 1) * 4].unsqueeze(2), [128, 4, 64]), ALU.mult, r=[bname, "cx"], w=["o_a"])
                for g in range(2):
                    ig = t["imp"][:, g * 32:(g + 1) * 32]
                    k.P.add("dve", (lambda g: lambda e: e.max(out=t["m8"][:, g * 8:(g + 1) * 8], in_=t["imp"][:, g * 32:(g + 1) * 32]))(g),
                            ["imp"], ["m8"])
                    k.ts("dve", t["sel"][:, g * 32:(g + 1) * 32], ig, t["m8"][:, g * 8 + 7:g * 8 + 8], None, ALU.is_ge,
                         r=["imp", "m8"], w=["sel"])
                gen = None
                if i + 1 < NT:
                    nrows = slice((i + 1) * 128, (i + 2) * 128)
                    k.dma("sp", t["cs"][:], d["rope"][nrows, :], w=["cs"])
                    gen = front(k, t, 128, d["xp"][nrows, :], "cs", sfx_of(i + 1), True)
                    next(gen, None)
                j0 = max(0, i - 4)
                items = []
                for g in range(2):
                    gs = slice(g * 64, (g + 1) * 64)
                    for j in range(j0, i + 1):
                        sl = j % 5
                        mms = [(t["KwT"][:, sl * 128:(sl + 1) * 128], qt[g], ["KwT", "QT"])]
                        if j == i:
                            mms.append((t["identb"][:, :], t["tnle"][:, :], ["identb", "tnle"]))
                        elif j == i - 4:
                            mms.append((t["identb"][:, :], t["tngt"][:, :], ["identb", "tngt"]))
                        items.append((g, 128, mms, t["Vw"][:, sl, g, :], "Vw", j == j0, j == i))
                run_branch(items, tail_hook)
                if tgen["g"] is not None:
                    for _ in tgen["g"]:
                        pass
                    tgen["g"] = None
                branch_finish(2)
                k.tr(t["pT"][0:64, 0:128], t["sel"][:], t["identb"][:], r=["sel", "identb"], w=["pT"])
                for g in range(2):
                    g32 = slice(g * 32, (g + 1) * 32)
                    k.ts("dve", v3(t["nsel4"][g32, g, :], 4), bc(t["pT"][g32, 0:128].unsqueeze(1), [32, 4, 128]), -1.0, 30000.0,
                         ALU.add, ALU.mult, r=["pT"], w=["nsel4"])
                items = []
                for g in range(2):
                    gs = slice(g * 64, (g + 1) * 64)
                    g32 = slice(g * 32, (g + 1) * 32)
                    for j in range(i + 1):
                        mms = [(t["KsT"][:, j * 128:(j + 1) * 128], qt[g], ["KsT", "QT"]),
                               (t["e2"][:, j * 128:(j + 1) * 128], t["nsel4"][:, g, :], ["e2", "nsel4"])]
                        if j == i:
                            mms.append((t["identb"][:, :], t["tnle"][:, :], ["identb", "tnle"]))
                        items.append((g, 128, mms, t["Vs"][:, j, g, :], "Vs", j == 0, j == i))
                acc = {"a": 0.0}

                def hook(nit):
                    if gen is None:
                        return
                    acc["a"] += 14.0 / nit
                    while acc["a"] >= 1.0:
                        acc["a"] -= 1.0
                        next(gen, None)

                run_branch(items, hook)
                if gen is not None:
                    for _ in gen:
                        pass
                branch_finish(1)
                tgen["g"] = tail(k, t, 128, d["yp"][rows, :], d["xp"][rows, :], sfx)
                tgen["fresh"] = True
            for _ in tgen["g"]:
                pass
            P.emit(es)
    return nc


def _consts():
    c = {}
    half = 32
    inv = (10000.0 ** (-np.arange(half, dtype=np.float32) * 2.0 / 64)).astype(np.float32)
    ang = np.arange(S + 1, dtype=np.float32)[:, None] * inv[None, :]
    c["rope"] = np.concatenate([np.cos(ang), np.sin(ang)], axis=1).astype(np.float32)
    kk = np.arange(128)[:, None]
    qq = np.arange(128)[None, :]
    c["tri_le"] = (kk <= qq).astype(np.float32)
    c["tri_gt"] = (kk > qq).astype(np.float32)
    c["identf_c"] = np.eye(128, dtype=np.float32)
    c["pool4"] = ((np.arange(128)[:, None] // 32) == np.arange(4)[None, :]).astype(np.float32) / 32.0
    c["pair"] = ((np.arange(64)[:, None] // 2) == np.arange(32)[None, :]).astype(np.float32)
    e = ((np.arange(2048)[None, :] // 64) == np.arange(32)[:, None]).astype(np.float32)
    c["e2"] = np.concatenate([e, e], axis=0)
    m = np.zeros((NT, 64, 128), np.float32)
    ib = np.zeros((NT, 128, 32), np.float32)
    for i in range(NT):
        pos = i * 128 + np.arange(128)
        cend = (np.arange(64) + 1) * 32 - 1
        m[i] = (cend[:, None] <= pos[None, :]).astype(np.float32)
        qblk = pos // 64
        blk = np.arange(32)
        forced = (blk[None, :] == 0) | (blk[None, :] == qblk[:, None])
        causal = blk[None, :] <= qblk[:, None]
        ib[i] = np.where(causal, 1.0e4 * forced, -1.0e30).astype(np.float32)
    base = np.zeros((128, 128), np.float32)
    base[68:, :] = -30000.0
    for u in range(64, 68):
        base[u, :] = np.where(np.arange(128) < 32 * (u - 64) + 31, -30000.0, 0.0)
    c["cmpbase"] = np.ascontiguousarray(np.tile(base[:, None, :], (1, 4, 1)).reshape(128, 512)).astype(np.float32)
    c["trineg_le"] = np.ascontiguousarray(np.tile(((1.0 - c["tri_le"]) * -30000.0)[:, None, :], (1, 4, 1)).reshape(128, 512)).astype(np.float32)
    c["trineg_gt"] = np.ascontiguousarray(np.tile(((1.0 - c["tri_gt"]) * -30000.0)[:, None, :], (1, 4, 1)).reshape(128, 512)).astype(np.float32)
    c["impbias"] = ib
    c["pool2"] = ((np.arange(128)[:, None] // 2) == np.arange(64)[None, :]).astype(np.float32) / 32.0
    c["e4"] = ((np.arange(128)[None, :] // 4) == np.arange(32)[:, None]).astype(np.float32)
    c["rmod8"] = (np.arange(128) % 8).astype(np.float32).reshape(128, 1)
    return c


_NC_CACHE = {}


def kernel(x_prompt, x_sample, cache_k_cmp, cache_v_cmp, cache_k_slc, cache_v_slc,
           cache_k_win, cache_v_win, page_table, c_prompt, c_sample,
           w_ada, b_ada, norm_g, w_in, q_norm_g, k_norm_g, cmp_pos_k, cmp_pos_v,
           w_cmp_k, w_cmp_v, vnorm_g, vnorm_b, w_s, b_s, w_br_a, w_br_b, w_out):
    f = lambda a: np.ascontiguousarray(np.asarray(a), dtype=np.float32)
    if "nc" not in _NC_CACHE:
        _NC_CACHE["nc"] = build_program()
    nc = _NC_CACHE["nc"]
    consts = _consts()
    pools = {nm: f(a).reshape(2560 * 8, 2048) for nm, a in
             (("kcmp", cache_k_cmp), ("vcmp", cache_v_cmp), ("kslc", cache_k_slc), ("vslc", cache_v_slc))}
    shared = dict(
        w_ada=f(w_ada)[0], b_ada=f(b_ada)[0].reshape(1, -1), norm_g=f(norm_g)[0].reshape(1, -1), w_in=f(w_in)[0],
        qg=f(q_norm_g)[0].reshape(1, -1), kg=f(k_norm_g)[0].reshape(1, -1), pek=f(cmp_pos_k)[0], pev=f(cmp_pos_v)[0],
        wck=f(w_cmp_k)[0], wcv=f(w_cmp_v)[0], vng=f(vnorm_g)[0].reshape(1, -1), vnb=f(vnorm_b)[0].reshape(1, -1),
        ws=f(w_s)[0], bs=f(b_s)[0], wbra=f(w_br_a)[0], wbrb=f(w_br_b)[0], wout=f(w_out)[0])
    shared.update(pools)
    shared.update(consts)
    xp, xs = f(x_prompt), f(x_sample)
    kw, vw = f(cache_k_win)[0], f(cache_v_win)[0]
    pt = np.asarray(page_table).astype(np.int32)
    cp, cs = f(c_prompt), f(c_sample)
    in_maps = []
    for c in range(8):
        sl = slice(c * NSMP, (c + 1) * NSMP)
        m = dict(shared)
        m["xp"] = xp[c]
        m["xs"] = np.ascontiguousarray(xs[sl, 0, :])
        m["cpv"] = np.ascontiguousarray(cp[c])
        m["csv"] = np.ascontiguousarray(cs[sl])
        m["kwin"] = np.ascontiguousarray(kw[sl].reshape(NSMP, 512, 128))
        m["vwin"] = np.ascontiguousarray(vw[sl].reshape(NSMP, 512, 128))
        m["ptrep"] = np.ascontiguousarray(np.repeat(pt[sl].T, 8, axis=0))
        in_maps.append(m)
    res = run_bass_kernel_spmd(nc, in_maps, core_ids=list(range(8)))
    R = res.results
    cat = lambda nm: np.stack([R[c][nm] for c in range(8)], axis=0)
    y_p = cat("yp")
    y_s = np.concatenate([R[c]["ys"] for c in range(8)], axis=0).reshape(128, 1, D)
    outs = [y_p, y_s]
    for nm in ("pk_cmp", "pv_cmp", "pk_slc", "pv_slc"):
        outs.append(cat(nm).reshape(1, 8, S, 2, 64))
    for nm in ("pk_win", "pv_win"):
        outs.append(cat(nm).reshape(1, 8, 512, 2, 64))
    for nm in ("sk_cmp", "sv_cmp", "sk_slc", "sv_slc"):
        outs.append(np.concatenate([R[c][nm] for c in range(8)], axis=0).reshape(1, 128, 1, 2, 64))
    for nm in ("sk_win", "sv_win"):
        outs.append(np.concatenate([R[c][nm] for c in range(8)], axis=0).reshape(1, 128, 512, 2, 64))
    outs.append(np.concatenate([R[c]["svch"] for c in range(8)], axis=0).reshape(1, 128, 1, 512))
    return tuple(o.astype(np.float32) for o in outs)
```
